# Optimizing a Trainium2 kernel written in Bass

```python
import math
import numpy as np
import jax
import jax.numpy as jnp
from jax import lax

D_MODEL = 1024
BATCH = 16
SEQ = 2048
DEPTH = 2
DEC_BATCH = 8
DEC_SEQ = 4096
PAST_LEN = 128

BRANCH_WIDTH = D_MODEL // 2
N_BRANCH = 3
N_DIR = 2
RMS_EPS = 1e-6
ATT_HEAD_DIM = 64
ATT_HEADS = BRANCH_WIDTH // ATT_HEAD_DIM
ATT_KV_HEADS = ATT_HEADS // 4
ATT_GROUP = ATT_HEADS // ATT_KV_HEADS
KV_WIDTH = ATT_KV_HEADS * ATT_HEAD_DIM
ROT_DIM = ATT_HEAD_DIM // 4
ROPE_THETA = 500000.0
WINDOW = 128
ATT_BLOCK = 128
MLSTM_HEADS = 4
MLSTM_HEAD_DIM = BRANCH_WIDTH // MLSTM_HEADS
MLSTM_CHUNK = 128
SSM_HEAD_DIM = 64
SSM_HEADS = BRANCH_WIDTH // SSM_HEAD_DIM
SSM_GROUPS = 2
SSM_HEADS_PER_GROUP = SSM_HEADS // SSM_GROUPS
SSM_STATE = 64
SSM_CONV = 5
SSM_CHUNK = 128
CONV_WIDTH = BRANCH_WIDTH + 2 * SSM_GROUPS * SSM_STATE
IN_SPLITS = (
    BRANCH_WIDTH, KV_WIDTH, KV_WIDTH, BRANCH_WIDTH,
    BRANCH_WIDTH, BRANCH_WIDTH, BRANCH_WIDTH, BRANCH_WIDTH,
    N_DIR * MLSTM_HEADS, N_DIR * MLSTM_HEADS, BRANCH_WIDTH,
    CONV_WIDTH, N_DIR * SSM_HEADS, BRANCH_WIDTH,
    N_BRANCH * D_MODEL,
)
IN_WIDTH = sum(IN_SPLITS)

kernel_name = "hybrid_bidir_attn_mlstm_ssd_encoder"


def rms_norm(x, g):
    xf = x.astype(jnp.float32)
    y = xf * lax.rsqrt(jnp.mean(xf * xf, axis=-1, keepdims=True) + RMS_EPS)
    return (y * g.astype(jnp.float32)).astype(x.dtype)


def partial_rope(x, pos):
    half = ROT_DIM // 2
    inv_freq = ROPE_THETA ** (-jnp.arange(half, dtype=jnp.float32) * 2.0 / ROT_DIM)
    ang = pos.astype(jnp.float32)[:, None] * inv_freq
    cos = jnp.cos(ang)[:, None, :].astype(x.dtype)
    sin = jnp.sin(ang)[:, None, :].astype(x.dtype)
    x1 = x[..., :half]
    x2 = x[..., half:ROT_DIM]
    return jnp.concatenate([x1 * cos - x2 * sin, x2 * cos + x1 * sin, x[..., ROT_DIM:]], axis=-1)


def window_attention(q, k, v, sink):
    bsz, s_len = q.shape[0], q.shape[1]
    w = ATT_BLOCK
    nb = s_len // w
    qb = q.reshape(bsz, nb, w, ATT_KV_HEADS, ATT_GROUP, ATT_HEAD_DIM)

    def key_windows(a):
        ap = jnp.pad(a, ((0, 0), (w, w), (0, 0), (0, 0))).reshape(bsz, nb + 2, w, ATT_KV_HEADS, ATT_HEAD_DIM)
        return jnp.concatenate([ap[:, :-2], ap[:, 1:-1], ap[:, 2:]], axis=2)

    kw = key_windows(k)
    vw = key_windows(v)
    s = jnp.einsum('bnqhgd,bnkhd->bnhgqk', qb, kw).astype(jnp.float32) * (ATT_HEAD_DIM ** -0.5)
    blk = jnp.arange(nb)[:, None] * w
    qpos = blk + jnp.arange(w)
    kpos = blk - w + jnp.arange(3 * w)
    valid = ((jnp.abs(qpos[:, :, None] - kpos[:, None, :]) <= WINDOW)
             & (kpos >= 0)[:, None, :] & (kpos < s_len)[:, None, :])
    s = jnp.where(valid[None, :, None, None], s, -jnp.inf)
    sk = sink.astype(jnp.float32).reshape(ATT_KV_HEADS, ATT_GROUP)[None, None, :, :, None, None]
    mx = jnp.maximum(jnp.max(s, axis=-1, keepdims=True), sk)
    p = jnp.exp(s - mx)
    p = p / (jnp.sum(p, axis=-1, keepdims=True) + jnp.exp(sk - mx))
    o = jnp.einsum('bnhgqk,bnkhd->bnqhgd', p.astype(v.dtype), vw)
    return o.reshape(bsz, s_len, ATT_HEADS * ATT_HEAD_DIM)


def mlstm_bidir(q, k, v, ig, fg):
    bsz, s_len, nh, dk = q.shape
    dv = v.shape[-1]
    L = MLSTM_CHUNK
    nc = s_len // L
    f32 = jnp.float32

    def both(a):
        return jnp.stack([a, jnp.flip(a, 1)])

    def per_dir(a):
        return jnp.stack([a[:, :, 0], jnp.flip(a[:, :, 1], 1)])

    def chunks(a):
        return a.reshape(N_DIR, bsz, nc, L, nh, -1).transpose(2, 0, 1, 4, 3, 5)

    qc = chunks(both(q.astype(f32)))
    kc = chunks(both(k.astype(f32))) * (dk ** -0.5)
    vc = chunks(both(v.astype(f32)))
    igc = chunks(per_dir(ig.astype(f32))[..., None])[..., 0]
    lfc = chunks(per_dir(jax.nn.log_sigmoid(fg.astype(f32)))[..., None])[..., 0]
    causal = jnp.tril(jnp.ones((L, L), dtype=bool))

    def step(carry, inp):
        c_st, n_st, m_st = carry
        qj, kj, vj, ij, lf = inp
        b = jnp.cumsum(lf, axis=-1)
        inter = b + m_st[..., None]
        log_d = jnp.where(causal, b[..., :, None] - b[..., None, :] + ij[..., None, :], -jnp.inf)
        m_t = jnp.maximum(inter, jnp.max(log_d, axis=-1))
        w_inter = jnp.exp(inter - m_t)
        sc = jnp.einsum('...td,...sd->...ts', qj, kj) * jnp.exp(log_d - m_t[..., None])
        num = jnp.einsum('...ts,...se->...te', sc, vj) + w_inter[..., None] * jnp.einsum('...td,...de->...te', qj, c_st)
        den = jnp.sum(sc, axis=-1) + w_inter * jnp.einsum('...td,...d->...t', qj, n_st)
        h = num / jnp.maximum(jnp.abs(den), jnp.exp(-m_t))[..., None]
        b_last = b[..., -1]
        g = b_last[..., None] - b + ij
        m_new = jnp.maximum(b_last + m_st, jnp.max(g, axis=-1))
        wk = jnp.exp(g - m_new[..., None])
        decay = jnp.exp(b_last + m_st - m_new)
        c_new = decay[..., None, None] * c_st + jnp.einsum('...sd,...se,...s->...de', kj, vj, wk)
        n_new = decay[..., None] * n_st + jnp.einsum('...sd,...s->...d', kj, wk)
        return (c_new, n_new, m_new), h

    init = (jnp.zeros((N_DIR, bsz, nh, dk, dv), f32),
            jnp.zeros((N_DIR, bsz, nh, dk), f32),
            jnp.zeros((N_DIR, bsz, nh), f32))
    _, h = lax.scan(step, init, (qc, kc, vc, igc, lfc))
    h = h.transpose(1, 2, 0, 4, 3, 5).reshape(N_DIR, bsz, s_len, nh, dv)
    return h[0] + jnp.flip(h[1], 1)


def centred_depthwise_conv(x, w, b):
    pad = SSM_CONV // 2
    y = lax.conv_general_dilated(x, w[:, None, :].astype(x.dtype), window_strides=(1,),
                                 padding=[(pad, pad)], dimension_numbers=('NWC', 'WIO', 'NWC'),
                                 feature_group_count=x.shape[-1])
    return y + b.astype(x.dtype)


def ssd_bidir(xs, bm, cm, dt, a, d_skip):
    bsz, s_len = xs.shape[0], xs.shape[1]
    L = SSM_CHUNK
    nc = s_len // L
    G, E, P, N = SSM_GROUPS, SSM_HEADS_PER_GROUP, SSM_HEAD_DIM, SSM_STATE
    Z = N_DIR * bsz
    f32 = jnp.float32

    def both(arr):
        return jnp.stack([arr, jnp.flip(arr, 1)])

    x2 = both(xs.astype(f32)).reshape(Z, nc, L, G, E, P)
    b2 = both(bm.astype(f32)).reshape(Z, nc, L, G, N)
    c2 = both(cm.astype(f32)).reshape(Z, nc, L, G, N)
    dtd = jnp.stack([dt[:, :, 0], jnp.flip(dt[:, :, 1], 1)])
    da = (dtd * a[:, None, None, :]).reshape(Z, nc, L, G, E)
    xdt = x2 * dtd.reshape(Z, nc, L, G, E)[..., None]
    acs = jnp.cumsum(da, axis=2)
    causal = jnp.tril(jnp.ones((L, L), dtype=bool))[:, :, None, None]
    lmat = jnp.exp(jnp.where(causal, acs[:, :, :, None] - acs[:, :, None, :], -jnp.inf))
    cb = jnp.einsum('zclgn,zcsgn->zclsg', c2, b2)
    y_diag = jnp.einsum('zclsg,zclsge,zcsgep->zclgep', cb, lmat, xdt)
    decay_states = jnp.exp(acs[:, :, -1:] - acs)
    states = jnp.einsum('zclgn,zclge,zclgep->zcgepn', b2, decay_states, xdt)
    chunk_decay = jnp.exp(acs[:, :, -1])

    def step(h, inp):
        st, dec = inp
        return dec[..., None, None] * h + st, h

    _, prev = lax.scan(step, jnp.zeros((Z, G, E, P, N), f32),
                       (states.transpose(1, 0, 2, 3, 4, 5), chunk_decay.transpose(1, 0, 2, 3)))
    prev = prev.transpose(1, 0, 2, 3, 4, 5)
    y_off = jnp.einsum('zclgn,zcgepn,zclge->zclgep', c2, prev, jnp.exp(acs))
    y = (y_diag + y_off).reshape(N_DIR, bsz, s_len, SSM_HEADS, P)
    return y[0] + jnp.flip(y[1], 1) + d_skip.astype(f32)[:, None] * xs.astype(f32)


def encoder_layer(x, norm_g, w_in, q_norm_g, k_norm_g, attn_sink, w_att_out,
                  mlstm_i_b, mlstm_f_b, mlstm_norm_g, w_mlstm_out,
                  conv_w, conv_b, a_log, dt_bias, d_skip, ssm_norm_g, w_ssm_out, w_out):
    bsz, s_len, _ = x.shape
    h = rms_norm(x, norm_g)
    proj = h @ w_in
    bounds = np.cumsum(IN_SPLITS)[:-1].tolist()
    (aq, ak, av, az, mq, mk, mv, mo, mi, mf, mz, sxbc, sdt, sz, gates) = jnp.split(proj, bounds, axis=-1)
    pos = jnp.arange(s_len)

    aq = partial_rope(rms_norm(aq.reshape(bsz, s_len, ATT_HEADS, ATT_HEAD_DIM), q_norm_g), pos)
    ak = partial_rope(rms_norm(ak.reshape(bsz, s_len, ATT_KV_HEADS, ATT_HEAD_DIM), k_norm_g), pos)
    av = av.reshape(bsz, s_len, ATT_KV_HEADS, ATT_HEAD_DIM)
    ya = window_attention(aq, ak, av, attn_sink) * jax.nn.silu(az)

    hm = mlstm_bidir(mq.reshape(bsz, s_len, MLSTM_HEADS, MLSTM_HEAD_DIM),
                     mk.reshape(bsz, s_len, MLSTM_HEADS, MLSTM_HEAD_DIM),
                     mv.reshape(bsz, s_len, MLSTM_HEADS, MLSTM_HEAD_DIM),
                     mi.reshape(bsz, s_len, N_DIR, MLSTM_HEADS) + mlstm_i_b,
                     mf.reshape(bsz, s_len, N_DIR, MLSTM_HEADS) + mlstm_f_b)
    hm = rms_norm(hm, mlstm_norm_g.reshape(MLSTM_HEADS, MLSTM_HEAD_DIM)).reshape(bsz, s_len, BRANCH_WIDTH).astype(x.dtype)
    yb = hm * jax.nn.sigmoid(mo) * jax.nn.silu(mz)

    xbc = jax.nn.silu(centred_depthwise_conv(sxbc, conv_w, conv_b))
    sx, sbm, scm = jnp.split(xbc, [BRANCH_WIDTH, BRANCH_WIDTH + SSM_GROUPS * SSM_STATE], axis=-1)
    dt = jax.nn.softplus(sdt.reshape(bsz, s_len, N_DIR, SSM_HEADS).astype(jnp.float32) + dt_bias.astype(jnp.float32))
    a = -jnp.exp(a_log.astype(jnp.float32))
    ys = ssd_bidir(sx.reshape(bsz, s_len, SSM_HEADS, SSM_HEAD_DIM),
                   sbm.reshape(bsz, s_len, SSM_GROUPS, SSM_STATE),
                   scm.reshape(bsz, s_len, SSM_GROUPS, SSM_STATE), dt, a, d_skip)
    ys = ys.reshape(bsz, s_len, BRANCH_WIDTH).astype(x.dtype)
    yc = rms_norm(ys * jax.nn.silu(sz), ssm_norm_g)

    ga, gb, gc = jnp.split(jax.nn.sigmoid(gates), N_BRANCH, axis=-1)
    merged = ga * (ya @ w_att_out) + gb * (yb @ w_mlstm_out) + gc * (yc @ w_ssm_out)
    return x + merged @ w_out


def setup_inputs(seed: int = 0) -> dict:
    key = jax.random.key(seed)
    ks = jax.random.split(key, 24)
    f32 = jnp.float32

    def nrm(k, shape, scale):
        return jax.random.normal(k, shape, f32) * scale

    x_prompt = jax.random.normal(ks[0], (BATCH, SEQ, D_MODEL), f32)
    x_sample = jax.random.normal(ks[1], (DEC_BATCH, DEC_SEQ, D_MODEL), f32)
    norm_g = 1.0 + nrm(ks[2], (DEPTH, D_MODEL), 0.02)
    w_in = nrm(ks[3], (DEPTH, D_MODEL, IN_WIDTH), D_MODEL ** -0.5)
    q_norm_g = 1.0 + nrm(ks[4], (DEPTH, ATT_HEAD_DIM), 0.02)
    k_norm_g = 1.0 + nrm(ks[5], (DEPTH, ATT_HEAD_DIM), 0.02)
    attn_sink = nrm(ks[6], (DEPTH, ATT_HEADS), 0.5)
    w_att_out = nrm(ks[7], (DEPTH, BRANCH_WIDTH, D_MODEL), BRANCH_WIDTH ** -0.5)
    mlstm_i_b = nrm(ks[8], (DEPTH, N_DIR, MLSTM_HEADS), 0.1)
    mlstm_f_b = jnp.linspace(3.0, 6.0, MLSTM_HEADS, dtype=f32) + nrm(ks[9], (DEPTH, N_DIR, MLSTM_HEADS), 0.1)
    mlstm_norm_g = 1.0 + nrm(ks[10], (DEPTH, BRANCH_WIDTH), 0.02)
    w_mlstm_out = nrm(ks[11], (DEPTH, BRANCH_WIDTH, D_MODEL), BRANCH_WIDTH ** -0.5)
    conv_w = nrm(ks[12], (DEPTH, SSM_CONV, CONV_WIDTH), SSM_CONV ** -0.5)
    conv_b = nrm(ks[13], (DEPTH, CONV_WIDTH), 0.02)
    a_log = jnp.log(jax.random.uniform(ks[14], (DEPTH, N_DIR, SSM_HEADS), f32, 1.0, 16.0))
    u = jax.random.uniform(ks[15], (DEPTH, N_DIR, SSM_HEADS), f32)
    dt0 = jnp.exp(u * (math.log(0.1) - math.log(0.001)) + math.log(0.001))
    dt_bias = dt0 + jnp.log(-jnp.expm1(-dt0))
    d_skip = 1.0 + nrm(ks[16], (DEPTH, SSM_HEADS), 0.1)
    ssm_norm_g = 1.0 + nrm(ks[17], (DEPTH, BRANCH_WIDTH), 0.02)
    w_ssm_out = nrm(ks[18], (DEPTH, BRANCH_WIDTH, D_MODEL), BRANCH_WIDTH ** -0.5)
    w_out = nrm(ks[19], (DEPTH, D_MODEL, D_MODEL), D_MODEL ** -0.5)
    return {"x_prompt": x_prompt, "x_sample": x_sample, "norm_g": norm_g, "w_in": w_in,
            "q_norm_g": q_norm_g, "k_norm_g": k_norm_g, "attn_sink": attn_sink, "w_att_out": w_att_out,
            "mlstm_i_b": mlstm_i_b, "mlstm_f_b": mlstm_f_b, "mlstm_norm_g": mlstm_norm_g,
            "w_mlstm_out": w_mlstm_out, "conv_w": conv_w, "conv_b": conv_b, "a_log": a_log,
            "dt_bias": dt_bias, "d_skip": d_skip, "ssm_norm_g": ssm_norm_g, "w_ssm_out": w_ssm_out,
            "w_out": w_out}


def reference(x_prompt, x_sample, norm_g, w_in, q_norm_g, k_norm_g, attn_sink, w_att_out,
              mlstm_i_b, mlstm_f_b, mlstm_norm_g, w_mlstm_out, conv_w, conv_b, a_log,
              dt_bias, d_skip, ssm_norm_g, w_ssm_out, w_out):
    y_prompt = x_prompt
    y_sample = x_sample
    for layer in range(DEPTH):
        params = (norm_g[layer], w_in[layer], q_norm_g[layer], k_norm_g[layer], attn_sink[layer],
                  w_att_out[layer], mlstm_i_b[layer], mlstm_f_b[layer], mlstm_norm_g[layer],
                  w_mlstm_out[layer], conv_w[layer], conv_b[layer], a_log[layer], dt_bias[layer],
                  d_skip[layer], ssm_norm_g[layer], w_ssm_out[layer], w_out[layer])
        y_prompt = encoder_layer(y_prompt, *params)
        y_sample = encoder_layer(y_sample, *params)
    return (y_prompt, y_sample)
```

```python
import contextlib
import math
import numpy as np
import concourse.bass as bass
import concourse.mybir as mybir
from concourse.bass_utils import run_bass_kernel_spmd

F32 = mybir.dt.float32
BF16 = mybir.dt.bfloat16
AF = mybir.ActivationFunctionType
ALU = mybir.AluOpType
AX = mybir.AxisListType

D = 1024
NCORES = 8
TM = 512
CH = 128
NPIECE = 25
ROPE_THETA = 500000.0
EPS = 1e-6


class Buf:
    __slots__ = ("name", "last_w", "readers")

    def __init__(self, name=""):
        self.name = name
        self.last_w = None
        self.readers = []


class Sched:
    ENGS = ("pe", "act", "dve", "pool", "sp")
    NPOOL = 4

    def __init__(self, nc, n_dma_sems=24):
        self.nc = nc
        self.nodes = []
        self.n_dma_sems = n_dma_sems
        self.order = None
        import os as _os2
        self.tagging = bool(_os2.environ.get('KTAG'))
        self.gaps = {}

    def _add(self, eng, fn, reads, writes, dma, cost):
        deps = set()
        for b in reads:
            if b.last_w is not None:
                deps.add(b.last_w)
        for b in writes:
            if b.last_w is not None:
                deps.add(b.last_w)
            deps.update(b.readers)
        gid = len(self.nodes)
        tag = ""
        if self.tagging:
            import sys as _sys
            f = _sys._getframe(2)
            names = []
            while f is not None and len(names) < 3:
                nm = f.f_code.co_name
                if nm not in ("op", "dma", "ACT", "TT", "TSC", "STT", "CP", "MM", "TR", "DMA", "SCAN", "RECIP", "<lambda>", "proj_feat", "proj_tok"):
                    names.append(nm)
                f = f.f_back
            tag = "/".join(names[:2])
        self.nodes.append(dict(eng=eng, fn=fn, dma=dma, deps=deps, cost=cost, tag=tag))
        for b in reads:
            b.readers.append(gid)
        for b in writes:
            b.last_w = gid
            b.readers = []
        return gid

    def op(self, eng, fn, reads=(), writes=(), cost=300.0):
        return self._add(eng, fn, reads, writes, False, cost)

    def dma(self, eng, fn, reads=(), writes=(), cost=3000.0):
        return self._add(eng, fn, reads, writes, True, cost)

    def schedule(self, window=24, enable=True):
        nodes = self.nodes
        per = {e: [] for e in self.ENGS}
        for g, n in enumerate(nodes):
            per[n["eng"]].append(g)
        if not enable:
            self.order = per
            return
        ptr = {e: 0 for e in self.ENGS}
        done = {e: [False] * len(per[e]) for e in self.ENGS}
        finish = [None] * len(nodes)
        t_eng = {e: 0.0 for e in self.ENGS}
        new = {e: [] for e in self.ENGS}
        remaining = len(nodes)
        LAT = 250.0
        W = {e: window for e in self.ENGS}
        W["sp"] = 6
        while remaining:
            best = None
            for e in self.ENGS:
                lst = per[e]
                p = ptr[e]
                while p < len(lst) and done[e][p]:
                    p += 1
                ptr[e] = p
                if p >= len(lst):
                    continue
                cnt = 0
                q = p
                cand = None
                while q < len(lst) and cnt < W[e]:
                    if not done[e][q]:
                        cnt += 1
                        g = lst[q]
                        nd = nodes[g]
                        ready = t_eng[e]
                        ok = True
                        for d in nd["deps"]:
                            f = finish[d]
                            if f is None:
                                ok = False
                                break
                            if nodes[d]["eng"] != e or nodes[d]["dma"]:
                                f += LAT
                            if f > ready:
                                ready = f
                        if ok:
                            key = (ready, q)
                            if cand is None or key < cand[0]:
                                cand = (key, q, g, ready)
                            if ready <= t_eng[e]:
                                break
                    q += 1
                if cand is not None:
                    if best is None or (cand[3], cand[2]) < (best[3], best[2]):
                        best = (e, cand[1], cand[2], cand[3])
            assert best is not None, "scheduler stuck"
            e, q, g, start = best
            nd = nodes[g]
            if self.tagging and e == "pe" and start > t_eng[e] + 500.0:
                dmax = max(nd["deps"], key=lambda d: finish[d])
                key = (nd["tag"], nodes[dmax]["eng"], nodes[dmax]["tag"])
                self.gaps[key] = self.gaps.get(key, 0.0) + (start - t_eng[e])
            if nd["dma"]:
                finish[g] = start + nd["cost"]
                t_eng[e] = start + 60.0
            else:
                finish[g] = start + nd["cost"]
                t_eng[e] = finish[g]
            done[e][q] = True
            new[e].append(g)
            remaining -= 1
        self.order = new
        self.est_time = max(f for f in finish if f is not None)
        self.busy = {e: sum(nodes[g]['cost'] for g in new[e] if not nodes[g]['dma']) for e in self.ENGS}

    def _plan(self):
        nodes = self.nodes
        pos = {}
        for e in self.ENGS:
            for i, g in enumerate(self.order[e]):
                pos[g] = i
        npool = self.NPOOL
        nsp = self.n_dma_sems - npool
        rr = {"sp": 0, "pool": 0}
        dma_val = [0] * self.n_dma_sems
        sem_of = {}
        for e in self.ENGS:
            for g in self.order[e]:
                if nodes[g]["dma"]:
                    if e == "pool":
                        s = nsp + rr["pool"]; rr["pool"] = (rr["pool"] + 1) % npool
                    else:
                        s = rr["sp"]; rr["sp"] = (rr["sp"] + 1) % nsp
                    prev = dma_val[s]
                    dma_val[s] += 16
                    sem_of[g] = (s, dma_val[s], prev)
        self.dma_val = dma_val
        flag = [False] * len(nodes)
        plan = {e: [] for e in self.ENGS}
        for e in self.ENGS:
            waited_c = {}
            waited_d = {}
            for g in self.order[e]:
                nd = nodes[g]
                waits = []
                deps = list(nd["deps"])
                for d in deps:
                    dn = nodes[d]
                    if dn["dma"]:
                        s, v, _ = sem_of[d]
                        if waited_d.get(s, 0) < v:
                            waited_d[s] = v
                            waits.append(("d", s, v))
                    else:
                        pe_ = dn["eng"]
                        if pe_ == e and e in ("pe", "sp"):
                            continue
                        if waited_c.get(pe_, -1) < pos[d]:
                            waited_c[pe_] = pos[d]
                            flag[d] = True
                            waits.append(("c", pe_, d))
                if nd["dma"]:
                    s, v, prev = sem_of[g]
                    if prev > 0 and waited_d.get(s, 0) < prev:
                        waited_d[s] = prev
                        waits.append(("d", s, prev))
                plan[e].append((g, waits))
        counts = {}
        for e in self.ENGS:
            c = 0
            for g in self.order[e]:
                if (not nodes[g]["dma"]) and flag[g]:
                    c += 1
                counts[g] = c
        self.plan, self.flag, self.counts, self.sem_of = plan, flag, counts, sem_of

    def check(self):
        nodes = self.nodes
        ptr = {e: 0 for e in self.ENGS}
        csem = {e: 0 for e in self.ENGS}
        dsem = [0] * self.n_dma_sems
        while True:
            prog = False
            for e in self.ENGS:
                pl = self.plan[e]
                while ptr[e] < len(pl):
                    g, waits = pl[ptr[e]]
                    ok = True
                    for w in waits:
                        if w[0] == "c":
                            if csem[w[1]] < self.counts[w[2]]:
                                ok = False
                        elif dsem[w[1]] < w[2]:
                            ok = False
                    if not ok:
                        break
                    if nodes[g]["dma"]:
                        dsem[self.sem_of[g][0]] += 16
                    elif self.flag[g]:
                        csem[e] += 1
                    ptr[e] += 1
                    prog = True
            if all(ptr[e] == len(self.plan[e]) for e in self.ENGS):
                return True
            if not prog:
                for e in self.ENGS:
                    if ptr[e] < len(self.plan[e]):
                        print("STUCK", e, ptr[e], "/", len(self.plan[e]), self.plan[e][ptr[e]][1])
                return False

    def emit(self, final_wait_eng="sp"):
        nc = self.nc
        nodes = self.nodes
        with contextlib.ExitStack() as st:
            csem = {e: st.enter_context(nc.semaphore("cs_" + e)) for e in self.ENGS}
            dsem = [st.enter_context(nc.semaphore("ds_%d" % i)) for i in range(self.n_dma_sems)]
            final = [(i, v) for i, v in enumerate(self.dma_val) if v > 0]
            block = st.enter_context(nc.Block())
            engobj = {"pe": "tensor", "act": "scalar", "dve": "vector", "pool": "gpsimd", "sp": "sync"}

            def make(e):
                def body(eng):
                    for g, waits in self.plan[e]:
                        for w in waits:
                            if w[0] == "c":
                                eng.wait_ge(csem[w[1]], self.counts[w[2]])
                            else:
                                eng.wait_ge(dsem[w[1]], w[2])
                        ins = nodes[g]["fn"](eng)
                        if nodes[g]["dma"]:
                            ins.then_inc(dsem[self.sem_of[g][0]], 16)
                        elif self.flag[g]:
                            ins.then_inc(csem[e], 1)
                    if e == final_wait_eng:
                        for (i, v) in final:
                            eng.wait_ge(dsem[i], v)
                return body

            for e in self.ENGS:
                if self.plan[e] or e == final_wait_eng:
                    getattr(block, engobj[e])(make(e))


class Tl:
    __slots__ = ("t", "b")

    def __init__(self, t, name=""):
        self.t = t
        self.b = Buf(name)


def make_consts(smax):
    c = {}
    c["c_ident"] = np.eye(128, dtype=np.float32)
    s = np.arange(128)[:, None]
    t = np.arange(128)[None, :]
    tri = np.zeros((128, 2, 512), np.float32)
    tri[:, 0, :] = np.tile((s <= t).astype(np.float32), (1, 4))
    tri[:, 1, :] = np.tile((s >= t).astype(np.float32), (1, 4))
    c["c_tri"] = tri
    f = np.arange(128)
    d = f % 64
    pos = np.arange(smax, dtype=np.float32)
    rope = np.zeros((128, 2, smax), np.float32)
    rope[:, 0, :] = 1.0
    inv_freq = (ROPE_THETA ** (-np.arange(8, dtype=np.float32) * 2.0 / 16.0)).astype(np.float32)
    for ff in range(128):
        if d[ff] < 16:
            ang = pos * inv_freq[d[ff] % 8]
            rope[ff, 0, :] = np.cos(ang)
            rope[ff, 1, :] = np.sin(ang)
    c["c_rope"] = rope
    rot = np.zeros((128, 128), np.float32)
    for ff in range(128):
        if d[ff] < 8:
            rot[ff + 8, ff] = -1.0
        elif d[ff] < 16:
            rot[ff - 8, ff] = 1.0
    c["c_rot"] = rot
    blk = np.zeros((128, 128), np.float32)
    blk[:64, :64] = 1.0 / 64
    blk[64:, 64:] = 1.0 / 64
    c["c_blk"] = blk
    sel = np.zeros((8, 8, 128), np.float32)
    for h in range(8):
        sel[h, h, :] = 1.0
    c["c_sel"] = sel
    rst = np.ones((8, 512), np.float32)
    rst[:, ::128] = 0.0
    c["c_reset"] = rst
    return c


O_AQ, O_AK, O_AV, O_AZ = 0, 512, 640, 768
O_MQ, O_MK, O_MV, O_MO = 1280, 1792, 2304, 2816
O_MI, O_MF, O_MZ = 3328, 3336, 3344
O_SX, O_SB, O_SC, O_SDT, O_SZ = 3856, 4368, 4496, 4624, 4640
O_G = 5152
P_AQ, P_KV, P_AZ, P_MQ, P_MK, P_MV, P_MO, P_MZ, P_SX, P_SBC, P_SZ = range(11)
P_G0 = 11
P_WO0, P_WO1 = 23, 24


def build(seq_lens, depth, enable=("att", "mlstm", "ssd"), debug=()):
    ntok = sum(seq_lens)
    smax = max(seq_lens)
    nc = bass.Bass("TRN2", target_bir_lowering=False)
    S = Sched(nc)
    es = contextlib.ExitStack()

    def din(name, shape, dt=F32):
        return nc.dram_tensor(name, list(shape), dt, kind="ExternalInput").ap()

    x_in = din("x", [ntok, D])
    y_out = nc.dram_tensor("y", [ntok, D], F32, kind="ExternalOutput").ap()
    W = dict(
        norm_g=din("norm_g", [depth, D]), w_in=din("w_in", [depth, D, 8224]),
        q_norm_g=din("q_norm_g", [depth, 64]), k_norm_g=din("k_norm_g", [depth, 64]),
        attn_sink=din("attn_sink", [depth, 8]), w_att_out=din("w_att_out", [depth, 512, D]),
        mlstm_i_b=din("mlstm_i_b", [depth, 2, 4]), mlstm_f_b=din("mlstm_f_b", [depth, 2, 4]),
        mlstm_norm_g=din("mlstm_norm_g", [depth, 512]), w_mlstm_out=din("w_mlstm_out", [depth, 512, D]),
        conv_w=din("conv_w", [depth, 5, 768]), conv_b=din("conv_b", [depth, 768]),
        a_log=din("a_log", [depth, 2, 8]), dt_bias=din("dt_bias", [depth, 2, 8]),
        d_skip=din("d_skip", [depth, 8]), ssm_norm_g=din("ssm_norm_g", [depth, 512]),
        w_ssm_out=din("w_ssm_out", [depth, 512, D]), w_out=din("w_out", [depth, D, D]),
    )
    C = dict(c_ident=din("c_ident", [128, 128]), c_tri=din("c_tri", [128, 2, 512]),
             c_rope=din("c_rope", [128, 2, smax]), c_rot=din("c_rot", [128, 128]),
             c_blk=din("c_blk", [128, 128]), c_sel=din("c_sel", [8, 8, 128]),
             c_reset=din("c_reset", [8, 512]))
    WS = nc.dram_tensor("ws_bf16", [depth, NPIECE, 128, 4096], BF16, kind="Internal").ap()
    ws_buf = [[[] for p in range(NPIECE)] for l in range(depth)]
    nscr = max(1, min(2, depth - 1))
    YS = [nc.dram_tensor("yscr%d" % i, [ntok, D], F32, kind="Internal").ap() for i in range(nscr)]
    ys_buf = [[[Buf("yscr"), Buf("yscr")] for c in range(ntok // CH)] for i in range(nscr)]
    nchmax = smax // CH
    BSTM = nc.dram_tensor("bst_m", [nchmax, 128, 516], BF16, kind="Internal").ap()
    BSTS = nc.dram_tensor("bst_s", [nchmax, 128, 256], BF16, kind="Internal").ap()
    bstm_buf = [Buf("bstm%d" % i) for i in range(nchmax)]
    bsts_buf = [Buf("bsts%d" % i) for i in range(nchmax)]

    dbg_done = {}
    import os
    KSKIP = set(os.environ.get("KSKIP", "").split(","))

    def dbg(name, ap, bufs, dt=F32):
        if name not in debug or name in dbg_done:
            return
        dbg_done[name] = True
        shp = list(ap.shape)
        o = nc.dram_tensor("dbg_" + name, shp, dt, kind="ExternalOutput").ap()
        S.dma("sp", lambda e: e.dma_start(out=o, in_=ap), bufs, [])

    def sb(name, shape, dt=F32):
        return Tl(es.enter_context(nc.sbuf_tensor(name, list(shape), dt)), name)

    def psum(name, shape, dt=F32):
        return Tl(es.enter_context(nc.psum_tensor(name, list(shape), dt)), name)

    def fsz(ap):
        n = 1
        for s_ in ap.shape[1:]:
            n *= s_
        return n

    def ecost(eng, ap, mult=1.0):
        n = fsz(ap)
        if eng == "act":
            return 220.0 + 0.85 * n
        if eng == "dve":
            return 60.0 + 1.3 * n * mult
        return 100.0 + 2.6 * n * mult

    def ACT(out, in_, func, rd, wr, bias=None, scale=None, accum=None):
        kw = {}
        if bias is not None:
            kw["bias"] = bias
        if scale is not None:
            kw["scale"] = scale
        if accum is not None:
            kw["accum_out"] = accum
        S.op("act", lambda e: e.activation(out=out, in_=in_, func=func, **kw), rd, wr, cost=ecost("act", out))

    def TT(eng, out, in0, in1, op, rd, wr):
        S.op(eng, lambda e: e.tensor_tensor(out=out, in0=in0, in1=in1, op=op), rd, wr, cost=ecost(eng, out))

    def TSC(eng, out, in0, s1, op0, rd, wr, s2=None, op1=None):
        if op1 is None:
            S.op(eng, lambda e: e.tensor_scalar(out=out, in0=in0, scalar1=s1, scalar2=None, op0=op0), rd, wr, cost=ecost(eng, out))
        else:
            S.op(eng, lambda e: e.tensor_scalar(out=out, in0=in0, scalar1=s1, scalar2=s2, op0=op0, op1=op1), rd, wr, cost=ecost(eng, out))

    def STT(eng, out, in0, scalar, in1, op0, op1, rd, wr):
        S.op(eng, lambda e: e.scalar_tensor_tensor(out=out, in0=in0, scalar=scalar, in1=in1, op0=op0, op1=op1), rd, wr,
             cost=ecost(eng, out))

    def CP(eng, out, in_, rd, wr):
        if eng == "act":
            S.op("act", lambda e: e.copy(out=out, in_=in_), rd, wr, cost=ecost("act", out))
        else:
            S.op(eng, lambda e: e.tensor_copy(out=out, in_=in_), rd, wr, cost=ecost(eng, out, 1.4 if eng == "pool" else 1.0))

    def RECIP(out, in_, rd, wr):
        S.op("dve", lambda e: e.reciprocal(out=out, in_=in_), rd, wr, cost=100.0 + 6.6 * fsz(out))

    def MM(out, lhsT, rhs, start, stop, rd, wr, skip=False):
        n = max(fsz(out), 32)
        passes = 4 if lhsT.dtype == F32 else 1
        c = 25.0 + 0.5 * n * passes
        if skip:
            S.op("pe", lambda e: e.matmul(out, lhsT, rhs, start=start, stop=stop, skip_group_check=True), rd, wr, cost=c)
        else:
            S.op("pe", lambda e: e.matmul(out, lhsT, rhs, start=start, stop=stop), rd, wr, cost=c)

    def TR(out, in_, ident, rd, wr):
        S.op("pe", lambda e: e.transpose(out, in_, ident), rd, wr, cost=90.0)

    def DMA(out, in_, rd, wr, eng="sp", slow=False):
        nbytes = out.shape[0] * fsz(out) * (2 if out.dtype == BF16 else 4)
        c = 2500.0 + nbytes / (40.0 if eng == "pool" else 120.0)
        if slow:
            S.dma(eng, lambda e: e.dma_start(out=out, in_=in_, allow_slow_non_contiguous=True), rd, wr, cost=c)
        else:
            S.dma(eng, lambda e: e.dma_start(out=out, in_=in_), rd, wr, cost=c)

    def SCAN(out, d0, d1, init, op0, op1, rd, wr):
        S.op("dve", lambda e: e.tensor_tensor_scan(out=out, data0=d0, data1=d1, initial=init, op0=op0, op1=op1), rd, wr,
             cost=100.0 + 2.0 * fsz(out))

    class Ring:
        def __init__(self, tiles):
            self.tiles = tiles
            self.i = 0

        def get(self):
            t = self.tiles[self.i]
            self.i = (self.i + 1) % len(self.tiles)
            return t

    with es:
        f32r = Ring([sb("f32r%d" % i, [128, 512]) for i in range(7)])
        b16r = Ring([sb("b16r%d" % i, [128, 512], BF16) for i in range(8)])
        identf = sb("identf", [128, 128])
        identb = sb("identb", [128, 128], BF16)
        trib = sb("trib", [128, 2, 512], BF16)
        rotb = sb("rotb", [128, 128], BF16)
        blkb = sb("blkb", [128, 128], BF16)
        self_ = sb("sel", [40, 8, 128])
        resetm = sb("resetm", [40, 512])
        zrow = sb("zrow", [8, 1])
        DMA(identf.t[:], C["c_ident"][:, :], [], [identf.b])
        CP("dve", identb.t[:], identf.t[:], [identf.b], [identb.b])
        for half in range(2):
            stg = f32r.get()
            DMA(stg.t[:, :], C["c_tri"][:, half, :], [], [stg.b])
            CP("dve", trib.t[:, half, :], stg.t[:, :], [stg.b], [trib.b])
        stg = f32r.get()
        DMA(stg.t[:, 0:128], C["c_rot"][:, :], [], [stg.b])
        CP("dve", rotb.t[:], stg.t[:, 0:128], [stg.b], [rotb.b])
        stg = f32r.get()
        DMA(stg.t[:, 0:128], C["c_blk"][:, :], [], [stg.b])
        CP("dve", blkb.t[:], stg.t[:, 0:128], [stg.b], [blkb.b])
        DMA(self_.t[32:40, :, :], C["c_sel"][:, :, :], [], [self_.b])
        DMA(resetm.t[32:40, :], C["c_reset"][:, :], [], [resetm.b])
        S.op("dve", lambda e: e.memset(zrow.t[:], 0.0), [], [zrow.b])
        epsc = sb("epsc", [128, 1])
        S.op("dve", lambda e: e.memset(epsc.t[:], EPS), [], [epsc.b])
        negm = sb("negm", [128, 2, 128], BF16)
        TSC("dve", negm.t[:, :, :], trib.t[:, :, 0:128], -1.0, ALU.add, [trib.b], [negm.b], s2=30000.0, op1=ALU.mult)

        pending_casts = {l: [] for l in range(depth)}
        _DMA_real = DMA

        def DMA(out, in_, rd, wr, eng="sp", slow=False, _defer=[None]):
            if _defer[0] is not None and eng == "pool":
                pending_casts[_defer[0]].append(lambda: _DMA_real(out, in_, rd, wr, eng=eng, slow=slow))
            else:
                _DMA_real(out, in_, rd, wr, eng=eng, slow=slow)
        _defer_box = DMA.__defaults__[2]
        for l in range(depth):
            _defer_box[0] = l if l >= 1 else None
            wi = W["w_in"][l].rearrange("(k p) c -> p k c", p=128)

            def wdst(p, off, n, cw=512, l=l):
                return WS[l, p].rearrange("p (k c) -> p k c", c=cw)[:, :, off:off + n]

            def cast(p, off, c0, n, l=l, wi=wi):
                b_ = Buf("ws"); ws_buf[l][p].append(b_)
                DMA(wdst(p, off, n), wi[:, :, c0:c0 + n], [], [b_], eng="pool")

            for i in range(4):
                cast(P_AQ, i * 128, O_AQ + i * 64, 64)
                cast(P_AQ, i * 128 + 64, O_AQ + (4 + i) * 64, 64)
            cast(P_KV, 0, O_AK, 256)
            for d_ in range(2):
                base_ = 256 + d_ * 72
                b0_ = Buf("ws"); ws_buf[l][P_KV].append(b0_)
                DMA(wdst(P_KV, base_, 72), wi[:, :, O_MI:O_MI + 72], [], [b0_], eng="pool")
                for (off_, c0_, n_) in ((0, O_MI + d_ * 4, 4), (32, O_MF + d_ * 4, 4), (64, O_SDT + d_ * 8, 8)):
                    b_ = Buf("ws"); ws_buf[l][P_KV].append(b_)
                    DMA(wdst(P_KV, base_ + off_, n_), wi[:, :, c0_:c0_ + n_], [b0_], [b_], eng="pool")
            cast(P_AZ, 0, O_AZ, 512)
            cast(P_MQ, 0, O_MQ, 512)
            cast(P_MK, 0, O_MK, 512)
            cast(P_MV, 0, O_MV, 512)
            cast(P_MO, 0, O_MO, 512)
            cast(P_MZ, 0, O_MZ, 512)
            cast(P_SX, 0, O_SX, 512)
            cast(P_SBC, 0, O_SB, 256)
            cast(P_SZ, 0, O_SZ, 512)
            for bi_, nm in enumerate(("w_att_out", "w_mlstm_out", "w_ssm_out")):
                src = W[nm][l].rearrange("(k p) c -> p k c", p=128)
                for q_ in range(4):
                    p = P_G0 + bi_ * 4 + q_
                    b_ = Buf("ws"); ws_buf[l][p].append(b_)
                    DMA(WS[l, p][:, 0:2048].rearrange("p (k c) -> p k c", c=256),
                        wi[:, :, O_G + bi_ * 1024 + q_ * 256:O_G + bi_ * 1024 + (q_ + 1) * 256], [], [b_], eng="pool")
                    b_ = Buf("ws"); ws_buf[l][p].append(b_)
                    DMA(WS[l, p][:, 2048:3072].rearrange("p (k c) -> p k c", c=256), src[:, :, q_ * 256:(q_ + 1) * 256], [], [b_], eng="pool")
            wo = W["w_out"][l].rearrange("(k p) c -> p k c", p=128)
            for p_, lo_ in ((P_WO0, 0), (P_WO1, 512)):
                b_ = Buf("ws"); ws_buf[l][p_].append(b_)
                DMA(wdst(p_, 0, 512), wo[:, :, lo_:lo_ + 512], [], [b_], eng="pool")

        _defer_box[0] = None
        NSLOT = 3
        wslots = [sb("wslot%d" % i, [128, 4096], BF16) for i in range(NSLOT)]

        ng_bc = sb("ng_bc", [128, D])
        gq = sb("gq", [128, 1]); gk = sb("gk", [128, 1])
        esink = sb("esink", [128, 8])
        ib = [sb("ib%d" % d, [4, 1]) for d in range(2)]
        nfb = [sb("nfb%d" % d, [4, 1]) for d in range(2)]
        mng_bc = sb("mng_bc", [128, 512]); sng_bc = sb("sng_bc", [128, 512])
        cw = sb("cw", [128, 6, 5]); cb = sb("cb", [128, 6])
        acoef = [sb("acoef%d" % d, [40, 1]) for d in range(2)]
        dtb = [sb("dtb%d" % d, [40, 1]) for d in range(2)]
        dsk_bc = sb("dsk_bc", [128, 8])

        def load_layer_params(l):
            DMA(ng_bc.t[:], W["norm_g"][l].partition_broadcast(128), [], [ng_bc.b])
            for half in range(2):
                DMA(gq.t[half * 64:(half + 1) * 64, :], W["q_norm_g"][l].rearrange("(d o) -> d o", o=1), [], [gq.b])
                DMA(gk.t[half * 64:(half + 1) * 64, :], W["k_norm_g"][l].rearrange("(d o) -> d o", o=1), [], [gk.b])
            S.op("act", lambda e: e.mul(out=gq.t[:], in_=gq.t[:], mul=0.125), [gq.b], [gq.b])
            DMA(esink.t[:], W["attn_sink"][l].partition_broadcast(128), [], [esink.b])
            ACT(esink.t[:], esink.t[:], AF.Exp, [esink.b], [esink.b])
            for d in range(2):
                DMA(ib[d].t[:], W["mlstm_i_b"][l, d].rearrange("(d o) -> d o", o=1), [], [ib[d].b])
                DMA(nfb[d].t[:], W["mlstm_f_b"][l, d].rearrange("(d o) -> d o", o=1), [], [nfb[d].b])
                S.op("act", (lambda d: lambda e: e.mul(out=nfb[d].t[:], in_=nfb[d].t[:], mul=-1.0))(d), [nfb[d].b], [nfb[d].b])
                DMA(acoef[d].t[32:40, :], W["a_log"][l, d].rearrange("(d o) -> d o", o=1), [], [acoef[d].b])
                ACT(acoef[d].t[32:40, :], acoef[d].t[32:40, :], AF.Exp, [acoef[d].b], [acoef[d].b])
                S.op("act", (lambda d: lambda e: e.mul(out=acoef[d].t[32:40, :], in_=acoef[d].t[32:40, :], mul=-1.0))(d), [acoef[d].b], [acoef[d].b])
                DMA(dtb[d].t[32:40, :], W["dt_bias"][l, d].rearrange("(d o) -> d o", o=1), [], [dtb[d].b])
            DMA(mng_bc.t[:], W["mlstm_norm_g"][l].partition_broadcast(128), [], [mng_bc.b])
            DMA(sng_bc.t[:], W["ssm_norm_g"][l].partition_broadcast(128), [], [sng_bc.b])
            for ti in range(6):
                DMA(cw.t[:, ti, :], W["conv_w"][l][:, ti * 128:(ti + 1) * 128].rearrange("k p -> p k"), [], [cw.b], slow=True)
                DMA(cb.t[:, ti:ti + 1], W["conv_b"][l][ti * 128:(ti + 1) * 128].rearrange("(p o) -> p o", o=1), [], [cb.b])
            DMA(dsk_bc.t[:], W["d_skip"][l].partition_broadcast(128), [], [dsk_bc.b])

        hT = sb("hT", [128, 8, 8 * CH], BF16)
        hT_b = [Buf("hT%d" % i) for i in range(8)]
        hslot_chunk = [None] * 8
        xring = Ring([sb("xt%d" % i, [128, D]) for i in range(2)])
        hbring = Ring([sb("hb%d" % i, [128, D], BF16) for i in range(2)])
        st4 = Ring([sb("st4_%d" % i, [128, 8]) for i in range(6)])

        pf = Ring([psum("pf%d" % i, [128, 512]) for i in range(4)])
        pacc = [psum("pacc%d" % i, [128, 512]) for i in range(2)]
        pb = Ring([psum("pb%d" % i, [128, 1024], BF16) for i in range(2)])

        mergedT = sb("mergedT", [128, 8, TM])
        GT = sb("GT", [128, 4096], BF16)

        class View:
            def __init__(self, ap, b):
                self.t = ap
                self.b = b
        mergedTb = View(GT.t[:, :].rearrange("p (k t) -> p k t", k=8), GT.b)
        gate_tok = [View(GT.t[:, i * 2048:(i + 1) * 2048].rearrange("p (j c) -> p j c", j=4), GT.b) for i in range(2)]
        ybT = sb("ybT", [128, 4, TM], BF16)
        rope_t = sb("rope_t", [128, 2, TM])
        kT = sb("kT", [128, 6 * CH], BF16)
        vaug = sb("vaug", [128, 6, 2, 65], BF16)
        mqT = sb("mqT", [128, 4, TM], BF16)
        qT = mqT
        mkT = sb("mkT", [128, 4, TM], BF16)
        mvaug = sb("mvaug", [128, 4, 4, 129], BF16)
        mktok = sb("mktok", [128, 4, 512], BF16)
        mC = [sb("mC%d" % d, [128, 516]) for d in range(2)]
        mCb = [sb("mCb%d" % d, [128, 516], BF16) for d in range(2)]
        srawr = Ring([sb("sraw%d" % i, [128, TM + 4]) for i in range(2)])
        sconv = sb("sconv", [128, 6, TM], BF16)
        sxtok = sb("sxtok", [128, 4, 512], BF16)
        szero = sb("szero", [128, 4, TM], BF16)
        S.op("pool", lambda e: e.memset(szero.t[:], 0.0), [], [szero.b])
        sbtok = sb("sbtok", [128, 4, 128], BF16)
        sS = [sb("sS%d" % d, [128, 256]) for d in range(2)]
        sSb = [sb("sSb%d" % d, [128, 256], BF16) for d in range(2)]
        RT = [sb("rt%d" % i, [40, 512]) for i in range(8)]
        carryB = [sb("carryB%d" % d, [4, 1]) for d in range(2)]
        carryM = [sb("carryM%d" % d, [4, 1]) for d in range(2)]
        mprev = sb("mprev", [4, 4])
        NTS = 52
        TSF = sb("TSF", [128, 4, NTS])
        TSBm = sb("TSBm", [128, 4, NTS])
        ACSF = sb("ACSF", [40, TM])
        ACSBm = sb("ACSBm", [40, TM])
        acs_hl = {0: (sb("acsfh", [40, TM], BF16), sb("acsfl", [40, TM], BF16)),
                  1: (sb("acsbh", [40, TM], BF16), sb("acsbl", [40, TM], BF16))}
        selb = sb("selb", [40, 8, 128], BF16)
        CP("dve", selb.t[32:40, :, :], self_.t[32:40, :, :], [self_.b], [selb.b])

        def mk_hilo(d, src):
            hi, lo = acs_hl[d]
            CP("act", hi.t[32:40, :], src.t[32:40, :], [src.b], [hi.b])
            TT("dve", lo.t[32:40, :], src.t[32:40, :], hi.t[32:40, :], ALU.subtract, [src.b, hi.b], [lo.b])
        nmmax = smax // TM
        TSD = nc.dram_tensor("tsd", [nmmax, 128, 4 * NTS], F32, kind="Internal").ap()
        ACSD = nc.dram_tensor("acsd", [nmmax, 8, TM], F32, kind="Internal").ap()
        tsd_buf = [Buf("tsd%d" % i) for i in range(nmmax)]
        acsd_buf = [Buf("acsd%d" % i) for i in range(nmmax)]

        stages = []

        def add_stage(l, piece, fn):
            stages.append((l, piece, fn))

        def run_stages():
            loads = [i for i, s_ in enumerate(stages) if s_[1] is not None]
            slot_of = {}
            nxt = 0
            PRE = 2
            for i, (l, piece, fn) in enumerate(stages):
                while nxt < len(loads) and (nxt < PRE or loads[nxt - PRE] <= i):
                    j = loads[nxt]
                    sl = wslots[nxt % NSLOT]
                    lj, pj, _ = stages[j]
                    if pj == P_KV or pj == P_SBC:
                        nc_ = 400 if pj == P_KV else 256
                        DMA(sl.t[:, :].rearrange("p (k c) -> p k c", c=512)[:, :, 0:nc_],
                            WS[lj, pj].rearrange("p (k c) -> p k c", c=512)[:, :, 0:nc_], ws_buf[lj][pj], [sl.b])
                    elif P_G0 <= pj < P_WO0:
                        DMA(sl.t[:, 0:3072], WS[lj, pj][:, 0:3072], ws_buf[lj][pj], [sl.b])
                    else:
                        DMA(sl.t[:], WS[lj, pj], ws_buf[lj][pj], [sl.b])
                    slot_of[j] = sl
                    nxt += 1
                fn(slot_of.get(i))

        def src_of(l):
            return x_in if l == 0 else YS[(l - 1) % nscr]

        def src_buf(l, r0):
            return [] if l == 0 else ys_buf[(l - 1) % nscr][r0 // CH]

        def dst_of(l):
            return y_out if l == depth - 1 else YS[l % nscr]

        def dst_buf2(l, r0, half):
            return [] if l == depth - 1 else [ys_buf[l % nscr][r0 // CH][half]]

        def rstd_from_sumsq(dst, ssum, n, rd, wr, npart=128):
            TSC("dve", dst, ssum, 1.0 / n, ALU.mult, rd, wr, s2=EPS, op1=ALU.add)
            ACT(dst, dst, AF.Sqrt, wr, wr)
            S.op("dve", lambda e: e.reciprocal(out=dst, in_=dst), wr, wr)

        def ensure_h(l, tok0, c):
            sl = c % 8
            if hslot_chunk[sl] == (l, tok0, c):
                return
            hslot_chunk[sl] = (l, tok0, c)
            xt = xring.get()
            r0 = tok0 + c * CH
            DMA(xt.t[:], src_of(l)[r0:r0 + CH, :], src_buf(l, r0), [xt.b])
            st = st4.get()
            hb = hbring.get()
            ACT(hb.t[:], xt.t[:], AF.Square, [xt.b], [hb.b, st.b], accum=st.t[:, 0:1])
            rstd_from_sumsq(st.t[:, 1:2], st.t[:, 0:1], D, [st.b], [st.b])
            STT("dve", hb.t[:], xt.t[:], st.t[:, 1:2], ng_bc.t[:], ALU.mult, ALU.mult, [xt.b, st.b, ng_bc.b], [hb.b])
            p = pb.get()
            for k in range(8):
                TR(p.t[:, k * 128:(k + 1) * 128], hb.t[:, k * 128:(k + 1) * 128], identb.t[:], [hb.b, identb.b], [p.b])
            CP("act", hT.t[:, :, sl * CH:(sl + 1) * CH], p.t[:].rearrange("p (k t) -> p k t", k=8), [p.b], [hT_b[sl]])
            if c == 0:
                dbg("hT", hT.t[:, :, sl * CH:(sl + 1) * CH], [hT_b[sl]], BF16)

        def h_rhs(k, c0, n):
            s0 = c0 % 8
            assert s0 + n <= 8
            return hT.t[:, k, s0 * CH:(s0 + n) * CH]

        def h_bufs(c0, n):
            return [hT_b[(c0 + i) % 8] for i in range(n)]

        def proj_feat(sl, col0, ncols, c0, nchunks, pt, pcol0=0):
            for k in range(8):
                MM(pt.t[0:ncols, pcol0:pcol0 + nchunks * CH], sl.t[:, k * 512 + col0:k * 512 + col0 + ncols],
                   h_rhs(k, c0, nchunks), k == 0, k == 7, [sl.b] + h_bufs(c0, nchunks), [pt.b])

        def proj_tok(sl, col0, ncols, c, pt):
            s0 = c % 8
            for k in range(8):
                MM(pt.t[:, 0:ncols], hT.t[:, k, s0 * CH:(s0 + 1) * CH], sl.t[:, k * 512 + col0:k * 512 + col0 + ncols],
                   k == 0, k == 7, [sl.b, hT_b[s0]], [pt.b])

        mL = slice(0, 4)
        sD = slice(32, 40)

        def gate_rows(l, d, sl, c0, first, ts_dst, ts_bufs, acs_dst, acs_bufs):
            rev = (d == 1)

            def rv(ap2d):
                return ap2d[:, ::-1] if rev else ap2d

            pg_ = pf.get()
            proj_feat(sl, 256 + d * 72, 72, c0, 4, pg_)
            pmi = pmf = pdt = pg_
            R = RT
            IG, L1, Bp, Mg, WI, DEC = R[0], R[1], R[2], R[3], R[4], R[5]
            ACT(IG.t[mL, :], rv(pmi.t[0:4, :]), AF.Identity, [pmi.b, ib[d].b], [IG.b], bias=ib[d].t[:, 0:1])
            ACT(L1.t[mL, :], rv(pmf.t[32:36, :]), AF.Exp, [pmf.b, nfb[d].b], [L1.b], bias=nfb[d].t[:, 0:1], scale=-1.0)
            ACT(L1.t[mL, :], L1.t[mL, :], AF.Ln, [L1.b], [L1.b], bias=1.0)
            if first:
                S.op("dve", lambda e: e.memset(carryB[d].t[:], 0.0), [], [carryB[d].b])
                S.op("dve", lambda e: e.memset(carryM[d].t[:], 0.0), [], [carryM[d].b])
            SCAN(Bp.t[mL, :], L1.t[mL, :], zrow.t[0:4, 0:1].to_broadcast([4, 512]), carryB[d].t[:, 0:1], ALU.add, ALU.add,
                 [L1.b, zrow.b, carryB[d].b], [Bp.b])
            A = IG
            TT("dve", A.t[mL, :], IG.t[mL, :], Bp.t[mL, :], ALU.add, [IG.b, Bp.b], [A.b])
            SCAN(Mg.t[mL, :], A.t[mL, :], A.t[mL, :], carryM[d].t[:, 0:1], ALU.max, ALU.max, [A.b, carryM[d].b], [Mg.b])

            def r3(tl):
                return tl.t[mL, :].rearrange("p (c t) -> p c t", c=4)

            Mg3 = r3(Mg)
            CP("dve", mprev.t[:, 0:1], carryM[d].t[:, 0:1], [carryM[d].b], [mprev.b])
            CP("dve", mprev.t[:, 1:4], Mg3[:, 0:3, 127], [Mg.b], [mprev.b])
            CP("dve", carryB[d].t[:, 0:1], Bp.t[mL, 511:512], [Bp.b], [carryB[d].b])
            CP("dve", carryM[d].t[:, 0:1], Mg.t[mL, 511:512], [Mg.b], [carryM[d].b])
            mend_bc = Mg3[:, :, 127:128].to_broadcast([4, 4, 128])
            mprev_bc = mprev.t[:, :].unsqueeze(2).to_broadcast([4, 4, 128])
            U = L1
            TT("dve", r3(U), r3(Mg), mend_bc, ALU.subtract, [Mg.b, L1.b], [U.b])
            TSC("dve", U.t[mL, :], U.t[mL, :], -60.0, ALU.max, [U.b], [U.b])
            ACT(U.t[mL, :], U.t[mL, :], AF.Exp, [U.b], [U.b], scale=-1.0)
            TT("dve", r3(WI), r3(Mg), mprev_bc, ALU.subtract, [Mg.b, mprev.b], [WI.b])
            ACT(WI.t[mL, :], WI.t[mL, :], AF.Exp, [WI.b], [WI.b], scale=-1.0)
            FL = Bp
            TT("dve", FL.t[mL, :], Bp.t[mL, :], Mg.t[mL, :], ALU.subtract, [Bp.b, Mg.b], [FL.b])
            ACT(FL.t[mL, :], FL.t[mL, :], AF.Exp, [FL.b], [FL.b])
            TT("dve", r3(DEC), mprev_bc, mend_bc, ALU.subtract, [Mg.b, mprev.b], [DEC.b])
            ACT(DEC.t[mL, :], DEC.t[mL, :], AF.Exp, [DEC.b], [DEC.b])
            WK = A
            TT("dve", r3(WK), r3(A), mend_bc, ALU.subtract, [A.b, Mg.b], [WK.b])
            ACT(WK.t[mL, :], WK.t[mL, :], AF.Exp, [WK.b], [WK.b])
            DT, LDT, DA, ACS, EA = R[0], R[1], R[2], R[3], R[4]
            ACT(DT.t[sD, :], rv(pdt.t[64:72, :]), AF.Exp, [pdt.b, dtb[d].b], [DT.b], bias=dtb[d].t[sD, 0:1])
            ACT(DT.t[sD, :], DT.t[sD, :], AF.Ln, [DT.b], [DT.b], bias=1.0)
            ACT(LDT.t[sD, :], DT.t[sD, :], AF.Ln, [DT.b], [LDT.b])
            TSC("dve", DA.t[sD, :], DT.t[sD, :], acoef[d].t[sD, 0:1], ALU.mult, [DT.b, acoef[d].b], [DA.b])
            SCAN(ACS.t[sD, :], resetm.t[sD, :], DA.t[sD, :], 0.0, ALU.mult, ALU.add, [resetm.b, DA.b], [ACS.b])

            def r8(tl):
                return tl.t[sD, :].rearrange("p (c t) -> p c t", c=4)

            aend_bc = r8(ACS)[:, :, 127:128].to_broadcast([8, 4, 128])
            BL = LDT
            TT("dve", BL.t[sD, :], LDT.t[sD, :], ACS.t[sD, :], ALU.subtract, [LDT.b, ACS.b], [BL.b])
            WST = DA
            TT("dve", r8(WST), aend_bc, r8(ACS), ALU.subtract, [ACS.b, DA.b], [WST.b])
            ACT(WST.t[sD, :], WST.t[sD, :], AF.Exp, [WST.b], [WST.b])
            TT("dve", WST.t[sD, :], WST.t[sD, :], DT.t[sD, :], ALU.mult, [WST.b, DT.b], [WST.b])
            ACT(EA.t[sD, :], ACS.t[sD, :], AF.Exp, [ACS.b], [EA.b])
            CD = DT
            CP("dve", r8(CD), aend_bc, [ACS.b, WST.b, DT.b], [CD.b])
            ACT(CD.t[sD, :], CD.t[sD, :], AF.Exp, [CD.b], [CD.b])
            quants = [(WK, mL, 4, 0), (U, mL, 4, 4), (WI, mL, 4, 8), (FL, mL, 4, 12), (DEC, mL, 4, 16),
                      (BL, sD, 8, 20), (WST, sD, 8, 28), (EA, sD, 8, 36), (CD, sD, 8, 44)]
            pts = pf.get()
            if rev:
                order_ = [R[0], R[1], R[2], R[4], R[5]]
                rmap = {}
                for ti_, tl_ in enumerate(order_):
                    q2 = R[6 + (ti_ % 2)]
                    rmap[id(tl_)] = (q2, ti_)
                quants = sorted(quants, key=lambda x: rmap[id(x[0])][1])
                done_ = set()
            for qi, (q, ps_, r, off) in enumerate(quants):
                if rev:
                    q2, ti_ = rmap[id(q)]
                    if ti_ not in done_:
                        done_.add(ti_)
                        CP("dve", q2.t[0:40, :], q.t[0:40, ::-1], [q.b], [q2.b])
                    q = q2
                for j in range(4):
                    MM(pts.t[:, j * 64 + off:j * 64 + off + r], q.t[ps_, j * CH:(j + 1) * CH], identf.t[ps_, ps_],
                       True, True, [q.b, identf.b], [pts.b])
            CP("act", ts_dst, pts.t[:, 0:256].rearrange("p (j c) -> p j c", j=4)[:, :, 0:NTS], [pts.b], ts_bufs)
            if rev:
                CP("pool", acs_dst, ACS.t[sD, ::-1], [ACS.b], acs_bufs)
            else:
                CP("pool", acs_dst, ACS.t[sD, :], [ACS.b], acs_bufs)

        def ssd_conv(l, slx, slbc, c0, nch_seq, tiles):
            for ti in tiles:
                if "conv" in KSKIP:
                    break
                sl, col0 = (slx, ti * 128) if ti < 4 else (slbc, (ti - 4) * 128)
                sraw = srawr.get()
                pm = pf.get()
                proj_feat(sl, col0, 128, c0, 4, pm)
                CP("act", sraw.t[:, 2:2 + TM], pm.t[:, :], [pm.b], [sraw.b])
                ph = pf.get()
                if c0 > 0:
                    s0 = (c0 - 1) % 8
                    for k in range(8):
                        MM(ph.t[:, 0:32], sl.t[:, k * 512 + col0:k * 512 + col0 + 128], hT.t[:, k, s0 * CH + 96:s0 * CH + 128],
                           k == 0, k == 7, [sl.b, hT_b[s0]], [ph.b])
                    CP("dve", sraw.t[:, 0:2], ph.t[:, 30:32], [ph.b], [sraw.b])
                else:
                    S.op("dve", (lambda sraw: lambda e: e.memset(sraw.t[:, 0:2], 0.0))(sraw), [], [sraw.b])
                if c0 + 4 < nch_seq:
                    s0 = (c0 + 4) % 8
                    for k in range(8):
                        MM(ph.t[:, 32:64], sl.t[:, k * 512 + col0:k * 512 + col0 + 128], hT.t[:, k, s0 * CH:s0 * CH + 32],
                           k == 0, k == 7, [sl.b, hT_b[s0]], [ph.b])
                    CP("dve", sraw.t[:, TM + 2:TM + 4], ph.t[:, 32:34], [ph.b], [sraw.b])
                else:
                    S.op("dve", (lambda sraw: lambda e: e.memset(sraw.t[:, TM + 2:TM + 4], 0.0))(sraw), [], [sraw.b])
                acc = f32r.get()
                TSC("dve", acc.t[:, :], sraw.t[:, 0:TM], cw.t[:, ti, 0:1], ALU.mult, [sraw.b, cw.b, cb.b], [acc.b],
                    s2=cb.t[:, ti:ti + 1], op1=ALU.add)
                for kk in range(1, 5):
                    STT("dve", acc.t[:, :], sraw.t[:, kk:kk + TM], cw.t[:, ti, kk:kk + 1], acc.t[:, :], ALU.mult, ALU.add,
                        [sraw.b, cw.b, acc.b], [acc.b])
                ACT(sconv.t[:, ti, :], acc.t[:, :], AF.Silu, [acc.b], [sconv.b])

        def ssd_tokmajor(j):
            if "tokm" in KSKIP:
                return
            p = pb.get()
            for ti in range(4):
                TR(p.t[:, ti * 128:(ti + 1) * 128], sconv.t[:, ti, j * CH:(j + 1) * CH], identb.t[:], [sconv.b, identb.b], [p.b])
            CP("act", sxtok.t[:, j, :], p.t[:, 0:512], [p.b], [sxtok.b])
            if "tokb" in KSKIP:
                return
            p2 = pb.get()
            TR(p2.t[:, 0:128], sconv.t[:, 4, j * CH:(j + 1) * CH], identb.t[:], [sconv.b, identb.b], [p2.b])
            CP("act", sbtok.t[:, j, :], p2.t[:, 0:128], [p2.b], [sbtok.b])

        def ssd_state_update(d, j, ts_ap, ts_rd):
            xw = b16r.get()
            TT("pool" if d == 0 else "dve", xw.t[:, :].rearrange("p (h e) -> p h e", h=8), sxtok.t[:, j, :].rearrange("p (h e) -> p h e", h=8),
               ts_ap[:, 28:36].unsqueeze(2).to_broadcast([128, 8, 64]), ALU.mult, [sxtok.b] + ts_rd, [xw.b])
            pS = pf.get()
            for g in range(2):
                MM(pS.t[:, g * 256:(g + 1) * 256], sbtok.t[:, j, :], xw.t[:, g * 256:(g + 1) * 256], True, True, [sbtok.b, xw.b], [pS.b])
            for g in range(2):
                ps_ = slice(g * 64, (g + 1) * 64)
                TT("dve", sS[d].t[ps_, :].rearrange("p (h e) -> p h e", h=4), sS[d].t[ps_, :].rearrange("p (h e) -> p h e", h=4),
                   ts_ap[ps_, 44 + g * 4:44 + g * 4 + 4].unsqueeze(2).to_broadcast([64, 4, 64]), ALU.mult, [sS[d].b] + ts_rd, [sS[d].b])
                TT("dve", sS[d].t[ps_, :], sS[d].t[ps_, :], pS.t[ps_, g * 256:(g + 1) * 256], ALU.add, [sS[d].b, pS.b], [sS[d].b])

        def mlstm_state_update(d, j, ts_ap, ts_rd, pre=None):
            if pre is not None:
                vw, wkb = pre["vw"], pre["wkb"]
            else:
                vw = b16r.get()
                TT("pool" if d == 0 else "dve", vw.t[:, :].rearrange("p (h e) -> p h e", h=4), mvaug.t[:, j, :, 0:128],
                   ts_ap[:, 0:4].unsqueeze(2).to_broadcast([128, 4, 128]), ALU.mult, [mvaug.b] + ts_rd, [vw.b])
                wkb = b16r.get()
                CP("dve", wkb.t[:, 0:4], ts_ap[:, 0:4], ts_rd, [wkb.b])
            pC = pf.get(); pn = pf.get()
            for h in range(4):
                MM(pC.t[:, h * 128:(h + 1) * 128], mktok.t[:, j, h * 128:(h + 1) * 128], vw.t[:, h * 128:(h + 1) * 128], True, True,
                   [mktok.b, vw.b], [pC.b])
                MM(pn.t[:, h:h + 1], mktok.t[:, j, h * 128:(h + 1) * 128], wkb.t[:, h:h + 1], True, True, [mktok.b, wkb.b], [pn.b])
            dec_bc = ts_ap[:, 16:20]
            TT("dve", mC[d].t[:, 0:512].rearrange("p (h e) -> p h e", h=4), mC[d].t[:, 0:512].rearrange("p (h e) -> p h e", h=4),
               dec_bc.unsqueeze(2).to_broadcast([128, 4, 128]), ALU.mult, [mC[d].b] + ts_rd, [mC[d].b])
            TT("dve", mC[d].t[:, 512:516], mC[d].t[:, 512:516], dec_bc, ALU.mult, [mC[d].b] + ts_rd, [mC[d].b])
            TT("dve", mC[d].t[:, 0:512], mC[d].t[:, 0:512], pC.t[:, :], ALU.add, [mC[d].b, pC.b], [mC[d].b])
            TT("dve", mC[d].t[:, 512:516], mC[d].t[:, 512:516], pn.t[:, 0:4], ALU.add, [mC[d].b, pn.b], [mC[d].b])

        def mlstm_v_tok(sl, c, j):
            pv = pf.get()
            proj_tok(sl, 0, 512, c, pv)
            CP("act", mvaug.t[:, j, :, 0:128], pv.t[:, :].rearrange("p (h e) -> p h e", h=4), [pv.b], [mvaug.b])

        def add_branch_merge(l, bi, first, c0):
            for q_ in range(4):
                def st_g(sl, q_=q_):
                    for oo in range(2):
                        o = q_ * 2 + oo
                        pg = pf.get()
                        for k_ in range(8):
                            MM(pg.t[:, :], sl.t[:, k_ * 256 + oo * 128:k_ * 256 + (oo + 1) * 128], h_rhs(k_, c0, 4), k_ == 0, k_ == 7,
                               [sl.b] + h_bufs(c0, 4), [pg.b])
                        gs = f32r.get()
                        ACT(gs.t[:, :], pg.t[:, :], AF.Sigmoid, [pg.b], [gs.b])
                        pp = pf.get()
                        for k_ in range(4):
                            MM(pp.t[:, :], sl.t[:, 2048 + k_ * 256 + oo * 128:2048 + k_ * 256 + (oo + 1) * 128], ybT.t[:, k_, :], k_ == 0, k_ == 3,
                               [sl.b, ybT.b], [pp.b])
                        if first:
                            TT("dve", mergedT.t[:, o, :], gs.t[:, :], pp.t[:, :], ALU.mult, [gs.b, pp.b], [mergedT.b])
                        else:
                            TT("dve", gs.t[:, :], gs.t[:, :], pp.t[:, :], ALU.mult, [gs.b, pp.b], [gs.b])
                            TT("pool", mergedT.t[:, o, :], mergedT.t[:, o, :], gs.t[:, :], ALU.add, [gs.b, mergedT.b], [mergedT.b])
                add_stage(l, P_G0 + bi * 4 + q_, st_g)

        def add_final(l, tok0, c0, nch):
            def st_pre(_sl):
                for c_ in range(c0 + 5, min(nch, c0 + 9)):
                    ensure_h(l, tok0, c_)
                dbg("ybT", ybT.t[:, :, :], [ybT.b], BF16)
                dbg("mergedT", mergedT.t[:, :, :], [mergedT.b])
                CP("act", mergedTb.t[:, 0:4, :], mergedT.t[:, 0:4, :], [mergedT.b], [mergedTb.b])
                CP("dve", mergedTb.t[:, 4:8, :], mergedT.t[:, 4:8, :], [mergedT.b], [mergedTb.b])
            add_stage(l, None, st_pre)
            for half in range(2):
                def st_o(sl, half=half):
                    hs = slice(half * 512, (half + 1) * 512)
                    for j in range(4):
                        r0 = tok0 + (c0 + j) * CH
                        xt = xring2.get()
                        DMA(xt.t[:, :], src_of(l)[r0:r0 + CH, hs], src_buf(l, r0), [xt.b])
                        po = pf.get()
                        for k in range(8):
                            MM(po.t[:, :], mergedTb.t[:, k, j * CH:(j + 1) * CH], sl.t[:, k * 512:(k + 1) * 512], k == 0, k == 7,
                               [mergedTb.b, sl.b], [po.b])
                        TT("dve", xt.t[:, :], xt.t[:, :], po.t[:, :], ALU.add, [xt.b, po.b], [xt.b])
                        DMA(dst_of(l)[r0:r0 + CH, hs], xt.t[:, :], [xt.b], dst_buf2(l, r0, half))
                add_stage(l, P_WO0 + half, st_o)

        xring2 = Ring([sb("xo%d" % i, [128, 512]) for i in range(3)])
        hsum = sb("hsum", [128, 512])
        tsf_b = TSF.b
        acsf_b = ACSF.b

        def seq_layer(l, tok0, slen):
            nch = slen // CH
            nm = slen // TM
            en_att, en_ml, en_ssd = ("att" in enable), ("mlstm" in enable), ("ssd" in enable)
            en_rec = en_ml or en_ssd

            if en_rec:
                for m in reversed(range(nm)):
                    c0 = 4 * m

                    def st_h(_sl, c0=c0):
                        for c in range(max(0, c0 - 1), min(nch, c0 + 5)):
                            ensure_h(l, tok0, c)
                    add_stage(l, None, st_h)

                    def st_gates(sl, c0=c0, m=m):
                        if m == nm - 1:
                            S.op("dve", lambda e: e.memset(mC[1].t[:], 0.0), [], [mC[1].b])
                            S.op("dve", lambda e: e.memset(sS[1].t[:], 0.0), [], [sS[1].b])
                            S.op("dve", lambda e: e.memset(mvaug.t[:, :, :, 128:129], 1.0), [], [mvaug.b])
                        gate_rows(l, 1, sl, c0, m == nm - 1, TSBm.t[:, :, :], [TSBm.b], ACSBm.t[sD, :], [ACSBm.b])
                        DMA(TSD[m].rearrange("p (j c) -> p j c", j=4), TSBm.t[:, :, :], [TSBm.b], [tsd_buf[m]])
                        DMA(ACSD[m], ACSBm.t[sD, :], [ACSBm.b], [acsd_buf[m]])
                    add_stage(l, P_KV, st_gates)

                    if en_ml:
                        def st_mk(sl, c0=c0):
                            for j in range(4):
                                pk = pf.get()
                                proj_tok(sl, 0, 512, c0 + j, pk)
                                S.op("act", (lambda j, pk: lambda e: e.mul(out=mktok.t[:, j, :], in_=pk.t[:, :], mul=128 ** -0.5))(j, pk),
                                     [pk.b], [mktok.b])
                        add_stage(l, P_MK, st_mk)

                        def st_mv(sl, c0=c0, m=m):
                            for j in range(4):
                                mlstm_v_tok(sl, c0 + j, j)
                            for j in reversed(range(4)):
                                c = c0 + j
                                CP("act", mCb[1].t[:, :], mC[1].t[:, :], [mC[1].b], [mCb[1].b])
                                DMA(BSTM[c], mCb[1].t[:, :], [mCb[1].b], [bstm_buf[c]])
                                mlstm_state_update(1, j, TSBm.t[:, j, :], [TSBm.b])
                        add_stage(l, P_MV, st_mv)

                    if en_ssd and "p2ssd" not in KSKIP:
                        def st_sx(sl, c0=c0):
                            ssd_conv(l, sl, None, c0, nch, [0, 1, 2, 3])
                        add_stage(l, P_SX, st_sx)

                        def st_sbc(sl, c0=c0, m=m):
                            ssd_conv(l, None, sl, c0, nch, [4])
                            for j in reversed(range(4)):
                                c = c0 + j
                                ssd_tokmajor(j)
                                CP("act", sSb[1].t[:, :], sS[1].t[:, :], [sS[1].b], [sSb[1].b])
                                DMA(BSTS[c], sSb[1].t[:, :], [sSb[1].b], [bsts_buf[c]])
                                ssd_state_update(1, j, TSBm.t[:, j, :], [TSBm.b])
                        add_stage(l, P_SBC, st_sbc)

                    def st_pref(_sl, c0=c0):
                        for c_ in range(max(0, c0 - 5), c0 - 1):
                            ensure_h(l, tok0, c_)
                    add_stage(l, None, st_pref)

            for m in range(nm):
                c0 = 4 * m
                lo = max(0, c0 - 1)
                hi = min(nch, c0 + 5)

                def st_h(_sl, c0=c0, lo=lo, hi=hi, m=m):
                    if l + 1 < depth:
                        for _ in range(casts_per_mt):
                            if pending_casts[l + 1]:
                                pending_casts[l + 1].pop(0)()
                    for c in range(lo, hi):
                        ensure_h(l, tok0, c)
                    DMA(rope_t.t[:], C["c_rope"][:, :, c0 * CH:c0 * CH + TM], [], [rope_t.b])
                    if m == 0:
                        S.op("dve", lambda e: e.memset(mC[0].t[:], 0.0), [], [mC[0].b])
                        S.op("dve", lambda e: e.memset(sS[0].t[:], 0.0), [], [sS[0].b])
                        S.op("dve", lambda e: e.memset(mCb[0].t[:], 0.0), [], [mCb[0].b])
                        S.op("dve", lambda e: e.memset(sSb[0].t[:], 0.0), [], [sSb[0].b])
                        S.op("dve", lambda e: e.memset(mvaug.t[:, :, :, 128:129], 1.0), [], [mvaug.b])
                        S.op("dve", lambda e: e.memset(vaug.t[:, :, :, 64:65], 1.0), [], [vaug.b])
                add_stage(l, None, st_h)

                def qk_norm_rope(pq, gain, ntok):
                    sq = b16r.get()
                    ACT(sq.t[:, 0:ntok], pq.t[:, 0:ntok], AF.Square, [pq.b], [sq.b])
                    pss = pf.get()
                    MM(pss.t[:, 0:ntok], blkb.t[:], sq.t[:, 0:ntok], True, True, [blkb.b, sq.b], [pss.b])
                    rs = f32r.get()
                    ACT(rs.t[:, 0:ntok], pss.t[:, 0:ntok], AF.Ln, [pss.b, epsc.b], [rs.b], bias=epsc.t[:, 0:1])
                    ACT(rs.t[:, 0:ntok], rs.t[:, 0:ntok], AF.Exp, [rs.b], [rs.b], scale=-0.5)
                    qn = f32r.get(); qnb = b16r.get()
                    STT("dve", qnb.t[:, 0:ntok], pq.t[:, 0:ntok], gain.t[:, 0:1], rs.t[:, 0:ntok], ALU.mult, ALU.mult,
                        [pq.b, gain.b, rs.b], [qnb.b])
                    pr = pf.get()
                    MM(pr.t[:, 0:ntok], rotb.t[:], qnb.t[:, 0:ntok], True, True, [rotb.b, qnb.b], [pr.b])
                    t1 = f32r.get()
                    return qn, pr, t1, qnb

                if en_att:
                    def st_aq(sl, c0=c0, qk_norm_rope=qk_norm_rope):
                        for i in range(4):
                            pq = pf.get()
                            proj_feat(sl, i * 128, 128, c0, 4, pq)
                            qn, pr, t1, qnb = qk_norm_rope(pq, gq, TM)
                            TT("dve", t1.t[:, :], pr.t[:, :], rope_t.t[:, 1, :], ALU.mult, [pr.b, rope_t.b], [t1.b])
                            TT("pool", qn.t[:, :], qnb.t[:, :], rope_t.t[:, 0, :], ALU.mult, [qnb.b, rope_t.b], [qn.b])
                            TT("pool", qT.t[:, i, :], qn.t[:, :], t1.t[:, :], ALU.add, [qn.b, t1.b], [qT.b])
                            if i == 0:
                                dbg("pq", pq.t[:, :], [pq.b])
                        dbg("qT", qT.t[:, :, :], [qT.b], BF16)
                    add_stage(l, P_AQ, st_aq)

                def st_kv(sl, c0=c0, lo=lo, hi=hi, m=m, qk_norm_rope=qk_norm_rope):
                    if en_att:
                        for (ca, n) in ((lo, c0 - lo), (c0, 4), (c0 + 4, hi - c0 - 4)):
                            if n <= 0:
                                continue
                            pk = pf.get()
                            proj_feat(sl, 0, 128, ca, n, pk)
                            ntk = n * CH
                            qn, pr, t1, qnb = qk_norm_rope(pk, gk, ntk)
                            if ca == c0:
                                cosap, sinap, rbufs = rope_t.t[:, 0, :], rope_t.t[:, 1, :], [rope_t.b]
                            else:
                                rt = f32r.get(); rt2 = f32r.get()
                                DMA(rt.t[:, 0:CH], C["c_rope"][:, 0, ca * CH:(ca + 1) * CH], [], [rt.b])
                                DMA(rt2.t[:, 0:CH], C["c_rope"][:, 1, ca * CH:(ca + 1) * CH], [], [rt2.b])
                                cosap, sinap, rbufs = rt.t[:, 0:CH], rt2.t[:, 0:CH], [rt.b, rt2.b]
                            TT("dve", t1.t[:, 0:ntk], pr.t[:, 0:ntk], sinap, ALU.mult, [pr.b] + rbufs, [t1.b])
                            TT("pool", qn.t[:, 0:ntk], qnb.t[:, 0:ntk], cosap, ALU.mult, [qnb.b] + rbufs, [qn.b])
                            o = (ca - lo) * CH
                            TT("pool", kT.t[:, o:o + ntk], qn.t[:, 0:ntk], t1.t[:, 0:ntk], ALU.add, [qn.b, t1.b], [kT.b])
                        for c in range(lo, hi):
                            pv = pf.get()
                            proj_tok(sl, 128, 128, c, pv)
                            CP("act", vaug.t[:, c - lo, :, 0:64], pv.t[:, 0:128].rearrange("p (g e) -> p g e", g=2), [pv.b], [vaug.b])
                        dbg("kT", kT.t[:, :], [kT.b], BF16)
                        dbg("vaug", vaug.t[:, :, :, :], [vaug.b], BF16)
                    if en_rec:
                        gate_rows(l, 0, sl, c0, m == 0, TSF.t[:, :, :], [tsf_b], ACSF.t[sD, :], [acsf_b])
                        DMA(TSBm.t[:, :, :], TSD[m].rearrange("p (j c) -> p j c", j=4), [tsd_buf[m]], [TSBm.b])
                        DMA(ACSBm.t[sD, :], ACSD[m], [acsd_buf[m]], [ACSBm.b])
                        mk_hilo(0, ACSF)
                        mk_hilo(1, ACSBm)
                add_stage(l, P_KV, st_kv)

                if en_att:
                    def st_az(sl, c0=c0, lo=lo, hi=hi):
                        for j in range(4):
                            pz = pf.get()
                            proj_tok(sl, 0, 512, c0 + j, pz)
                            ACT(gate_tok[0].t[:, j, :], pz.t[:, :], AF.Silu, [pz.b], [gate_tok[0].b])
                        for j in range(4):
                            c = c0 + j
                            po = pacc
                            kblocks = [cc for cc in (c - 1, c, c + 1) if 0 <= cc < nch]
                            for g in range(2):
                                pr_ = slice(g * 64, (g + 1) * 64)
                                for bi, cc in enumerate(kblocks):
                                    ps_ = pf.get()
                                    ko = (cc - lo) * CH
                                    for i in range(4):
                                        MM(ps_.t[:, i * 128:(i + 1) * 128], kT.t[pr_, ko:ko + CH], qT.t[pr_, i, j * CH:(j + 1) * CH],
                                           True, True, [kT.b, qT.b], [ps_.b])
                                    pt_ = b16r.get()
                                    ACT(pt_.t[:, :], ps_.t[:, :], AF.Exp, [ps_.b], [pt_.b])
                                    if cc != c:
                                        mi_ = 1 if cc < c else 0
                                        TT("pool", pt_.t[:, :], pt_.t[:, :], trib.t[:, mi_, :], ALU.mult, [pt_.b, trib.b], [pt_.b])
                                    for i in range(4):
                                        MM(po[g].t[:, i * 65:(i + 1) * 65], pt_.t[:, i * 128:(i + 1) * 128], vaug.t[:, cc - lo, g, :],
                                           bi == 0 and i == 0, bi == len(kblocks) - 1, [pt_.b, vaug.b], [po[g].b], skip=True)
                            ya = f32r.get(); den = st4.get()
                            for g in range(2):
                                pv3 = po[g].t[:, 0:260].rearrange("p (i e) -> p i e", i=4)
                                TT("dve", den.t[:, g * 4:(g + 1) * 4], pv3[:, :, 64], esink.t[:, g * 4:(g + 1) * 4], ALU.add,
                                   [po[g].b, esink.b], [den.b])
                            S.op("dve", (lambda den: lambda e: e.reciprocal(out=den.t[:, 0:8], in_=den.t[:, 0:8]))(den), [den.b], [den.b])
                            for g in range(2):
                                pv3 = po[g].t[:, 0:260].rearrange("p (i e) -> p i e", i=4)
                                TT("dve", ya.t[:, g * 256:(g + 1) * 256].rearrange("p (i e) -> p i e", i=4), pv3[:, :, 0:64],
                                   den.t[:, g * 4:(g + 1) * 4].unsqueeze(2).to_broadcast([128, 4, 64]), ALU.mult, [po[g].b, den.b], [ya.b])
                            dbg("ya", ya.t[:, :], [ya.b])
                            yg = b16r.get()
                            TT("pool", yg.t[:, :], ya.t[:, :], gate_tok[0].t[:, j, :], ALU.mult, [ya.b, gate_tok[0].b], [yg.b])
                            dbg("yg", yg.t[:, :], [yg.b], BF16)
                            p = pb.get()
                            for i in range(4):
                                TR(p.t[:, i * 128:(i + 1) * 128], yg.t[:, i * 128:(i + 1) * 128], identb.t[:], [yg.b, identb.b], [p.b])
                            CP("act", ybT.t[:, :, j * CH:(j + 1) * CH], p.t[:, 0:512].rearrange("p (i t) -> p i t", i=4), [p.b], [ybT.b])
                    add_stage(l, P_AZ, st_az)
                    add_branch_merge(l, 0, True, c0)

                if en_ml:
                    def st_mq(sl, c0=c0):
                        for h in range(4):
                            pq = pf.get()
                            proj_feat(sl, h * 128, 128, c0, 4, pq)
                            CP("act", mqT.t[:, h, :], pq.t[:, :], [pq.b], [mqT.b])
                    add_stage(l, P_MQ, st_mq)

                    def st_mk(sl, c0=c0):
                        for h in range(4):
                            pk = pf.get()
                            proj_feat(sl, h * 128, 128, c0, 4, pk)
                            S.op("act", (lambda h, pk: lambda e: e.mul(out=mkT.t[:, h, :], in_=pk.t[:, :], mul=128 ** -0.5))(h, pk),
                                 [pk.b], [mkT.b])
                        for j in range(4):
                            p = pb.get()
                            for h in range(4):
                                TR(p.t[:, h * 128:(h + 1) * 128], mkT.t[:, h, j * CH:(j + 1) * CH], identb.t[:], [mkT.b, identb.b], [p.b])
                            CP("dve", mktok.t[:, j, :], p.t[:, 0:512], [p.b], [mktok.b])
                    add_stage(l, P_MK, st_mk)

                    def st_mv(sl, c0=c0):
                        for j in range(4):
                            mlstm_v_tok(sl, c0 + j, j)
                    add_stage(l, P_MV, st_mv)

                    def st_mo(sl, c0=c0):
                        for j in range(4):
                            pz = pf.get()
                            proj_tok(sl, 0, 512, c0 + j, pz)
                            ACT(gate_tok[0].t[:, j, :], pz.t[:, :], AF.Sigmoid, [pz.b], [gate_tok[0].b])
                    add_stage(l, P_MO, st_mo)

                    def st_mz(sl, c0=c0, m=m):
                        for j in range(4):
                            pz = pf.get()
                            proj_tok(sl, 0, 512, c0 + j, pz)
                            ACT(gate_tok[1].t[:, j, :], pz.t[:, :], AF.Silu, [pz.b], [gate_tok[1].b])
                        for j in range(4):
                            c = c0 + j
                            tok = slice(j * CH, (j + 1) * CH)
                            DMA(mCb[1].t[:, :], BSTM[c], [bstm_buf[c]], [mCb[1].b])
                            ps = pf.get()
                            for h in range(4):
                                MM(ps.t[:, h * 128:(h + 1) * 128], mkT.t[:, h, tok], mqT.t[:, h, tok], True, True, [mkT.b, mqT.b], [ps.b])
                            g01 = f32r.get()
                            TT("pool", g01.t[:, :], gate_tok[0].t[:, j, :], gate_tok[1].t[:, j, :], ALU.mult, [gate_tok[0].b], [g01.b])
                            TT("pool", g01.t[:, :], g01.t[:, :], mng_bc.t[:, :], ALU.mult, [g01.b, mng_bc.b], [g01.b])
                            fwd_vw = {}
                            for d in range(2):
                                ts_ap = TSF.t[:, j, :] if d == 0 else TSBm.t[:, j, :]
                                ts_rd = [tsf_b] if d == 0 else [TSBm.b]
                                scm = b16r.get()
                                TT("dve", scm.t[:, :], ps.t[:, :], trib.t[:, d, :], ALU.mult, [ps.b, trib.b], [scm.b])
                                vw = b16r.get(); wkb = b16r.get()
                                if d == 0:
                                    fwd_vw["vw"], fwd_vw["wkb"] = vw, wkb
                                TT("pool", vw.t[:, :].rearrange("p (h e) -> p h e", h=4), mvaug.t[:, j, :, 0:128],
                                   ts_ap[:, 0:4].unsqueeze(2).to_broadcast([128, 4, 128]), ALU.mult, [mvaug.b] + ts_rd, [vw.b])
                                CP("dve", wkb.t[:, 0:4], ts_ap[:, 0:4], ts_rd, [wkb.b])
                                pnum, pint = pacc[0], pacc[1]
                                pden = pf.get()
                                for h in range(4):
                                    hs = slice(h * 128, (h + 1) * 128)
                                    MM(pnum.t[:, hs], scm.t[:, hs], vw.t[:, hs], True, True, [scm.b, vw.b], [pnum.b])
                                    MM(pden.t[:, h:h + 1], scm.t[:, hs], wkb.t[:, h:h + 1], True, True, [scm.b, wkb.b], [pden.b])
                                    MM(pint.t[:, hs], mqT.t[:, h, tok], mCb[d].t[:, hs], True, True, [mqT.b, mCb[d].b], [pint.b])
                                    MM(pden.t[:, 4 + h:5 + h], mqT.t[:, h, tok], mCb[d].t[:, 512 + h:513 + h], True, True,
                                       [mqT.b, mCb[d].b], [pden.b])
                                n1 = f32r.get(); n2 = f32r.get(); dd = st4.get()
                                u_bc = ts_ap[:, 4:8].unsqueeze(2).to_broadcast([128, 4, 128])
                                wi_bc = ts_ap[:, 8:12].unsqueeze(2).to_broadcast([128, 4, 128])
                                TT("dve", n1.t[:, :].rearrange("p (h e) -> p h e", h=4), pnum.t[:, :].rearrange("p (h e) -> p h e", h=4),
                                   u_bc, ALU.mult, [pnum.b] + ts_rd, [n1.b])
                                TT("dve", n2.t[:, :].rearrange("p (h e) -> p h e", h=4), pint.t[:, :].rearrange("p (h e) -> p h e", h=4),
                                   wi_bc, ALU.mult, [pint.b] + ts_rd, [n2.b])
                                TT("pool", n1.t[:, :], n1.t[:, :], n2.t[:, :], ALU.add, [n1.b, n2.b], [n1.b])
                                TT("dve", dd.t[:, 0:4], pden.t[:, 0:4], ts_ap[:, 4:8], ALU.mult, [pden.b] + ts_rd, [dd.b])
                                TT("dve", dd.t[:, 4:8], pden.t[:, 4:8], ts_ap[:, 8:12], ALU.mult, [pden.b] + ts_rd, [dd.b])
                                TT("dve", dd.t[:, 0:4], dd.t[:, 0:4], dd.t[:, 4:8], ALU.add, [dd.b], [dd.b])
                                ACT(dd.t[:, 0:4], dd.t[:, 0:4], AF.Abs, [dd.b], [dd.b])
                                TT("dve", dd.t[:, 0:4], dd.t[:, 0:4], ts_ap[:, 12:16], ALU.max, [dd.b] + ts_rd, [dd.b])
                                S.op("dve", (lambda dd: lambda e: e.reciprocal(out=dd.t[:, 0:4], in_=dd.t[:, 0:4]))(dd), [dd.b], [dd.b])
                                r_bc = dd.t[:, 0:4].unsqueeze(2).to_broadcast([128, 4, 128])
                                if d == 0:
                                    TT("pool", hsum.t[:, :].rearrange("p (h e) -> p h e", h=4), n1.t[:, :].rearrange("p (h e) -> p h e", h=4),
                                       r_bc, ALU.mult, [n1.b, dd.b], [hsum.b])
                                else:
                                    TT("pool", n1.t[:, :].rearrange("p (h e) -> p h e", h=4), n1.t[:, :].rearrange("p (h e) -> p h e", h=4),
                                       r_bc, ALU.mult, [n1.b, dd.b], [n1.b])
                                    TT("pool", hsum.t[:, :], hsum.t[:, :], n1.t[:, :], ALU.add, [hsum.b, n1.b], [hsum.b])
                            sq = f32r.get(); ss = st4.get()
                            ACT(sq.t[:, :], hsum.t[:, :], AF.Square, [hsum.b], [sq.b])
                            S.op("dve", (lambda sq, ss: lambda e: e.reduce_sum(out=ss.t[:, 0:4], in_=sq.t[:, :].rearrange("p (h e) -> p h e", h=4),
                                                                         axis=AX.X))(sq, ss), [sq.b], [ss.b])
                            rstd_from_sumsq(ss.t[:, 4:8], ss.t[:, 0:4], 128, [ss.b], [ss.b])
                            hn = f32r.get()
                            TT("dve", hn.t[:, :].rearrange("p (h e) -> p h e", h=4), hsum.t[:, :].rearrange("p (h e) -> p h e", h=4),
                               ss.t[:, 4:8].unsqueeze(2).to_broadcast([128, 4, 128]), ALU.mult, [hsum.b, ss.b], [hn.b])
                            yg = b16r.get()
                            TT("pool", yg.t[:, :], hn.t[:, :], g01.t[:, :], ALU.mult, [hn.b, g01.b], [yg.b])
                            p = pb.get()
                            for i in range(4):
                                TR(p.t[:, i * 128:(i + 1) * 128], yg.t[:, i * 128:(i + 1) * 128], identb.t[:], [yg.b, identb.b], [p.b])
                            CP("act", ybT.t[:, :, tok], p.t[:, 0:512].rearrange("p (i t) -> p i t", i=4), [p.b], [ybT.b])
                            mlstm_state_update(0, j, TSF.t[:, j, :], [tsf_b], pre=fwd_vw)
                            CP("act", mCb[0].t[:, :], mC[0].t[:, :], [mC[0].b], [mCb[0].b])
                    add_stage(l, P_MZ, st_mz)
                    add_branch_merge(l, 1, not en_att, c0)

                if en_ssd:
                    def st_sx(sl, c0=c0):
                        ssd_conv(l, sl, None, c0, nch, [0, 1, 2, 3])
                    add_stage(l, P_SX, st_sx)

                    def st_sbc(sl, c0=c0):
                        ssd_conv(l, None, sl, c0, nch, [4, 5])
                        for which in range(2):
                            for g in range(2):
                                gs_ = slice(g * 64, (g + 1) * 64)
                                CP("pool", szero.t[gs_, which * 2 + g, :], sconv.t[gs_, 4 + which, :], [sconv.b], [szero.b])
                        for j in range(4):
                            ssd_tokmajor(j)
                    add_stage(l, P_SBC, st_sbc)

                    def st_sz(sl, c0=c0, m=m):
                        for j in range(4):
                            pz = pf.get()
                            proj_tok(sl, 0, 512, c0 + j, pz)
                            ACT(gate_tok[0].t[:, j, :], pz.t[:, :], AF.Silu, [pz.b], [gate_tok[0].b])
                        for j in range(4):
                            if "sz" in KSKIP:
                                break
                            c = c0 + j
                            tok = slice(j * CH, (j + 1) * CH)
                            if "cDma" not in KSKIP:
                                DMA(sSb[1].t[:, :], BSTS[c], [bsts_buf[c]], [sSb[1].b])
                            pcb = pf.get()
                            for g in range(2):
                                if "cPcb" in KSKIP:
                                    break
                                gs_ = slice(g * 64, (g + 1) * 64)
                                MM(pcb.t[:, g * 128:(g + 1) * 128], szero.t[:, g, tok], sconv.t[:, 5, tok], True, True, [sconv.b, szero.b], [pcb.b])
                            py = pf.get()
                            toff = []
                            for d in range(2):
                                ts_ap = TSF.t[:, j, :] if d == 0 else TSBm.t[:, j, :]
                                ts_rd = [tsf_b] if d == 0 else [TSBm.b]
                                acs_hi, acs_lo = acs_hl[d]
                                if d == 0:
                                    cbm = f32r.get()
                                    CP("act", cbm.t[:, 0:256], pcb.t[:, 0:256], [pcb.b], [cbm.b])
                                for h in range(8):
                                    if "cSel" in KSKIP:
                                        break
                                    pe_ = pacc[h // 4]
                                    hb_ = slice((h % 4) * 128, (h % 4 + 1) * 128)
                                    MM(pe_.t[:, hb_], selb.t[sD, h, :], acs_hi.t[sD, tok], h % 4 == 0, False, [selb.b, acs_hi.b], [pe_.b], skip=True)
                                    MM(pe_.t[:, hb_], selb.t[sD, h, :], acs_lo.t[sD, tok], False, False, [selb.b, acs_lo.b], [pe_.b], skip=True)
                                    MM(pe_.t[:, hb_], identb.t[:], negm.t[:, d, :], False, True, [identb.b, negm.b], [pe_.b], skip=True)
                                mts = []
                                if "cA" in KSKIP:
                                    continue
                                for g in range(2):
                                    lt = f32r.get()
                                    for e_ in range(4):
                                        h = g * 4 + e_
                                        ACT(lt.t[:, e_ * 128:(e_ + 1) * 128], pacc[g].t[:, e_ * 128:(e_ + 1) * 128], AF.Exp, [pacc[g].b] + ts_rd, [lt.b],
                                            bias=ts_ap[:, 20 + h:21 + h])
                                    mt = b16r.get()
                                    TT("dve" if g == 0 else "pool", mt.t[:, :].rearrange("p (h e) -> p h e", h=4), lt.t[:, :].rearrange("p (h e) -> p h e", h=4),
                                       cbm.t[:, g * 128:(g + 1) * 128].unsqueeze(1).to_broadcast([128, 4, 128]), ALU.mult, [lt.b, cbm.b], [mt.b])
                                    mts.append(mt)
                                if "cC" in KSKIP:
                                    continue
                                poff = pf.get()
                                for g in range(2):
                                    gs_ = slice(g * 64, (g + 1) * 64)
                                    MM(poff.t[:, g * 256:(g + 1) * 256], szero.t[:, 2 + g, tok], sSb[d].t[:, :], True, True, [szero.b, sSb[d].b], [poff.b])
                                for h in range(8):
                                    MM(py.t[:, h * 64:(h + 1) * 64], mts[h // 4].t[:, (h % 4) * 128:(h % 4 + 1) * 128], sxtok.t[:, j, h * 64:(h + 1) * 64],
                                       d == 0 and h == 0, d == 1, [mts[h // 4].b, sxtok.b], [py.b], skip=True)
                                to = f32r.get()
                                TT("dve", to.t[:, :].rearrange("p (h e) -> p h e", h=8), poff.t[:, :].rearrange("p (h e) -> p h e", h=8),
                                   ts_ap[:, 36:44].unsqueeze(2).to_broadcast([128, 8, 64]), ALU.mult, [poff.b] + ts_rd, [to.b])
                                toff.append(to)
                            if "cA" in KSKIP or "cC" in KSKIP or "cD" in KSKIP:
                                continue
                            y = f32r.get()
                            TT("dve", y.t[:, :], py.t[:, :], toff[0].t[:, :], ALU.add, [py.b, toff[0].b], [y.b])
                            TT("pool", y.t[:, :], y.t[:, :], toff[1].t[:, :], ALU.add, [y.b, toff[1].b], [y.b])
                            xs = toff[0]
                            TT("pool", xs.t[:, :].rearrange("p (h e) -> p h e", h=8), sxtok.t[:, j, :].rearrange("p (h e) -> p h e", h=8),
                               dsk_bc.t[:, 0:8].unsqueeze(2).to_broadcast([128, 8, 64]), ALU.mult, [sxtok.b, dsk_bc.b], [xs.b])
                            TT("pool", y.t[:, :], y.t[:, :], xs.t[:, :], ALU.add, [y.b, xs.b], [y.b])
                            TT("pool", y.t[:, :], y.t[:, :], gate_tok[0].t[:, j, :], ALU.mult, [y.b, gate_tok[0].b], [y.b])
                            sq = toff[1]; ss = st4.get()
                            ACT(sq.t[:, :], y.t[:, :], AF.Square, [y.b], [sq.b, ss.b], accum=ss.t[:, 0:1])
                            rstd_from_sumsq(ss.t[:, 1:2], ss.t[:, 0:1], 512, [ss.b], [ss.b])
                            yg = b16r.get()
                            STT("dve", yg.t[:, :], y.t[:, :], ss.t[:, 1:2], sng_bc.t[:, :], ALU.mult, ALU.mult, [y.b, ss.b, sng_bc.b], [yg.b])
                            p = pb.get()
                            for i in range(4):
                                TR(p.t[:, i * 128:(i + 1) * 128], yg.t[:, i * 128:(i + 1) * 128], identb.t[:], [yg.b, identb.b], [p.b])
                            CP("act", ybT.t[:, :, tok], p.t[:, 0:512].rearrange("p (i t) -> p i t", i=4), [p.b], [ybT.b])
                            if "cE" in KSKIP:
                                continue
                            ssd_state_update(0, j, TSF.t[:, j, :], [tsf_b])
                            CP("act", sSb[0].t[:, :], sS[0].t[:, :], [sS[0].b], [sSb[0].b])
                    add_stage(l, P_SZ, st_sz)
                    add_branch_merge(l, 2, not (en_att or en_ml), c0)

                if not (en_att or en_ml or en_ssd):
                    def st_zero(_sl):
                        S.op("dve", lambda e: e.memset(mergedT.t[:], 0.0), [], [mergedT.b])
                    add_stage(l, None, st_zero)
                add_final(l, tok0, c0, nch)

        n_mt_layer = sum(sl_ // TM for sl_ in seq_lens)
        casts_per_mt = 0
        if depth > 1:
            casts_per_mt = max(len(v_) for v_ in pending_casts.values()) // max(1, n_mt_layer - 1) + 1

        def flush_casts(l):
            def f(_sl):
                while pending_casts[l]:
                    pending_casts[l].pop(0)()
            return f
        for l in range(depth):
            if l >= 1:
                add_stage(l, None, flush_casts(l))
            add_stage(l, None, (lambda l: lambda _sl: load_layer_params(l))(l))
            pos = 0
            for slen in seq_lens:
                seq_layer(l, pos, slen)
                pos += slen
        run_stages()
        import os as _os
        S.schedule(window=int(_os.environ.get('KWIN', '24')), enable=_os.environ.get('KSCHED', '1') == '1')
        S._plan()
        print('sched ok:', S.check(), {e: len(S.order[e]) for e in S.ENGS}, 'est_us', getattr(S, 'est_time', 0) / 1e3, 'busy_us', {e: round(v / 1e3) for e, v in getattr(S, 'busy', {}).items()})
        if S.tagging:
            for kk_, v_ in sorted(S.gaps.items(), key=lambda x: -x[1])[:40]:
                print('GAP', kk_, round(v_ / 1e3, 1))
        S.emit()
    return nc


_CACHE = {}


def _run(seq_lens, depth, per_core_x, weights, enable=("att", "mlstm", "ssd"), debug=(), full=False):
    key = (tuple(seq_lens), depth, tuple(enable), tuple(debug))
    if key not in _CACHE:
        _CACHE[key] = build(list(seq_lens), depth, enable, debug)
    nc = _CACHE[key]
    consts = make_consts(max(seq_lens))
    in_maps = []
    for xc in per_core_x:
        m = {"x": np.ascontiguousarray(xc, dtype=np.float32)}
        for k, v in weights.items():
            m[k] = np.ascontiguousarray(v, dtype=np.float32)
        m.update(consts)
        in_maps.append(m)
    res = run_bass_kernel_spmd(nc, in_maps, core_ids=list(range(len(per_core_x))))
    if full:
        return res.results
    return [r["y"] for r in res.results]


def kernel(x_prompt, x_sample, norm_g, w_in, q_norm_g, k_norm_g, attn_sink, w_att_out,
           mlstm_i_b, mlstm_f_b, mlstm_norm_g, w_mlstm_out, conv_w, conv_b, a_log,
           dt_bias, d_skip, ssm_norm_g, w_ssm_out, w_out):
    x_prompt = np.asarray(x_prompt, dtype=np.float32)
    x_sample = np.asarray(x_sample, dtype=np.float32)
    weights = dict(norm_g=norm_g, w_in=w_in, q_norm_g=q_norm_g, k_norm_g=k_norm_g, attn_sink=attn_sink,
                   w_att_out=w_att_out, mlstm_i_b=mlstm_i_b, mlstm_f_b=mlstm_f_b, mlstm_norm_g=mlstm_norm_g,
                   w_mlstm_out=w_mlstm_out, conv_w=conv_w, conv_b=conv_b, a_log=a_log, dt_bias=dt_bias,
                   d_skip=d_skip, ssm_norm_g=ssm_norm_g, w_ssm_out=w_ssm_out, w_out=w_out)
    weights = {k: np.asarray(v, dtype=np.float32) for k, v in weights.items()}
    depth = weights["w_in"].shape[0]
    nb, sp = x_prompt.shape[0], x_prompt.shape[1]
    ns, ss = x_sample.shape[0], x_sample.shape[1]
    ppc, spc = nb // NCORES, ns // NCORES
    seq_lens = [sp] * ppc + [ss] * spc
    per_core = []
    for c in range(NCORES):
        parts = [x_prompt[c * ppc + i] for i in range(ppc)] + [x_sample[c * spc + i] for i in range(spc)]
        per_core.append(np.concatenate(parts, axis=0))
    outs = _run(seq_lens, depth, per_core, weights)
    y_prompt = np.empty_like(x_prompt)
    y_sample = np.empty_like(x_sample)
    for c in range(NCORES):
        o = outs[c]
        pos = 0
        for i in range(ppc):
            y_prompt[c * ppc + i] = o[pos:pos + sp]; pos += sp
        for i in range(spc):
            y_sample[c * spc + i] = o[pos:pos + ss]; pos += ss
    return (y_prompt, y_sample)
```

```python
import contextlib
import math
import numpy as np
import concourse.bass as bass
import concourse.mybir as mybir
from concourse.bass_utils import run_bass_kernel_spmd

F32 = mybir.dt.float32
BF16 = mybir.dt.bfloat16
AF = mybir.ActivationFunctionType
ALU = mybir.AluOpType
AX = mybir.AxisListType

D = 1024
NCORES = 8
TM = 512
CH = 128
NPIECE = 25
ROPE_THETA = 500000.0
EPS = 1e-6


class Buf:
    __slots__ = ("name", "last_w", "readers")

    def __init__(self, name=""):
        self.name = name
        self.last_w = None
        self.readers = []


class Sched:
    ENGS = ("pe", "act", "dve", "pool", "sp")
    NPOOL = 4

    def __init__(self, nc, n_dma_sems=24):
        self.nc = nc
        self.nodes = []
        self.n_dma_sems = n_dma_sems
        self.order = None
        import os as _os2
        self.tagging = bool(_os2.environ.get('KTAG'))
        self.gaps = {}

    def _add(self, eng, fn, reads, writes, dma, cost):
        deps = set()
        for b in reads:
            if b.last_w is not None:
                deps.add(b.last_w)
        for b in writes:
            if b.last_w is not None:
                deps.add(b.last_w)
            deps.update(b.readers)
        gid = len(self.nodes)
        tag = ""
        if self.tagging:
            import sys as _sys
            f = _sys._getframe(2)
            names = []
            while f is not None and len(names) < 3:
                nm = f.f_code.co_name
                if nm not in ("op", "dma", "ACT", "TT", "TSC", "STT", "CP", "MM", "TR", "DMA", "SCAN", "RECIP", "<lambda>", "proj_feat", "proj_tok"):
                    names.append(nm)
                f = f.f_back
            tag = "/".join(names[:2])
        self.nodes.append(dict(eng=eng, fn=fn, dma=dma, deps=deps, cost=cost, tag=tag))
        for b in reads:
            b.readers.append(gid)
        for b in writes:
            b.last_w = gid
            b.readers = []
        return gid

    def op(self, eng, fn, reads=(), writes=(), cost=300.0):
        return self._add(eng, fn, reads, writes, False, cost)

    def dma(self, eng, fn, reads=(), writes=(), cost=3000.0):
        return self._add(eng, fn, reads, writes, True, cost)

    def schedule(self, window=24, enable=True):
        nodes = self.nodes
        per = {e: [] for e in self.ENGS}
        for g, n in enumerate(nodes):
            per[n["eng"]].append(g)
        if not enable:
            self.order = per
            return
        ptr = {e: 0 for e in self.ENGS}
        done = {e: [False] * len(per[e]) for e in self.ENGS}
        finish = [None] * len(nodes)
        t_eng = {e: 0.0 for e in self.ENGS}
        new = {e: [] for e in self.ENGS}
        remaining = len(nodes)
        LAT = 250.0
        W = {e: window for e in self.ENGS}
        W["sp"] = 6
        while remaining:
            best = None
            for e in self.ENGS:
                lst = per[e]
                p = ptr[e]
                while p < len(lst) and done[e][p]:
                    p += 1
                ptr[e] = p
                if p >= len(lst):
                    continue
                cnt = 0
                q = p
                cand = None
                while q < len(lst) and cnt < W[e]:
                    if not done[e][q]:
                        cnt += 1
                        g = lst[q]
                        nd = nodes[g]
                        ready = t_eng[e]
                        ok = True
                        for d in nd["deps"]:
                            f = finish[d]
                            if f is None:
                                ok = False
                                break
                            if nodes[d]["eng"] != e or nodes[d]["dma"]:
                                f += LAT
                            if f > ready:
                                ready = f
                        if ok:
                            key = (ready, q)
                            if cand is None or key < cand[0]:
                                cand = (key, q, g, ready)
                            if ready <= t_eng[e]:
                                break
                    q += 1
                if cand is not None:
                    if best is None or (cand[3], cand[2]) < (best[3], best[2]):
                        best = (e, cand[1], cand[2], cand[3])
            assert best is not None, "scheduler stuck"
            e, q, g, start = best
            nd = nodes[g]
            if self.tagging and e == "pe" and start > t_eng[e] + 500.0:
                dmax = max(nd["deps"], key=lambda d: finish[d])
                key = (nd["tag"], nodes[dmax]["eng"], nodes[dmax]["tag"])
                self.gaps[key] = self.gaps.get(key, 0.0) + (start - t_eng[e])
            if nd["dma"]:
                finish[g] = start + nd["cost"]
                t_eng[e] = start + 60.0
            else:
                finish[g] = start + nd["cost"]
                t_eng[e] = finish[g]
            done[e][q] = True
            new[e].append(g)
            remaining -= 1
        self.order = new
        self.est_time = max(f for f in finish if f is not None)
        self.busy = {e: sum(nodes[g]['cost'] for g in new[e] if not nodes[g]['dma']) for e in self.ENGS}

    def _plan(self):
        nodes = self.nodes
        pos = {}
        for e in self.ENGS:
            for i, g in enumerate(self.order[e]):
                pos[g] = i
        npool = self.NPOOL
        nsp = self.n_dma_sems - npool
        rr = {"sp": 0, "pool": 0}
        dma_val = [0] * self.n_dma_sems
        sem_of = {}
        for e in self.ENGS:
            for g in self.order[e]:
                if nodes[g]["dma"]:
                    if e == "pool":
                        s = nsp + rr["pool"]; rr["pool"] = (rr["pool"] + 1) % npool
                    else:
                        s = rr["sp"]; rr["sp"] = (rr["sp"] + 1) % nsp
                    prev = dma_val[s]
                    dma_val[s] += 16
                    sem_of[g] = (s, dma_val[s], prev)
        self.dma_val = dma_val
        flag = [False] * len(nodes)
        plan = {e: [] for e in self.ENGS}
        for e in self.ENGS:
            waited_c = {}
            waited_d = {}
            for g in self.order[e]:
                nd = nodes[g]
                waits = []
                deps = list(nd["deps"])
                for d in deps:
                    dn = nodes[d]
                    if dn["dma"]:
                        s, v, _ = sem_of[d]
                        if waited_d.get(s, 0) < v:
                            waited_d[s] = v
                            waits.append(("d", s, v))
                    else:
                        pe_ = dn["eng"]
                        if pe_ == e and e in ("pe", "sp"):
                            continue
                        if waited_c.get(pe_, -1) < pos[d]:
                            waited_c[pe_] = pos[d]
                            flag[d] = True
                            waits.append(("c", pe_, d))
                if nd["dma"]:
                    s, v, prev = sem_of[g]
                    if prev > 0 and waited_d.get(s, 0) < prev:
                        waited_d[s] = prev
                        waits.append(("d", s, prev))
                plan[e].append((g, waits))
        counts = {}
        for e in self.ENGS:
            c = 0
            for g in self.order[e]:
                if (not nodes[g]["dma"]) and flag[g]:
                    c += 1
                counts[g] = c
        self.plan, self.flag, self.counts, self.sem_of = plan, flag, counts, sem_of

    def check(self):
        nodes = self.nodes
        ptr = {e: 0 for e in self.ENGS}
        csem = {e: 0 for e in self.ENGS}
        dsem = [0] * self.n_dma_sems
        while True:
            prog = False
            for e in self.ENGS:
                pl = self.plan[e]
                while ptr[e] < len(pl):
                    g, waits = pl[ptr[e]]
                    ok = True
                    for w in waits:
                        if w[0] == "c":
                            if csem[w[1]] < self.counts[w[2]]:
                                ok = False
                        elif dsem[w[1]] < w[2]:
                            ok = False
                    if not ok:
                        break
                    if nodes[g]["dma"]:
                        dsem[self.sem_of[g][0]] += 16
                    elif self.flag[g]:
                        csem[e] += 1
                    ptr[e] += 1
                    prog = True
            if all(ptr[e] == len(self.plan[e]) for e in self.ENGS):
                return True
            if not prog:
                for e in self.ENGS:
                    if ptr[e] < len(self.plan[e]):
                        print("STUCK", e, ptr[e], "/", len(self.plan[e]), self.plan[e][ptr[e]][1])
                return False

    def emit(self, final_wait_eng="sp"):
        nc = self.nc
        nodes = self.nodes
        with contextlib.ExitStack() as st:
            csem = {e: st.enter_context(nc.semaphore("cs_" + e)) for e in self.ENGS}
            dsem = [st.enter_context(nc.semaphore("ds_%d" % i)) for i in range(self.n_dma_sems)]
            final = [(i, v) for i, v in enumerate(self.dma_val) if v > 0]
            block = st.enter_context(nc.Block())
            engobj = {"pe": "tensor", "act": "scalar", "dve": "vector", "pool": "gpsimd", "sp": "sync"}

            def make(e):
                def body(eng):
                    for g, waits in self.plan[e]:
                        for w in waits:
                            if w[0] == "c":
                                eng.wait_ge(csem[w[1]], self.counts[w[2]])
                            else:
                                eng.wait_ge(dsem[w[1]], w[2])
                        ins = nodes[g]["fn"](eng)
                        if nodes[g]["dma"]:
                            ins.then_inc(dsem[self.sem_of[g][0]], 16)
                        elif self.flag[g]:
                            ins.then_inc(csem[e], 1)
                    if e == final_wait_eng:
                        for (i, v) in final:
                            eng.wait_ge(dsem[i], v)
                return body

            for e in self.ENGS:
                if self.plan[e] or e == final_wait_eng:
                    getattr(block, engobj[e])(make(e))


class Tl:
    __slots__ = ("t", "b")

    def __init__(self, t, name=""):
        self.t = t
        self.b = Buf(name)


def make_consts(smax):
    c = {}
    c["c_ident"] = np.eye(128, dtype=np.float32)
    s = np.arange(128)[:, None]
    t = np.arange(128)[None, :]
    tri = np.zeros((128, 2, 512), np.float32)
    tri[:, 0, :] = np.tile((s <= t).astype(np.float32), (1, 4))
    tri[:, 1, :] = np.tile((s >= t).astype(np.float32), (1, 4))
    c["c_tri"] = tri
    f = np.arange(128)
    d = f % 64
    pos = np.arange(smax, dtype=np.float32)
    rope = np.zeros((128, 2, smax), np.float32)
    rope[:, 0, :] = 1.0
    inv_freq = (ROPE_THETA ** (-np.arange(8, dtype=np.float32) * 2.0 / 16.0)).astype(np.float32)
    for ff in range(128):
        if d[ff] < 16:
            ang = pos * inv_freq[d[ff] % 8]
            rope[ff, 0, :] = np.cos(ang)
            rope[ff, 1, :] = np.sin(ang)
    c["c_rope"] = rope
    rot = np.zeros((128, 128), np.float32)
    for ff in range(128):
        if d[ff] < 8:
            rot[ff + 8, ff] = -1.0
        elif d[ff] < 16:
            rot[ff - 8, ff] = 1.0
    c["c_rot"] = rot
    blk = np.zeros((128, 128), np.float32)
    blk[:64, :64] = 1.0 / 64
    blk[64:, 64:] = 1.0 / 64
    c["c_blk"] = blk
    sel = np.zeros((8, 8, 128), np.float32)
    for h in range(8):
        sel[h, h, :] = 1.0
    c["c_sel"] = sel
    rst = np.ones((8, 512), np.float32)
    rst[:, ::128] = 0.0
    c["c_reset"] = rst
    return c


O_AQ, O_AK, O_AV, O_AZ = 0, 512, 640, 768
O_MQ, O_MK, O_MV, O_MO = 1280, 1792, 2304, 2816
O_MI, O_MF, O_MZ = 3328, 3336, 3344
O_SX, O_SB, O_SC, O_SDT, O_SZ = 3856, 4368, 4496, 4624, 4640
O_G = 5152
P_AQ, P_KV, P_AZ, P_MQ, P_MK, P_MV, P_MO, P_MZ, P_SX, P_SBC, P_SZ = range(11)
P_G0 = 11
P_WO0, P_WO1 = 23, 24


def build(seq_lens, depth, enable=("att", "mlstm", "ssd"), debug=()):
    ntok = sum(seq_lens)
    smax = max(seq_lens)
    nc = bass.Bass("TRN2", target_bir_lowering=False)
    S = Sched(nc)
    es = contextlib.ExitStack()

    def din(name, shape, dt=F32):
        return nc.dram_tensor(name, list(shape), dt, kind="ExternalInput").ap()

    x_in = din("x", [ntok, D])
    y_out = nc.dram_tensor("y", [ntok, D], F32, kind="ExternalOutput").ap()
    W = dict(
        norm_g=din("norm_g", [depth, D]), w_in=din("w_in", [depth, D, 8224]),
        q_norm_g=din("q_norm_g", [depth, 64]), k_norm_g=din("k_norm_g", [depth, 64]),
        attn_sink=din("attn_sink", [depth, 8]), w_att_out=din("w_att_out", [depth, 512, D]),
        mlstm_i_b=din("mlstm_i_b", [depth, 2, 4]), mlstm_f_b=din("mlstm_f_b", [depth, 2, 4]),
        mlstm_norm_g=din("mlstm_norm_g", [depth, 512]), w_mlstm_out=din("w_mlstm_out", [depth, 512, D]),
        conv_w=din("conv_w", [depth, 5, 768]), conv_b=din("conv_b", [depth, 768]),
        a_log=din("a_log", [depth, 2, 8]), dt_bias=din("dt_bias", [depth, 2, 8]),
        d_skip=din("d_skip", [depth, 8]), ssm_norm_g=din("ssm_norm_g", [depth, 512]),
        w_ssm_out=din("w_ssm_out", [depth, 512, D]), w_out=din("w_out", [depth, D, D]),
    )
    C = dict(c_ident=din("c_ident", [128, 128]), c_tri=din("c_tri", [128, 2, 512]),
             c_rope=din("c_rope", [128, 2, smax]), c_rot=din("c_rot", [128, 128]),
             c_blk=din("c_blk", [128, 128]), c_sel=din("c_sel", [8, 8, 128]),
             c_reset=din("c_reset", [8, 512]))
    WS = nc.dram_tensor("ws_bf16", [depth, NPIECE, 128, 4096], BF16, kind="Internal").ap()
    ws_buf = [[[] for p in range(NPIECE)] for l in range(depth)]
    nscr = max(1, min(2, depth - 1))
    YS = [nc.dram_tensor("yscr%d" % i, [ntok, D], F32, kind="Internal").ap() for i in range(nscr)]
    ys_buf = [[[Buf("yscr"), Buf("yscr")] for c in range(ntok // CH)] for i in range(nscr)]
    nchmax = smax // CH
    BSTM = nc.dram_tensor("bst_m", [nchmax, 128, 516], BF16, kind="Internal").ap()
    BSTS = nc.dram_tensor("bst_s", [nchmax, 128, 256], BF16, kind="Internal").ap()
    bstm_buf = [Buf("bstm%d" % i) for i in range(nchmax)]
    bsts_buf = [Buf("bsts%d" % i) for i in range(nchmax)]

    dbg_done = {}
    import os
    KSKIP = set(os.environ.get("KSKIP", "").split(","))

    def dbg(name, ap, bufs, dt=F32):
        if name not in debug or name in dbg_done:
            return
        dbg_done[name] = True
        shp = list(ap.shape)
        o = nc.dram_tensor("dbg_" + name, shp, dt, kind="ExternalOutput").ap()
        S.dma("sp", lambda e: e.dma_start(out=o, in_=ap), bufs, [])

    def sb(name, shape, dt=F32):
        return Tl(es.enter_context(nc.sbuf_tensor(name, list(shape), dt)), name)

    def psum(name, shape, dt=F32):
        return Tl(es.enter_context(nc.psum_tensor(name, list(shape), dt)), name)

    def fsz(ap):
        n = 1
        for s_ in ap.shape[1:]:
            n *= s_
        return n

    def ecost(eng, ap, mult=1.0):
        n = fsz(ap)
        if eng == "act":
            return 220.0 + 0.85 * n
        if eng == "dve":
            return 60.0 + 1.3 * n * mult
        return 100.0 + 2.6 * n * mult

    def ACT(out, in_, func, rd, wr, bias=None, scale=None, accum=None):
        kw = {}
        if bias is not None:
            kw["bias"] = bias
        if scale is not None:
            kw["scale"] = scale
        if accum is not None:
            kw["accum_out"] = accum
        S.op("act", lambda e: e.activation(out=out, in_=in_, func=func, **kw), rd, wr, cost=ecost("act", out))

    def TT(eng, out, in0, in1, op, rd, wr):
        S.op(eng, lambda e: e.tensor_tensor(out=out, in0=in0, in1=in1, op=op), rd, wr, cost=ecost(eng, out))

    def TSC(eng, out, in0, s1, op0, rd, wr, s2=None, op1=None):
        if op1 is None:
            S.op(eng, lambda e: e.tensor_scalar(out=out, in0=in0, scalar1=s1, scalar2=None, op0=op0), rd, wr, cost=ecost(eng, out))
        else:
            S.op(eng, lambda e: e.tensor_scalar(out=out, in0=in0, scalar1=s1, scalar2=s2, op0=op0, op1=op1), rd, wr, cost=ecost(eng, out))

    def STT(eng, out, in0, scalar, in1, op0, op1, rd, wr):
        S.op(eng, lambda e: e.scalar_tensor_tensor(out=out, in0=in0, scalar=scalar, in1=in1, op0=op0, op1=op1), rd, wr,
             cost=ecost(eng, out))

    def CP(eng, out, in_, rd, wr):
        if eng == "act":
            S.op("act", lambda e: e.copy(out=out, in_=in_), rd, wr, cost=ecost("act", out))
        else:
            S.op(eng, lambda e: e.tensor_copy(out=out, in_=in_), rd, wr, cost=ecost(eng, out, 1.4 if eng == "pool" else 1.0))

    def RECIP(out, in_, rd, wr):
        S.op("dve", lambda e: e.reciprocal(out=out, in_=in_), rd, wr, cost=100.0 + 6.6 * fsz(out))

    def MM(out, lhsT, rhs, start, stop, rd, wr, skip=False):
        n = max(fsz(out), 32)
        passes = 4 if lhsT.dtype == F32 else 1
        c = 25.0 + 0.5 * n * passes
        if skip:
            S.op("pe", lambda e: e.matmul(out, lhsT, rhs, start=start, stop=stop, skip_group_check=True), rd, wr, cost=c)
        else:
            S.op("pe", lambda e: e.matmul(out, lhsT, rhs, start=start, stop=stop), rd, wr, cost=c)

    def TR(out, in_, ident, rd, wr):
        S.op("pe", lambda e: e.transpose(out, in_, ident), rd, wr, cost=90.0)

    def DMA(out, in_, rd, wr, eng="sp", slow=False):
        nbytes = out.shape[0] * fsz(out) * (2 if out.dtype == BF16 else 4)
        c = 2500.0 + nbytes / (40.0 if eng == "pool" else 120.0)
        if slow:
            S.dma(eng, lambda e: e.dma_start(out=out, in_=in_, allow_slow_non_contiguous=True), rd, wr, cost=c)
        else:
            S.dma(eng, lambda e: e.dma_start(out=out, in_=in_), rd, wr, cost=c)

    def SCAN(out, d0, d1, init, op0, op1, rd, wr):
        S.op("dve", lambda e: e.tensor_tensor_scan(out=out, data0=d0, data1=d1, initial=init, op0=op0, op1=op1), rd, wr,
             cost=100.0 + 2.0 * fsz(out))

    class Ring:
        def __init__(self, tiles):
            self.tiles = tiles
            self.i = 0

        def get(self):
            t = self.tiles[self.i]
            self.i = (self.i + 1) % len(self.tiles)
            return t

    with es:
        f32r = Ring([sb("f32r%d" % i, [128, 512]) for i in range(7)])
        b16r = Ring([sb("b16r%d" % i, [128, 512], BF16) for i in range(8)])
        identf = sb("identf", [128, 128])
        identb = sb("identb", [128, 128], BF16)
        trib = sb("trib", [128, 2, 512], BF16)
        rotb = sb("rotb", [128, 128], BF16)
        blkb = sb("blkb", [128, 128], BF16)
        self_ = sb("sel", [40, 8, 128])
        resetm = sb("resetm", [40, 512])
        zrow = sb("zrow", [8, 1])
        DMA(identf.t[:], C["c_ident"][:, :], [], [identf.b])
        CP("dve", identb.t[:], identf.t[:], [identf.b], [identb.b])
        for half in range(2):
            stg = f32r.get()
            DMA(stg.t[:, :], C["c_tri"][:, half, :], [], [stg.b])
            CP("dve", trib.t[:, half, :], stg.t[:, :], [stg.b], [trib.b])
        stg = f32r.get()
        DMA(stg.t[:, 0:128], C["c_rot"][:, :], [], [stg.b])
        CP("dve", rotb.t[:], stg.t[:, 0:128], [stg.b], [rotb.b])
        stg = f32r.get()
        DMA(stg.t[:, 0:128], C["c_blk"][:, :], [], [stg.b])
        CP("dve", blkb.t[:], stg.t[:, 0:128], [stg.b], [blkb.b])
        DMA(self_.t[32:40, :, :], C["c_sel"][:, :, :], [], [self_.b])
        DMA(resetm.t[32:40, :], C["c_reset"][:, :], [], [resetm.b])
        S.op("dve", lambda e: e.memset(zrow.t[:], 0.0), [], [zrow.b])
        epsc = sb("epsc", [128, 1])
        S.op("dve", lambda e: e.memset(epsc.t[:], EPS), [], [epsc.b])
        negm = sb("negm", [128, 2, 128], BF16)
        TSC("dve", negm.t[:, :, :], trib.t[:, :, 0:128], -1.0, ALU.add, [trib.b], [negm.b], s2=30000.0, op1=ALU.mult)

        pending_casts = {l: [] for l in range(depth)}
        _DMA_real = DMA

        def DMA(out, in_, rd, wr, eng="sp", slow=False, _defer=[None]):
            if _defer[0] is not None and eng == "pool":
                pending_casts[_defer[0]].append(lambda: _DMA_real(out, in_, rd, wr, eng=eng, slow=slow))
            else:
                _DMA_real(out, in_, rd, wr, eng=eng, slow=slow)
        _defer_box = DMA.__defaults__[2]
        for l in range(depth):
            _defer_box[0] = l if l >= 1 else None
            wi = W["w_in"][l].rearrange("(k p) c -> p k c", p=128)

            def wdst(p, off, n, cw=512, l=l):
                return WS[l, p].rearrange("p (k c) -> p k c", c=cw)[:, :, off:off + n]

            def cast(p, off, c0, n, l=l, wi=wi):
                b_ = Buf("ws"); ws_buf[l][p].append(b_)
                DMA(wdst(p, off, n), wi[:, :, c0:c0 + n], [], [b_], eng="pool")

            for i in range(4):
                cast(P_AQ, i * 128, O_AQ + i * 64, 64)
                cast(P_AQ, i * 128 + 64, O_AQ + (4 + i) * 64, 64)
            cast(P_KV, 0, O_AK, 256)
            for d_ in range(2):
                base_ = 256 + d_ * 72
                b0_ = Buf("ws"); ws_buf[l][P_KV].append(b0_)
                DMA(wdst(P_KV, base_, 72), wi[:, :, O_MI:O_MI + 72], [], [b0_], eng="pool")
                for (off_, c0_, n_) in ((0, O_MI + d_ * 4, 4), (32, O_MF + d_ * 4, 4), (64, O_SDT + d_ * 8, 8)):
                    b_ = Buf("ws"); ws_buf[l][P_KV].append(b_)
                    DMA(wdst(P_KV, base_ + off_, n_), wi[:, :, c0_:c0_ + n_], [b0_], [b_], eng="pool")
            cast(P_AZ, 0, O_AZ, 512)
            cast(P_MQ, 0, O_MQ, 512)
            cast(P_MK, 0, O_MK, 512)
            cast(P_MV, 0, O_MV, 512)
            cast(P_MO, 0, O_MO, 512)
            cast(P_MZ, 0, O_MZ, 512)
            cast(P_SX, 0, O_SX, 512)
            cast(P_SBC, 0, O_SB, 256)
            cast(P_SZ, 0, O_SZ, 512)
            for bi_, nm in enumerate(("w_att_out", "w_mlstm_out", "w_ssm_out")):
                src = W[nm][l].rearrange("(k p) c -> p k c", p=128)
                for q_ in range(4):
                    p = P_G0 + bi_ * 4 + q_
                    b_ = Buf("ws"); ws_buf[l][p].append(b_)
                    DMA(WS[l, p][:, 0:2048].rearrange("p (k c) -> p k c", c=256),
                        wi[:, :, O_G + bi_ * 1024 + q_ * 256:O_G + bi_ * 1024 + (q_ + 1) * 256], [], [b_], eng="pool")
                    b_ = Buf("ws"); ws_buf[l][p].append(b_)
                    DMA(WS[l, p][:, 2048:3072].rearrange("p (k c) -> p k c", c=256), src[:, :, q_ * 256:(q_ + 1) * 256], [], [b_], eng="pool")
            wo = W["w_out"][l].rearrange("(k p) c -> p k c", p=128)
            for p_, lo_ in ((P_WO0, 0), (P_WO1, 512)):
                b_ = Buf("ws"); ws_buf[l][p_].append(b_)
                DMA(wdst(p_, 0, 512), wo[:, :, lo_:lo_ + 512], [], [b_], eng="pool")

        _defer_box[0] = None
        NSLOT = 3
        wslots = [sb("wslot%d" % i, [128, 4096], BF16) for i in range(NSLOT)]

        ng_bc = sb("ng_bc", [128, D])
        gq = sb("gq", [128, 1]); gk = sb("gk", [128, 1])
        esink = sb("esink", [128, 8])
        ib = [sb("ib%d" % d, [4, 1]) for d in range(2)]
        nfb = [sb("nfb%d" % d, [4, 1]) for d in range(2)]
        mng_bc = sb("mng_bc", [128, 512]); sng_bc = sb("sng_bc", [128, 512])
        cw = sb("cw", [128, 6, 5]); cb = sb("cb", [128, 6])
        acoef = [sb("acoef%d" % d, [40, 1]) for d in range(2)]
        dtb = [sb("dtb%d" % d, [40, 1]) for d in range(2)]
        dsk_bc = sb("dsk_bc", [128, 8])

        def load_layer_params(l):
            DMA(ng_bc.t[:], W["norm_g"][l].partition_broadcast(128), [], [ng_bc.b])
            for half in range(2):
                DMA(gq.t[half * 64:(half + 1) * 64, :], W["q_norm_g"][l].rearrange("(d o) -> d o", o=1), [], [gq.b])
                DMA(gk.t[half * 64:(half + 1) * 64, :], W["k_norm_g"][l].rearrange("(d o) -> d o", o=1), [], [gk.b])
            S.op("act", lambda e: e.mul(out=gq.t[:], in_=gq.t[:], mul=0.125), [gq.b], [gq.b])
            DMA(esink.t[:], W["attn_sink"][l].partition_broadcast(128), [], [esink.b])
            ACT(esink.t[:], esink.t[:], AF.Exp, [esink.b], [esink.b])
            for d in range(2):
                DMA(ib[d].t[:], W["mlstm_i_b"][l, d].rearrange("(d o) -> d o", o=1), [], [ib[d].b])
                DMA(nfb[d].t[:], W["mlstm_f_b"][l, d].rearrange("(d o) -> d o", o=1), [], [nfb[d].b])
                S.op("act", (lambda d: lambda e: e.mul(out=nfb[d].t[:], in_=nfb[d].t[:], mul=-1.0))(d), [nfb[d].b], [nfb[d].b])
                DMA(acoef[d].t[32:40, :], W["a_log"][l, d].rearrange("(d o) -> d o", o=1), [], [acoef[d].b])
                ACT(acoef[d].t[32:40, :], acoef[d].t[32:40, :], AF.Exp, [acoef[d].b], [acoef[d].b])
                S.op("act", (lambda d: lambda e: e.mul(out=acoef[d].t[32:40, :], in_=acoef[d].t[32:40, :], mul=-1.0))(d), [acoef[d].b], [acoef[d].b])
                DMA(dtb[d].t[32:40, :], W["dt_bias"][l, d].rearrange("(d o) -> d o", o=1), [], [dtb[d].b])
            DMA(mng_bc.t[:], W["mlstm_norm_g"][l].partition_broadcast(128), [], [mng_bc.b])
            DMA(sng_bc.t[:], W["ssm_norm_g"][l].partition_broadcast(128), [], [sng_bc.b])
            for ti in range(6):
                DMA(cw.t[:, ti, :], W["conv_w"][l][:, ti * 128:(ti + 1) * 128].rearrange("k p -> p k"), [], [cw.b], slow=True)
                DMA(cb.t[:, ti:ti + 1], W["conv_b"][l][ti * 128:(ti + 1) * 128].rearrange("(p o) -> p o", o=1), [], [cb.b])
            DMA(dsk_bc.t[:], W["d_skip"][l].partition_broadcast(128), [], [dsk_bc.b])

        hT = sb("hT", [128, 8, 8 * CH], BF16)
        hT_b = [Buf("hT%d" % i) for i in range(8)]
        hslot_chunk = [None] * 8
        xring = Ring([sb("xt%d" % i, [128, D]) for i in range(2)])
        hbring = Ring([sb("hb%d" % i, [128, D], BF16) for i in range(2)])
        st4 = Ring([sb("st4_%d" % i, [128, 8]) for i in range(6)])

        pf = Ring([psum("pf%d" % i, [128, 512]) for i in range(4)])
        pacc = [psum("pacc%d" % i, [128, 512]) for i in range(2)]
        pb = Ring([psum("pb%d" % i, [128, 1024], BF16) for i in range(2)])

        mergedT = sb("mergedT", [128, 8, TM])
        GT = sb("GT", [128, 4096], BF16)

        class View:
            def __init__(self, ap, b):
                self.t = ap
                self.b = b
        mergedTb = View(GT.t[:, :].rearrange("p (k t) -> p k t", k=8), GT.b)
        gate_tok = [View(GT.t[:, i * 2048:(i + 1) * 2048].rearrange("p (j c) -> p j c", j=4), GT.b) for i in range(2)]
        ybT = sb("ybT", [128, 4, TM], BF16)
        rope_t = sb("rope_t", [128, 2, TM])
        kT = sb("kT", [128, 6 * CH], BF16)
        vaug = sb("vaug", [128, 6, 2, 65], BF16)
        mqT = sb("mqT", [128, 4, TM], BF16)
        qT = mqT
        mkT = sb("mkT", [128, 4, TM], BF16)
        mvaug = sb("mvaug", [128, 4, 4, 129], BF16)
        mktok = sb("mktok", [128, 4, 512], BF16)
        mC = [sb("mC%d" % d, [128, 516]) for d in range(2)]
        mCb = [sb("mCb%d" % d, [128, 516], BF16) for d in range(2)]
        srawr = Ring([sb("sraw%d" % i, [128, TM + 4]) for i in range(2)])
        sconv = sb("sconv", [128, 6, TM], BF16)
        sxtok = sb("sxtok", [128, 4, 512], BF16)
        szero = sb("szero", [128, 4, TM], BF16)
        S.op("pool", lambda e: e.memset(szero.t[:], 0.0), [], [szero.b])
        sbtok = sb("sbtok", [128, 4, 128], BF16)
        sS = [sb("sS%d" % d, [128, 256]) for d in range(2)]
        sSb = [sb("sSb%d" % d, [128, 256], BF16) for d in range(2)]
        RT = [sb("rt%d" % i, [40, 512]) for i in range(8)]
        for rt_ in RT:
            S.op("pool", (lambda rt_: lambda e: e.memset(rt_.t[:], 0.0))(rt_), [], [rt_.b])
        carryB = [sb("carryB%d" % d, [4, 1]) for d in range(2)]
        carryM = [sb("carryM%d" % d, [4, 1]) for d in range(2)]
        mprev = sb("mprev", [4, 4])
        NTS = 52
        TSF = sb("TSF", [128, 4, NTS])
        TSBm = sb("TSBm", [128, 4, NTS])
        ACSF = sb("ACSF", [40, TM])
        ACSBm = sb("ACSBm", [40, TM])
        acs_hl = {0: (sb("acsfh", [40, TM], BF16), sb("acsfl", [40, TM], BF16)),
                  1: (sb("acsbh", [40, TM], BF16), sb("acsbl", [40, TM], BF16))}
        selb = sb("selb", [40, 8, 128], BF16)
        CP("dve", selb.t[32:40, :, :], self_.t[32:40, :, :], [self_.b], [selb.b])

        def mk_hilo(d, src):
            hi, lo = acs_hl[d]
            CP("act", hi.t[32:40, :], src.t[32:40, :], [src.b], [hi.b])
            TT("dve", lo.t[32:40, :], src.t[32:40, :], hi.t[32:40, :], ALU.subtract, [src.b, hi.b], [lo.b])
        nmmax = smax // TM
        TSD = nc.dram_tensor("tsd", [nmmax, 128, 4 * NTS], F32, kind="Internal").ap()
        ACSD = nc.dram_tensor("acsd", [nmmax, 8, TM], F32, kind="Internal").ap()
        tsd_buf = [Buf("tsd%d" % i) for i in range(nmmax)]
        acsd_buf = [Buf("acsd%d" % i) for i in range(nmmax)]

        stages = []

        def add_stage(l, piece, fn):
            stages.append((l, piece, fn))

        def run_stages():
            loads = [i for i, s_ in enumerate(stages) if s_[1] is not None]
            slot_of = {}
            nxt = 0
            PRE = 2
            for i, (l, piece, fn) in enumerate(stages):
                while nxt < len(loads) and (nxt < PRE or loads[nxt - PRE] <= i):
                    j = loads[nxt]
                    sl = wslots[nxt % NSLOT]
                    lj, pj, _ = stages[j]
                    if pj == P_KV or pj == P_SBC:
                        nc_ = 400 if pj == P_KV else 256
                        DMA(sl.t[:, :].rearrange("p (k c) -> p k c", c=512)[:, :, 0:nc_],
                            WS[lj, pj].rearrange("p (k c) -> p k c", c=512)[:, :, 0:nc_], ws_buf[lj][pj], [sl.b])
                    elif P_G0 <= pj < P_WO0:
                        DMA(sl.t[:, 0:3072], WS[lj, pj][:, 0:3072], ws_buf[lj][pj], [sl.b])
                    else:
                        DMA(sl.t[:], WS[lj, pj], ws_buf[lj][pj], [sl.b])
                    slot_of[j] = sl
                    nxt += 1
                fn(slot_of.get(i))

        def src_of(l):
            return x_in if l == 0 else YS[(l - 1) % nscr]

        def src_buf(l, r0):
            return [] if l == 0 else ys_buf[(l - 1) % nscr][r0 // CH]

        def dst_of(l):
            return y_out if l == depth - 1 else YS[l % nscr]

        def dst_buf2(l, r0, half):
            return [] if l == depth - 1 else [ys_buf[l % nscr][r0 // CH][half]]

        def rstd_from_sumsq(dst, ssum, n, rd, wr, npart=128):
            TSC("dve", dst, ssum, 1.0 / n, ALU.mult, rd, wr, s2=EPS, op1=ALU.add)
            ACT(dst, dst, AF.Sqrt, wr, wr)
            S.op("dve", lambda e: e.reciprocal(out=dst, in_=dst), wr, wr)

        def ensure_h(l, tok0, c):
            sl = c % 8
            if hslot_chunk[sl] == (l, tok0, c):
                return
            hslot_chunk[sl] = (l, tok0, c)
            xt = xring.get()
            r0 = tok0 + c * CH
            DMA(xt.t[:], src_of(l)[r0:r0 + CH, :], src_buf(l, r0), [xt.b])
            st = st4.get()
            hb = hbring.get()
            ACT(hb.t[:], xt.t[:], AF.Square, [xt.b], [hb.b, st.b], accum=st.t[:, 0:1])
            rstd_from_sumsq(st.t[:, 1:2], st.t[:, 0:1], D, [st.b], [st.b])
            STT("dve", hb.t[:], xt.t[:], st.t[:, 1:2], ng_bc.t[:], ALU.mult, ALU.mult, [xt.b, st.b, ng_bc.b], [hb.b])
            p = pb.get()
            for k in range(8):
                TR(p.t[:, k * 128:(k + 1) * 128], hb.t[:, k * 128:(k + 1) * 128], identb.t[:], [hb.b, identb.b], [p.b])
            CP("act", hT.t[:, :, sl * CH:(sl + 1) * CH], p.t[:].rearrange("p (k t) -> p k t", k=8), [p.b], [hT_b[sl]])
            if c == 0:
                dbg("hT", hT.t[:, :, sl * CH:(sl + 1) * CH], [hT_b[sl]], BF16)

        def h_rhs(k, c0, n):
            s0 = c0 % 8
            assert s0 + n <= 8
            return hT.t[:, k, s0 * CH:(s0 + n) * CH]

        def h_bufs(c0, n):
            return [hT_b[(c0 + i) % 8] for i in range(n)]

        def proj_feat(sl, col0, ncols, c0, nchunks, pt, pcol0=0):
            for k in range(8):
                MM(pt.t[0:ncols, pcol0:pcol0 + nchunks * CH], sl.t[:, k * 512 + col0:k * 512 + col0 + ncols],
                   h_rhs(k, c0, nchunks), k == 0, k == 7, [sl.b] + h_bufs(c0, nchunks), [pt.b])

        def proj_tok(sl, col0, ncols, c, pt):
            s0 = c % 8
            for k in range(8):
                MM(pt.t[:, 0:ncols], hT.t[:, k, s0 * CH:(s0 + 1) * CH], sl.t[:, k * 512 + col0:k * 512 + col0 + ncols],
                   k == 0, k == 7, [sl.b, hT_b[s0]], [pt.b])

        mL = slice(0, 4)
        sD = slice(32, 40)

        def gate_rows(l, d, sl, c0, first, ts_dst, ts_bufs, acs_dst, acs_bufs):
            rev = (d == 1)

            def rv(ap2d):
                return ap2d[:, ::-1] if rev else ap2d

            pg_ = pf.get()
            proj_feat(sl, 256 + d * 72, 72, c0, 4, pg_)
            pmi = pmf = pdt = pg_
            R = RT
            IG, L1, Bp, Mg, WI, DEC = R[0], R[1], R[2], R[3], R[4], R[5]
            ACT(IG.t[mL, :], rv(pmi.t[0:4, :]), AF.Identity, [pmi.b, ib[d].b], [IG.b], bias=ib[d].t[:, 0:1])
            ACT(L1.t[mL, :], rv(pmf.t[32:36, :]), AF.Exp, [pmf.b, nfb[d].b], [L1.b], bias=nfb[d].t[:, 0:1], scale=-1.0)
            ACT(L1.t[mL, :], L1.t[mL, :], AF.Ln, [L1.b], [L1.b], bias=1.0)
            if first:
                S.op("dve", lambda e: e.memset(carryB[d].t[:], 0.0), [], [carryB[d].b])
                S.op("dve", lambda e: e.memset(carryM[d].t[:], 0.0), [], [carryM[d].b])
            SCAN(Bp.t[mL, :], L1.t[mL, :], zrow.t[0:4, 0:1].to_broadcast([4, 512]), carryB[d].t[:, 0:1], ALU.add, ALU.add,
                 [L1.b, zrow.b, carryB[d].b], [Bp.b])
            A = IG
            TT("dve", A.t[mL, :], IG.t[mL, :], Bp.t[mL, :], ALU.add, [IG.b, Bp.b], [A.b])
            SCAN(Mg.t[mL, :], A.t[mL, :], A.t[mL, :], carryM[d].t[:, 0:1], ALU.max, ALU.max, [A.b, carryM[d].b], [Mg.b])

            def r3(tl):
                return tl.t[mL, :].rearrange("p (c t) -> p c t", c=4)

            Mg3 = r3(Mg)
            CP("dve", mprev.t[:, 0:1], carryM[d].t[:, 0:1], [carryM[d].b], [mprev.b])
            CP("dve", mprev.t[:, 1:4], Mg3[:, 0:3, 127], [Mg.b], [mprev.b])
            CP("dve", carryB[d].t[:, 0:1], Bp.t[mL, 511:512], [Bp.b], [carryB[d].b])
            CP("dve", carryM[d].t[:, 0:1], Mg.t[mL, 511:512], [Mg.b], [carryM[d].b])
            mend_bc = Mg3[:, :, 127:128].to_broadcast([4, 4, 128])
            mprev_bc = mprev.t[:, :].unsqueeze(2).to_broadcast([4, 4, 128])
            U = L1
            TT("dve", r3(U), r3(Mg), mend_bc, ALU.subtract, [Mg.b, L1.b], [U.b])
            TSC("dve", U.t[mL, :], U.t[mL, :], -60.0, ALU.max, [U.b], [U.b])
            ACT(U.t[mL, :], U.t[mL, :], AF.Exp, [U.b], [U.b], scale=-1.0)
            TT("dve", r3(WI), r3(Mg), mprev_bc, ALU.subtract, [Mg.b, mprev.b], [WI.b])
            ACT(WI.t[mL, :], WI.t[mL, :], AF.Exp, [WI.b], [WI.b], scale=-1.0)
            FL = Bp
            TT("dve", FL.t[mL, :], Bp.t[mL, :], Mg.t[mL, :], ALU.subtract, [Bp.b, Mg.b], [FL.b])
            ACT(FL.t[mL, :], FL.t[mL, :], AF.Exp, [FL.b], [FL.b])
            TT("dve", r3(DEC), mprev_bc, mend_bc, ALU.subtract, [Mg.b, mprev.b], [DEC.b])
            ACT(DEC.t[mL, :], DEC.t[mL, :], AF.Exp, [DEC.b], [DEC.b])
            WK = A
            TT("dve", r3(WK), r3(A), mend_bc, ALU.subtract, [A.b, Mg.b], [WK.b])
            ACT(WK.t[mL, :], WK.t[mL, :], AF.Exp, [WK.b], [WK.b])
            DT, LDT, DA, ACS, EA = R[0], R[1], R[2], R[3], R[4]
            ACT(DT.t[sD, :], rv(pdt.t[64:72, :]), AF.Exp, [pdt.b, dtb[d].b], [DT.b], bias=dtb[d].t[sD, 0:1])
            ACT(DT.t[sD, :], DT.t[sD, :], AF.Ln, [DT.b], [DT.b], bias=1.0)
            ACT(LDT.t[sD, :], DT.t[sD, :], AF.Ln, [DT.b], [LDT.b])
            TSC("dve", DA.t[sD, :], DT.t[sD, :], acoef[d].t[sD, 0:1], ALU.mult, [DT.b, acoef[d].b], [DA.b])
            SCAN(ACS.t[sD, :], resetm.t[sD, :], DA.t[sD, :], 0.0, ALU.mult, ALU.add, [resetm.b, DA.b], [ACS.b])

            def r8(tl):
                return tl.t[sD, :].rearrange("p (c t) -> p c t", c=4)

            aend_bc = r8(ACS)[:, :, 127:128].to_broadcast([8, 4, 128])
            BL = LDT
            TT("dve", BL.t[sD, :], LDT.t[sD, :], ACS.t[sD, :], ALU.subtract, [LDT.b, ACS.b], [BL.b])
            WST = DA
            TT("dve", r8(WST), aend_bc, r8(ACS), ALU.subtract, [ACS.b, DA.b], [WST.b])
            ACT(WST.t[sD, :], WST.t[sD, :], AF.Exp, [WST.b], [WST.b])
            TT("dve", WST.t[sD, :], WST.t[sD, :], DT.t[sD, :], ALU.mult, [WST.b, DT.b], [WST.b])
            ACT(EA.t[sD, :], ACS.t[sD, :], AF.Exp, [ACS.b], [EA.b])
            CD = DT
            CP("dve", r8(CD), aend_bc, [ACS.b, WST.b, DT.b], [CD.b])
            ACT(CD.t[sD, :], CD.t[sD, :], AF.Exp, [CD.b], [CD.b])
            quants = [(WK, mL, 4, 0), (U, mL, 4, 4), (WI, mL, 4, 8), (FL, mL, 4, 12), (DEC, mL, 4, 16),
                      (BL, sD, 8, 20), (WST, sD, 8, 28), (EA, sD, 8, 36), (CD, sD, 8, 44)]
            pts = pf.get()
            if rev:
                order_ = [R[0], R[1], R[2], R[4], R[5]]
                rmap = {}
                for ti_, tl_ in enumerate(order_):
                    q2 = R[6 + (ti_ % 2)]
                    rmap[id(tl_)] = (q2, ti_)
                quants = sorted(quants, key=lambda x: rmap[id(x[0])][1])
                done_ = set()
            for qi, (q, ps_, r, off) in enumerate(quants):
                if rev:
                    q2, ti_ = rmap[id(q)]
                    if ti_ not in done_:
                        done_.add(ti_)
                        CP("dve", q2.t[0:40, :], q.t[0:40, ::-1], [q.b], [q2.b])
                    q = q2
                for j in range(4):
                    MM(pts.t[:, j * 64 + off:j * 64 + off + r], q.t[ps_, j * CH:(j + 1) * CH], identf.t[ps_, ps_],
                       True, True, [q.b, identf.b], [pts.b])
            CP("act", ts_dst, pts.t[:, 0:256].rearrange("p (j c) -> p j c", j=4)[:, :, 0:NTS], [pts.b], ts_bufs)
            if rev:
                CP("pool", acs_dst, ACS.t[sD, ::-1], [ACS.b], acs_bufs)
            else:
                CP("pool", acs_dst, ACS.t[sD, :], [ACS.b], acs_bufs)

        def ssd_conv(l, slx, slbc, c0, nch_seq, tiles):
            for ti in tiles:
                if "conv" in KSKIP:
                    break
                sl, col0 = (slx, ti * 128) if ti < 4 else (slbc, (ti - 4) * 128)
                sraw = srawr.get()
                pm = pf.get()
                proj_feat(sl, col0, 128, c0, 4, pm)
                CP("act", sraw.t[:, 2:2 + TM], pm.t[:, :], [pm.b], [sraw.b])
                ph = pf.get()
                if c0 > 0:
                    s0 = (c0 - 1) % 8
                    for k in range(8):
                        MM(ph.t[:, 0:32], sl.t[:, k * 512 + col0:k * 512 + col0 + 128], hT.t[:, k, s0 * CH + 96:s0 * CH + 128],
                           k == 0, k == 7, [sl.b, hT_b[s0]], [ph.b])
                    CP("dve", sraw.t[:, 0:2], ph.t[:, 30:32], [ph.b], [sraw.b])
                else:
                    S.op("dve", (lambda sraw: lambda e: e.memset(sraw.t[:, 0:2], 0.0))(sraw), [], [sraw.b])
                if c0 + 4 < nch_seq:
                    s0 = (c0 + 4) % 8
                    for k in range(8):
                        MM(ph.t[:, 32:64], sl.t[:, k * 512 + col0:k * 512 + col0 + 128], hT.t[:, k, s0 * CH:s0 * CH + 32],
                           k == 0, k == 7, [sl.b, hT_b[s0]], [ph.b])
                    CP("dve", sraw.t[:, TM + 2:TM + 4], ph.t[:, 32:34], [ph.b], [sraw.b])
                else:
                    S.op("dve", (lambda sraw: lambda e: e.memset(sraw.t[:, TM + 2:TM + 4], 0.0))(sraw), [], [sraw.b])
                acc = f32r.get()
                TSC("dve", acc.t[:, :], sraw.t[:, 0:TM], cw.t[:, ti, 0:1], ALU.mult, [sraw.b, cw.b, cb.b], [acc.b],
                    s2=cb.t[:, ti:ti + 1], op1=ALU.add)
                for kk in range(1, 5):
                    STT("dve", acc.t[:, :], sraw.t[:, kk:kk + TM], cw.t[:, ti, kk:kk + 1], acc.t[:, :], ALU.mult, ALU.add,
                        [sraw.b, cw.b, acc.b], [acc.b])
                ACT(sconv.t[:, ti, :], acc.t[:, :], AF.Silu, [acc.b], [sconv.b])

        def ssd_tokmajor(j):
            if "tokm" in KSKIP:
                return
            p = pb.get()
            for ti in range(4):
                TR(p.t[:, ti * 128:(ti + 1) * 128], sconv.t[:, ti, j * CH:(j + 1) * CH], identb.t[:], [sconv.b, identb.b], [p.b])
            CP("act", sxtok.t[:, j, :], p.t[:, 0:512], [p.b], [sxtok.b])
            if "tokb" in KSKIP:
                return
            p2 = pb.get()
            TR(p2.t[:, 0:128], sconv.t[:, 4, j * CH:(j + 1) * CH], identb.t[:], [sconv.b, identb.b], [p2.b])
            CP("act", sbtok.t[:, j, :], p2.t[:, 0:128], [p2.b], [sbtok.b])

        def ssd_state_update(d, j, ts_ap, ts_rd):
            xw = b16r.get()
            TT("pool" if d == 0 else "dve", xw.t[:, :].rearrange("p (h e) -> p h e", h=8), sxtok.t[:, j, :].rearrange("p (h e) -> p h e", h=8),
               ts_ap[:, 28:36].unsqueeze(2).to_broadcast([128, 8, 64]), ALU.mult, [sxtok.b] + ts_rd, [xw.b])
            pS = pf.get()
            for g in range(2):
                MM(pS.t[:, g * 256:(g + 1) * 256], sbtok.t[:, j, :], xw.t[:, g * 256:(g + 1) * 256], True, True, [sbtok.b, xw.b], [pS.b])
            for g in range(2):
                ps_ = slice(g * 64, (g + 1) * 64)
                TT("dve", sS[d].t[ps_, :].rearrange("p (h e) -> p h e", h=4), sS[d].t[ps_, :].rearrange("p (h e) -> p h e", h=4),
                   ts_ap[ps_, 44 + g * 4:44 + g * 4 + 4].unsqueeze(2).to_broadcast([64, 4, 64]), ALU.mult, [sS[d].b] + ts_rd, [sS[d].b])
                TT("dve", sS[d].t[ps_, :], sS[d].t[ps_, :], pS.t[ps_, g * 256:(g + 1) * 256], ALU.add, [sS[d].b, pS.b], [sS[d].b])

        def mlstm_state_update(d, j, ts_ap, ts_rd, pre=None):
            if pre is not None:
                vw, wkb = pre["vw"], pre["wkb"]
            else:
                vw = b16r.get()
                TT("pool" if d == 0 else "dve", vw.t[:, :].rearrange("p (h e) -> p h e", h=4), mvaug.t[:, j, :, 0:128],
                   ts_ap[:, 0:4].unsqueeze(2).to_broadcast([128, 4, 128]), ALU.mult, [mvaug.b] + ts_rd, [vw.b])
                wkb = b16r.get()
                CP("dve", wkb.t[:, 0:4], ts_ap[:, 0:4], ts_rd, [wkb.b])
            pC = pf.get(); pn = pf.get()
            for h in range(4):
                MM(pC.t[:, h * 128:(h + 1) * 128], mktok.t[:, j, h * 128:(h + 1) * 128], vw.t[:, h * 128:(h + 1) * 128], True, True,
                   [mktok.b, vw.b], [pC.b])
                MM(pn.t[:, h:h + 1], mktok.t[:, j, h * 128:(h + 1) * 128], wkb.t[:, h:h + 1], True, True, [mktok.b, wkb.b], [pn.b])
            dec_bc = ts_ap[:, 16:20]
            TT("dve", mC[d].t[:, 0:512].rearrange("p (h e) -> p h e", h=4), mC[d].t[:, 0:512].rearrange("p (h e) -> p h e", h=4),
               dec_bc.unsqueeze(2).to_broadcast([128, 4, 128]), ALU.mult, [mC[d].b] + ts_rd, [mC[d].b])
            TT("dve", mC[d].t[:, 512:516], mC[d].t[:, 512:516], dec_bc, ALU.mult, [mC[d].b] + ts_rd, [mC[d].b])
            TT("dve", mC[d].t[:, 0:512], mC[d].t[:, 0:512], pC.t[:, :], ALU.add, [mC[d].b, pC.b], [mC[d].b])
            TT("dve", mC[d].t[:, 512:516], mC[d].t[:, 512:516], pn.t[:, 0:4], ALU.add, [mC[d].b, pn.b], [mC[d].b])

        def mlstm_v_tok(sl, c, j):
            pv = pf.get()
            proj_tok(sl, 0, 512, c, pv)
            CP("act", mvaug.t[:, j, :, 0:128], pv.t[:, :].rearrange("p (h e) -> p h e", h=4), [pv.b], [mvaug.b])

        def add_branch_merge(l, bi, first, c0):
            for q_ in range(4):
                def st_g(sl, q_=q_):
                    for oo in range(2):
                        o = q_ * 2 + oo
                        pg = pf.get()
                        for k_ in range(8):
                            MM(pg.t[:, :], sl.t[:, k_ * 256 + oo * 128:k_ * 256 + (oo + 1) * 128], h_rhs(k_, c0, 4), k_ == 0, k_ == 7,
                               [sl.b] + h_bufs(c0, 4), [pg.b])
                        gs = f32r.get()
                        ACT(gs.t[:, :], pg.t[:, :], AF.Sigmoid, [pg.b], [gs.b])
                        pp = pf.get()
                        for k_ in range(4):
                            MM(pp.t[:, :], sl.t[:, 2048 + k_ * 256 + oo * 128:2048 + k_ * 256 + (oo + 1) * 128], ybT.t[:, k_, :], k_ == 0, k_ == 3,
                               [sl.b, ybT.b], [pp.b])
                        if first:
                            TT("dve", mergedT.t[:, o, :], gs.t[:, :], pp.t[:, :], ALU.mult, [gs.b, pp.b], [mergedT.b])
                        else:
                            TT("dve", gs.t[:, :], gs.t[:, :], pp.t[:, :], ALU.mult, [gs.b, pp.b], [gs.b])
                            TT("pool", mergedT.t[:, o, :], mergedT.t[:, o, :], gs.t[:, :], ALU.add, [gs.b, mergedT.b], [mergedT.b])
                add_stage(l, P_G0 + bi * 4 + q_, st_g)

        def add_final(l, tok0, c0, nch):
            def st_pre(_sl):
                for c_ in range(c0 + 5, min(nch, c0 + 9)):
                    ensure_h(l, tok0, c_)
                dbg("ybT", ybT.t[:, :, :], [ybT.b], BF16)
                dbg("mergedT", mergedT.t[:, :, :], [mergedT.b])
                CP("act", mergedTb.t[:, 0:4, :], mergedT.t[:, 0:4, :], [mergedT.b], [mergedTb.b])
                CP("dve", mergedTb.t[:, 4:8, :], mergedT.t[:, 4:8, :], [mergedT.b], [mergedTb.b])
            add_stage(l, None, st_pre)
            for half in range(2):
                def st_o(sl, half=half):
                    hs = slice(half * 512, (half + 1) * 512)
                    for j in range(4):
                        r0 = tok0 + (c0 + j) * CH
                        xt = xring2.get()
                        DMA(xt.t[:, :], src_of(l)[r0:r0 + CH, hs], src_buf(l, r0), [xt.b])
                        po = pf.get()
                        for k in range(8):
                            MM(po.t[:, :], mergedTb.t[:, k, j * CH:(j + 1) * CH], sl.t[:, k * 512:(k + 1) * 512], k == 0, k == 7,
                               [mergedTb.b, sl.b], [po.b])
                        TT("dve", xt.t[:, :], xt.t[:, :], po.t[:, :], ALU.add, [xt.b, po.b], [xt.b])
                        DMA(dst_of(l)[r0:r0 + CH, hs], xt.t[:, :], [xt.b], dst_buf2(l, r0, half))
                add_stage(l, P_WO0 + half, st_o)

        xring2 = Ring([sb("xo%d" % i, [128, 512]) for i in range(3)])
        hsum = sb("hsum", [128, 512])
        tsf_b = TSF.b
        acsf_b = ACSF.b

        def seq_layer(l, tok0, slen):
            nch = slen // CH
            nm = slen // TM
            en_att, en_ml, en_ssd = ("att" in enable), ("mlstm" in enable), ("ssd" in enable)
            en_rec = en_ml or en_ssd

            if en_rec:
                for m in reversed(range(nm)):
                    c0 = 4 * m

                    def st_h(_sl, c0=c0):
                        for c in range(max(0, c0 - 1), min(nch, c0 + 5)):
                            ensure_h(l, tok0, c)
                    add_stage(l, None, st_h)

                    def st_gates(sl, c0=c0, m=m):
                        if m == nm - 1:
                            S.op("dve", lambda e: e.memset(mC[1].t[:], 0.0), [], [mC[1].b])
                            S.op("dve", lambda e: e.memset(sS[1].t[:], 0.0), [], [sS[1].b])
                            S.op("dve", lambda e: e.memset(mvaug.t[:, :, :, 128:129], 1.0), [], [mvaug.b])
                        gate_rows(l, 1, sl, c0, m == nm - 1, TSBm.t[:, :, :], [TSBm.b], ACSBm.t[sD, :], [ACSBm.b])
                        DMA(TSD[m].rearrange("p (j c) -> p j c", j=4), TSBm.t[:, :, :], [TSBm.b], [tsd_buf[m]])
                        DMA(ACSD[m], ACSBm.t[sD, :], [ACSBm.b], [acsd_buf[m]])
                    add_stage(l, P_KV, st_gates)

                    if en_ml:
                        def st_mk(sl, c0=c0):
                            for j in range(4):
                                pk = pf.get()
                                proj_tok(sl, 0, 512, c0 + j, pk)
                                S.op("act", (lambda j, pk: lambda e: e.mul(out=mktok.t[:, j, :], in_=pk.t[:, :], mul=128 ** -0.5))(j, pk),
                                     [pk.b], [mktok.b])
                        add_stage(l, P_MK, st_mk)

                        def st_mv(sl, c0=c0, m=m):
                            for j in range(4):
                                mlstm_v_tok(sl, c0 + j, j)
                            for j in reversed(range(4)):
                                c = c0 + j
                                CP("act", mCb[1].t[:, :], mC[1].t[:, :], [mC[1].b], [mCb[1].b])
                                DMA(BSTM[c], mCb[1].t[:, :], [mCb[1].b], [bstm_buf[c]])
                                mlstm_state_update(1, j, TSBm.t[:, j, :], [TSBm.b])
                        add_stage(l, P_MV, st_mv)

                    if en_ssd and "p2ssd" not in KSKIP:
                        def st_sx(sl, c0=c0):
                            ssd_conv(l, sl, None, c0, nch, [0, 1, 2, 3])
                        add_stage(l, P_SX, st_sx)

                        def st_sbc(sl, c0=c0, m=m):
                            ssd_conv(l, None, sl, c0, nch, [4])
                            for j in reversed(range(4)):
                                c = c0 + j
                                ssd_tokmajor(j)
                                CP("act", sSb[1].t[:, :], sS[1].t[:, :], [sS[1].b], [sSb[1].b])
                                DMA(BSTS[c], sSb[1].t[:, :], [sSb[1].b], [bsts_buf[c]])
                                ssd_state_update(1, j, TSBm.t[:, j, :], [TSBm.b])
                        add_stage(l, P_SBC, st_sbc)

                    def st_pref(_sl, c0=c0):
                        for c_ in range(max(0, c0 - 5), c0 - 1):
                            ensure_h(l, tok0, c_)
                    add_stage(l, None, st_pref)

            for m in range(nm):
                c0 = 4 * m
                lo = max(0, c0 - 1)
                hi = min(nch, c0 + 5)

                def st_h(_sl, c0=c0, lo=lo, hi=hi, m=m):
                    if l + 1 < depth:
                        for _ in range(casts_per_mt):
                            if pending_casts[l + 1]:
                                pending_casts[l + 1].pop(0)()
                    for c in range(lo, hi):
                        ensure_h(l, tok0, c)
                    DMA(rope_t.t[:], C["c_rope"][:, :, c0 * CH:c0 * CH + TM], [], [rope_t.b])
                    if m == 0:
                        S.op("dve", lambda e: e.memset(mC[0].t[:], 0.0), [], [mC[0].b])
                        S.op("dve", lambda e: e.memset(sS[0].t[:], 0.0), [], [sS[0].b])
                        S.op("dve", lambda e: e.memset(mCb[0].t[:], 0.0), [], [mCb[0].b])
                        S.op("dve", lambda e: e.memset(sSb[0].t[:], 0.0), [], [sSb[0].b])
                        S.op("dve", lambda e: e.memset(mvaug.t[:, :, :, 128:129], 1.0), [], [mvaug.b])
                        S.op("dve", lambda e: e.memset(vaug.t[:, :, :, 64:65], 1.0), [], [vaug.b])
                add_stage(l, None, st_h)

                def qk_norm_rope(pq, gain, ntok):
                    sq = b16r.get()
                    ACT(sq.t[:, 0:ntok], pq.t[:, 0:ntok], AF.Square, [pq.b], [sq.b])
                    pss = pf.get()
                    MM(pss.t[:, 0:ntok], blkb.t[:], sq.t[:, 0:ntok], True, True, [blkb.b, sq.b], [pss.b])
                    rs = f32r.get()
                    ACT(rs.t[:, 0:ntok], pss.t[:, 0:ntok], AF.Ln, [pss.b, epsc.b], [rs.b], bias=epsc.t[:, 0:1])
                    ACT(rs.t[:, 0:ntok], rs.t[:, 0:ntok], AF.Exp, [rs.b], [rs.b], scale=-0.5)
                    qn = f32r.get(); qnb = b16r.get()
                    STT("dve", qnb.t[:, 0:ntok], pq.t[:, 0:ntok], gain.t[:, 0:1], rs.t[:, 0:ntok], ALU.mult, ALU.mult,
                        [pq.b, gain.b, rs.b], [qnb.b])
                    pr = pf.get()
                    MM(pr.t[:, 0:ntok], rotb.t[:], qnb.t[:, 0:ntok], True, True, [rotb.b, qnb.b], [pr.b])
                    t1 = f32r.get()
                    return qn, pr, t1, qnb

                if en_att:
                    def st_aq(sl, c0=c0, qk_norm_rope=qk_norm_rope):
                        for i in range(4):
                            pq = pf.get()
                            proj_feat(sl, i * 128, 128, c0, 4, pq)
                            qn, pr, t1, qnb = qk_norm_rope(pq, gq, TM)
                            TT("dve", t1.t[:, :], pr.t[:, :], rope_t.t[:, 1, :], ALU.mult, [pr.b, rope_t.b], [t1.b])
                            TT("pool", qn.t[:, :], qnb.t[:, :], rope_t.t[:, 0, :], ALU.mult, [qnb.b, rope_t.b], [qn.b])
                            TT("pool", qT.t[:, i, :], qn.t[:, :], t1.t[:, :], ALU.add, [qn.b, t1.b], [qT.b])
                            if i == 0:
                                dbg("pq", pq.t[:, :], [pq.b])
                        dbg("qT", qT.t[:, :, :], [qT.b], BF16)
                    add_stage(l, P_AQ, st_aq)

                def st_kv(sl, c0=c0, lo=lo, hi=hi, m=m, qk_norm_rope=qk_norm_rope):
                    if en_att:
                        for (ca, n) in ((lo, c0 - lo), (c0, 4), (c0 + 4, hi - c0 - 4)):
                            if n <= 0:
                                continue
                            pk = pf.get()
                            proj_feat(sl, 0, 128, ca, n, pk)
                            ntk = n * CH
                            qn, pr, t1, qnb = qk_norm_rope(pk, gk, ntk)
                            if ca == c0:
                                cosap, sinap, rbufs = rope_t.t[:, 0, :], rope_t.t[:, 1, :], [rope_t.b]
                            else:
                                rt = f32r.get(); rt2 = f32r.get()
                                DMA(rt.t[:, 0:CH], C["c_rope"][:, 0, ca * CH:(ca + 1) * CH], [], [rt.b])
                                DMA(rt2.t[:, 0:CH], C["c_rope"][:, 1, ca * CH:(ca + 1) * CH], [], [rt2.b])
                                cosap, sinap, rbufs = rt.t[:, 0:CH], rt2.t[:, 0:CH], [rt.b, rt2.b]
                            TT("dve", t1.t[:, 0:ntk], pr.t[:, 0:ntk], sinap, ALU.mult, [pr.b] + rbufs, [t1.b])
                            TT("pool", qn.t[:, 0:ntk], qnb.t[:, 0:ntk], cosap, ALU.mult, [qnb.b] + rbufs, [qn.b])
                            o = (ca - lo) * CH
                            TT("pool", kT.t[:, o:o + ntk], qn.t[:, 0:ntk], t1.t[:, 0:ntk], ALU.add, [qn.b, t1.b], [kT.b])
                        for c in range(lo, hi):
                            pv = pf.get()
                            proj_tok(sl, 128, 128, c, pv)
                            CP("act", vaug.t[:, c - lo, :, 0:64], pv.t[:, 0:128].rearrange("p (g e) -> p g e", g=2), [pv.b], [vaug.b])
                        dbg("kT", kT.t[:, :], [kT.b], BF16)
                        dbg("vaug", vaug.t[:, :, :, :], [vaug.b], BF16)
                    if en_rec:
                        gate_rows(l, 0, sl, c0, m == 0, TSF.t[:, :, :], [tsf_b], ACSF.t[sD, :], [acsf_b])
                        DMA(TSBm.t[:, :, :], TSD[m].rearrange("p (j c) -> p j c", j=4), [tsd_buf[m]], [TSBm.b])
                        DMA(ACSBm.t[sD, :], ACSD[m], [acsd_buf[m]], [ACSBm.b])
                        mk_hilo(0, ACSF)
                        mk_hilo(1, ACSBm)
                add_stage(l, P_KV, st_kv)

                if en_att:
                    def st_az(sl, c0=c0, lo=lo, hi=hi):
                        for j in range(4):
                            pz = pf.get()
                            proj_tok(sl, 0, 512, c0 + j, pz)
                            ACT(gate_tok[0].t[:, j, :], pz.t[:, :], AF.Silu, [pz.b], [gate_tok[0].b])
                        for j in range(4):
                            c = c0 + j
                            po = pacc
                            kblocks = [cc for cc in (c - 1, c, c + 1) if 0 <= cc < nch]
                            for g in range(2):
                                pr_ = slice(g * 64, (g + 1) * 64)
                                for bi, cc in enumerate(kblocks):
                                    ps_ = pf.get()
                                    ko = (cc - lo) * CH
                                    for i in range(4):
                                        MM(ps_.t[:, i * 128:(i + 1) * 128], kT.t[pr_, ko:ko + CH], qT.t[pr_, i, j * CH:(j + 1) * CH],
                                           True, True, [kT.b, qT.b], [ps_.b])
                                    pt_ = b16r.get()
                                    ACT(pt_.t[:, :], ps_.t[:, :], AF.Exp, [ps_.b], [pt_.b])
                                    if cc != c:
                                        mi_ = 1 if cc < c else 0
                                        TT("pool", pt_.t[:, :], pt_.t[:, :], trib.t[:, mi_, :], ALU.mult, [pt_.b, trib.b], [pt_.b])
                                    for i in range(4):
                                        MM(po[g].t[:, i * 65:(i + 1) * 65], pt_.t[:, i * 128:(i + 1) * 128], vaug.t[:, cc - lo, g, :],
                                           bi == 0 and i == 0, bi == len(kblocks) - 1, [pt_.b, vaug.b], [po[g].b], skip=True)
                            ya = f32r.get(); den = st4.get()
                            for g in range(2):
                                pv3 = po[g].t[:, 0:260].rearrange("p (i e) -> p i e", i=4)
                                TT("dve", den.t[:, g * 4:(g + 1) * 4], pv3[:, :, 64], esink.t[:, g * 4:(g + 1) * 4], ALU.add,
                                   [po[g].b, esink.b], [den.b])
                            S.op("dve", (lambda den: lambda e: e.reciprocal(out=den.t[:, 0:8], in_=den.t[:, 0:8]))(den), [den.b], [den.b])
                            for g in range(2):
                                pv3 = po[g].t[:, 0:260].rearrange("p (i e) -> p i e", i=4)
                                TT("dve", ya.t[:, g * 256:(g + 1) * 256].rearrange("p (i e) -> p i e", i=4), pv3[:, :, 0:64],
                                   den.t[:, g * 4:(g + 1) * 4].unsqueeze(2).to_broadcast([128, 4, 64]), ALU.mult, [po[g].b, den.b], [ya.b])
                            dbg("ya", ya.t[:, :], [ya.b])
                            yg = b16r.get()
                            TT("pool", yg.t[:, :], ya.t[:, :], gate_tok[0].t[:, j, :], ALU.mult, [ya.b, gate_tok[0].b], [yg.b])
                            dbg("yg", yg.t[:, :], [yg.b], BF16)
                            p = pb.get()
                            for i in range(4):
                                TR(p.t[:, i * 128:(i + 1) * 128], yg.t[:, i * 128:(i + 1) * 128], identb.t[:], [yg.b, identb.b], [p.b])
                            CP("act", ybT.t[:, :, j * CH:(j + 1) * CH], p.t[:, 0:512].rearrange("p (i t) -> p i t", i=4), [p.b], [ybT.b])
                    add_stage(l, P_AZ, st_az)
                    add_branch_merge(l, 0, True, c0)

                if en_ml:
                    def st_mq(sl, c0=c0):
                        for h in range(4):
                            pq = pf.get()
                            proj_feat(sl, h * 128, 128, c0, 4, pq)
                            CP("act", mqT.t[:, h, :], pq.t[:, :], [pq.b], [mqT.b])
                    add_stage(l, P_MQ, st_mq)

                    def st_mk(sl, c0=c0):
                        for h in range(4):
                            pk = pf.get()
                            proj_feat(sl, h * 128, 128, c0, 4, pk)
                            S.op("act", (lambda h, pk: lambda e: e.mul(out=mkT.t[:, h, :], in_=pk.t[:, :], mul=128 ** -0.5))(h, pk),
                                 [pk.b], [mkT.b])
                        for j in range(4):
                            p = pb.get()
                            for h in range(4):
                                TR(p.t[:, h * 128:(h + 1) * 128], mkT.t[:, h, j * CH:(j + 1) * CH], identb.t[:], [mkT.b, identb.b], [p.b])
                            CP("dve", mktok.t[:, j, :], p.t[:, 0:512], [p.b], [mktok.b])
                    add_stage(l, P_MK, st_mk)

                    def st_mv(sl, c0=c0):
                        for j in range(4):
                            mlstm_v_tok(sl, c0 + j, j)
                    add_stage(l, P_MV, st_mv)

                    def st_mo(sl, c0=c0):
                        for j in range(4):
                            pz = pf.get()
                            proj_tok(sl, 0, 512, c0 + j, pz)
                            ACT(gate_tok[0].t[:, j, :], pz.t[:, :], AF.Sigmoid, [pz.b], [gate_tok[0].b])
                    add_stage(l, P_MO, st_mo)

                    def st_mz(sl, c0=c0, m=m):
                        for j in range(4):
                            pz = pf.get()
                            proj_tok(sl, 0, 512, c0 + j, pz)
                            ACT(gate_tok[1].t[:, j, :], pz.t[:, :], AF.Silu, [pz.b], [gate_tok[1].b])
                        for j in range(4):
                            c = c0 + j
                            tok = slice(j * CH, (j + 1) * CH)
                            DMA(mCb[1].t[:, :], BSTM[c], [bstm_buf[c]], [mCb[1].b])
                            ps = pf.get()
                            for h in range(4):
                                MM(ps.t[:, h * 128:(h + 1) * 128], mkT.t[:, h, tok], mqT.t[:, h, tok], True, True, [mkT.b, mqT.b], [ps.b])
                            g01 = f32r.get()
                            TT("pool", g01.t[:, :], gate_tok[0].t[:, j, :], gate_tok[1].t[:, j, :], ALU.mult, [gate_tok[0].b], [g01.b])
                            TT("pool", g01.t[:, :], g01.t[:, :], mng_bc.t[:, :], ALU.mult, [g01.b, mng_bc.b], [g01.b])
                            fwd_vw = {}
                            for d in range(2):
                                ts_ap = TSF.t[:, j, :] if d == 0 else TSBm.t[:, j, :]
                                ts_rd = [tsf_b] if d == 0 else [TSBm.b]
                                scm = b16r.get()
                                TT("dve", scm.t[:, :], ps.t[:, :], trib.t[:, d, :], ALU.mult, [ps.b, trib.b], [scm.b])
                                vw = b16r.get(); wkb = b16r.get()
                                if d == 0:
                                    fwd_vw["vw"], fwd_vw["wkb"] = vw, wkb
                                TT("pool", vw.t[:, :].rearrange("p (h e) -> p h e", h=4), mvaug.t[:, j, :, 0:128],
                                   ts_ap[:, 0:4].unsqueeze(2).to_broadcast([128, 4, 128]), ALU.mult, [mvaug.b] + ts_rd, [vw.b])
                                CP("dve", wkb.t[:, 0:4], ts_ap[:, 0:4], ts_rd, [wkb.b])
                                pnum, pint = pacc[0], pacc[1]
                                pden = pf.get()
                                for h in range(4):
                                    hs = slice(h * 128, (h + 1) * 128)
                                    MM(pnum.t[:, hs], scm.t[:, hs], vw.t[:, hs], True, True, [scm.b, vw.b], [pnum.b])
                                    MM(pden.t[:, h:h + 1], scm.t[:, hs], wkb.t[:, h:h + 1], True, True, [scm.b, wkb.b], [pden.b])
                                    MM(pint.t[:, hs], mqT.t[:, h, tok], mCb[d].t[:, hs], True, True, [mqT.b, mCb[d].b], [pint.b])
                                    MM(pden.t[:, 4 + h:5 + h], mqT.t[:, h, tok], mCb[d].t[:, 512 + h:513 + h], True, True,
                                       [mqT.b, mCb[d].b], [pden.b])
                                n1 = f32r.get(); n2 = f32r.get(); dd = st4.get()
                                u_bc = ts_ap[:, 4:8].unsqueeze(2).to_broadcast([128, 4, 128])
                                wi_bc = ts_ap[:, 8:12].unsqueeze(2).to_broadcast([128, 4, 128])
                                TT("dve", n1.t[:, :].rearrange("p (h e) -> p h e", h=4), pnum.t[:, :].rearrange("p (h e) -> p h e", h=4),
                                   u_bc, ALU.mult, [pnum.b] + ts_rd, [n1.b])
                                TT("dve", n2.t[:, :].rearrange("p (h e) -> p h e", h=4), pint.t[:, :].rearrange("p (h e) -> p h e", h=4),
                                   wi_bc, ALU.mult, [pint.b] + ts_rd, [n2.b])
                                TT("pool", n1.t[:, :], n1.t[:, :], n2.t[:, :], ALU.add, [n1.b, n2.b], [n1.b])
                                TT("dve", dd.t[:, 0:4], pden.t[:, 0:4], ts_ap[:, 4:8], ALU.mult, [pden.b] + ts_rd, [dd.b])
                                TT("dve", dd.t[:, 4:8], pden.t[:, 4:8], ts_ap[:, 8:12], ALU.mult, [pden.b] + ts_rd, [dd.b])
                                TT("dve", dd.t[:, 0:4], dd.t[:, 0:4], dd.t[:, 4:8], ALU.add, [dd.b], [dd.b])
                                ACT(dd.t[:, 0:4], dd.t[:, 0:4], AF.Abs, [dd.b], [dd.b])
                                TT("dve", dd.t[:, 0:4], dd.t[:, 0:4], ts_ap[:, 12:16], ALU.max, [dd.b] + ts_rd, [dd.b])
                                S.op("dve", (lambda dd: lambda e: e.reciprocal(out=dd.t[:, 0:4], in_=dd.t[:, 0:4]))(dd), [dd.b], [dd.b])
                                r_bc = dd.t[:, 0:4].unsqueeze(2).to_broadcast([128, 4, 128])
                                if d == 0:
                                    TT("pool", hsum.t[:, :].rearrange("p (h e) -> p h e", h=4), n1.t[:, :].rearrange("p (h e) -> p h e", h=4),
                                       r_bc, ALU.mult, [n1.b, dd.b], [hsum.b])
                                else:
                                    TT("pool", n1.t[:, :].rearrange("p (h e) -> p h e", h=4), n1.t[:, :].rearrange("p (h e) -> p h e", h=4),
                                       r_bc, ALU.mult, [n1.b, dd.b], [n1.b])
                                    TT("pool", hsum.t[:, :], hsum.t[:, :], n1.t[:, :], ALU.add, [hsum.b, n1.b], [hsum.b])
                            sq = f32r.get(); ss = st4.get()
                            ACT(sq.t[:, :], hsum.t[:, :], AF.Square, [hsum.b], [sq.b])
                            S.op("dve", (lambda sq, ss: lambda e: e.reduce_sum(out=ss.t[:, 0:4], in_=sq.t[:, :].rearrange("p (h e) -> p h e", h=4),
                                                                         axis=AX.X))(sq, ss), [sq.b], [ss.b])
                            rstd_from_sumsq(ss.t[:, 4:8], ss.t[:, 0:4], 128, [ss.b], [ss.b])
                            hn = f32r.get()
                            TT("dve", hn.t[:, :].rearrange("p (h e) -> p h e", h=4), hsum.t[:, :].rearrange("p (h e) -> p h e", h=4),
                               ss.t[:, 4:8].unsqueeze(2).to_broadcast([128, 4, 128]), ALU.mult, [hsum.b, ss.b], [hn.b])
                            yg = b16r.get()
                            TT("pool", yg.t[:, :], hn.t[:, :], g01.t[:, :], ALU.mult, [hn.b, g01.b], [yg.b])
                            p = pb.get()
                            for i in range(4):
                                TR(p.t[:, i * 128:(i + 1) * 128], yg.t[:, i * 128:(i + 1) * 128], identb.t[:], [yg.b, identb.b], [p.b])
                            CP("act", ybT.t[:, :, tok], p.t[:, 0:512].rearrange("p (i t) -> p i t", i=4), [p.b], [ybT.b])
                            mlstm_state_update(0, j, TSF.t[:, j, :], [tsf_b], pre=fwd_vw)
                            CP("act", mCb[0].t[:, :], mC[0].t[:, :], [mC[0].b], [mCb[0].b])
                    add_stage(l, P_MZ, st_mz)
                    add_branch_merge(l, 1, not en_att, c0)

                if en_ssd:
                    def st_sx(sl, c0=c0):
                        ssd_conv(l, sl, None, c0, nch, [0, 1, 2, 3])
                    add_stage(l, P_SX, st_sx)

                    def st_sbc(sl, c0=c0):
                        ssd_conv(l, None, sl, c0, nch, [4, 5])
                        for which in range(2):
                            for g in range(2):
                                gs_ = slice(g * 64, (g + 1) * 64)
                                CP("pool", szero.t[gs_, which * 2 + g, :], sconv.t[gs_, 4 + which, :], [sconv.b], [szero.b])
                        for j in range(4):
                            ssd_tokmajor(j)
                    add_stage(l, P_SBC, st_sbc)

                    def st_sz(sl, c0=c0, m=m):
                        for j in range(4):
                            pz = pf.get()
                            proj_tok(sl, 0, 512, c0 + j, pz)
                            ACT(gate_tok[0].t[:, j, :], pz.t[:, :], AF.Silu, [pz.b], [gate_tok[0].b])
                        for j in range(4):
                            if "sz" in KSKIP:
                                break
                            c = c0 + j
                            tok = slice(j * CH, (j + 1) * CH)
                            if "cDma" not in KSKIP:
                                DMA(sSb[1].t[:, :], BSTS[c], [bsts_buf[c]], [sSb[1].b])
                            pcb = pf.get()
                            for g in range(2):
                                if "cPcb" in KSKIP:
                                    break
                                gs_ = slice(g * 64, (g + 1) * 64)
                                MM(pcb.t[:, g * 128:(g + 1) * 128], szero.t[:, g, tok], sconv.t[:, 5, tok], True, True, [sconv.b, szero.b], [pcb.b])
                            py = pf.get()
                            toff = []
                            for d in range(2):
                                ts_ap = TSF.t[:, j, :] if d == 0 else TSBm.t[:, j, :]
                                ts_rd = [tsf_b] if d == 0 else [TSBm.b]
                                acs_hi, acs_lo = acs_hl[d]
                                if d == 0:
                                    cbm = f32r.get()
                                    CP("act", cbm.t[:, 0:256], pcb.t[:, 0:256], [pcb.b], [cbm.b])
                                for h in range(8):
                                    if "cSel" in KSKIP:
                                        break
                                    pe_ = pacc[h // 4]
                                    hb_ = slice((h % 4) * 128, (h % 4 + 1) * 128)
                                    MM(pe_.t[:, hb_], selb.t[sD, h, :], acs_hi.t[sD, tok], h % 4 == 0, False, [selb.b, acs_hi.b], [pe_.b], skip=True)
                                    MM(pe_.t[:, hb_], selb.t[sD, h, :], acs_lo.t[sD, tok], False, False, [selb.b, acs_lo.b], [pe_.b], skip=True)
                                    MM(pe_.t[:, hb_], identb.t[:], negm.t[:, d, :], False, True, [identb.b, negm.b], [pe_.b], skip=True)
                                mts = []
                                if "cA" in KSKIP:
                                    continue
                                for g in range(2):
                                    lt = f32r.get()
                                    for e_ in range(4):
                                        h = g * 4 + e_
                                        ACT(lt.t[:, e_ * 128:(e_ + 1) * 128], pacc[g].t[:, e_ * 128:(e_ + 1) * 128], AF.Exp, [pacc[g].b] + ts_rd, [lt.b],
                                            bias=ts_ap[:, 20 + h:21 + h])
                                    mt = b16r.get()
                                    TT("dve" if g == 0 else "pool", mt.t[:, :].rearrange("p (h e) -> p h e", h=4), lt.t[:, :].rearrange("p (h e) -> p h e", h=4),
                                       cbm.t[:, g * 128:(g + 1) * 128].unsqueeze(1).to_broadcast([128, 4, 128]), ALU.mult, [lt.b, cbm.b], [mt.b])
                                    mts.append(mt)
                                if "cC" in KSKIP:
                                    continue
                                poff = pf.get()
                                for g in range(2):
                                    gs_ = slice(g * 64, (g + 1) * 64)
                                    MM(poff.t[:, g * 256:(g + 1) * 256], szero.t[:, 2 + g, tok], sSb[d].t[:, :], True, True, [szero.b, sSb[d].b], [poff.b])
                                for h in range(8):
                                    MM(py.t[:, h * 64:(h + 1) * 64], mts[h // 4].t[:, (h % 4) * 128:(h % 4 + 1) * 128], sxtok.t[:, j, h * 64:(h + 1) * 64],
                                       d == 0 and h == 0, d == 1, [mts[h // 4].b, sxtok.b], [py.b], skip=True)
                                to = f32r.get()
                                TT("dve", to.t[:, :].rearrange("p (h e) -> p h e", h=8), poff.t[:, :].rearrange("p (h e) -> p h e", h=8),
                                   ts_ap[:, 36:44].unsqueeze(2).to_broadcast([128, 8, 64]), ALU.mult, [poff.b] + ts_rd, [to.b])
                                toff.append(to)
                            if "cA" in KSKIP or "cC" in KSKIP or "cD" in KSKIP:
                                continue
                            y = f32r.get()
                            TT("dve", y.t[:, :], py.t[:, :], toff[0].t[:, :], ALU.add, [py.b, toff[0].b], [y.b])
                            TT("pool", y.t[:, :], y.t[:, :], toff[1].t[:, :], ALU.add, [y.b, toff[1].b], [y.b])
                            xs = toff[0]
                            TT("pool", xs.t[:, :].rearrange("p (h e) -> p h e", h=8), sxtok.t[:, j, :].rearrange("p (h e) -> p h e", h=8),
                               dsk_bc.t[:, 0:8].unsqueeze(2).to_broadcast([128, 8, 64]), ALU.mult, [sxtok.b, dsk_bc.b], [xs.b])
                            TT("pool", y.t[:, :], y.t[:, :], xs.t[:, :], ALU.add, [y.b, xs.b], [y.b])
                            TT("pool", y.t[:, :], y.t[:, :], gate_tok[0].t[:, j, :], ALU.mult, [y.b, gate_tok[0].b], [y.b])
                            sq = toff[1]; ss = st4.get()
                            ACT(sq.t[:, :], y.t[:, :], AF.Square, [y.b], [sq.b, ss.b], accum=ss.t[:, 0:1])
                            rstd_from_sumsq(ss.t[:, 1:2], ss.t[:, 0:1], 512, [ss.b], [ss.b])
                            yg = b16r.get()
                            STT("dve", yg.t[:, :], y.t[:, :], ss.t[:, 1:2], sng_bc.t[:, :], ALU.mult, ALU.mult, [y.b, ss.b, sng_bc.b], [yg.b])
                            p = pb.get()
                            for i in range(4):
                                TR(p.t[:, i * 128:(i + 1) * 128], yg.t[:, i * 128:(i + 1) * 128], identb.t[:], [yg.b, identb.b], [p.b])
                            CP("act", ybT.t[:, :, tok], p.t[:, 0:512].rearrange("p (i t) -> p i t", i=4), [p.b], [ybT.b])
                            if "cE" in KSKIP:
                                continue
                            ssd_state_update(0, j, TSF.t[:, j, :], [tsf_b])
                            CP("act", sSb[0].t[:, :], sS[0].t[:, :], [sS[0].b], [sSb[0].b])
                    add_stage(l, P_SZ, st_sz)
                    add_branch_merge(l, 2, not (en_att or en_ml), c0)

                if not (en_att or en_ml or en_ssd):
                    def st_zero(_sl):
                        S.op("dve", lambda e: e.memset(mergedT.t[:], 0.0), [], [mergedT.b])
                    add_stage(l, None, st_zero)
                add_final(l, tok0, c0, nch)

        n_mt_layer = sum(sl_ // TM for sl_ in seq_lens)
        casts_per_mt = 0
        if depth > 1:
            casts_per_mt = max(len(v_) for v_ in pending_casts.values()) // max(1, n_mt_layer - 1) + 1

        def flush_casts(l):
            def f(_sl):
                while pending_casts[l]:
                    pending_casts[l].pop(0)()
            return f
        for l in range(depth):
            if l >= 1:
                add_stage(l, None, flush_casts(l))
            add_stage(l, None, (lambda l: lambda _sl: load_layer_params(l))(l))
            pos = 0
            for slen in seq_lens:
                seq_layer(l, pos, slen)
                pos += slen
        run_stages()
        import os as _os
        S.schedule(window=int(_os.environ.get('KWIN', '24')), enable=_os.environ.get('KSCHED', '1') == '1')
        S._plan()
        print('sched ok:', S.check(), {e: len(S.order[e]) for e in S.ENGS}, 'est_us', getattr(S, 'est_time', 0) / 1e3, 'busy_us', {e: round(v / 1e3) for e, v in getattr(S, 'busy', {}).items()})
        if S.tagging:
            for kk_, v_ in sorted(S.gaps.items(), key=lambda x: -x[1])[:40]:
                print('GAP', kk_, round(v_ / 1e3, 1))
        S.emit()
    return nc


_CACHE = {}


def _run(seq_lens, depth, per_core_x, weights, enable=("att", "mlstm", "ssd"), debug=(), full=False):
    key = (tuple(seq_lens), depth, tuple(enable), tuple(debug))
    if key not in _CACHE:
        _CACHE[key] = build(list(seq_lens), depth, enable, debug)
    nc = _CACHE[key]
    consts = make_consts(max(seq_lens))
    in_maps = []
    for xc in per_core_x:
        m = {"x": np.ascontiguousarray(xc, dtype=np.float32)}
        for k, v in weights.items():
            m[k] = np.ascontiguousarray(v, dtype=np.float32)
        m.update(consts)
        in_maps.append(m)
    res = run_bass_kernel_spmd(nc, in_maps, core_ids=list(range(len(per_core_x))))
    if full:
        return res.results
    return [r["y"] for r in res.results]


def kernel(x_prompt, x_sample, norm_g, w_in, q_norm_g, k_norm_g, attn_sink, w_att_out,
           mlstm_i_b, mlstm_f_b, mlstm_norm_g, w_mlstm_out, conv_w, conv_b, a_log,
           dt_bias, d_skip, ssm_norm_g, w_ssm_out, w_out):
    x_prompt = np.asarray(x_prompt, dtype=np.float32)
    x_sample = np.asarray(x_sample, dtype=np.float32)
    weights = dict(norm_g=norm_g, w_in=w_in, q_norm_g=q_norm_g, k_norm_g=k_norm_g, attn_sink=attn_sink,
                   w_att_out=w_att_out, mlstm_i_b=mlstm_i_b, mlstm_f_b=mlstm_f_b, mlstm_norm_g=mlstm_norm_g,
                   w_mlstm_out=w_mlstm_out, conv_w=conv_w, conv_b=conv_b, a_log=a_log, dt_bias=dt_bias,
                   d_skip=d_skip, ssm_norm_g=ssm_norm_g, w_ssm_out=w_ssm_out, w_out=w_out)
    weights = {k: np.asarray(v, dtype=np.float32) for k, v in weights.items()}
    depth = weights["w_in"].shape[0]
    nb, sp = x_prompt.shape[0], x_prompt.shape[1]
    ns, ss = x_sample.shape[0], x_sample.shape[1]
    ppc, spc = nb // NCORES, ns // NCORES
    seq_lens = [sp] * ppc + [ss] * spc
    per_core = []
    for c in range(NCORES):
        parts = [x_prompt[c * ppc + i] for i in range(ppc)] + [x_sample[c * spc + i] for i in range(spc)]
        per_core.append(np.concatenate(parts, axis=0))
    outs = _run(seq_lens, depth, per_core, weights)
    y_prompt = np.empty_like(x_prompt)
    y_sample = np.empty_like(x_sample)
    for c in range(NCORES):
        o = outs[c]
        pos = 0
        for i in range(ppc):
            y_prompt[c * ppc + i] = o[pos:pos + sp]; pos += sp
        for i in range(spc):
            y_sample[c * spc + i] = o[pos:pos + ss]; pos += ss
    return (y_prompt, y_sample)
```

```python
import contextlib
import math
import numpy as np
import concourse.bass as bass
import concourse.mybir as mybir
from concourse.bass_utils import run_bass_kernel_spmd

F32 = mybir.dt.float32
BF16 = mybir.dt.bfloat16
AF = mybir.ActivationFunctionType
ALU = mybir.AluOpType
AX = mybir.AxisListType

D = 1024
NCORES = 8
TM = 512
CH = 128
NPIECE = 25
ROPE_THETA = 500000.0
EPS = 1e-6


class Buf:
    __slots__ = ("name", "last_w", "readers")

    def __init__(self, name=""):
        self.name = name
        self.last_w = None
        self.readers = []


class Sched:
    ENGS = ("pe", "act", "dve", "pool", "sp")
    NPOOL = 4

    def __init__(self, nc, n_dma_sems=24):
        self.nc = nc
        self.nodes = []
        self.n_dma_sems = n_dma_sems
        self.order = None
        import os as _os2
        self.tagging = bool(_os2.environ.get('KTAG'))
        self.gaps = {}

    def _add(self, eng, fn, reads, writes, dma, cost):
        deps = set()
        for b in reads:
            if b.last_w is not None:
                deps.add(b.last_w)
        for b in writes:
            if b.last_w is not None:
                deps.add(b.last_w)
            deps.update(b.readers)
        gid = len(self.nodes)
        tag = ""
        if self.tagging:
            import sys as _sys
            f = _sys._getframe(2)
            names = []
            while f is not None and len(names) < 3:
                nm = f.f_code.co_name
                if nm not in ("op", "dma", "ACT", "TT", "TSC", "STT", "CP", "MM", "TR", "DMA", "SCAN", "RECIP", "<lambda>", "proj_feat", "proj_tok"):
                    names.append(nm)
                f = f.f_back
            tag = "/".join(names[:2])
        self.nodes.append(dict(eng=eng, fn=fn, dma=dma, deps=deps, cost=cost, tag=tag))
        for b in reads:
            b.readers.append(gid)
        for b in writes:
            b.last_w = gid
            b.readers = []
        return gid

    def op(self, eng, fn, reads=(), writes=(), cost=300.0):
        return self._add(eng, fn, reads, writes, False, cost)

    def dma(self, eng, fn, reads=(), writes=(), cost=3000.0):
        return self._add(eng, fn, reads, writes, True, cost)

    def schedule(self, window=24, enable=True):
        nodes = self.nodes
        per = {e: [] for e in self.ENGS}
        for g, n in enumerate(nodes):
            per[n["eng"]].append(g)
        if not enable:
            self.order = per
            return
        ptr = {e: 0 for e in self.ENGS}
        done = {e: [False] * len(per[e]) for e in self.ENGS}
        finish = [None] * len(nodes)
        t_eng = {e: 0.0 for e in self.ENGS}
        new = {e: [] for e in self.ENGS}
        remaining = len(nodes)
        LAT = 250.0
        W = {e: window for e in self.ENGS}
        W["sp"] = 6
        while remaining:
            best = None
            for e in self.ENGS:
                lst = per[e]
                p = ptr[e]
                while p < len(lst) and done[e][p]:
                    p += 1
                ptr[e] = p
                if p >= len(lst):
                    continue
                cnt = 0
                q = p
                cand = None
                while q < len(lst) and cnt < W[e]:
                    if not done[e][q]:
                        cnt += 1
                        g = lst[q]
                        nd = nodes[g]
                        ready = t_eng[e]
                        ok = True
                        for d in nd["deps"]:
                            f = finish[d]
                            if f is None:
                                ok = False
                                break
                            if nodes[d]["eng"] != e or nodes[d]["dma"]:
                                f += LAT
                            if f > ready:
                                ready = f
                        if ok:
                            key = (ready, q)
                            if cand is None or key < cand[0]:
                                cand = (key, q, g, ready)
                            if ready <= t_eng[e]:
                                break
                    q += 1
                if cand is not None:
                    if best is None or (cand[3], cand[2]) < (best[3], best[2]):
                        best = (e, cand[1], cand[2], cand[3])
            assert best is not None, "scheduler stuck"
            e, q, g, start = best
            nd = nodes[g]
            if self.tagging and e == "pe" and start > t_eng[e] + 500.0:
                dmax = max(nd["deps"], key=lambda d: finish[d])
                key = (nd["tag"], nodes[dmax]["eng"], nodes[dmax]["tag"])
                self.gaps[key] = self.gaps.get(key, 0.0) + (start - t_eng[e])
            if nd["dma"]:
                finish[g] = start + nd["cost"]
                t_eng[e] = start + 60.0
            else:
                finish[g] = start + nd["cost"]
                t_eng[e] = finish[g]
            done[e][q] = True
            new[e].append(g)
            remaining -= 1
        self.order = new
        self.est_time = max(f for f in finish if f is not None)
        self.busy = {e: sum(nodes[g]['cost'] for g in new[e] if not nodes[g]['dma']) for e in self.ENGS}

    def _plan(self):
        nodes = self.nodes
        pos = {}
        for e in self.ENGS:
            for i, g in enumerate(self.order[e]):
                pos[g] = i
        npool = self.NPOOL
        nsp = self.n_dma_sems - npool
        rr = {"sp": 0, "pool": 0}
        dma_val = [0] * self.n_dma_sems
        sem_of = {}
        for e in self.ENGS:
            for g in self.order[e]:
                if nodes[g]["dma"]:
                    if e == "pool":
                        s = nsp + rr["pool"]; rr["pool"] = (rr["pool"] + 1) % npool
                    else:
                        s = rr["sp"]; rr["sp"] = (rr["sp"] + 1) % nsp
                    prev = dma_val[s]
                    dma_val[s] += 16
                    sem_of[g] = (s, dma_val[s], prev)
        self.dma_val = dma_val
        flag = [False] * len(nodes)
        plan = {e: [] for e in self.ENGS}
        for e in self.ENGS:
            waited_c = {}
            waited_d = {}
            for g in self.order[e]:
                nd = nodes[g]
                waits = []
                deps = list(nd["deps"])
                for d in deps:
                    dn = nodes[d]
                    if dn["dma"]:
                        s, v, _ = sem_of[d]
                        if waited_d.get(s, 0) < v:
                            waited_d[s] = v
                            waits.append(("d", s, v))
                    else:
                        pe_ = dn["eng"]
                        if pe_ == e and e in ("pe", "sp"):
                            continue
                        if waited_c.get(pe_, -1) < pos[d]:
                            waited_c[pe_] = pos[d]
                            flag[d] = True
                            waits.append(("c", pe_, d))
                if nd["dma"]:
                    s, v, prev = sem_of[g]
                    if prev > 0 and waited_d.get(s, 0) < prev:
                        waited_d[s] = prev
                        waits.append(("d", s, prev))
                plan[e].append((g, waits))
        counts = {}
        for e in self.ENGS:
            c = 0
            for g in self.order[e]:
                if (not nodes[g]["dma"]) and flag[g]:
                    c += 1
                counts[g] = c
        self.plan, self.flag, self.counts, self.sem_of = plan, flag, counts, sem_of

    def check(self):
        nodes = self.nodes
        ptr = {e: 0 for e in self.ENGS}
        csem = {e: 0 for e in self.ENGS}
        dsem = [0] * self.n_dma_sems
        while True:
            prog = False
            for e in self.ENGS:
                pl = self.plan[e]
                while ptr[e] < len(pl):
                    g, waits = pl[ptr[e]]
                    ok = True
                    for w in waits:
                        if w[0] == "c":
                            if csem[w[1]] < self.counts[w[2]]:
                                ok = False
                        elif dsem[w[1]] < w[2]:
                            ok = False
                    if not ok:
                        break
                    if nodes[g]["dma"]:
                        dsem[self.sem_of[g][0]] += 16
                    elif self.flag[g]:
                        csem[e] += 1
                    ptr[e] += 1
                    prog = True
            if all(ptr[e] == len(self.plan[e]) for e in self.ENGS):
                return True
            if not prog:
                for e in self.ENGS:
                    if ptr[e] < len(self.plan[e]):
                        print("STUCK", e, ptr[e], "/", len(self.plan[e]), self.plan[e][ptr[e]][1])
                return False

    def emit(self, final_wait_eng="sp"):
        nc = self.nc
        nodes = self.nodes
        with contextlib.ExitStack() as st:
            csem = {e: st.enter_context(nc.semaphore("cs_" + e)) for e in self.ENGS}
            dsem = [st.enter_context(nc.semaphore("ds_%d" % i)) for i in range(self.n_dma_sems)]
            final = [(i, v) for i, v in enumerate(self.dma_val) if v > 0]
            block = st.enter_context(nc.Block())
            engobj = {"pe": "tensor", "act": "scalar", "dve": "vector", "pool": "gpsimd", "sp": "sync"}

            def make(e):
                def body(eng):
                    for g, waits in self.plan[e]:
                        for w in waits:
                            if w[0] == "c":
                                eng.wait_ge(csem[w[1]], self.counts[w[2]])
                            else:
                                eng.wait_ge(dsem[w[1]], w[2])
                        ins = nodes[g]["fn"](eng)
                        if nodes[g]["dma"]:
                            ins.then_inc(dsem[self.sem_of[g][0]], 16)
                        elif self.flag[g]:
                            ins.then_inc(csem[e], 1)
                    if e == final_wait_eng:
                        for (i, v) in final:
                            eng.wait_ge(dsem[i], v)
                return body

            for e in self.ENGS:
                if self.plan[e] or e == final_wait_eng:
                    getattr(block, engobj[e])(make(e))


class Tl:
    __slots__ = ("t", "b")

    def __init__(self, t, name=""):
        self.t = t
        self.b = Buf(name)


def make_consts(smax):
    c = {}
    c["c_ident"] = np.eye(128, dtype=np.float32)
    s = np.arange(128)[:, None]
    t = np.arange(128)[None, :]
    tri = np.zeros((128, 2, 512), np.float32)
    tri[:, 0, :] = np.tile((s <= t).astype(np.float32), (1, 4))
    tri[:, 1, :] = np.tile((s >= t).astype(np.float32), (1, 4))
    c["c_tri"] = tri
    f = np.arange(128)
    d = f % 64
    pos = np.arange(smax, dtype=np.float32)
    rope = np.zeros((128, 2, smax), np.float32)
    rope[:, 0, :] = 1.0
    inv_freq = (ROPE_THETA ** (-np.arange(8, dtype=np.float32) * 2.0 / 16.0)).astype(np.float32)
    for ff in range(128):
        if d[ff] < 16:
            ang = pos * inv_freq[d[ff] % 8]
            rope[ff, 0, :] = np.cos(ang)
            rope[ff, 1, :] = np.sin(ang)
    c["c_rope"] = rope
    rot = np.zeros((128, 128), np.float32)
    for ff in range(128):
        if d[ff] < 8:
            rot[ff + 8, ff] = -1.0
        elif d[ff] < 16:
            rot[ff - 8, ff] = 1.0
    c["c_rot"] = rot
    blk = np.zeros((128, 128), np.float32)
    blk[:64, :64] = 1.0 / 64
    blk[64:, 64:] = 1.0 / 64
    c["c_blk"] = blk
    sel = np.zeros((8, 8, 128), np.float32)
    for h in range(8):
        sel[h, h, :] = 1.0
    c["c_sel"] = sel
    rst = np.ones((8, 512), np.float32)
    rst[:, ::128] = 0.0
    c["c_reset"] = rst
    return c


O_AQ, O_AK, O_AV, O_AZ = 0, 512, 640, 768
O_MQ, O_MK, O_MV, O_MO = 1280, 1792, 2304, 2816
O_MI, O_MF, O_MZ = 3328, 3336, 3344
O_SX, O_SB, O_SC, O_SDT, O_SZ = 3856, 4368, 4496, 4624, 4640
O_G = 5152
P_AQ, P_KV, P_AZ, P_MQ, P_MK, P_MV, P_MO, P_MZ, P_SX, P_SBC, P_SZ = range(11)
P_G0 = 11
P_WO0, P_WO1 = 23, 24


def build(seq_lens, depth, enable=("att", "mlstm", "ssd"), debug=()):
    ntok = sum(seq_lens)
    smax = max(seq_lens)
    nc = bass.Bass("TRN2", target_bir_lowering=False)
    S = Sched(nc)
    es = contextlib.ExitStack()

    def din(name, shape, dt=F32):
        return nc.dram_tensor(name, list(shape), dt, kind="ExternalInput").ap()

    x_in = din("x", [ntok, D])
    y_out = nc.dram_tensor("y", [ntok, D], F32, kind="ExternalOutput").ap()
    W = dict(
        norm_g=din("norm_g", [depth, D]), w_in=din("w_in", [depth, D, 8224]),
        q_norm_g=din("q_norm_g", [depth, 64]), k_norm_g=din("k_norm_g", [depth, 64]),
        attn_sink=din("attn_sink", [depth, 8]), w_att_out=din("w_att_out", [depth, 512, D]),
        mlstm_i_b=din("mlstm_i_b", [depth, 2, 4]), mlstm_f_b=din("mlstm_f_b", [depth, 2, 4]),
        mlstm_norm_g=din("mlstm_norm_g", [depth, 512]), w_mlstm_out=din("w_mlstm_out", [depth, 512, D]),
        conv_w=din("conv_w", [depth, 5, 768]), conv_b=din("conv_b", [depth, 768]),
        a_log=din("a_log", [depth, 2, 8]), dt_bias=din("dt_bias", [depth, 2, 8]),
        d_skip=din("d_skip", [depth, 8]), ssm_norm_g=din("ssm_norm_g", [depth, 512]),
        w_ssm_out=din("w_ssm_out", [depth, 512, D]), w_out=din("w_out", [depth, D, D]),
    )
    C = dict(c_ident=din("c_ident", [128, 128]), c_tri=din("c_tri", [128, 2, 512]),
             c_rope=din("c_rope", [128, 2, smax]), c_rot=din("c_rot", [128, 128]),
             c_blk=din("c_blk", [128, 128]), c_sel=din("c_sel", [8, 8, 128]),
             c_reset=din("c_reset", [8, 512]))
    WS = nc.dram_tensor("ws_bf16", [depth, NPIECE, 128, 4096], BF16, kind="Internal").ap()
    ws_buf = [[[] for p in range(NPIECE)] for l in range(depth)]
    nscr = max(1, min(2, depth - 1))
    YS = [nc.dram_tensor("yscr%d" % i, [ntok, D], F32, kind="Internal").ap() for i in range(nscr)]
    ys_buf = [[[Buf("yscr"), Buf("yscr")] for c in range(ntok // CH)] for i in range(nscr)]
    nchmax = smax // CH
    BSTM = nc.dram_tensor("bst_m", [nchmax, 128, 516], BF16, kind="Internal").ap()
    BSTS = nc.dram_tensor("bst_s", [nchmax, 128, 256], BF16, kind="Internal").ap()
    KTOK = nc.dram_tensor("ktok", [nchmax, 128, 512], BF16, kind="Internal").ap()
    VTOK = nc.dram_tensor("vtok", [nchmax, 128, 512], BF16, kind="Internal").ap()
    XTOK = nc.dram_tensor("xtok", [nchmax, 128, 512], BF16, kind="Internal").ap()
    ktok_buf = [Buf("ktok%d" % i) for i in range(nchmax)]
    vtok_buf = [Buf("vtok%d" % i) for i in range(nchmax)]
    xtok_buf = [Buf("xtok%d" % i) for i in range(nchmax)]
    bstm_buf = [Buf("bstm%d" % i) for i in range(nchmax)]
    bsts_buf = [Buf("bsts%d" % i) for i in range(nchmax)]

    dbg_done = {}
    import os
    KSKIP = set(os.environ.get("KSKIP", "").split(","))

    def dbg(name, ap, bufs, dt=F32):
        if name not in debug or name in dbg_done:
            return
        dbg_done[name] = True
        shp = list(ap.shape)
        o = nc.dram_tensor("dbg_" + name, shp, dt, kind="ExternalOutput").ap()
        S.dma("sp", lambda e: e.dma_start(out=o, in_=ap), bufs, [])

    def sb(name, shape, dt=F32):
        return Tl(es.enter_context(nc.sbuf_tensor(name, list(shape), dt)), name)

    def psum(name, shape, dt=F32):
        return Tl(es.enter_context(nc.psum_tensor(name, list(shape), dt)), name)

    def fsz(ap):
        n = 1
        for s_ in ap.shape[1:]:
            n *= s_
        return n

    def ecost(eng, ap, mult=1.0):
        n = fsz(ap)
        if eng == "act":
            return 220.0 + 0.85 * n
        if eng == "dve":
            return 60.0 + 1.3 * n * mult
        return 100.0 + 2.6 * n * mult

    def ACT(out, in_, func, rd, wr, bias=None, scale=None, accum=None):
        kw = {}
        if bias is not None:
            kw["bias"] = bias
        if scale is not None:
            kw["scale"] = scale
        if accum is not None:
            kw["accum_out"] = accum
        S.op("act", lambda e: e.activation(out=out, in_=in_, func=func, **kw), rd, wr, cost=ecost("act", out))

    def TT(eng, out, in0, in1, op, rd, wr):
        S.op(eng, lambda e: e.tensor_tensor(out=out, in0=in0, in1=in1, op=op), rd, wr, cost=ecost(eng, out))

    def TSC(eng, out, in0, s1, op0, rd, wr, s2=None, op1=None):
        if op1 is None:
            S.op(eng, lambda e: e.tensor_scalar(out=out, in0=in0, scalar1=s1, scalar2=None, op0=op0), rd, wr, cost=ecost(eng, out))
        else:
            S.op(eng, lambda e: e.tensor_scalar(out=out, in0=in0, scalar1=s1, scalar2=s2, op0=op0, op1=op1), rd, wr, cost=ecost(eng, out))

    def STT(eng, out, in0, scalar, in1, op0, op1, rd, wr):
        S.op(eng, lambda e: e.scalar_tensor_tensor(out=out, in0=in0, scalar=scalar, in1=in1, op0=op0, op1=op1), rd, wr,
             cost=ecost(eng, out))

    def CP(eng, out, in_, rd, wr):
        if eng == "act":
            S.op("act", lambda e: e.copy(out=out, in_=in_), rd, wr, cost=ecost("act", out))
        else:
            S.op(eng, lambda e: e.tensor_copy(out=out, in_=in_), rd, wr, cost=ecost(eng, out, 1.4 if eng == "pool" else 1.0))

    def RECIP(out, in_, rd, wr):
        S.op("dve", lambda e: e.reciprocal(out=out, in_=in_), rd, wr, cost=100.0 + 6.6 * fsz(out))

    def MM(out, lhsT, rhs, start, stop, rd, wr, skip=False):
        n = max(fsz(out), 32)
        passes = 4 if lhsT.dtype == F32 else 1
        c = 25.0 + 0.5 * n * passes
        if skip:
            S.op("pe", lambda e: e.matmul(out, lhsT, rhs, start=start, stop=stop, skip_group_check=True), rd, wr, cost=c)
        else:
            S.op("pe", lambda e: e.matmul(out, lhsT, rhs, start=start, stop=stop), rd, wr, cost=c)

    def TR(out, in_, ident, rd, wr):
        S.op("pe", lambda e: e.transpose(out, in_, ident), rd, wr, cost=90.0)

    def DMA(out, in_, rd, wr, eng="sp", slow=False):
        nbytes = out.shape[0] * fsz(out) * (2 if out.dtype == BF16 else 4)
        c = 2500.0 + nbytes / (40.0 if eng == "pool" else 120.0)
        if slow:
            S.dma(eng, lambda e: e.dma_start(out=out, in_=in_, allow_slow_non_contiguous=True), rd, wr, cost=c)
        else:
            S.dma(eng, lambda e: e.dma_start(out=out, in_=in_), rd, wr, cost=c)

    def SCAN(out, d0, d1, init, op0, op1, rd, wr):
        S.op("dve", lambda e: e.tensor_tensor_scan(out=out, data0=d0, data1=d1, initial=init, op0=op0, op1=op1), rd, wr,
             cost=100.0 + 2.0 * fsz(out))

    class Ring:
        def __init__(self, tiles):
            self.tiles = tiles
            self.i = 0

        def get(self):
            t = self.tiles[self.i]
            self.i = (self.i + 1) % len(self.tiles)
            return t

    with es:
        f32r = Ring([sb("f32r%d" % i, [128, 512]) for i in range(7)])
        b16r = Ring([sb("b16r%d" % i, [128, 512], BF16) for i in range(8)])
        identf = sb("identf", [128, 128])
        identb = sb("identb", [128, 128], BF16)
        trib = sb("trib", [128, 2, 512], BF16)
        rotb = sb("rotb", [128, 128], BF16)
        blkb = sb("blkb", [128, 128], BF16)
        self_ = sb("sel", [40, 8, 128])
        resetm = sb("resetm", [40, 512])
        zrow = sb("zrow", [8, 1])
        DMA(identf.t[:], C["c_ident"][:, :], [], [identf.b])
        CP("dve", identb.t[:], identf.t[:], [identf.b], [identb.b])
        for half in range(2):
            stg = f32r.get()
            DMA(stg.t[:, :], C["c_tri"][:, half, :], [], [stg.b])
            CP("dve", trib.t[:, half, :], stg.t[:, :], [stg.b], [trib.b])
        stg = f32r.get()
        DMA(stg.t[:, 0:128], C["c_rot"][:, :], [], [stg.b])
        CP("dve", rotb.t[:], stg.t[:, 0:128], [stg.b], [rotb.b])
        stg = f32r.get()
        DMA(stg.t[:, 0:128], C["c_blk"][:, :], [], [stg.b])
        CP("dve", blkb.t[:], stg.t[:, 0:128], [stg.b], [blkb.b])
        DMA(self_.t[32:40, :, :], C["c_sel"][:, :, :], [], [self_.b])
        DMA(resetm.t[32:40, :], C["c_reset"][:, :], [], [resetm.b])
        S.op("dve", lambda e: e.memset(zrow.t[:], 0.0), [], [zrow.b])
        epsc = sb("epsc", [128, 1])
        S.op("dve", lambda e: e.memset(epsc.t[:], EPS), [], [epsc.b])
        negm = sb("negm", [128, 2, 128], BF16)
        TSC("dve", negm.t[:, :, :], trib.t[:, :, 0:128], -1.0, ALU.add, [trib.b], [negm.b], s2=30000.0, op1=ALU.mult)

        pending_casts = {l: [] for l in range(depth)}
        _DMA_real = DMA

        def DMA(out, in_, rd, wr, eng="sp", slow=False, _defer=[None]):
            if _defer[0] is not None and eng == "pool":
                pending_casts[_defer[0]].append(lambda: _DMA_real(out, in_, rd, wr, eng=eng, slow=slow))
            else:
                _DMA_real(out, in_, rd, wr, eng=eng, slow=slow)
        _defer_box = DMA.__defaults__[2]
        for l in range(depth):
            _defer_box[0] = l if l >= 1 else None
            wi = W["w_in"][l].rearrange("(k p) c -> p k c", p=128)

            def wdst(p, off, n, cw=512, l=l):
                return WS[l, p].rearrange("p (k c) -> p k c", c=cw)[:, :, off:off + n]

            def cast(p, off, c0, n, l=l, wi=wi):
                b_ = Buf("ws"); ws_buf[l][p].append(b_)
                DMA(wdst(p, off, n), wi[:, :, c0:c0 + n], [], [b_], eng="pool")

            for i in range(4):
                cast(P_AQ, i * 128, O_AQ + i * 64, 64)
                cast(P_AQ, i * 128 + 64, O_AQ + (4 + i) * 64, 64)
            cast(P_KV, 0, O_AK, 256)
            for d_ in range(2):
                base_ = 256 + d_ * 72
                b0_ = Buf("ws"); ws_buf[l][P_KV].append(b0_)
                DMA(wdst(P_KV, base_, 72), wi[:, :, O_MI:O_MI + 72], [], [b0_], eng="pool")
                for (off_, c0_, n_) in ((0, O_MI + d_ * 4, 4), (32, O_MF + d_ * 4, 4), (64, O_SDT + d_ * 8, 8)):
                    b_ = Buf("ws"); ws_buf[l][P_KV].append(b_)
                    DMA(wdst(P_KV, base_ + off_, n_), wi[:, :, c0_:c0_ + n_], [b0_], [b_], eng="pool")
            cast(P_AZ, 0, O_AZ, 512)
            cast(P_MQ, 0, O_MQ, 512)
            cast(P_MK, 0, O_MK, 512)
            cast(P_MV, 0, O_MV, 512)
            cast(P_MO, 0, O_MO, 512)
            cast(P_MZ, 0, O_MZ, 512)
            cast(P_SX, 0, O_SX, 512)
            cast(P_SBC, 0, O_SB, 256)
            cast(P_SZ, 0, O_SZ, 512)
            for bi_, nm in enumerate(("w_att_out", "w_mlstm_out", "w_ssm_out")):
                src = W[nm][l].rearrange("(k p) c -> p k c", p=128)
                for q_ in range(4):
                    p = P_G0 + bi_ * 4 + q_
                    b_ = Buf("ws"); ws_buf[l][p].append(b_)
                    DMA(WS[l, p][:, 0:2048].rearrange("p (k c) -> p k c", c=256),
                        wi[:, :, O_G + bi_ * 1024 + q_ * 256:O_G + bi_ * 1024 + (q_ + 1) * 256], [], [b_], eng="pool")
                    b_ = Buf("ws"); ws_buf[l][p].append(b_)
                    DMA(WS[l, p][:, 2048:3072].rearrange("p (k c) -> p k c", c=256), src[:, :, q_ * 256:(q_ + 1) * 256], [], [b_], eng="pool")
            wo = W["w_out"][l].rearrange("(k p) c -> p k c", p=128)
            for p_, lo_ in ((P_WO0, 0), (P_WO1, 512)):
                b_ = Buf("ws"); ws_buf[l][p_].append(b_)
                DMA(wdst(p_, 0, 512), wo[:, :, lo_:lo_ + 512], [], [b_], eng="pool")

        _defer_box[0] = None
        NSLOT = 3
        wslots = [sb("wslot%d" % i, [128, 4096], BF16) for i in range(NSLOT)]

        ng_bc = sb("ng_bc", [128, D])
        gq = sb("gq", [128, 1]); gk = sb("gk", [128, 1])
        esink = sb("esink", [128, 8])
        ib = [sb("ib%d" % d, [4, 1]) for d in range(2)]
        nfb = [sb("nfb%d" % d, [4, 1]) for d in range(2)]
        mng_bc = sb("mng_bc", [128, 512]); sng_bc = sb("sng_bc", [128, 512])
        cw = sb("cw", [128, 6, 5]); cb = sb("cb", [128, 6])
        acoef = [sb("acoef%d" % d, [40, 1]) for d in range(2)]
        dtb = [sb("dtb%d" % d, [40, 1]) for d in range(2)]
        dsk_bc = sb("dsk_bc", [128, 8])

        def load_layer_params(l):
            DMA(ng_bc.t[:], W["norm_g"][l].partition_broadcast(128), [], [ng_bc.b])
            for half in range(2):
                DMA(gq.t[half * 64:(half + 1) * 64, :], W["q_norm_g"][l].rearrange("(d o) -> d o", o=1), [], [gq.b])
                DMA(gk.t[half * 64:(half + 1) * 64, :], W["k_norm_g"][l].rearrange("(d o) -> d o", o=1), [], [gk.b])
            S.op("act", lambda e: e.mul(out=gq.t[:], in_=gq.t[:], mul=0.125), [gq.b], [gq.b])
            DMA(esink.t[:], W["attn_sink"][l].partition_broadcast(128), [], [esink.b])
            ACT(esink.t[:], esink.t[:], AF.Exp, [esink.b], [esink.b])
            for d in range(2):
                DMA(ib[d].t[:], W["mlstm_i_b"][l, d].rearrange("(d o) -> d o", o=1), [], [ib[d].b])
                DMA(nfb[d].t[:], W["mlstm_f_b"][l, d].rearrange("(d o) -> d o", o=1), [], [nfb[d].b])
                S.op("act", (lambda d: lambda e: e.mul(out=nfb[d].t[:], in_=nfb[d].t[:], mul=-1.0))(d), [nfb[d].b], [nfb[d].b])
                DMA(acoef[d].t[32:40, :], W["a_log"][l, d].rearrange("(d o) -> d o", o=1), [], [acoef[d].b])
                ACT(acoef[d].t[32:40, :], acoef[d].t[32:40, :], AF.Exp, [acoef[d].b], [acoef[d].b])
                S.op("act", (lambda d: lambda e: e.mul(out=acoef[d].t[32:40, :], in_=acoef[d].t[32:40, :], mul=-1.0))(d), [acoef[d].b], [acoef[d].b])
                DMA(dtb[d].t[32:40, :], W["dt_bias"][l, d].rearrange("(d o) -> d o", o=1), [], [dtb[d].b])
            DMA(mng_bc.t[:], W["mlstm_norm_g"][l].partition_broadcast(128), [], [mng_bc.b])
            DMA(sng_bc.t[:], W["ssm_norm_g"][l].partition_broadcast(128), [], [sng_bc.b])
            for ti in range(6):
                DMA(cw.t[:, ti, :], W["conv_w"][l][:, ti * 128:(ti + 1) * 128].rearrange("k p -> p k"), [], [cw.b], slow=True)
                DMA(cb.t[:, ti:ti + 1], W["conv_b"][l][ti * 128:(ti + 1) * 128].rearrange("(p o) -> p o", o=1), [], [cb.b])
            DMA(dsk_bc.t[:], W["d_skip"][l].partition_broadcast(128), [], [dsk_bc.b])

        hT = sb("hT", [128, 8, 8 * CH], BF16)
        hT_b = [Buf("hT%d" % i) for i in range(8)]
        hslot_chunk = [None] * 8
        xring = Ring([sb("xt%d" % i, [128, D]) for i in range(2)])
        hbring = Ring([sb("hb%d" % i, [128, D], BF16) for i in range(2)])
        st4 = Ring([sb("st4_%d" % i, [128, 8]) for i in range(6)])

        pf = Ring([psum("pf%d" % i, [128, 512]) for i in range(4)])
        pacc = [psum("pacc%d" % i, [128, 512]) for i in range(2)]
        pb = Ring([psum("pb%d" % i, [128, 1024], BF16) for i in range(2)])

        mergedT = sb("mergedT", [128, 8, TM])
        GT = sb("GT", [128, 4096], BF16)

        class View:
            def __init__(self, ap, b):
                self.t = ap
                self.b = b
        mergedTb = View(GT.t[:, :].rearrange("p (k t) -> p k t", k=8), GT.b)
        gate_tok = [View(GT.t[:, i * 2048:(i + 1) * 2048].rearrange("p (j c) -> p j c", j=4), GT.b) for i in range(2)]
        ybT = sb("ybT", [128, 4, TM], BF16)
        rope_t = sb("rope_t", [128, 2, TM])
        kT = sb("kT", [128, 6 * CH], BF16)
        vaug = sb("vaug", [128, 6, 2, 65], BF16)
        mqT = sb("mqT", [128, 4, TM], BF16)
        qT = mqT
        mkT = sb("mkT", [128, 4, TM], BF16)
        mvaug = sb("mvaug", [128, 4, 4, 129], BF16)
        mktok = sb("mktok", [128, 4, 512], BF16)
        mC = [sb("mC%d" % d, [128, 516]) for d in range(2)]
        mCb = [sb("mCb%d" % d, [128, 516], BF16) for d in range(2)]
        srawr = Ring([sb("sraw%d" % i, [128, TM + 4]) for i in range(2)])
        sconv = sb("sconv", [128, 6, TM], BF16)
        sxtok = sb("sxtok", [128, 4, 512], BF16)
        szero = sb("szero", [128, 4, TM], BF16)
        S.op("pool", lambda e: e.memset(szero.t[:], 0.0), [], [szero.b])
        sbtok = sb("sbtok", [128, 4, 128], BF16)
        sS = [sb("sS%d" % d, [128, 256]) for d in range(2)]
        sSb = [sb("sSb%d" % d, [128, 256], BF16) for d in range(2)]
        RT = [sb("rt%d" % i, [40, 512]) for i in range(8)]
        for rt_ in RT:
            S.op("pool", (lambda rt_: lambda e: e.memset(rt_.t[:], 0.0))(rt_), [], [rt_.b])
        carryB = [sb("carryB%d" % d, [4, 1]) for d in range(2)]
        carryM = [sb("carryM%d" % d, [4, 1]) for d in range(2)]
        mprev = sb("mprev", [4, 4])
        NTS = 52
        TSF = sb("TSF", [128, 4, NTS])
        TSBm = sb("TSBm", [128, 4, NTS])
        ACSF = sb("ACSF", [40, TM])
        ACSBm = sb("ACSBm", [40, TM])
        acs_hl = {0: (sb("acsfh", [40, TM], BF16), sb("acsfl", [40, TM], BF16)),
                  1: (sb("acsbh", [40, TM], BF16), sb("acsbl", [40, TM], BF16))}
        selb = sb("selb", [40, 8, 128], BF16)
        CP("dve", selb.t[32:40, :, :], self_.t[32:40, :, :], [self_.b], [selb.b])

        def mk_hilo(d, src):
            hi, lo = acs_hl[d]
            CP("act", hi.t[32:40, :], src.t[32:40, :], [src.b], [hi.b])
            TT("dve", lo.t[32:40, :], src.t[32:40, :], hi.t[32:40, :], ALU.subtract, [src.b, hi.b], [lo.b])
        nmmax = smax // TM
        TSD = nc.dram_tensor("tsd", [nmmax, 128, 4 * NTS], F32, kind="Internal").ap()
        ACSD = nc.dram_tensor("acsd", [nmmax, 8, TM], F32, kind="Internal").ap()
        tsd_buf = [Buf("tsd%d" % i) for i in range(nmmax)]
        acsd_buf = [Buf("acsd%d" % i) for i in range(nmmax)]

        stages = []

        def add_stage(l, piece, fn):
            stages.append((l, piece, fn))

        def run_stages():
            loads = [i for i, s_ in enumerate(stages) if s_[1] is not None]
            slot_of = {}
            nxt = 0
            PRE = 2
            for i, (l, piece, fn) in enumerate(stages):
                while nxt < len(loads) and (nxt < PRE or loads[nxt - PRE] <= i):
                    j = loads[nxt]
                    sl = wslots[nxt % NSLOT]
                    lj, pj, _ = stages[j]
                    if pj == P_KV or pj == P_SBC:
                        nc_ = 400 if pj == P_KV else 256
                        DMA(sl.t[:, :].rearrange("p (k c) -> p k c", c=512)[:, :, 0:nc_],
                            WS[lj, pj].rearrange("p (k c) -> p k c", c=512)[:, :, 0:nc_], ws_buf[lj][pj], [sl.b])
                    elif P_G0 <= pj < P_WO0:
                        DMA(sl.t[:, 0:3072], WS[lj, pj][:, 0:3072], ws_buf[lj][pj], [sl.b])
                    else:
                        DMA(sl.t[:], WS[lj, pj], ws_buf[lj][pj], [sl.b])
                    slot_of[j] = sl
                    nxt += 1
                fn(slot_of.get(i))

        def src_of(l):
            return x_in if l == 0 else YS[(l - 1) % nscr]

        def src_buf(l, r0):
            return [] if l == 0 else ys_buf[(l - 1) % nscr][r0 // CH]

        def dst_of(l):
            return y_out if l == depth - 1 else YS[l % nscr]

        def dst_buf2(l, r0, half):
            return [] if l == depth - 1 else [ys_buf[l % nscr][r0 // CH][half]]

        def rstd_from_sumsq(dst, ssum, n, rd, wr, npart=128):
            TSC("dve", dst, ssum, 1.0 / n, ALU.mult, rd, wr, s2=EPS, op1=ALU.add)
            ACT(dst, dst, AF.Sqrt, wr, wr)
            S.op("dve", lambda e: e.reciprocal(out=dst, in_=dst), wr, wr)

        def ensure_h(l, tok0, c):
            sl = c % 8
            if hslot_chunk[sl] == (l, tok0, c):
                return
            hslot_chunk[sl] = (l, tok0, c)
            xt = xring.get()
            r0 = tok0 + c * CH
            DMA(xt.t[:], src_of(l)[r0:r0 + CH, :], src_buf(l, r0), [xt.b])
            st = st4.get()
            hb = hbring.get()
            ACT(hb.t[:], xt.t[:], AF.Square, [xt.b], [hb.b, st.b], accum=st.t[:, 0:1])
            rstd_from_sumsq(st.t[:, 1:2], st.t[:, 0:1], D, [st.b], [st.b])
            STT("dve", hb.t[:], xt.t[:], st.t[:, 1:2], ng_bc.t[:], ALU.mult, ALU.mult, [xt.b, st.b, ng_bc.b], [hb.b])
            p = pb.get()
            for k in range(8):
                TR(p.t[:, k * 128:(k + 1) * 128], hb.t[:, k * 128:(k + 1) * 128], identb.t[:], [hb.b, identb.b], [p.b])
            CP("act", hT.t[:, :, sl * CH:(sl + 1) * CH], p.t[:].rearrange("p (k t) -> p k t", k=8), [p.b], [hT_b[sl]])
            if c == 0:
                dbg("hT", hT.t[:, :, sl * CH:(sl + 1) * CH], [hT_b[sl]], BF16)

        def h_rhs(k, c0, n):
            s0 = c0 % 8
            assert s0 + n <= 8
            return hT.t[:, k, s0 * CH:(s0 + n) * CH]

        def h_bufs(c0, n):
            return [hT_b[(c0 + i) % 8] for i in range(n)]

        def proj_feat(sl, col0, ncols, c0, nchunks, pt, pcol0=0):
            for k in range(8):
                MM(pt.t[0:ncols, pcol0:pcol0 + nchunks * CH], sl.t[:, k * 512 + col0:k * 512 + col0 + ncols],
                   h_rhs(k, c0, nchunks), k == 0, k == 7, [sl.b] + h_bufs(c0, nchunks), [pt.b])

        def proj_tok(sl, col0, ncols, c, pt):
            s0 = c % 8
            for k in range(8):
                MM(pt.t[:, 0:ncols], hT.t[:, k, s0 * CH:(s0 + 1) * CH], sl.t[:, k * 512 + col0:k * 512 + col0 + ncols],
                   k == 0, k == 7, [sl.b, hT_b[s0]], [pt.b])

        mL = slice(0, 4)
        sD = slice(32, 40)

        def gate_rows(l, d, sl, c0, first, ts_dst, ts_bufs, acs_dst, acs_bufs):
            rev = (d == 1)

            def rv(ap2d):
                return ap2d[:, ::-1] if rev else ap2d

            pg_ = pf.get()
            proj_feat(sl, 256 + d * 72, 72, c0, 4, pg_)
            pmi = pmf = pdt = pg_
            R = RT
            IG, L1, Bp, Mg, WI, DEC = R[0], R[1], R[2], R[3], R[4], R[5]
            ACT(IG.t[mL, :], rv(pmi.t[0:4, :]), AF.Identity, [pmi.b, ib[d].b], [IG.b], bias=ib[d].t[:, 0:1])
            ACT(L1.t[mL, :], rv(pmf.t[32:36, :]), AF.Exp, [pmf.b, nfb[d].b], [L1.b], bias=nfb[d].t[:, 0:1], scale=-1.0)
            ACT(L1.t[mL, :], L1.t[mL, :], AF.Ln, [L1.b], [L1.b], bias=1.0)
            if first:
                S.op("dve", lambda e: e.memset(carryB[d].t[:], 0.0), [], [carryB[d].b])
                S.op("dve", lambda e: e.memset(carryM[d].t[:], 0.0), [], [carryM[d].b])
            SCAN(Bp.t[mL, :], L1.t[mL, :], zrow.t[0:4, 0:1].to_broadcast([4, 512]), carryB[d].t[:, 0:1], ALU.add, ALU.add,
                 [L1.b, zrow.b, carryB[d].b], [Bp.b])
            A = IG
            TT("dve", A.t[mL, :], IG.t[mL, :], Bp.t[mL, :], ALU.add, [IG.b, Bp.b], [A.b])
            SCAN(Mg.t[mL, :], A.t[mL, :], A.t[mL, :], carryM[d].t[:, 0:1], ALU.max, ALU.max, [A.b, carryM[d].b], [Mg.b])

            def r3(tl):
                return tl.t[mL, :].rearrange("p (c t) -> p c t", c=4)

            Mg3 = r3(Mg)
            CP("dve", mprev.t[:, 0:1], carryM[d].t[:, 0:1], [carryM[d].b], [mprev.b])
            CP("dve", mprev.t[:, 1:4], Mg3[:, 0:3, 127], [Mg.b], [mprev.b])
            CP("dve", carryB[d].t[:, 0:1], Bp.t[mL, 511:512], [Bp.b], [carryB[d].b])
            CP("dve", carryM[d].t[:, 0:1], Mg.t[mL, 511:512], [Mg.b], [carryM[d].b])
            mend_bc = Mg3[:, :, 127:128].to_broadcast([4, 4, 128])
            mprev_bc = mprev.t[:, :].unsqueeze(2).to_broadcast([4, 4, 128])
            U = L1
            TT("dve", r3(U), r3(Mg), mend_bc, ALU.subtract, [Mg.b, L1.b], [U.b])
            TSC("dve", U.t[mL, :], U.t[mL, :], -60.0, ALU.max, [U.b], [U.b])
            ACT(U.t[mL, :], U.t[mL, :], AF.Exp, [U.b], [U.b], scale=-1.0)
            TT("dve", r3(WI), r3(Mg), mprev_bc, ALU.subtract, [Mg.b, mprev.b], [WI.b])
            ACT(WI.t[mL, :], WI.t[mL, :], AF.Exp, [WI.b], [WI.b], scale=-1.0)
            FL = Bp
            TT("dve", FL.t[mL, :], Bp.t[mL, :], Mg.t[mL, :], ALU.subtract, [Bp.b, Mg.b], [FL.b])
            ACT(FL.t[mL, :], FL.t[mL, :], AF.Exp, [FL.b], [FL.b])
            TT("dve", r3(DEC), mprev_bc, mend_bc, ALU.subtract, [Mg.b, mprev.b], [DEC.b])
            ACT(DEC.t[mL, :], DEC.t[mL, :], AF.Exp, [DEC.b], [DEC.b])
            WK = A
            TT("dve", r3(WK), r3(A), mend_bc, ALU.subtract, [A.b, Mg.b], [WK.b])
            ACT(WK.t[mL, :], WK.t[mL, :], AF.Exp, [WK.b], [WK.b])
            DT, LDT, DA, ACS, EA = R[0], R[1], R[2], R[3], R[4]
            ACT(DT.t[sD, :], rv(pdt.t[64:72, :]), AF.Exp, [pdt.b, dtb[d].b], [DT.b], bias=dtb[d].t[sD, 0:1])
            ACT(DT.t[sD, :], DT.t[sD, :], AF.Ln, [DT.b], [DT.b], bias=1.0)
            ACT(LDT.t[sD, :], DT.t[sD, :], AF.Ln, [DT.b], [LDT.b])
            TSC("dve", DA.t[sD, :], DT.t[sD, :], acoef[d].t[sD, 0:1], ALU.mult, [DT.b, acoef[d].b], [DA.b])
            SCAN(ACS.t[sD, :], resetm.t[sD, :], DA.t[sD, :], 0.0, ALU.mult, ALU.add, [resetm.b, DA.b], [ACS.b])

            def r8(tl):
                return tl.t[sD, :].rearrange("p (c t) -> p c t", c=4)

            aend_bc = r8(ACS)[:, :, 127:128].to_broadcast([8, 4, 128])
            BL = LDT
            TT("dve", BL.t[sD, :], LDT.t[sD, :], ACS.t[sD, :], ALU.subtract, [LDT.b, ACS.b], [BL.b])
            WST = DA
            TT("dve", r8(WST), aend_bc, r8(ACS), ALU.subtract, [ACS.b, DA.b], [WST.b])
            ACT(WST.t[sD, :], WST.t[sD, :], AF.Exp, [WST.b], [WST.b])
            TT("dve", WST.t[sD, :], WST.t[sD, :], DT.t[sD, :], ALU.mult, [WST.b, DT.b], [WST.b])
            ACT(EA.t[sD, :], ACS.t[sD, :], AF.Exp, [ACS.b], [EA.b])
            CD = DT
            CP("dve", r8(CD), aend_bc, [ACS.b, WST.b, DT.b], [CD.b])
            ACT(CD.t[sD, :], CD.t[sD, :], AF.Exp, [CD.b], [CD.b])
            quants = [(WK, mL, 4, 0), (U, mL, 4, 4), (WI, mL, 4, 8), (FL, mL, 4, 12), (DEC, mL, 4, 16),
                      (BL, sD, 8, 20), (WST, sD, 8, 28), (EA, sD, 8, 36), (CD, sD, 8, 44)]
            pts = pf.get()
            if rev:
                order_ = [R[0], R[1], R[2], R[4], R[5]]
                rmap = {}
                for ti_, tl_ in enumerate(order_):
                    q2 = R[6 + (ti_ % 2)]
                    rmap[id(tl_)] = (q2, ti_)
                quants = sorted(quants, key=lambda x: rmap[id(x[0])][1])
                done_ = set()
            for qi, (q, ps_, r, off) in enumerate(quants):
                if rev:
                    q2, ti_ = rmap[id(q)]
                    if ti_ not in done_:
                        done_.add(ti_)
                        CP("dve", q2.t[0:40, :], q.t[0:40, ::-1], [q.b], [q2.b])
                    q = q2
                for j in range(4):
                    MM(pts.t[:, j * 64 + off:j * 64 + off + r], q.t[ps_, j * CH:(j + 1) * CH], identf.t[ps_, ps_],
                       True, True, [q.b, identf.b], [pts.b])
            CP("act", ts_dst, pts.t[:, 0:256].rearrange("p (j c) -> p j c", j=4)[:, :, 0:NTS], [pts.b], ts_bufs)
            if rev:
                CP("pool", acs_dst, ACS.t[sD, ::-1], [ACS.b], acs_bufs)
            else:
                CP("pool", acs_dst, ACS.t[sD, :], [ACS.b], acs_bufs)

        def ssd_conv(l, slx, slbc, c0, nch_seq, tiles):
            for ti in tiles:
                if "conv" in KSKIP:
                    break
                sl, col0 = (slx, ti * 128) if ti < 4 else (slbc, (ti - 4) * 128)
                sraw = srawr.get()
                pm = pf.get()
                proj_feat(sl, col0, 128, c0, 4, pm)
                CP("act", sraw.t[:, 2:2 + TM], pm.t[:, :], [pm.b], [sraw.b])
                ph = pf.get()
                if c0 > 0:
                    s0 = (c0 - 1) % 8
                    for k in range(8):
                        MM(ph.t[:, 0:32], sl.t[:, k * 512 + col0:k * 512 + col0 + 128], hT.t[:, k, s0 * CH + 96:s0 * CH + 128],
                           k == 0, k == 7, [sl.b, hT_b[s0]], [ph.b])
                    CP("dve", sraw.t[:, 0:2], ph.t[:, 30:32], [ph.b], [sraw.b])
                else:
                    S.op("dve", (lambda sraw: lambda e: e.memset(sraw.t[:, 0:2], 0.0))(sraw), [], [sraw.b])
                if c0 + 4 < nch_seq:
                    s0 = (c0 + 4) % 8
                    for k in range(8):
                        MM(ph.t[:, 32:64], sl.t[:, k * 512 + col0:k * 512 + col0 + 128], hT.t[:, k, s0 * CH:s0 * CH + 32],
                           k == 0, k == 7, [sl.b, hT_b[s0]], [ph.b])
                    CP("dve", sraw.t[:, TM + 2:TM + 4], ph.t[:, 32:34], [ph.b], [sraw.b])
                else:
                    S.op("dve", (lambda sraw: lambda e: e.memset(sraw.t[:, TM + 2:TM + 4], 0.0))(sraw), [], [sraw.b])
                acc = f32r.get()
                TSC("dve", acc.t[:, :], sraw.t[:, 0:TM], cw.t[:, ti, 0:1], ALU.mult, [sraw.b, cw.b, cb.b], [acc.b],
                    s2=cb.t[:, ti:ti + 1], op1=ALU.add)
                for kk in range(1, 5):
                    STT("dve", acc.t[:, :], sraw.t[:, kk:kk + TM], cw.t[:, ti, kk:kk + 1], acc.t[:, :], ALU.mult, ALU.add,
                        [sraw.b, cw.b, acc.b], [acc.b])
                ACT(sconv.t[:, ti, :], acc.t[:, :], AF.Silu, [acc.b], [sconv.b])

        def ssd_tokmajor(j, do_x=True):
            if "tokm" in KSKIP:
                return
            if do_x:
                p = pb.get()
                for ti in range(4):
                    TR(p.t[:, ti * 128:(ti + 1) * 128], sconv.t[:, ti, j * CH:(j + 1) * CH], identb.t[:], [sconv.b, identb.b], [p.b])
                CP("act", sxtok.t[:, j, :], p.t[:, 0:512], [p.b], [sxtok.b])
            if "tokb" in KSKIP:
                return
            p2 = pb.get()
            TR(p2.t[:, 0:128], sconv.t[:, 4, j * CH:(j + 1) * CH], identb.t[:], [sconv.b, identb.b], [p2.b])
            CP("act", sbtok.t[:, j, :], p2.t[:, 0:128], [p2.b], [sbtok.b])

        def ssd_state_update(d, j, ts_ap, ts_rd):
            xw = b16r.get()
            TT("pool" if d == 0 else "dve", xw.t[:, :].rearrange("p (h e) -> p h e", h=8), sxtok.t[:, j, :].rearrange("p (h e) -> p h e", h=8),
               ts_ap[:, 28:36].unsqueeze(2).to_broadcast([128, 8, 64]), ALU.mult, [sxtok.b] + ts_rd, [xw.b])
            pS = pf.get()
            for g in range(2):
                MM(pS.t[:, g * 256:(g + 1) * 256], sbtok.t[:, j, :], xw.t[:, g * 256:(g + 1) * 256], True, True, [sbtok.b, xw.b], [pS.b])
            for g in range(2):
                ps_ = slice(g * 64, (g + 1) * 64)
                TT("dve", sS[d].t[ps_, :].rearrange("p (h e) -> p h e", h=4), sS[d].t[ps_, :].rearrange("p (h e) -> p h e", h=4),
                   ts_ap[ps_, 44 + g * 4:44 + g * 4 + 4].unsqueeze(2).to_broadcast([64, 4, 64]), ALU.mult, [sS[d].b] + ts_rd, [sS[d].b])
                TT("dve", sS[d].t[ps_, :], sS[d].t[ps_, :], pS.t[ps_, g * 256:(g + 1) * 256], ALU.add, [sS[d].b, pS.b], [sS[d].b])

        def mlstm_state_update(d, j, ts_ap, ts_rd, pre=None):
            if pre is not None:
                vw, wkb = pre["vw"], pre["wkb"]
            else:
                vw = b16r.get()
                TT("pool" if d == 0 else "dve", vw.t[:, :].rearrange("p (h e) -> p h e", h=4), mvaug.t[:, j, :, 0:128],
                   ts_ap[:, 0:4].unsqueeze(2).to_broadcast([128, 4, 128]), ALU.mult, [mvaug.b] + ts_rd, [vw.b])
                wkb = b16r.get()
                CP("dve", wkb.t[:, 0:4], ts_ap[:, 0:4], ts_rd, [wkb.b])
            pC = pf.get(); pn = pf.get()
            for h in range(4):
                MM(pC.t[:, h * 128:(h + 1) * 128], mktok.t[:, j, h * 128:(h + 1) * 128], vw.t[:, h * 128:(h + 1) * 128], True, True,
                   [mktok.b, vw.b], [pC.b])
                MM(pn.t[:, h:h + 1], mktok.t[:, j, h * 128:(h + 1) * 128], wkb.t[:, h:h + 1], True, True, [mktok.b, wkb.b], [pn.b])
            dec_bc = ts_ap[:, 16:20]
            TT("dve", mC[d].t[:, 0:512].rearrange("p (h e) -> p h e", h=4), mC[d].t[:, 0:512].rearrange("p (h e) -> p h e", h=4),
               dec_bc.unsqueeze(2).to_broadcast([128, 4, 128]), ALU.mult, [mC[d].b] + ts_rd, [mC[d].b])
            TT("dve", mC[d].t[:, 512:516], mC[d].t[:, 512:516], dec_bc, ALU.mult, [mC[d].b] + ts_rd, [mC[d].b])
            TT("dve", mC[d].t[:, 0:512], mC[d].t[:, 0:512], pC.t[:, :], ALU.add, [mC[d].b, pC.b], [mC[d].b])
            TT("dve", mC[d].t[:, 512:516], mC[d].t[:, 512:516], pn.t[:, 0:4], ALU.add, [mC[d].b, pn.b], [mC[d].b])

        def mlstm_v_tok(sl, c, j):
            pv = pf.get()
            proj_tok(sl, 0, 512, c, pv)
            CP("act", mvaug.t[:, j, :, 0:128], pv.t[:, :].rearrange("p (h e) -> p h e", h=4), [pv.b], [mvaug.b])

        def add_branch_merge(l, bi, first, c0):
            for q_ in range(4):
                def st_g(sl, q_=q_):
                    for oo in range(2):
                        o = q_ * 2 + oo
                        pg = pf.get()
                        for k_ in range(8):
                            MM(pg.t[:, :], sl.t[:, k_ * 256 + oo * 128:k_ * 256 + (oo + 1) * 128], h_rhs(k_, c0, 4), k_ == 0, k_ == 7,
                               [sl.b] + h_bufs(c0, 4), [pg.b])
                        gs = f32r.get()
                        ACT(gs.t[:, :], pg.t[:, :], AF.Sigmoid, [pg.b], [gs.b])
                        pp = pf.get()
                        for k_ in range(4):
                            MM(pp.t[:, :], sl.t[:, 2048 + k_ * 256 + oo * 128:2048 + k_ * 256 + (oo + 1) * 128], ybT.t[:, k_, :], k_ == 0, k_ == 3,
                               [sl.b, ybT.b], [pp.b])
                        if first:
                            TT("dve", mergedT.t[:, o, :], gs.t[:, :], pp.t[:, :], ALU.mult, [gs.b, pp.b], [mergedT.b])
                        else:
                            TT("dve", gs.t[:, :], gs.t[:, :], pp.t[:, :], ALU.mult, [gs.b, pp.b], [gs.b])
                            TT("pool", mergedT.t[:, o, :], mergedT.t[:, o, :], gs.t[:, :], ALU.add, [gs.b, mergedT.b], [mergedT.b])
                add_stage(l, P_G0 + bi * 4 + q_, st_g)

        def add_final(l, tok0, c0, nch):
            def st_pre(_sl):
                for c_ in range(c0 + 5, min(nch, c0 + 9)):
                    ensure_h(l, tok0, c_)
                dbg("ybT", ybT.t[:, :, :], [ybT.b], BF16)
                dbg("mergedT", mergedT.t[:, :, :], [mergedT.b])
                CP("act", mergedTb.t[:, 0:4, :], mergedT.t[:, 0:4, :], [mergedT.b], [mergedTb.b])
                CP("dve", mergedTb.t[:, 4:8, :], mergedT.t[:, 4:8, :], [mergedT.b], [mergedTb.b])
            add_stage(l, None, st_pre)
            for half in range(2):
                def st_o(sl, half=half):
                    hs = slice(half * 512, (half + 1) * 512)
                    for j in range(4):
                        r0 = tok0 + (c0 + j) * CH
                        xt = xring2.get()
                        DMA(xt.t[:, :], src_of(l)[r0:r0 + CH, hs], src_buf(l, r0), [xt.b])
                        po = pf.get()
                        for k in range(8):
                            MM(po.t[:, :], mergedTb.t[:, k, j * CH:(j + 1) * CH], sl.t[:, k * 512:(k + 1) * 512], k == 0, k == 7,
                               [mergedTb.b, sl.b], [po.b])
                        TT("dve", xt.t[:, :], xt.t[:, :], po.t[:, :], ALU.add, [xt.b, po.b], [xt.b])
                        DMA(dst_of(l)[r0:r0 + CH, hs], xt.t[:, :], [xt.b], dst_buf2(l, r0, half))
                add_stage(l, P_WO0 + half, st_o)

        xring2 = Ring([sb("xo%d" % i, [128, 512]) for i in range(3)])
        hsum = sb("hsum", [128, 512])
        tsf_b = TSF.b
        acsf_b = ACSF.b

        def seq_layer(l, tok0, slen):
            nch = slen // CH
            nm = slen // TM
            en_att, en_ml, en_ssd = ("att" in enable), ("mlstm" in enable), ("ssd" in enable)
            en_rec = en_ml or en_ssd

            if en_rec:
                for m in reversed(range(nm)):
                    c0 = 4 * m

                    def st_h(_sl, c0=c0):
                        for c in range(max(0, c0 - 1), min(nch, c0 + 5)):
                            ensure_h(l, tok0, c)
                    add_stage(l, None, st_h)

                    def st_gates(sl, c0=c0, m=m):
                        if m == nm - 1:
                            S.op("dve", lambda e: e.memset(mC[1].t[:], 0.0), [], [mC[1].b])
                            S.op("dve", lambda e: e.memset(sS[1].t[:], 0.0), [], [sS[1].b])
                            S.op("dve", lambda e: e.memset(mvaug.t[:, :, :, 128:129], 1.0), [], [mvaug.b])
                        gate_rows(l, 1, sl, c0, m == nm - 1, TSBm.t[:, :, :], [TSBm.b], ACSBm.t[sD, :], [ACSBm.b])
                        DMA(TSD[m].rearrange("p (j c) -> p j c", j=4), TSBm.t[:, :, :], [TSBm.b], [tsd_buf[m]])
                        DMA(ACSD[m], ACSBm.t[sD, :], [ACSBm.b], [acsd_buf[m]])
                    add_stage(l, P_KV, st_gates)

                    if en_ml:
                        def st_mk(sl, c0=c0):
                            for j in range(4):
                                pk = pf.get()
                                proj_tok(sl, 0, 512, c0 + j, pk)
                                S.op("act", (lambda j, pk: lambda e: e.mul(out=mktok.t[:, j, :], in_=pk.t[:, :], mul=128 ** -0.5))(j, pk),
                                     [pk.b], [mktok.b])
                                DMA(KTOK[c0 + j], mktok.t[:, j, :], [mktok.b], [ktok_buf[c0 + j]])
                        add_stage(l, P_MK, st_mk)

                        def st_mv(sl, c0=c0, m=m):
                            for j in range(4):
                                mlstm_v_tok(sl, c0 + j, j)
                                DMA(VTOK[c0 + j].rearrange("p (h e) -> p h e", h=4), mvaug.t[:, j, :, 0:128], [mvaug.b], [vtok_buf[c0 + j]])
                            for j in reversed(range(4)):
                                c = c0 + j
                                CP("act", mCb[1].t[:, :], mC[1].t[:, :], [mC[1].b], [mCb[1].b])
                                DMA(BSTM[c], mCb[1].t[:, :], [mCb[1].b], [bstm_buf[c]])
                                mlstm_state_update(1, j, TSBm.t[:, j, :], [TSBm.b])
                        add_stage(l, P_MV, st_mv)

                    if en_ssd and "p2ssd" not in KSKIP:
                        def st_sx(sl, c0=c0):
                            ssd_conv(l, sl, None, c0, nch, [0, 1, 2, 3])
                        add_stage(l, P_SX, st_sx)

                        def st_sbc(sl, c0=c0, m=m):
                            ssd_conv(l, None, sl, c0, nch, [4])
                            for j in reversed(range(4)):
                                c = c0 + j
                                ssd_tokmajor(j)
                                DMA(XTOK[c], sxtok.t[:, j, :], [sxtok.b], [xtok_buf[c]])
                                CP("act", sSb[1].t[:, :], sS[1].t[:, :], [sS[1].b], [sSb[1].b])
                                DMA(BSTS[c], sSb[1].t[:, :], [sSb[1].b], [bsts_buf[c]])
                                ssd_state_update(1, j, TSBm.t[:, j, :], [TSBm.b])
                        add_stage(l, P_SBC, st_sbc)

                    def st_pref(_sl, c0=c0):
                        for c_ in range(max(0, c0 - 5), c0 - 1):
                            ensure_h(l, tok0, c_)
                    add_stage(l, None, st_pref)

            for m in range(nm):
                c0 = 4 * m
                lo = max(0, c0 - 1)
                hi = min(nch, c0 + 5)

                def st_h(_sl, c0=c0, lo=lo, hi=hi, m=m):
                    if l + 1 < depth:
                        for _ in range(casts_per_mt):
                            if pending_casts[l + 1]:
                                pending_casts[l + 1].pop(0)()
                    for c in range(lo, hi):
                        ensure_h(l, tok0, c)
                    DMA(rope_t.t[:], C["c_rope"][:, :, c0 * CH:c0 * CH + TM], [], [rope_t.b])
                    if m == 0:
                        S.op("dve", lambda e: e.memset(mC[0].t[:], 0.0), [], [mC[0].b])
                        S.op("dve", lambda e: e.memset(sS[0].t[:], 0.0), [], [sS[0].b])
                        S.op("dve", lambda e: e.memset(mCb[0].t[:], 0.0), [], [mCb[0].b])
                        S.op("dve", lambda e: e.memset(sSb[0].t[:], 0.0), [], [sSb[0].b])
                        S.op("dve", lambda e: e.memset(mvaug.t[:, :, :, 128:129], 1.0), [], [mvaug.b])
                        S.op("dve", lambda e: e.memset(vaug.t[:, :, :, 64:65], 1.0), [], [vaug.b])
                add_stage(l, None, st_h)

                def qk_norm_rope(pq, gain, ntok):
                    sq = b16r.get()
                    ACT(sq.t[:, 0:ntok], pq.t[:, 0:ntok], AF.Square, [pq.b], [sq.b])
                    pss = pf.get()
                    MM(pss.t[:, 0:ntok], blkb.t[:], sq.t[:, 0:ntok], True, True, [blkb.b, sq.b], [pss.b])
                    rs = f32r.get()
                    ACT(rs.t[:, 0:ntok], pss.t[:, 0:ntok], AF.Ln, [pss.b, epsc.b], [rs.b], bias=epsc.t[:, 0:1])
                    ACT(rs.t[:, 0:ntok], rs.t[:, 0:ntok], AF.Exp, [rs.b], [rs.b], scale=-0.5)
                    qn = f32r.get(); qnb = b16r.get()
                    STT("dve", qnb.t[:, 0:ntok], pq.t[:, 0:ntok], gain.t[:, 0:1], rs.t[:, 0:ntok], ALU.mult, ALU.mult,
                        [pq.b, gain.b, rs.b], [qnb.b])
                    pr = pf.get()
                    MM(pr.t[:, 0:ntok], rotb.t[:], qnb.t[:, 0:ntok], True, True, [rotb.b, qnb.b], [pr.b])
                    t1 = f32r.get()
                    return qn, pr, t1, qnb

                if en_att:
                    def st_aq(sl, c0=c0, qk_norm_rope=qk_norm_rope):
                        for i in range(4):
                            pq = pf.get()
                            proj_feat(sl, i * 128, 128, c0, 4, pq)
                            qn, pr, t1, qnb = qk_norm_rope(pq, gq, TM)
                            TT("dve", t1.t[:, :], pr.t[:, :], rope_t.t[:, 1, :], ALU.mult, [pr.b, rope_t.b], [t1.b])
                            TT("pool", qn.t[:, :], qnb.t[:, :], rope_t.t[:, 0, :], ALU.mult, [qnb.b, rope_t.b], [qn.b])
                            TT("pool", qT.t[:, i, :], qn.t[:, :], t1.t[:, :], ALU.add, [qn.b, t1.b], [qT.b])
                            if i == 0:
                                dbg("pq", pq.t[:, :], [pq.b])
                        dbg("qT", qT.t[:, :, :], [qT.b], BF16)
                    add_stage(l, P_AQ, st_aq)

                def st_kv(sl, c0=c0, lo=lo, hi=hi, m=m, qk_norm_rope=qk_norm_rope):
                    if en_att:
                        for (ca, n) in ((lo, c0 - lo), (c0, 4), (c0 + 4, hi - c0 - 4)):
                            if n <= 0:
                                continue
                            pk = pf.get()
                            proj_feat(sl, 0, 128, ca, n, pk)
                            ntk = n * CH
                            qn, pr, t1, qnb = qk_norm_rope(pk, gk, ntk)
                            if ca == c0:
                                cosap, sinap, rbufs = rope_t.t[:, 0, :], rope_t.t[:, 1, :], [rope_t.b]
                            else:
                                rt = f32r.get(); rt2 = f32r.get()
                                DMA(rt.t[:, 0:CH], C["c_rope"][:, 0, ca * CH:(ca + 1) * CH], [], [rt.b])
                                DMA(rt2.t[:, 0:CH], C["c_rope"][:, 1, ca * CH:(ca + 1) * CH], [], [rt2.b])
                                cosap, sinap, rbufs = rt.t[:, 0:CH], rt2.t[:, 0:CH], [rt.b, rt2.b]
                            TT("dve", t1.t[:, 0:ntk], pr.t[:, 0:ntk], sinap, ALU.mult, [pr.b] + rbufs, [t1.b])
                            TT("pool", qn.t[:, 0:ntk], qnb.t[:, 0:ntk], cosap, ALU.mult, [qnb.b] + rbufs, [qn.b])
                            o = (ca - lo) * CH
                            TT("pool", kT.t[:, o:o + ntk], qn.t[:, 0:ntk], t1.t[:, 0:ntk], ALU.add, [qn.b, t1.b], [kT.b])
                        for c in range(lo, hi):
                            pv = pf.get()
                            proj_tok(sl, 128, 128, c, pv)
                            CP("act", vaug.t[:, c - lo, :, 0:64], pv.t[:, 0:128].rearrange("p (g e) -> p g e", g=2), [pv.b], [vaug.b])
                        dbg("kT", kT.t[:, :], [kT.b], BF16)
                        dbg("vaug", vaug.t[:, :, :, :], [vaug.b], BF16)
                    if en_rec:
                        gate_rows(l, 0, sl, c0, m == 0, TSF.t[:, :, :], [tsf_b], ACSF.t[sD, :], [acsf_b])
                        DMA(TSBm.t[:, :, :], TSD[m].rearrange("p (j c) -> p j c", j=4), [tsd_buf[m]], [TSBm.b])
                        DMA(ACSBm.t[sD, :], ACSD[m], [acsd_buf[m]], [ACSBm.b])
                        mk_hilo(0, ACSF)
                        mk_hilo(1, ACSBm)
                add_stage(l, P_KV, st_kv)

                if en_att:
                    def st_az(sl, c0=c0, lo=lo, hi=hi):
                        for j in range(4):
                            pz = pf.get()
                            proj_tok(sl, 0, 512, c0 + j, pz)
                            ACT(gate_tok[0].t[:, j, :], pz.t[:, :], AF.Silu, [pz.b], [gate_tok[0].b])
                        for j in range(4):
                            c = c0 + j
                            po = pacc
                            kblocks = [cc for cc in (c - 1, c, c + 1) if 0 <= cc < nch]
                            for g in range(2):
                                pr_ = slice(g * 64, (g + 1) * 64)
                                for bi, cc in enumerate(kblocks):
                                    ps_ = pf.get()
                                    ko = (cc - lo) * CH
                                    for i in range(4):
                                        MM(ps_.t[:, i * 128:(i + 1) * 128], kT.t[pr_, ko:ko + CH], qT.t[pr_, i, j * CH:(j + 1) * CH],
                                           True, True, [kT.b, qT.b], [ps_.b])
                                    pt_ = b16r.get()
                                    ACT(pt_.t[:, :], ps_.t[:, :], AF.Exp, [ps_.b], [pt_.b])
                                    if cc != c:
                                        mi_ = 1 if cc < c else 0
                                        TT("pool", pt_.t[:, :], pt_.t[:, :], trib.t[:, mi_, :], ALU.mult, [pt_.b, trib.b], [pt_.b])
                                    for i in range(4):
                                        MM(po[g].t[:, i * 65:(i + 1) * 65], pt_.t[:, i * 128:(i + 1) * 128], vaug.t[:, cc - lo, g, :],
                                           bi == 0 and i == 0, bi == len(kblocks) - 1, [pt_.b, vaug.b], [po[g].b], skip=True)
                            ya = f32r.get(); den = st4.get()
                            for g in range(2):
                                pv3 = po[g].t[:, 0:260].rearrange("p (i e) -> p i e", i=4)
                                TT("dve", den.t[:, g * 4:(g + 1) * 4], pv3[:, :, 64], esink.t[:, g * 4:(g + 1) * 4], ALU.add,
                                   [po[g].b, esink.b], [den.b])
                            S.op("dve", (lambda den: lambda e: e.reciprocal(out=den.t[:, 0:8], in_=den.t[:, 0:8]))(den), [den.b], [den.b])
                            for g in range(2):
                                pv3 = po[g].t[:, 0:260].rearrange("p (i e) -> p i e", i=4)
                                TT("dve", ya.t[:, g * 256:(g + 1) * 256].rearrange("p (i e) -> p i e", i=4), pv3[:, :, 0:64],
                                   den.t[:, g * 4:(g + 1) * 4].unsqueeze(2).to_broadcast([128, 4, 64]), ALU.mult, [po[g].b, den.b], [ya.b])
                            dbg("ya", ya.t[:, :], [ya.b])
                            yg = b16r.get()
                            TT("pool", yg.t[:, :], ya.t[:, :], gate_tok[0].t[:, j, :], ALU.mult, [ya.b, gate_tok[0].b], [yg.b])
                            dbg("yg", yg.t[:, :], [yg.b], BF16)
                            p = pb.get()
                            for i in range(4):
                                TR(p.t[:, i * 128:(i + 1) * 128], yg.t[:, i * 128:(i + 1) * 128], identb.t[:], [yg.b, identb.b], [p.b])
                            CP("act", ybT.t[:, :, j * CH:(j + 1) * CH], p.t[:, 0:512].rearrange("p (i t) -> p i t", i=4), [p.b], [ybT.b])
                    add_stage(l, P_AZ, st_az)
                    add_branch_merge(l, 0, True, c0)

                if en_ml:
                    def st_mq(sl, c0=c0):
                        for h in range(4):
                            pq = pf.get()
                            proj_feat(sl, h * 128, 128, c0, 4, pq)
                            CP("act", mqT.t[:, h, :], pq.t[:, :], [pq.b], [mqT.b])
                    add_stage(l, P_MQ, st_mq)

                    def st_mk(_sl, c0=c0):
                        for j in range(4):
                            DMA(mktok.t[:, j, :], KTOK[c0 + j], [ktok_buf[c0 + j]], [mktok.b])
                            DMA(mvaug.t[:, j, :, 0:128], VTOK[c0 + j].rearrange("p (h e) -> p h e", h=4), [vtok_buf[c0 + j]], [mvaug.b])
                        for j in range(4):
                            p = pb.get()
                            for h in range(4):
                                TR(p.t[:, h * 128:(h + 1) * 128], mktok.t[:, j, h * 128:(h + 1) * 128], identb.t[:], [mktok.b, identb.b], [p.b])
                            CP("dve", mkT.t[:, :, j * CH:(j + 1) * CH], p.t[:, 0:512].rearrange("p (h t) -> p h t", h=4), [p.b], [mkT.b])
                    add_stage(l, None, st_mk)

                    def st_mo(sl, c0=c0):
                        for j in range(4):
                            pz = pf.get()
                            proj_tok(sl, 0, 512, c0 + j, pz)
                            ACT(gate_tok[0].t[:, j, :], pz.t[:, :], AF.Sigmoid, [pz.b], [gate_tok[0].b])
                    add_stage(l, P_MO, st_mo)

                    def st_mz(sl, c0=c0, m=m):
                        for j in range(4):
                            pz = pf.get()
                            proj_tok(sl, 0, 512, c0 + j, pz)
                            ACT(gate_tok[1].t[:, j, :], pz.t[:, :], AF.Silu, [pz.b], [gate_tok[1].b])
                        for j in range(4):
                            c = c0 + j
                            tok = slice(j * CH, (j + 1) * CH)
                            DMA(mCb[1].t[:, :], BSTM[c], [bstm_buf[c]], [mCb[1].b])
                            ps = pf.get()
                            for h in range(4):
                                MM(ps.t[:, h * 128:(h + 1) * 128], mkT.t[:, h, tok], mqT.t[:, h, tok], True, True, [mkT.b, mqT.b], [ps.b])
                            g01 = f32r.get()
                            TT("pool", g01.t[:, :], gate_tok[0].t[:, j, :], gate_tok[1].t[:, j, :], ALU.mult, [gate_tok[0].b], [g01.b])
                            TT("pool", g01.t[:, :], g01.t[:, :], mng_bc.t[:, :], ALU.mult, [g01.b, mng_bc.b], [g01.b])
                            fwd_vw = {}
                            for d in range(2):
                                ts_ap = TSF.t[:, j, :] if d == 0 else TSBm.t[:, j, :]
                                ts_rd = [tsf_b] if d == 0 else [TSBm.b]
                                scm = b16r.get()
                                TT("dve", scm.t[:, :], ps.t[:, :], trib.t[:, d, :], ALU.mult, [ps.b, trib.b], [scm.b])
                                vw = b16r.get(); wkb = b16r.get()
                                if d == 0:
                                    fwd_vw["vw"], fwd_vw["wkb"] = vw, wkb
                                TT("pool", vw.t[:, :].rearrange("p (h e) -> p h e", h=4), mvaug.t[:, j, :, 0:128],
                                   ts_ap[:, 0:4].unsqueeze(2).to_broadcast([128, 4, 128]), ALU.mult, [mvaug.b] + ts_rd, [vw.b])
                                CP("dve", wkb.t[:, 0:4], ts_ap[:, 0:4], ts_rd, [wkb.b])
                                pnum, pint = pacc[0], pacc[1]
                                pden = pf.get()
                                for h in range(4):
                                    hs = slice(h * 128, (h + 1) * 128)
                                    MM(pnum.t[:, hs], scm.t[:, hs], vw.t[:, hs], True, True, [scm.b, vw.b], [pnum.b])
                                    MM(pden.t[:, h:h + 1], scm.t[:, hs], wkb.t[:, h:h + 1], True, True, [scm.b, wkb.b], [pden.b])
                                    MM(pint.t[:, hs], mqT.t[:, h, tok], mCb[d].t[:, hs], True, True, [mqT.b, mCb[d].b], [pint.b])
                                    MM(pden.t[:, 4 + h:5 + h], mqT.t[:, h, tok], mCb[d].t[:, 512 + h:513 + h], True, True,
                                       [mqT.b, mCb[d].b], [pden.b])
                                n1 = f32r.get(); n2 = f32r.get(); dd = st4.get()
                                u_bc = ts_ap[:, 4:8].unsqueeze(2).to_broadcast([128, 4, 128])
                                wi_bc = ts_ap[:, 8:12].unsqueeze(2).to_broadcast([128, 4, 128])
                                TT("dve", n1.t[:, :].rearrange("p (h e) -> p h e", h=4), pnum.t[:, :].rearrange("p (h e) -> p h e", h=4),
                                   u_bc, ALU.mult, [pnum.b] + ts_rd, [n1.b])
                                TT("dve", n2.t[:, :].rearrange("p (h e) -> p h e", h=4), pint.t[:, :].rearrange("p (h e) -> p h e", h=4),
                                   wi_bc, ALU.mult, [pint.b] + ts_rd, [n2.b])
                                TT("pool", n1.t[:, :], n1.t[:, :], n2.t[:, :], ALU.add, [n1.b, n2.b], [n1.b])
                                TT("dve", dd.t[:, 0:4], pden.t[:, 0:4], ts_ap[:, 4:8], ALU.mult, [pden.b] + ts_rd, [dd.b])
                                TT("dve", dd.t[:, 4:8], pden.t[:, 4:8], ts_ap[:, 8:12], ALU.mult, [pden.b] + ts_rd, [dd.b])
                                TT("dve", dd.t[:, 0:4], dd.t[:, 0:4], dd.t[:, 4:8], ALU.add, [dd.b], [dd.b])
                                ACT(dd.t[:, 0:4], dd.t[:, 0:4], AF.Abs, [dd.b], [dd.b])
                                TT("dve", dd.t[:, 0:4], dd.t[:, 0:4], ts_ap[:, 12:16], ALU.max, [dd.b] + ts_rd, [dd.b])
                                S.op("dve", (lambda dd: lambda e: e.reciprocal(out=dd.t[:, 0:4], in_=dd.t[:, 0:4]))(dd), [dd.b], [dd.b])
                                r_bc = dd.t[:, 0:4].unsqueeze(2).to_broadcast([128, 4, 128])
                                if d == 0:
                                    TT("pool", hsum.t[:, :].rearrange("p (h e) -> p h e", h=4), n1.t[:, :].rearrange("p (h e) -> p h e", h=4),
                                       r_bc, ALU.mult, [n1.b, dd.b], [hsum.b])
                                else:
                                    TT("pool", n1.t[:, :].rearrange("p (h e) -> p h e", h=4), n1.t[:, :].rearrange("p (h e) -> p h e", h=4),
                                       r_bc, ALU.mult, [n1.b, dd.b], [n1.b])
                                    TT("pool", hsum.t[:, :], hsum.t[:, :], n1.t[:, :], ALU.add, [hsum.b, n1.b], [hsum.b])
                            sq = f32r.get(); ss = st4.get()
                            ACT(sq.t[:, :], hsum.t[:, :], AF.Square, [hsum.b], [sq.b])
                            S.op("dve", (lambda sq, ss: lambda e: e.reduce_sum(out=ss.t[:, 0:4], in_=sq.t[:, :].rearrange("p (h e) -> p h e", h=4),
                                                                         axis=AX.X))(sq, ss), [sq.b], [ss.b])
                            rstd_from_sumsq(ss.t[:, 4:8], ss.t[:, 0:4], 128, [ss.b], [ss.b])
                            hn = f32r.get()
                            TT("dve", hn.t[:, :].rearrange("p (h e) -> p h e", h=4), hsum.t[:, :].rearrange("p (h e) -> p h e", h=4),
                               ss.t[:, 4:8].unsqueeze(2).to_broadcast([128, 4, 128]), ALU.mult, [hsum.b, ss.b], [hn.b])
                            yg = b16r.get()
                            TT("pool", yg.t[:, :], hn.t[:, :], g01.t[:, :], ALU.mult, [hn.b, g01.b], [yg.b])
                            p = pb.get()
                            for i in range(4):
                                TR(p.t[:, i * 128:(i + 1) * 128], yg.t[:, i * 128:(i + 1) * 128], identb.t[:], [yg.b, identb.b], [p.b])
                            CP("act", ybT.t[:, :, tok], p.t[:, 0:512].rearrange("p (i t) -> p i t", i=4), [p.b], [ybT.b])
                            mlstm_state_update(0, j, TSF.t[:, j, :], [tsf_b], pre=fwd_vw)
                            CP("act", mCb[0].t[:, :], mC[0].t[:, :], [mC[0].b], [mCb[0].b])
                    add_stage(l, P_MZ, st_mz)
                    add_branch_merge(l, 1, not en_att, c0)

                if en_ssd:
                    def st_sbc(sl, c0=c0):
                        for j in range(4):
                            DMA(sxtok.t[:, j, :], XTOK[c0 + j], [xtok_buf[c0 + j]], [sxtok.b])
                        ssd_conv(l, None, sl, c0, nch, [4, 5])
                        for which in range(2):
                            for g in range(2):
                                gs_ = slice(g * 64, (g + 1) * 64)
                                CP("pool", szero.t[gs_, which * 2 + g, :], sconv.t[gs_, 4 + which, :], [sconv.b], [szero.b])
                        for j in range(4):
                            ssd_tokmajor(j, do_x=False)
                    add_stage(l, P_SBC, st_sbc)

                    def st_sz(sl, c0=c0, m=m):
                        for j in range(4):
                            pz = pf.get()
                            proj_tok(sl, 0, 512, c0 + j, pz)
                            ACT(gate_tok[0].t[:, j, :], pz.t[:, :], AF.Silu, [pz.b], [gate_tok[0].b])
                        for j in range(4):
                            if "sz" in KSKIP:
                                break
                            c = c0 + j
                            tok = slice(j * CH, (j + 1) * CH)
                            if "cDma" not in KSKIP:
                                DMA(sSb[1].t[:, :], BSTS[c], [bsts_buf[c]], [sSb[1].b])
                            pcb = pf.get()
                            for g in range(2):
                                if "cPcb" in KSKIP:
                                    break
                                gs_ = slice(g * 64, (g + 1) * 64)
                                MM(pcb.t[:, g * 128:(g + 1) * 128], szero.t[:, g, tok], sconv.t[:, 5, tok], True, True, [sconv.b, szero.b], [pcb.b])
                            py = pf.get()
                            toff = []
                            for d in range(2):
                                ts_ap = TSF.t[:, j, :] if d == 0 else TSBm.t[:, j, :]
                                ts_rd = [tsf_b] if d == 0 else [TSBm.b]
                                acs_hi, acs_lo = acs_hl[d]
                                if d == 0:
                                    cbm = f32r.get()
                                    CP("act", cbm.t[:, 0:256], pcb.t[:, 0:256], [pcb.b], [cbm.b])
                                for h in range(8):
                                    if "cSel" in KSKIP:
                                        break
                                    pe_ = pacc[h // 4]
                                    hb_ = slice((h % 4) * 128, (h % 4 + 1) * 128)
                                    MM(pe_.t[:, hb_], selb.t[sD, h, :], acs_hi.t[sD, tok], h % 4 == 0, False, [selb.b, acs_hi.b], [pe_.b], skip=True)
                                    MM(pe_.t[:, hb_], selb.t[sD, h, :], acs_lo.t[sD, tok], False, False, [selb.b, acs_lo.b], [pe_.b], skip=True)
                                    MM(pe_.t[:, hb_], identb.t[:], negm.t[:, d, :], False, True, [identb.b, negm.b], [pe_.b], skip=True)
                                mts = []
                                if "cA" in KSKIP:
                                    continue
                                for g in range(2):
                                    lt = f32r.get()
                                    for e_ in range(4):
                                        h = g * 4 + e_
                                        ACT(lt.t[:, e_ * 128:(e_ + 1) * 128], pacc[g].t[:, e_ * 128:(e_ + 1) * 128], AF.Exp, [pacc[g].b] + ts_rd, [lt.b],
                                            bias=ts_ap[:, 20 + h:21 + h])
                                    mt = b16r.get()
                                    TT("dve" if g == 0 else "pool", mt.t[:, :].rearrange("p (h e) -> p h e", h=4), lt.t[:, :].rearrange("p (h e) -> p h e", h=4),
                                       cbm.t[:, g * 128:(g + 1) * 128].unsqueeze(1).to_broadcast([128, 4, 128]), ALU.mult, [lt.b, cbm.b], [mt.b])
                                    mts.append(mt)
                                if "cC" in KSKIP:
                                    continue
                                poff = pf.get()
                                for g in range(2):
                                    gs_ = slice(g * 64, (g + 1) * 64)
                                    MM(poff.t[:, g * 256:(g + 1) * 256], szero.t[:, 2 + g, tok], sSb[d].t[:, :], True, True, [szero.b, sSb[d].b], [poff.b])
                                for h in range(8):
                                    MM(py.t[:, h * 64:(h + 1) * 64], mts[h // 4].t[:, (h % 4) * 128:(h % 4 + 1) * 128], sxtok.t[:, j, h * 64:(h + 1) * 64],
                                       d == 0 and h == 0, d == 1, [mts[h // 4].b, sxtok.b], [py.b], skip=True)
                                to = f32r.get()
                                TT("dve", to.t[:, :].rearrange("p (h e) -> p h e", h=8), poff.t[:, :].rearrange("p (h e) -> p h e", h=8),
                                   ts_ap[:, 36:44].unsqueeze(2).to_broadcast([128, 8, 64]), ALU.mult, [poff.b] + ts_rd, [to.b])
                                toff.append(to)
                            if "cA" in KSKIP or "cC" in KSKIP or "cD" in KSKIP:
                                continue
                            y = f32r.get()
                            TT("dve", y.t[:, :], py.t[:, :], toff[0].t[:, :], ALU.add, [py.b, toff[0].b], [y.b])
                            TT("pool", y.t[:, :], y.t[:, :], toff[1].t[:, :], ALU.add, [y.b, toff[1].b], [y.b])
                            xs = toff[0]
                            TT("pool", xs.t[:, :].rearrange("p (h e) -> p h e", h=8), sxtok.t[:, j, :].rearrange("p (h e) -> p h e", h=8),
                               dsk_bc.t[:, 0:8].unsqueeze(2).to_broadcast([128, 8, 64]), ALU.mult, [sxtok.b, dsk_bc.b], [xs.b])
                            TT("pool", y.t[:, :], y.t[:, :], xs.t[:, :], ALU.add, [y.b, xs.b], [y.b])
                            TT("pool", y.t[:, :], y.t[:, :], gate_tok[0].t[:, j, :], ALU.mult, [y.b, gate_tok[0].b], [y.b])
                            sq = toff[1]; ss = st4.get()
                            ACT(sq.t[:, :], y.t[:, :], AF.Square, [y.b], [sq.b, ss.b], accum=ss.t[:, 0:1])
                            rstd_from_sumsq(ss.t[:, 1:2], ss.t[:, 0:1], 512, [ss.b], [ss.b])
                            yg = b16r.get()
                            STT("dve", yg.t[:, :], y.t[:, :], ss.t[:, 1:2], sng_bc.t[:, :], ALU.mult, ALU.mult, [y.b, ss.b, sng_bc.b], [yg.b])
                            p = pb.get()
                            for i in range(4):
                                TR(p.t[:, i * 128:(i + 1) * 128], yg.t[:, i * 128:(i + 1) * 128], identb.t[:], [yg.b, identb.b], [p.b])
                            CP("act", ybT.t[:, :, tok], p.t[:, 0:512].rearrange("p (i t) -> p i t", i=4), [p.b], [ybT.b])
                            if "cE" in KSKIP:
                                continue
                            ssd_state_update(0, j, TSF.t[:, j, :], [tsf_b])
                            CP("act", sSb[0].t[:, :], sS[0].t[:, :], [sS[0].b], [sSb[0].b])
                    add_stage(l, P_SZ, st_sz)
                    add_branch_merge(l, 2, not (en_att or en_ml), c0)

                if not (en_att or en_ml or en_ssd):
                    def st_zero(_sl):
                        S.op("dve", lambda e: e.memset(mergedT.t[:], 0.0), [], [mergedT.b])
                    add_stage(l, None, st_zero)
                add_final(l, tok0, c0, nch)

        n_mt_layer = sum(sl_ // TM for sl_ in seq_lens)
        casts_per_mt = 0
        if depth > 1:
            casts_per_mt = max(len(v_) for v_ in pending_casts.values()) // max(1, n_mt_layer - 1) + 1

        def flush_casts(l):
            def f(_sl):
                while pending_casts[l]:
                    pending_casts[l].pop(0)()
            return f
        for l in range(depth):
            if l >= 1:
                add_stage(l, None, flush_casts(l))
            add_stage(l, None, (lambda l: lambda _sl: load_layer_params(l))(l))
            pos = 0
            for slen in seq_lens:
                seq_layer(l, pos, slen)
                pos += slen
        run_stages()
        import os as _os
        S.schedule(window=int(_os.environ.get('KWIN', '24')), enable=_os.environ.get('KSCHED', '1') == '1')
        S._plan()
        print('sched ok:', S.check(), {e: len(S.order[e]) for e in S.ENGS}, 'est_us', getattr(S, 'est_time', 0) / 1e3, 'busy_us', {e: round(v / 1e3) for e, v in getattr(S, 'busy', {}).items()})
        if S.tagging:
            for kk_, v_ in sorted(S.gaps.items(), key=lambda x: -x[1])[:40]:
                print('GAP', kk_, round(v_ / 1e3, 1))
        S.emit()
    return nc


_CACHE = {}


def _run(seq_lens, depth, per_core_x, weights, enable=("att", "mlstm", "ssd"), debug=(), full=False):
    key = (tuple(seq_lens), depth, tuple(enable), tuple(debug))
    if key not in _CACHE:
        _CACHE[key] = build(list(seq_lens), depth, enable, debug)
    nc = _CACHE[key]
    consts = make_consts(max(seq_lens))
    in_maps = []
    for xc in per_core_x:
        m = {"x": np.ascontiguousarray(xc, dtype=np.float32)}
        for k, v in weights.items():
            m[k] = np.ascontiguousarray(v, dtype=np.float32)
        m.update(consts)
        in_maps.append(m)
    res = run_bass_kernel_spmd(nc, in_maps, core_ids=list(range(len(per_core_x))))
    if full:
        return res.results
    return [r["y"] for r in res.results]


def kernel(x_prompt, x_sample, norm_g, w_in, q_norm_g, k_norm_g, attn_sink, w_att_out,
           mlstm_i_b, mlstm_f_b, mlstm_norm_g, w_mlstm_out, conv_w, conv_b, a_log,
           dt_bias, d_skip, ssm_norm_g, w_ssm_out, w_out):
    x_prompt = np.asarray(x_prompt, dtype=np.float32)
    x_sample = np.asarray(x_sample, dtype=np.float32)
    weights = dict(norm_g=norm_g, w_in=w_in, q_norm_g=q_norm_g, k_norm_g=k_norm_g, attn_sink=attn_sink,
                   w_att_out=w_att_out, mlstm_i_b=mlstm_i_b, mlstm_f_b=mlstm_f_b, mlstm_norm_g=mlstm_norm_g,
                   w_mlstm_out=w_mlstm_out, conv_w=conv_w, conv_b=conv_b, a_log=a_log, dt_bias=dt_bias,
                   d_skip=d_skip, ssm_norm_g=ssm_norm_g, w_ssm_out=w_ssm_out, w_out=w_out)
    weights = {k: np.asarray(v, dtype=np.float32) for k, v in weights.items()}
    depth = weights["w_in"].shape[0]
    nb, sp = x_prompt.shape[0], x_prompt.shape[1]
    ns, ss = x_sample.shape[0], x_sample.shape[1]
    ppc, spc = nb // NCORES, ns // NCORES
    seq_lens = [sp] * ppc + [ss] * spc
    per_core = []
    for c in range(NCORES):
        parts = [x_prompt[c * ppc + i] for i in range(ppc)] + [x_sample[c * spc + i] for i in range(spc)]
        per_core.append(np.concatenate(parts, axis=0))
    outs = _run(seq_lens, depth, per_core, weights)
    y_prompt = np.empty_like(x_prompt)
    y_sample = np.empty_like(x_sample)
    for c in range(NCORES):
        o = outs[c]
        pos = 0
        for i in range(ppc):
            y_prompt[c * ppc + i] = o[pos:pos + sp]; pos += sp
        for i in range(spc):
            y_sample[c * spc + i] = o[pos:pos + ss]; pos += ss
    return (y_prompt, y_sample)
```

```python
import contextlib
import math
import numpy as np
import concourse.bass as bass
import concourse.mybir as mybir
from concourse.bass_utils import run_bass_kernel_spmd

F32 = mybir.dt.float32
BF16 = mybir.dt.bfloat16
AF = mybir.ActivationFunctionType
ALU = mybir.AluOpType
AX = mybir.AxisListType

D = 1024
NCORES = 8
TM = 512
CH = 128
NPIECE = 25
ROPE_THETA = 500000.0
EPS = 1e-6


class Buf:
    __slots__ = ("name", "last_w", "readers")

    def __init__(self, name=""):
        self.name = name
        self.last_w = None
        self.readers = []


class Sched:
    ENGS = ("pe", "act", "dve", "pool", "sp")
    NPOOL = 4

    def __init__(self, nc, n_dma_sems=24):
        self.nc = nc
        self.nodes = []
        self.n_dma_sems = n_dma_sems
        self.order = None
        import os as _os2
        self.tagging = bool(_os2.environ.get('KTAG'))
        self.gaps = {}

    def _add(self, eng, fn, reads, writes, dma, cost):
        deps = set()
        for b in reads:
            if b.last_w is not None:
                deps.add(b.last_w)
        for b in writes:
            if b.last_w is not None:
                deps.add(b.last_w)
            deps.update(b.readers)
        gid = len(self.nodes)
        tag = ""
        if self.tagging:
            import sys as _sys
            f = _sys._getframe(2)
            names = []
            while f is not None and len(names) < 3:
                nm = f.f_code.co_name
                if nm not in ("op", "dma", "ACT", "TT", "TSC", "STT", "CP", "MM", "TR", "DMA", "SCAN", "RECIP", "<lambda>", "proj_feat", "proj_tok"):
                    names.append(nm)
                f = f.f_back
            tag = "/".join(names[:2])
        self.nodes.append(dict(eng=eng, fn=fn, dma=dma, deps=deps, cost=cost, tag=tag))
        for b in reads:
            b.readers.append(gid)
        for b in writes:
            b.last_w = gid
            b.readers = []
        return gid

    def op(self, eng, fn, reads=(), writes=(), cost=300.0):
        return self._add(eng, fn, reads, writes, False, cost)

    def dma(self, eng, fn, reads=(), writes=(), cost=3000.0):
        return self._add(eng, fn, reads, writes, True, cost)

    def schedule(self, window=24, enable=True):
        nodes = self.nodes
        per = {e: [] for e in self.ENGS}
        for g, n in enumerate(nodes):
            per[n["eng"]].append(g)
        if not enable:
            self.order = per
            return
        ptr = {e: 0 for e in self.ENGS}
        done = {e: [False] * len(per[e]) for e in self.ENGS}
        finish = [None] * len(nodes)
        t_eng = {e: 0.0 for e in self.ENGS}
        new = {e: [] for e in self.ENGS}
        remaining = len(nodes)
        LAT = 250.0
        W = {e: window for e in self.ENGS}
        W["sp"] = 6
        while remaining:
            best = None
            for e in self.ENGS:
                lst = per[e]
                p = ptr[e]
                while p < len(lst) and done[e][p]:
                    p += 1
                ptr[e] = p
                if p >= len(lst):
                    continue
                cnt = 0
                q = p
                cand = None
                while q < len(lst) and cnt < W[e]:
                    if not done[e][q]:
                        cnt += 1
                        g = lst[q]
                        nd = nodes[g]
                        ready = t_eng[e]
                        ok = True
                        for d in nd["deps"]:
                            f = finish[d]
                            if f is None:
                                ok = False
                                break
                            if nodes[d]["eng"] != e or nodes[d]["dma"]:
                                f += LAT
                            if f > ready:
                                ready = f
                        if ok:
                            key = (ready, q)
                            if cand is None or key < cand[0]:
                                cand = (key, q, g, ready)
                            if ready <= t_eng[e]:
                                break
                    q += 1
                if cand is not None:
                    if best is None or (cand[3], cand[2]) < (best[3], best[2]):
                        best = (e, cand[1], cand[2], cand[3])
            assert best is not None, "scheduler stuck"
            e, q, g, start = best
            nd = nodes[g]
            if self.tagging and e == "pe" and start > t_eng[e] + 500.0:
                dmax = max(nd["deps"], key=lambda d: finish[d])
                key = (nd["tag"], nodes[dmax]["eng"], nodes[dmax]["tag"])
                self.gaps[key] = self.gaps.get(key, 0.0) + (start - t_eng[e])
            if nd["dma"]:
                finish[g] = start + nd["cost"]
                t_eng[e] = start + 60.0
            else:
                finish[g] = start + nd["cost"]
                t_eng[e] = finish[g]
            done[e][q] = True
            new[e].append(g)
            remaining -= 1
        self.order = new
        self.est_time = max(f for f in finish if f is not None)
        self.busy = {e: sum(nodes[g]['cost'] for g in new[e] if not nodes[g]['dma']) for e in self.ENGS}

    def _plan(self):
        nodes = self.nodes
        pos = {}
        for e in self.ENGS:
            for i, g in enumerate(self.order[e]):
                pos[g] = i
        npool = self.NPOOL
        nsp = self.n_dma_sems - npool
        rr = {"sp": 0, "pool": 0}
        dma_val = [0] * self.n_dma_sems
        sem_of = {}
        for e in self.ENGS:
            for g in self.order[e]:
                if nodes[g]["dma"]:
                    if e == "pool":
                        s = nsp + rr["pool"]; rr["pool"] = (rr["pool"] + 1) % npool
                    else:
                        s = rr["sp"]; rr["sp"] = (rr["sp"] + 1) % nsp
                    prev = dma_val[s]
                    dma_val[s] += 16
                    sem_of[g] = (s, dma_val[s], prev)
        self.dma_val = dma_val
        flag = [False] * len(nodes)
        plan = {e: [] for e in self.ENGS}
        for e in self.ENGS:
            waited_c = {}
            waited_d = {}
            for g in self.order[e]:
                nd = nodes[g]
                waits = []
                deps = list(nd["deps"])
                for d in deps:
                    dn = nodes[d]
                    if dn["dma"]:
                        s, v, _ = sem_of[d]
                        if waited_d.get(s, 0) < v:
                            waited_d[s] = v
                            waits.append(("d", s, v))
                    else:
                        pe_ = dn["eng"]
                        if pe_ == e and e in ("pe", "sp"):
                            continue
                        if waited_c.get(pe_, -1) < pos[d]:
                            waited_c[pe_] = pos[d]
                            flag[d] = True
                            waits.append(("c", pe_, d))
                if nd["dma"]:
                    s, v, prev = sem_of[g]
                    if prev > 0 and waited_d.get(s, 0) < prev:
                        waited_d[s] = prev
                        waits.append(("d", s, prev))
                plan[e].append((g, waits))
        counts = {}
        for e in self.ENGS:
            c = 0
            for g in self.order[e]:
                if (not nodes[g]["dma"]) and flag[g]:
                    c += 1
                counts[g] = c
        self.plan, self.flag, self.counts, self.sem_of = plan, flag, counts, sem_of

    def check(self):
        nodes = self.nodes
        ptr = {e: 0 for e in self.ENGS}
        csem = {e: 0 for e in self.ENGS}
        dsem = [0] * self.n_dma_sems
        while True:
            prog = False
            for e in self.ENGS:
                pl = self.plan[e]
                while ptr[e] < len(pl):
                    g, waits = pl[ptr[e]]
                    ok = True
                    for w in waits:
                        if w[0] == "c":
                            if csem[w[1]] < self.counts[w[2]]:
                                ok = False
                        elif dsem[w[1]] < w[2]:
                            ok = False
                    if not ok:
                        break
                    if nodes[g]["dma"]:
                        dsem[self.sem_of[g][0]] += 16
                    elif self.flag[g]:
                        csem[e] += 1
                    ptr[e] += 1
                    prog = True
            if all(ptr[e] == len(self.plan[e]) for e in self.ENGS):
                return True
            if not prog:
                for e in self.ENGS:
                    if ptr[e] < len(self.plan[e]):
                        print("STUCK", e, ptr[e], "/", len(self.plan[e]), self.plan[e][ptr[e]][1])
                return False

    def emit(self, final_wait_eng="sp"):
        nc = self.nc
        nodes = self.nodes
        with contextlib.ExitStack() as st:
            csem = {e: st.enter_context(nc.semaphore("cs_" + e)) for e in self.ENGS}
            dsem = [st.enter_context(nc.semaphore("ds_%d" % i)) for i in range(self.n_dma_sems)]
            final = [(i, v) for i, v in enumerate(self.dma_val) if v > 0]
            block = st.enter_context(nc.Block())
            engobj = {"pe": "tensor", "act": "scalar", "dve": "vector", "pool": "gpsimd", "sp": "sync"}

            def make(e):
                def body(eng):
                    for g, waits in self.plan[e]:
                        for w in waits:
                            if w[0] == "c":
                                eng.wait_ge(csem[w[1]], self.counts[w[2]])
                            else:
                                eng.wait_ge(dsem[w[1]], w[2])
                        ins = nodes[g]["fn"](eng)
                        if nodes[g]["dma"]:
                            ins.then_inc(dsem[self.sem_of[g][0]], 16)
                        elif self.flag[g]:
                            ins.then_inc(csem[e], 1)
                    if e == final_wait_eng:
                        for (i, v) in final:
                            eng.wait_ge(dsem[i], v)
                return body

            for e in self.ENGS:
                if self.plan[e] or e == final_wait_eng:
                    getattr(block, engobj[e])(make(e))


class Tl:
    __slots__ = ("t", "b")

    def __init__(self, t, name=""):
        self.t = t
        self.b = Buf(name)


def make_consts(smax):
    c = {}
    c["c_ident"] = np.eye(128, dtype=np.float32)
    s = np.arange(128)[:, None]
    t = np.arange(128)[None, :]
    tri = np.zeros((128, 2, 512), np.float32)
    tri[:, 0, :] = np.tile((s <= t).astype(np.float32), (1, 4))
    tri[:, 1, :] = np.tile((s >= t).astype(np.float32), (1, 4))
    c["c_tri"] = tri
    f = np.arange(128)
    d = f % 64
    pos = np.arange(smax, dtype=np.float32)
    rope = np.zeros((128, 2, smax), np.float32)
    rope[:, 0, :] = 1.0
    inv_freq = (ROPE_THETA ** (-np.arange(8, dtype=np.float32) * 2.0 / 16.0)).astype(np.float32)
    for ff in range(128):
        if d[ff] < 16:
            ang = pos * inv_freq[d[ff] % 8]
            rope[ff, 0, :] = np.cos(ang)
            rope[ff, 1, :] = np.sin(ang)
    c["c_rope"] = rope
    rot = np.zeros((128, 128), np.float32)
    for ff in range(128):
        if d[ff] < 8:
            rot[ff + 8, ff] = -1.0
        elif d[ff] < 16:
            rot[ff - 8, ff] = 1.0
    c["c_rot"] = rot
    blk = np.zeros((128, 128), np.float32)
    blk[:64, :64] = 1.0 / 64
    blk[64:, 64:] = 1.0 / 64
    c["c_blk"] = blk
    sel = np.zeros((8, 8, 128), np.float32)
    for h in range(8):
        sel[h, h, :] = 1.0
    c["c_sel"] = sel
    rst = np.ones((8, 512), np.float32)
    rst[:, ::128] = 0.0
    c["c_reset"] = rst
    return c


O_AQ, O_AK, O_AV, O_AZ = 0, 512, 640, 768
O_MQ, O_MK, O_MV, O_MO = 1280, 1792, 2304, 2816
O_MI, O_MF, O_MZ = 3328, 3336, 3344
O_SX, O_SB, O_SC, O_SDT, O_SZ = 3856, 4368, 4496, 4624, 4640
O_G = 5152
P_AQ, P_KV, P_AZ, P_MQ, P_MK, P_MV, P_MO, P_MZ, P_SX, P_SBC, P_SZ = range(11)
P_G0 = 11
P_WO0, P_WO1 = 23, 24


def build(seq_lens, depth, enable=("att", "mlstm", "ssd"), debug=()):
    ntok = sum(seq_lens)
    smax = max(seq_lens)
    nc = bass.Bass("TRN2", target_bir_lowering=False)
    S = Sched(nc)
    es = contextlib.ExitStack()

    def din(name, shape, dt=F32):
        return nc.dram_tensor(name, list(shape), dt, kind="ExternalInput").ap()

    x_in = din("x", [ntok, D])
    y_out = nc.dram_tensor("y", [ntok, D], F32, kind="ExternalOutput").ap()
    W = dict(
        norm_g=din("norm_g", [depth, D]), w_in=din("w_in", [depth, D, 8224]),
        q_norm_g=din("q_norm_g", [depth, 64]), k_norm_g=din("k_norm_g", [depth, 64]),
        attn_sink=din("attn_sink", [depth, 8]), w_att_out=din("w_att_out", [depth, 512, D]),
        mlstm_i_b=din("mlstm_i_b", [depth, 2, 4]), mlstm_f_b=din("mlstm_f_b", [depth, 2, 4]),
        mlstm_norm_g=din("mlstm_norm_g", [depth, 512]), w_mlstm_out=din("w_mlstm_out", [depth, 512, D]),
        conv_w=din("conv_w", [depth, 5, 768]), conv_b=din("conv_b", [depth, 768]),
        a_log=din("a_log", [depth, 2, 8]), dt_bias=din("dt_bias", [depth, 2, 8]),
        d_skip=din("d_skip", [depth, 8]), ssm_norm_g=din("ssm_norm_g", [depth, 512]),
        w_ssm_out=din("w_ssm_out", [depth, 512, D]), w_out=din("w_out", [depth, D, D]),
    )
    C = dict(c_ident=din("c_ident", [128, 128]), c_tri=din("c_tri", [128, 2, 512]),
             c_rope=din("c_rope", [128, 2, smax]), c_rot=din("c_rot", [128, 128]),
             c_blk=din("c_blk", [128, 128]), c_sel=din("c_sel", [8, 8, 128]),
             c_reset=din("c_reset", [8, 512]))
    WS = nc.dram_tensor("ws_bf16", [depth, NPIECE, 128, 4096], BF16, kind="Internal").ap()
    ws_buf = [[[] for p in range(NPIECE)] for l in range(depth)]
    nscr = max(1, min(2, depth - 1))
    YS = [nc.dram_tensor("yscr%d" % i, [ntok, D], F32, kind="Internal").ap() for i in range(nscr)]
    ys_buf = [[[Buf("yscr"), Buf("yscr")] for c in range(ntok // CH)] for i in range(nscr)]
    nchmax = smax // CH
    BSTM = nc.dram_tensor("bst_m", [nchmax, 128, 516], BF16, kind="Internal").ap()
    BSTS = nc.dram_tensor("bst_s", [nchmax, 128, 256], BF16, kind="Internal").ap()
    KTOK = nc.dram_tensor("ktok", [nchmax, 128, 512], BF16, kind="Internal").ap()
    VTOK = nc.dram_tensor("vtok", [nchmax, 128, 512], BF16, kind="Internal").ap()
    XTOK = nc.dram_tensor("xtok", [nchmax, 128, 512], BF16, kind="Internal").ap()
    ktok_buf = [Buf("ktok%d" % i) for i in range(nchmax)]
    vtok_buf = [Buf("vtok%d" % i) for i in range(nchmax)]
    xtok_buf = [Buf("xtok%d" % i) for i in range(nchmax)]
    bstm_buf = [Buf("bstm%d" % i) for i in range(nchmax)]
    bsts_buf = [Buf("bsts%d" % i) for i in range(nchmax)]

    dbg_done = {}
    import os
    KSKIP = set(os.environ.get("KSKIP", "").split(","))

    def dbg(name, ap, bufs, dt=F32):
        if name not in debug or name in dbg_done:
            return
        dbg_done[name] = True
        shp = list(ap.shape)
        o = nc.dram_tensor("dbg_" + name, shp, dt, kind="ExternalOutput").ap()
        S.dma("sp", lambda e: e.dma_start(out=o, in_=ap), bufs, [])

    def sb(name, shape, dt=F32):
        return Tl(es.enter_context(nc.sbuf_tensor(name, list(shape), dt)), name)

    def psum(name, shape, dt=F32):
        return Tl(es.enter_context(nc.psum_tensor(name, list(shape), dt)), name)

    def fsz(ap):
        n = 1
        for s_ in ap.shape[1:]:
            n *= s_
        return n

    def ecost(eng, ap, mult=1.0):
        n = fsz(ap)
        if eng == "act":
            return 220.0 + 0.85 * n
        if eng == "dve":
            return 60.0 + 1.3 * n * mult
        return 100.0 + 2.6 * n * mult

    def ACT(out, in_, func, rd, wr, bias=None, scale=None, accum=None):
        kw = {}
        if bias is not None:
            kw["bias"] = bias
        if scale is not None:
            kw["scale"] = scale
        if accum is not None:
            kw["accum_out"] = accum
        S.op("act", lambda e: e.activation(out=out, in_=in_, func=func, **kw), rd, wr, cost=ecost("act", out))

    def TT(eng, out, in0, in1, op, rd, wr):
        S.op(eng, lambda e: e.tensor_tensor(out=out, in0=in0, in1=in1, op=op), rd, wr, cost=ecost(eng, out))

    def TSC(eng, out, in0, s1, op0, rd, wr, s2=None, op1=None):
        if op1 is None:
            S.op(eng, lambda e: e.tensor_scalar(out=out, in0=in0, scalar1=s1, scalar2=None, op0=op0), rd, wr, cost=ecost(eng, out))
        else:
            S.op(eng, lambda e: e.tensor_scalar(out=out, in0=in0, scalar1=s1, scalar2=s2, op0=op0, op1=op1), rd, wr, cost=ecost(eng, out))

    def STT(eng, out, in0, scalar, in1, op0, op1, rd, wr):
        S.op(eng, lambda e: e.scalar_tensor_tensor(out=out, in0=in0, scalar=scalar, in1=in1, op0=op0, op1=op1), rd, wr,
             cost=ecost(eng, out))

    def CP(eng, out, in_, rd, wr):
        if eng == "act":
            S.op("act", lambda e: e.copy(out=out, in_=in_), rd, wr, cost=ecost("act", out))
        else:
            S.op(eng, lambda e: e.tensor_copy(out=out, in_=in_), rd, wr, cost=ecost(eng, out, 1.4 if eng == "pool" else 1.0))

    def RECIP(out, in_, rd, wr):
        S.op("dve", lambda e: e.reciprocal(out=out, in_=in_), rd, wr, cost=100.0 + 6.6 * fsz(out))

    def MM(out, lhsT, rhs, start, stop, rd, wr, skip=False):
        n = max(fsz(out), 32)
        passes = 4 if lhsT.dtype == F32 else 1
        c = 25.0 + 0.5 * n * passes
        if skip:
            S.op("pe", lambda e: e.matmul(out, lhsT, rhs, start=start, stop=stop, skip_group_check=True), rd, wr, cost=c)
        else:
            S.op("pe", lambda e: e.matmul(out, lhsT, rhs, start=start, stop=stop), rd, wr, cost=c)

    def TR(out, in_, ident, rd, wr):
        S.op("pe", lambda e: e.transpose(out, in_, ident), rd, wr, cost=90.0)

    def DMA(out, in_, rd, wr, eng="sp", slow=False):
        nbytes = out.shape[0] * fsz(out) * (2 if out.dtype == BF16 else 4)
        c = 2500.0 + nbytes / (40.0 if eng == "pool" else 120.0)
        if slow:
            S.dma(eng, lambda e: e.dma_start(out=out, in_=in_, allow_slow_non_contiguous=True), rd, wr, cost=c)
        else:
            S.dma(eng, lambda e: e.dma_start(out=out, in_=in_), rd, wr, cost=c)

    def SCAN(out, d0, d1, init, op0, op1, rd, wr):
        S.op("dve", lambda e: e.tensor_tensor_scan(out=out, data0=d0, data1=d1, initial=init, op0=op0, op1=op1), rd, wr,
             cost=100.0 + 2.0 * fsz(out))

    class Ring:
        def __init__(self, tiles):
            self.tiles = tiles
            self.i = 0

        def get(self):
            t = self.tiles[self.i]
            self.i = (self.i + 1) % len(self.tiles)
            return t

    with es:
        f32r = Ring([sb("f32r%d" % i, [128, 512]) for i in range(7)])
        b16r = Ring([sb("b16r%d" % i, [128, 512], BF16) for i in range(8)])
        identf = sb("identf", [128, 128])
        identb = sb("identb", [128, 128], BF16)
        trib = sb("trib", [128, 2, 512], BF16)
        rotb = sb("rotb", [128, 128], BF16)
        blkb = sb("blkb", [128, 128], BF16)
        self_ = sb("sel", [40, 8, 128])
        resetm = sb("resetm", [40, 512])
        zrow = sb("zrow", [8, 1])
        DMA(identf.t[:], C["c_ident"][:, :], [], [identf.b])
        CP("dve", identb.t[:], identf.t[:], [identf.b], [identb.b])
        for half in range(2):
            stg = f32r.get()
            DMA(stg.t[:, :], C["c_tri"][:, half, :], [], [stg.b])
            CP("dve", trib.t[:, half, :], stg.t[:, :], [stg.b], [trib.b])
        stg = f32r.get()
        DMA(stg.t[:, 0:128], C["c_rot"][:, :], [], [stg.b])
        CP("dve", rotb.t[:], stg.t[:, 0:128], [stg.b], [rotb.b])
        stg = f32r.get()
        DMA(stg.t[:, 0:128], C["c_blk"][:, :], [], [stg.b])
        CP("dve", blkb.t[:], stg.t[:, 0:128], [stg.b], [blkb.b])
        DMA(self_.t[32:40, :, :], C["c_sel"][:, :, :], [], [self_.b])
        DMA(resetm.t[32:40, :], C["c_reset"][:, :], [], [resetm.b])
        S.op("dve", lambda e: e.memset(zrow.t[:], 0.0), [], [zrow.b])
        epsc = sb("epsc", [128, 1])
        S.op("dve", lambda e: e.memset(epsc.t[:], EPS), [], [epsc.b])
        negm = sb("negm", [128, 2, 128], BF16)
        TSC("dve", negm.t[:, :, :], trib.t[:, :, 0:128], -1.0, ALU.add, [trib.b], [negm.b], s2=30000.0, op1=ALU.mult)

        pending_casts = {l: [] for l in range(depth)}
        _DMA_real = DMA

        def DMA(out, in_, rd, wr, eng="sp", slow=False, _defer=[None]):
            if _defer[0] is not None and eng == "pool":
                pending_casts[_defer[0]].append(lambda: _DMA_real(out, in_, rd, wr, eng=eng, slow=slow))
            else:
                _DMA_real(out, in_, rd, wr, eng=eng, slow=slow)
        _defer_box = DMA.__defaults__[2]
        for l in range(depth):
            _defer_box[0] = l if l >= 1 else None
            wi = W["w_in"][l].rearrange("(k p) c -> p k c", p=128)

            def wdst(p, off, n, cw=512, l=l):
                return WS[l, p].rearrange("p (k c) -> p k c", c=cw)[:, :, off:off + n]

            def cast(p, off, c0, n, l=l, wi=wi):
                b_ = Buf("ws"); ws_buf[l][p].append(b_)
                DMA(wdst(p, off, n), wi[:, :, c0:c0 + n], [], [b_], eng="pool")

            for i in range(4):
                cast(P_AQ, i * 128, O_AQ + i * 64, 64)
                cast(P_AQ, i * 128 + 64, O_AQ + (4 + i) * 64, 64)
            cast(P_KV, 0, O_AK, 256)
            for d_ in range(2):
                base_ = 256 + d_ * 72
                b0_ = Buf("ws"); ws_buf[l][P_KV].append(b0_)
                DMA(wdst(P_KV, base_, 72), wi[:, :, O_MI:O_MI + 72], [], [b0_], eng="pool")
                for (off_, c0_, n_) in ((0, O_MI + d_ * 4, 4), (32, O_MF + d_ * 4, 4), (64, O_SDT + d_ * 8, 8)):
                    b_ = Buf("ws"); ws_buf[l][P_KV].append(b_)
                    DMA(wdst(P_KV, base_ + off_, n_), wi[:, :, c0_:c0_ + n_], [b0_], [b_], eng="pool")
            cast(P_AZ, 0, O_AZ, 512)
            cast(P_MQ, 0, O_MQ, 512)
            cast(P_MK, 0, O_MK, 512)
            cast(P_MV, 0, O_MV, 512)
            cast(P_MO, 0, O_MO, 512)
            cast(P_MZ, 0, O_MZ, 512)
            cast(P_SX, 0, O_SX, 512)
            cast(P_SBC, 0, O_SB, 256)
            cast(P_SZ, 0, O_SZ, 512)
            for bi_, nm in enumerate(("w_att_out", "w_mlstm_out", "w_ssm_out")):
                src = W[nm][l].rearrange("(k p) c -> p k c", p=128)
                for q_ in range(4):
                    p = P_G0 + bi_ * 4 + q_
                    b_ = Buf("ws"); ws_buf[l][p].append(b_)
                    DMA(WS[l, p][:, 0:2048].rearrange("p (k c) -> p k c", c=256),
                        wi[:, :, O_G + bi_ * 1024 + q_ * 256:O_G + bi_ * 1024 + (q_ + 1) * 256], [], [b_], eng="pool")
                    b_ = Buf("ws"); ws_buf[l][p].append(b_)
                    DMA(WS[l, p][:, 2048:3072].rearrange("p (k c) -> p k c", c=256), src[:, :, q_ * 256:(q_ + 1) * 256], [], [b_], eng="pool")
            wo = W["w_out"][l].rearrange("(k p) c -> p k c", p=128)
            for p_, lo_ in ((P_WO0, 0), (P_WO1, 512)):
                b_ = Buf("ws"); ws_buf[l][p_].append(b_)
                DMA(wdst(p_, 0, 512), wo[:, :, lo_:lo_ + 512], [], [b_], eng="pool")

        _defer_box[0] = None
        NSLOT = 3
        wslots = [sb("wslot%d" % i, [128, 4096], BF16) for i in range(NSLOT)]

        ng_bc = sb("ng_bc", [128, D])
        gq = sb("gq", [128, 1]); gk = sb("gk", [128, 1])
        esink = sb("esink", [128, 8])
        ib = [sb("ib%d" % d, [4, 1]) for d in range(2)]
        nfb = [sb("nfb%d" % d, [4, 1]) for d in range(2)]
        mng_bc = sb("mng_bc", [128, 512]); sng_bc = sb("sng_bc", [128, 512])
        cw = sb("cw", [128, 6, 5]); cb = sb("cb", [128, 6])
        acoef = [sb("acoef%d" % d, [40, 1]) for d in range(2)]
        dtb = [sb("dtb%d" % d, [40, 1]) for d in range(2)]
        dsk_bc = sb("dsk_bc", [128, 8])

        def load_layer_params(l):
            DMA(ng_bc.t[:], W["norm_g"][l].partition_broadcast(128), [], [ng_bc.b])
            for half in range(2):
                DMA(gq.t[half * 64:(half + 1) * 64, :], W["q_norm_g"][l].rearrange("(d o) -> d o", o=1), [], [gq.b])
                DMA(gk.t[half * 64:(half + 1) * 64, :], W["k_norm_g"][l].rearrange("(d o) -> d o", o=1), [], [gk.b])
            S.op("act", lambda e: e.mul(out=gq.t[:], in_=gq.t[:], mul=0.125), [gq.b], [gq.b])
            DMA(esink.t[:], W["attn_sink"][l].partition_broadcast(128), [], [esink.b])
            ACT(esink.t[:], esink.t[:], AF.Exp, [esink.b], [esink.b])
            for d in range(2):
                DMA(ib[d].t[:], W["mlstm_i_b"][l, d].rearrange("(d o) -> d o", o=1), [], [ib[d].b])
                DMA(nfb[d].t[:], W["mlstm_f_b"][l, d].rearrange("(d o) -> d o", o=1), [], [nfb[d].b])
                S.op("act", (lambda d: lambda e: e.mul(out=nfb[d].t[:], in_=nfb[d].t[:], mul=-1.0))(d), [nfb[d].b], [nfb[d].b])
                DMA(acoef[d].t[32:40, :], W["a_log"][l, d].rearrange("(d o) -> d o", o=1), [], [acoef[d].b])
                ACT(acoef[d].t[32:40, :], acoef[d].t[32:40, :], AF.Exp, [acoef[d].b], [acoef[d].b])
                S.op("act", (lambda d: lambda e: e.mul(out=acoef[d].t[32:40, :], in_=acoef[d].t[32:40, :], mul=-1.0))(d), [acoef[d].b], [acoef[d].b])
                DMA(dtb[d].t[32:40, :], W["dt_bias"][l, d].rearrange("(d o) -> d o", o=1), [], [dtb[d].b])
            DMA(mng_bc.t[:], W["mlstm_norm_g"][l].partition_broadcast(128), [], [mng_bc.b])
            DMA(sng_bc.t[:], W["ssm_norm_g"][l].partition_broadcast(128), [], [sng_bc.b])
            for ti in range(6):
                DMA(cw.t[:, ti, :], W["conv_w"][l][:, ti * 128:(ti + 1) * 128].rearrange("k p -> p k"), [], [cw.b], slow=True)
                DMA(cb.t[:, ti:ti + 1], W["conv_b"][l][ti * 128:(ti + 1) * 128].rearrange("(p o) -> p o", o=1), [], [cb.b])
            DMA(dsk_bc.t[:], W["d_skip"][l].partition_broadcast(128), [], [dsk_bc.b])

        hT = sb("hT", [128, 8, 8 * CH], BF16)
        hT_b = [Buf("hT%d" % i) for i in range(8)]
        hslot_chunk = [None] * 8
        xring = Ring([sb("xt%d" % i, [128, D]) for i in range(2)])
        hbring = Ring([sb("hb%d" % i, [128, D], BF16) for i in range(2)])
        st4 = Ring([sb("st4_%d" % i, [128, 8]) for i in range(6)])

        pf = Ring([psum("pf%d" % i, [128, 512]) for i in range(4)])
        pacc = [psum("pacc%d" % i, [128, 512]) for i in range(2)]
        pb = Ring([psum("pb%d" % i, [128, 1024], BF16) for i in range(2)])

        mergedT = sb("mergedT", [128, 8, TM], BF16)
        for i_ in range(2):
            f32r.tiles.append(sb("f32r_y%d" % i_, [128, 512]))
        for i_ in range(4):
            b16r.tiles.append(sb("b16r_y%d" % i_, [128, 512], BF16))
        GT = sb("GT", [128, 4096], BF16)

        class View:
            def __init__(self, ap, b):
                self.t = ap
                self.b = b
        gate_tok = [View(GT.t[:, i * 2048:(i + 1) * 2048].rearrange("p (j c) -> p j c", j=4), GT.b) for i in range(2)]
        ybT = sb("ybT", [128, 4, TM], BF16)
        rope_t = sb("rope_t", [128, 2, TM])
        kT = sb("kT", [128, 6 * CH], BF16)
        vaug = sb("vaug", [128, 6, 2, 65], BF16)
        mqT = sb("mqT", [128, 4, TM], BF16)
        qT = mqT
        mkT = sb("mkT", [128, 4, TM], BF16)
        mvaug = sb("mvaug", [128, 4, 4, 129], BF16)
        mktok = sb("mktok", [128, 4, 512], BF16)
        mC = [sb("mC%d" % d, [128, 516]) for d in range(2)]
        mCb = [sb("mCb%d" % d, [128, 516], BF16) for d in range(2)]
        srawr = Ring([sb("sraw%d" % i, [128, TM + 4]) for i in range(2)])
        sconv = sb("sconv", [128, 6, TM], BF16)
        sxtok = sb("sxtok", [128, 4, 512], BF16)
        szero = sb("szero", [128, 4, TM], BF16)
        S.op("pool", lambda e: e.memset(szero.t[:], 0.0), [], [szero.b])
        sbtok = sb("sbtok", [128, 4, 128], BF16)
        sS = [sb("sS%d" % d, [128, 256]) for d in range(2)]
        sSb = [sb("sSb%d" % d, [128, 256], BF16) for d in range(2)]
        RT = [sb("rt%d" % i, [40, 512]) for i in range(8)]
        for rt_ in RT:
            S.op("pool", (lambda rt_: lambda e: e.memset(rt_.t[:], 0.0))(rt_), [], [rt_.b])
        carryB = [sb("carryB%d" % d, [4, 1]) for d in range(2)]
        carryM = [sb("carryM%d" % d, [4, 1]) for d in range(2)]
        mprev = sb("mprev", [4, 4])
        NTS = 52
        TSF = sb("TSF", [128, 4, NTS])
        TSBm = sb("TSBm", [128, 4, NTS])
        ACSF = sb("ACSF", [40, TM])
        ACSBm = sb("ACSBm", [40, TM])
        acs_hl = {0: (sb("acsfh", [40, TM], BF16), sb("acsfl", [40, TM], BF16)),
                  1: (sb("acsbh", [40, TM], BF16), sb("acsbl", [40, TM], BF16))}
        selb = sb("selb", [40, 8, 128], BF16)
        CP("dve", selb.t[32:40, :, :], self_.t[32:40, :, :], [self_.b], [selb.b])

        def mk_hilo(d, src):
            hi, lo = acs_hl[d]
            CP("act", hi.t[32:40, :], src.t[32:40, :], [src.b], [hi.b])
            TT("dve", lo.t[32:40, :], src.t[32:40, :], hi.t[32:40, :], ALU.subtract, [src.b, hi.b], [lo.b])
        nmmax = smax // TM
        TSD = nc.dram_tensor("tsd", [nmmax, 128, 4 * NTS], F32, kind="Internal").ap()
        ACSD = nc.dram_tensor("acsd", [nmmax, 8, TM], F32, kind="Internal").ap()
        tsd_buf = [Buf("tsd%d" % i) for i in range(nmmax)]
        acsd_buf = [Buf("acsd%d" % i) for i in range(nmmax)]

        stages = []

        def add_stage(l, piece, fn):
            stages.append((l, piece, fn))

        def run_stages():
            loads = [i for i, s_ in enumerate(stages) if s_[1] is not None]
            slot_of = {}
            nxt = 0
            PRE = 2
            for i, (l, piece, fn) in enumerate(stages):
                while nxt < len(loads) and (nxt < PRE or loads[nxt - PRE] <= i):
                    j = loads[nxt]
                    sl = wslots[nxt % NSLOT]
                    lj, pj, _ = stages[j]
                    if pj == P_KV or pj == P_SBC:
                        nc_ = 400 if pj == P_KV else 256
                        DMA(sl.t[:, :].rearrange("p (k c) -> p k c", c=512)[:, :, 0:nc_],
                            WS[lj, pj].rearrange("p (k c) -> p k c", c=512)[:, :, 0:nc_], ws_buf[lj][pj], [sl.b])
                    elif P_G0 <= pj < P_WO0:
                        DMA(sl.t[:, 0:3072], WS[lj, pj][:, 0:3072], ws_buf[lj][pj], [sl.b])
                    else:
                        DMA(sl.t[:], WS[lj, pj], ws_buf[lj][pj], [sl.b])
                    slot_of[j] = sl
                    nxt += 1
                fn(slot_of.get(i))

        def src_of(l):
            return x_in if l == 0 else YS[(l - 1) % nscr]

        def src_buf(l, r0):
            return [] if l == 0 else ys_buf[(l - 1) % nscr][r0 // CH]

        def dst_of(l):
            return y_out if l == depth - 1 else YS[l % nscr]

        def dst_buf2(l, r0, half):
            return [] if l == depth - 1 else [ys_buf[l % nscr][r0 // CH][half]]

        def rstd_from_sumsq(dst, ssum, n, rd, wr, npart=128):
            TSC("dve", dst, ssum, 1.0 / n, ALU.mult, rd, wr, s2=EPS, op1=ALU.add)
            ACT(dst, dst, AF.Sqrt, wr, wr)
            S.op("dve", lambda e: e.reciprocal(out=dst, in_=dst), wr, wr)

        def ensure_h(l, tok0, c):
            sl = c % 8
            if hslot_chunk[sl] == (l, tok0, c):
                return
            hslot_chunk[sl] = (l, tok0, c)
            xt = xring.get()
            r0 = tok0 + c * CH
            DMA(xt.t[:], src_of(l)[r0:r0 + CH, :], src_buf(l, r0), [xt.b])
            st = st4.get()
            hb = hbring.get()
            ACT(hb.t[:], xt.t[:], AF.Square, [xt.b], [hb.b, st.b], accum=st.t[:, 0:1])
            rstd_from_sumsq(st.t[:, 1:2], st.t[:, 0:1], D, [st.b], [st.b])
            STT("dve", hb.t[:], xt.t[:], st.t[:, 1:2], ng_bc.t[:], ALU.mult, ALU.mult, [xt.b, st.b, ng_bc.b], [hb.b])
            p = pb.get()
            for k in range(8):
                TR(p.t[:, k * 128:(k + 1) * 128], hb.t[:, k * 128:(k + 1) * 128], identb.t[:], [hb.b, identb.b], [p.b])
            CP("act", hT.t[:, :, sl * CH:(sl + 1) * CH], p.t[:].rearrange("p (k t) -> p k t", k=8), [p.b], [hT_b[sl]])
            if c == 0:
                dbg("hT", hT.t[:, :, sl * CH:(sl + 1) * CH], [hT_b[sl]], BF16)

        def h_rhs(k, c0, n):
            s0 = c0 % 8
            assert s0 + n <= 8
            return hT.t[:, k, s0 * CH:(s0 + n) * CH]

        def h_bufs(c0, n):
            return [hT_b[(c0 + i) % 8] for i in range(n)]

        def proj_feat(sl, col0, ncols, c0, nchunks, pt, pcol0=0):
            for k in range(8):
                MM(pt.t[0:ncols, pcol0:pcol0 + nchunks * CH], sl.t[:, k * 512 + col0:k * 512 + col0 + ncols],
                   h_rhs(k, c0, nchunks), k == 0, k == 7, [sl.b] + h_bufs(c0, nchunks), [pt.b])

        def proj_tok(sl, col0, ncols, c, pt):
            s0 = c % 8
            for k in range(8):
                MM(pt.t[:, 0:ncols], hT.t[:, k, s0 * CH:(s0 + 1) * CH], sl.t[:, k * 512 + col0:k * 512 + col0 + ncols],
                   k == 0, k == 7, [sl.b, hT_b[s0]], [pt.b])

        mL = slice(0, 4)
        sD = slice(32, 40)

        def gate_rows(l, d, sl, c0, first, ts_dst, ts_bufs, acs_dst, acs_bufs):
            rev = (d == 1)

            def rv(ap2d):
                return ap2d[:, ::-1] if rev else ap2d

            pg_ = pf.get()
            proj_feat(sl, 256 + d * 72, 72, c0, 4, pg_)
            pmi = pmf = pdt = pg_
            R = RT
            IG, L1, Bp, Mg, WI, DEC = R[0], R[1], R[2], R[3], R[4], R[5]
            ACT(IG.t[mL, :], rv(pmi.t[0:4, :]), AF.Identity, [pmi.b, ib[d].b], [IG.b], bias=ib[d].t[:, 0:1])
            ACT(L1.t[mL, :], rv(pmf.t[32:36, :]), AF.Exp, [pmf.b, nfb[d].b], [L1.b], bias=nfb[d].t[:, 0:1], scale=-1.0)
            ACT(L1.t[mL, :], L1.t[mL, :], AF.Ln, [L1.b], [L1.b], bias=1.0)
            if first:
                S.op("dve", lambda e: e.memset(carryB[d].t[:], 0.0), [], [carryB[d].b])
                S.op("dve", lambda e: e.memset(carryM[d].t[:], 0.0), [], [carryM[d].b])
            SCAN(Bp.t[mL, :], L1.t[mL, :], zrow.t[0:4, 0:1].to_broadcast([4, 512]), carryB[d].t[:, 0:1], ALU.add, ALU.add,
                 [L1.b, zrow.b, carryB[d].b], [Bp.b])
            A = IG
            TT("dve", A.t[mL, :], IG.t[mL, :], Bp.t[mL, :], ALU.add, [IG.b, Bp.b], [A.b])
            SCAN(Mg.t[mL, :], A.t[mL, :], A.t[mL, :], carryM[d].t[:, 0:1], ALU.max, ALU.max, [A.b, carryM[d].b], [Mg.b])

            def r3(tl):
                return tl.t[mL, :].rearrange("p (c t) -> p c t", c=4)

            Mg3 = r3(Mg)
            CP("dve", mprev.t[:, 0:1], carryM[d].t[:, 0:1], [carryM[d].b], [mprev.b])
            CP("dve", mprev.t[:, 1:4], Mg3[:, 0:3, 127], [Mg.b], [mprev.b])
            CP("dve", carryB[d].t[:, 0:1], Bp.t[mL, 511:512], [Bp.b], [carryB[d].b])
            CP("dve", carryM[d].t[:, 0:1], Mg.t[mL, 511:512], [Mg.b], [carryM[d].b])
            mend_bc = Mg3[:, :, 127:128].to_broadcast([4, 4, 128])
            mprev_bc = mprev.t[:, :].unsqueeze(2).to_broadcast([4, 4, 128])
            U = L1
            TT("dve", r3(U), r3(Mg), mend_bc, ALU.subtract, [Mg.b, L1.b], [U.b])
            TSC("dve", U.t[mL, :], U.t[mL, :], -60.0, ALU.max, [U.b], [U.b])
            ACT(U.t[mL, :], U.t[mL, :], AF.Exp, [U.b], [U.b], scale=-1.0)
            TT("dve", r3(WI), r3(Mg), mprev_bc, ALU.subtract, [Mg.b, mprev.b], [WI.b])
            ACT(WI.t[mL, :], WI.t[mL, :], AF.Exp, [WI.b], [WI.b], scale=-1.0)
            FL = Bp
            TT("dve", FL.t[mL, :], Bp.t[mL, :], Mg.t[mL, :], ALU.subtract, [Bp.b, Mg.b], [FL.b])
            ACT(FL.t[mL, :], FL.t[mL, :], AF.Exp, [FL.b], [FL.b])
            TT("dve", r3(DEC), mprev_bc, mend_bc, ALU.subtract, [Mg.b, mprev.b], [DEC.b])
            ACT(DEC.t[mL, :], DEC.t[mL, :], AF.Exp, [DEC.b], [DEC.b])
            WK = A
            TT("dve", r3(WK), r3(A), mend_bc, ALU.subtract, [A.b, Mg.b], [WK.b])
            ACT(WK.t[mL, :], WK.t[mL, :], AF.Exp, [WK.b], [WK.b])
            DT, LDT, DA, ACS, EA = R[0], R[1], R[2], R[3], R[4]
            ACT(DT.t[sD, :], rv(pdt.t[64:72, :]), AF.Exp, [pdt.b, dtb[d].b], [DT.b], bias=dtb[d].t[sD, 0:1])
            ACT(DT.t[sD, :], DT.t[sD, :], AF.Ln, [DT.b], [DT.b], bias=1.0)
            ACT(LDT.t[sD, :], DT.t[sD, :], AF.Ln, [DT.b], [LDT.b])
            TSC("dve", DA.t[sD, :], DT.t[sD, :], acoef[d].t[sD, 0:1], ALU.mult, [DT.b, acoef[d].b], [DA.b])
            SCAN(ACS.t[sD, :], resetm.t[sD, :], DA.t[sD, :], 0.0, ALU.mult, ALU.add, [resetm.b, DA.b], [ACS.b])

            def r8(tl):
                return tl.t[sD, :].rearrange("p (c t) -> p c t", c=4)

            aend_bc = r8(ACS)[:, :, 127:128].to_broadcast([8, 4, 128])
            BL = LDT
            TT("dve", BL.t[sD, :], LDT.t[sD, :], ACS.t[sD, :], ALU.subtract, [LDT.b, ACS.b], [BL.b])
            WST = DA
            TT("dve", r8(WST), aend_bc, r8(ACS), ALU.subtract, [ACS.b, DA.b], [WST.b])
            ACT(WST.t[sD, :], WST.t[sD, :], AF.Exp, [WST.b], [WST.b])
            TT("dve", WST.t[sD, :], WST.t[sD, :], DT.t[sD, :], ALU.mult, [WST.b, DT.b], [WST.b])
            ACT(EA.t[sD, :], ACS.t[sD, :], AF.Exp, [ACS.b], [EA.b])
            CD = DT
            CP("dve", r8(CD), aend_bc, [ACS.b, WST.b, DT.b], [CD.b])
            ACT(CD.t[sD, :], CD.t[sD, :], AF.Exp, [CD.b], [CD.b])
            quants = [(WK, mL, 4, 0), (U, mL, 4, 4), (WI, mL, 4, 8), (FL, mL, 4, 12), (DEC, mL, 4, 16),
                      (BL, sD, 8, 20), (WST, sD, 8, 28), (EA, sD, 8, 36), (CD, sD, 8, 44)]
            pts = pf.get()
            if rev:
                order_ = [R[0], R[1], R[2], R[4], R[5]]
                rmap = {}
                for ti_, tl_ in enumerate(order_):
                    q2 = R[6 + (ti_ % 2)]
                    rmap[id(tl_)] = (q2, ti_)
                quants = sorted(quants, key=lambda x: rmap[id(x[0])][1])
                done_ = set()
            for qi, (q, ps_, r, off) in enumerate(quants):
                if rev:
                    q2, ti_ = rmap[id(q)]
                    if ti_ not in done_:
                        done_.add(ti_)
                        CP("dve", q2.t[0:40, :], q.t[0:40, ::-1], [q.b], [q2.b])
                    q = q2
                for j in range(4):
                    MM(pts.t[:, j * 64 + off:j * 64 + off + r], q.t[ps_, j * CH:(j + 1) * CH], identf.t[ps_, ps_],
                       True, True, [q.b, identf.b], [pts.b])
            CP("act", ts_dst, pts.t[:, 0:256].rearrange("p (j c) -> p j c", j=4)[:, :, 0:NTS], [pts.b], ts_bufs)
            if rev:
                CP("pool", acs_dst, ACS.t[sD, ::-1], [ACS.b], acs_bufs)
            else:
                CP("pool", acs_dst, ACS.t[sD, :], [ACS.b], acs_bufs)

        def ssd_conv(l, slx, slbc, c0, nch_seq, tiles):
            for ti in tiles:
                if "conv" in KSKIP:
                    break
                sl, col0 = (slx, ti * 128) if ti < 4 else (slbc, (ti - 4) * 128)
                sraw = srawr.get()
                pm = pf.get()
                proj_feat(sl, col0, 128, c0, 4, pm)
                CP("act", sraw.t[:, 2:2 + TM], pm.t[:, :], [pm.b], [sraw.b])
                ph = pf.get()
                if c0 > 0:
                    s0 = (c0 - 1) % 8
                    for k in range(8):
                        MM(ph.t[:, 0:32], sl.t[:, k * 512 + col0:k * 512 + col0 + 128], hT.t[:, k, s0 * CH + 96:s0 * CH + 128],
                           k == 0, k == 7, [sl.b, hT_b[s0]], [ph.b])
                    CP("dve", sraw.t[:, 0:2], ph.t[:, 30:32], [ph.b], [sraw.b])
                else:
                    S.op("dve", (lambda sraw: lambda e: e.memset(sraw.t[:, 0:2], 0.0))(sraw), [], [sraw.b])
                if c0 + 4 < nch_seq:
                    s0 = (c0 + 4) % 8
                    for k in range(8):
                        MM(ph.t[:, 32:64], sl.t[:, k * 512 + col0:k * 512 + col0 + 128], hT.t[:, k, s0 * CH:s0 * CH + 32],
                           k == 0, k == 7, [sl.b, hT_b[s0]], [ph.b])
                    CP("dve", sraw.t[:, TM + 2:TM + 4], ph.t[:, 32:34], [ph.b], [sraw.b])
                else:
                    S.op("dve", (lambda sraw: lambda e: e.memset(sraw.t[:, TM + 2:TM + 4], 0.0))(sraw), [], [sraw.b])
                acc = f32r.get()
                TSC("dve", acc.t[:, :], sraw.t[:, 0:TM], cw.t[:, ti, 0:1], ALU.mult, [sraw.b, cw.b, cb.b], [acc.b],
                    s2=cb.t[:, ti:ti + 1], op1=ALU.add)
                for kk in range(1, 5):
                    STT("dve", acc.t[:, :], sraw.t[:, kk:kk + TM], cw.t[:, ti, kk:kk + 1], acc.t[:, :], ALU.mult, ALU.add,
                        [sraw.b, cw.b, acc.b], [acc.b])
                ACT(sconv.t[:, ti, :], acc.t[:, :], AF.Silu, [acc.b], [sconv.b])

        def ssd_tokmajor(j, do_x=True):
            if "tokm" in KSKIP:
                return
            if do_x:
                p = pb.get()
                for ti in range(4):
                    TR(p.t[:, ti * 128:(ti + 1) * 128], sconv.t[:, ti, j * CH:(j + 1) * CH], identb.t[:], [sconv.b, identb.b], [p.b])
                CP("act", sxtok.t[:, j, :], p.t[:, 0:512], [p.b], [sxtok.b])
            if "tokb" in KSKIP:
                return
            p2 = pb.get()
            TR(p2.t[:, 0:128], sconv.t[:, 4, j * CH:(j + 1) * CH], identb.t[:], [sconv.b, identb.b], [p2.b])
            CP("act", sbtok.t[:, j, :], p2.t[:, 0:128], [p2.b], [sbtok.b])

        def ssd_state_update(d, j, ts_ap, ts_rd):
            xw = b16r.get()
            TT("pool" if d == 0 else "dve", xw.t[:, :].rearrange("p (h e) -> p h e", h=8), sxtok.t[:, j, :].rearrange("p (h e) -> p h e", h=8),
               ts_ap[:, 28:36].unsqueeze(2).to_broadcast([128, 8, 64]), ALU.mult, [sxtok.b] + ts_rd, [xw.b])
            pS = pf.get()
            for g in range(2):
                MM(pS.t[:, g * 256:(g + 1) * 256], sbtok.t[:, j, :], xw.t[:, g * 256:(g + 1) * 256], True, True, [sbtok.b, xw.b], [pS.b])
            for g in range(2):
                ps_ = slice(g * 64, (g + 1) * 64)
                TT("dve", sS[d].t[ps_, :].rearrange("p (h e) -> p h e", h=4), sS[d].t[ps_, :].rearrange("p (h e) -> p h e", h=4),
                   ts_ap[ps_, 44 + g * 4:44 + g * 4 + 4].unsqueeze(2).to_broadcast([64, 4, 64]), ALU.mult, [sS[d].b] + ts_rd, [sS[d].b])
                TT("dve", sS[d].t[ps_, :], sS[d].t[ps_, :], pS.t[ps_, g * 256:(g + 1) * 256], ALU.add, [sS[d].b, pS.b], [sS[d].b])

        def mlstm_state_update(d, j, ts_ap, ts_rd, pre=None):
            if pre is not None:
                vw, wkb = pre["vw"], pre["wkb"]
            else:
                vw = b16r.get()
                TT("pool" if d == 0 else "dve", vw.t[:, :].rearrange("p (h e) -> p h e", h=4), mvaug.t[:, j, :, 0:128],
                   ts_ap[:, 0:4].unsqueeze(2).to_broadcast([128, 4, 128]), ALU.mult, [mvaug.b] + ts_rd, [vw.b])
                wkb = b16r.get()
                CP("dve", wkb.t[:, 0:4], ts_ap[:, 0:4], ts_rd, [wkb.b])
            pC = pf.get(); pn = pf.get()
            for h in range(4):
                MM(pC.t[:, h * 128:(h + 1) * 128], mktok.t[:, j, h * 128:(h + 1) * 128], vw.t[:, h * 128:(h + 1) * 128], True, True,
                   [mktok.b, vw.b], [pC.b])
                MM(pn.t[:, h:h + 1], mktok.t[:, j, h * 128:(h + 1) * 128], wkb.t[:, h:h + 1], True, True, [mktok.b, wkb.b], [pn.b])
            dec_bc = ts_ap[:, 16:20]
            TT("dve", mC[d].t[:, 0:512].rearrange("p (h e) -> p h e", h=4), mC[d].t[:, 0:512].rearrange("p (h e) -> p h e", h=4),
               dec_bc.unsqueeze(2).to_broadcast([128, 4, 128]), ALU.mult, [mC[d].b] + ts_rd, [mC[d].b])
            TT("dve", mC[d].t[:, 512:516], mC[d].t[:, 512:516], dec_bc, ALU.mult, [mC[d].b] + ts_rd, [mC[d].b])
            TT("dve", mC[d].t[:, 0:512], mC[d].t[:, 0:512], pC.t[:, :], ALU.add, [mC[d].b, pC.b], [mC[d].b])
            TT("dve", mC[d].t[:, 512:516], mC[d].t[:, 512:516], pn.t[:, 0:4], ALU.add, [mC[d].b, pn.b], [mC[d].b])

        def mlstm_v_tok(sl, c, j):
            pv = pf.get()
            proj_tok(sl, 0, 512, c, pv)
            CP("act", mvaug.t[:, j, :, 0:128], pv.t[:, :].rearrange("p (h e) -> p h e", h=4), [pv.b], [mvaug.b])

        def add_branch_merge(l, bi, first, c0):
            for q_ in range(4):
                def st_g(sl, q_=q_):
                    for oo in range(2):
                        o = q_ * 2 + oo
                        pg = pf.get()
                        for k_ in range(8):
                            MM(pg.t[:, :], sl.t[:, k_ * 256 + oo * 128:k_ * 256 + (oo + 1) * 128], h_rhs(k_, c0, 4), k_ == 0, k_ == 7,
                               [sl.b] + h_bufs(c0, 4), [pg.b])
                        gs = f32r.get()
                        ACT(gs.t[:, :], pg.t[:, :], AF.Sigmoid, [pg.b], [gs.b])
                        pp = pf.get()
                        for k_ in range(4):
                            MM(pp.t[:, :], sl.t[:, 2048 + k_ * 256 + oo * 128:2048 + k_ * 256 + (oo + 1) * 128], ybT.t[:, k_, :], k_ == 0, k_ == 3,
                               [sl.b, ybT.b], [pp.b])
                        if first:
                            TT("dve", mergedT.t[:, o, :], gs.t[:, :], pp.t[:, :], ALU.mult, [gs.b, pp.b], [mergedT.b])
                        else:
                            TT("dve", gs.t[:, :], gs.t[:, :], pp.t[:, :], ALU.mult, [gs.b, pp.b], [gs.b])
                            TT("pool", mergedT.t[:, o, :], mergedT.t[:, o, :], gs.t[:, :], ALU.add, [gs.b, mergedT.b], [mergedT.b])
                add_stage(l, P_G0 + bi * 4 + q_, st_g)

        def add_final(l, tok0, c0, nch):
            def st_pre(_sl):
                for c_ in range(c0 + 5, min(nch, c0 + 9)):
                    ensure_h(l, tok0, c_)
                dbg("ybT", ybT.t[:, :, :], [ybT.b], BF16)
            add_stage(l, None, st_pre)
            for half in range(2):
                def st_o(sl, half=half):
                    hs = slice(half * 512, (half + 1) * 512)
                    for j in range(4):
                        r0 = tok0 + (c0 + j) * CH
                        xt = xring2.get()
                        DMA(xt.t[:, :], src_of(l)[r0:r0 + CH, hs], src_buf(l, r0), [xt.b])
                        po = pf.get()
                        for k in range(8):
                            MM(po.t[:, :], mergedT.t[:, k, j * CH:(j + 1) * CH], sl.t[:, k * 512:(k + 1) * 512], k == 0, k == 7,
                               [mergedT.b, sl.b], [po.b])
                        TT("dve", xt.t[:, :], xt.t[:, :], po.t[:, :], ALU.add, [xt.b, po.b], [xt.b])
                        DMA(dst_of(l)[r0:r0 + CH, hs], xt.t[:, :], [xt.b], dst_buf2(l, r0, half))
                add_stage(l, P_WO0 + half, st_o)

        xring2 = Ring([sb("xo%d" % i, [128, 512]) for i in range(3)])
        f32r.tiles.append(sb("f32r_x", [128, 512]))
        tsf_b = TSF.b
        acsf_b = ACSF.b

        def seq_layer(l, tok0, slen):
            nch = slen // CH
            nm = slen // TM
            en_att, en_ml, en_ssd = ("att" in enable), ("mlstm" in enable), ("ssd" in enable)
            en_rec = en_ml or en_ssd

            if en_rec:
                for m in reversed(range(nm)):
                    c0 = 4 * m

                    def st_h(_sl, c0=c0):
                        for c in range(max(0, c0 - 1), min(nch, c0 + 5)):
                            ensure_h(l, tok0, c)
                    add_stage(l, None, st_h)

                    def st_gates(sl, c0=c0, m=m):
                        if m == nm - 1:
                            S.op("dve", lambda e: e.memset(mC[1].t[:], 0.0), [], [mC[1].b])
                            S.op("dve", lambda e: e.memset(sS[1].t[:], 0.0), [], [sS[1].b])
                            S.op("dve", lambda e: e.memset(mvaug.t[:, :, :, 128:129], 1.0), [], [mvaug.b])
                        gate_rows(l, 1, sl, c0, m == nm - 1, TSBm.t[:, :, :], [TSBm.b], ACSBm.t[sD, :], [ACSBm.b])
                        DMA(TSD[m].rearrange("p (j c) -> p j c", j=4), TSBm.t[:, :, :], [TSBm.b], [tsd_buf[m]])
                        DMA(ACSD[m], ACSBm.t[sD, :], [ACSBm.b], [acsd_buf[m]])
                    add_stage(l, P_KV, st_gates)

                    if en_ml:
                        def st_mk(sl, c0=c0):
                            for j in range(4):
                                pk = pf.get()
                                proj_tok(sl, 0, 512, c0 + j, pk)
                                S.op("act", (lambda j, pk: lambda e: e.mul(out=mktok.t[:, j, :], in_=pk.t[:, :], mul=128 ** -0.5))(j, pk),
                                     [pk.b], [mktok.b])
                                DMA(KTOK[c0 + j], mktok.t[:, j, :], [mktok.b], [ktok_buf[c0 + j]])
                        add_stage(l, P_MK, st_mk)

                        def st_mv(sl, c0=c0, m=m):
                            for j in range(4):
                                mlstm_v_tok(sl, c0 + j, j)
                                DMA(VTOK[c0 + j].rearrange("p (h e) -> p h e", h=4), mvaug.t[:, j, :, 0:128], [mvaug.b], [vtok_buf[c0 + j]])
                            for j in reversed(range(4)):
                                c = c0 + j
                                CP("act", mCb[1].t[:, :], mC[1].t[:, :], [mC[1].b], [mCb[1].b])
                                DMA(BSTM[c], mCb[1].t[:, :], [mCb[1].b], [bstm_buf[c]])
                                mlstm_state_update(1, j, TSBm.t[:, j, :], [TSBm.b])
                        add_stage(l, P_MV, st_mv)

                    if en_ssd and "p2ssd" not in KSKIP:
                        def st_sx(sl, c0=c0):
                            ssd_conv(l, sl, None, c0, nch, [0, 1, 2, 3])
                        add_stage(l, P_SX, st_sx)

                        def st_sbc(sl, c0=c0, m=m):
                            ssd_conv(l, None, sl, c0, nch, [4])
                            for j in reversed(range(4)):
                                c = c0 + j
                                ssd_tokmajor(j)
                                DMA(XTOK[c], sxtok.t[:, j, :], [sxtok.b], [xtok_buf[c]])
                                CP("act", sSb[1].t[:, :], sS[1].t[:, :], [sS[1].b], [sSb[1].b])
                                DMA(BSTS[c], sSb[1].t[:, :], [sSb[1].b], [bsts_buf[c]])
                                ssd_state_update(1, j, TSBm.t[:, j, :], [TSBm.b])
                        add_stage(l, P_SBC, st_sbc)

                    def st_pref(_sl, c0=c0):
                        for c_ in range(max(0, c0 - 5), c0 - 1):
                            ensure_h(l, tok0, c_)
                    add_stage(l, None, st_pref)

            for m in range(nm):
                c0 = 4 * m
                lo = max(0, c0 - 1)
                hi = min(nch, c0 + 5)

                def st_h(_sl, c0=c0, lo=lo, hi=hi, m=m):
                    if l + 1 < depth:
                        for _ in range(casts_per_mt):
                            if pending_casts[l + 1]:
                                pending_casts[l + 1].pop(0)()
                    for c in range(lo, hi):
                        ensure_h(l, tok0, c)
                    DMA(rope_t.t[:], C["c_rope"][:, :, c0 * CH:c0 * CH + TM], [], [rope_t.b])
                    if m == 0:
                        S.op("dve", lambda e: e.memset(mC[0].t[:], 0.0), [], [mC[0].b])
                        S.op("dve", lambda e: e.memset(sS[0].t[:], 0.0), [], [sS[0].b])
                        S.op("dve", lambda e: e.memset(mCb[0].t[:], 0.0), [], [mCb[0].b])
                        S.op("dve", lambda e: e.memset(sSb[0].t[:], 0.0), [], [sSb[0].b])
                        S.op("dve", lambda e: e.memset(mvaug.t[:, :, :, 128:129], 1.0), [], [mvaug.b])
                        S.op("dve", lambda e: e.memset(vaug.t[:, :, :, 64:65], 1.0), [], [vaug.b])
                add_stage(l, None, st_h)

                def qk_norm_rope(pq, gain, ntok):
                    sq = b16r.get()
                    ACT(sq.t[:, 0:ntok], pq.t[:, 0:ntok], AF.Square, [pq.b], [sq.b])
                    pss = pf.get()
                    MM(pss.t[:, 0:ntok], blkb.t[:], sq.t[:, 0:ntok], True, True, [blkb.b, sq.b], [pss.b])
                    rs = f32r.get()
                    ACT(rs.t[:, 0:ntok], pss.t[:, 0:ntok], AF.Ln, [pss.b, epsc.b], [rs.b], bias=epsc.t[:, 0:1])
                    ACT(rs.t[:, 0:ntok], rs.t[:, 0:ntok], AF.Exp, [rs.b], [rs.b], scale=-0.5)
                    qn = f32r.get(); qnb = b16r.get()
                    STT("dve", qnb.t[:, 0:ntok], pq.t[:, 0:ntok], gain.t[:, 0:1], rs.t[:, 0:ntok], ALU.mult, ALU.mult,
                        [pq.b, gain.b, rs.b], [qnb.b])
                    pr = pf.get()
                    MM(pr.t[:, 0:ntok], rotb.t[:], qnb.t[:, 0:ntok], True, True, [rotb.b, qnb.b], [pr.b])
                    t1 = f32r.get()
                    return qn, pr, t1, qnb

                if en_att:
                    def st_aq(sl, c0=c0, qk_norm_rope=qk_norm_rope):
                        for i in range(4):
                            pq = pf.get()
                            proj_feat(sl, i * 128, 128, c0, 4, pq)
                            qn, pr, t1, qnb = qk_norm_rope(pq, gq, TM)
                            TT("dve", t1.t[:, :], pr.t[:, :], rope_t.t[:, 1, :], ALU.mult, [pr.b, rope_t.b], [t1.b])
                            TT("pool", qn.t[:, :], qnb.t[:, :], rope_t.t[:, 0, :], ALU.mult, [qnb.b, rope_t.b], [qn.b])
                            TT("pool", qT.t[:, i, :], qn.t[:, :], t1.t[:, :], ALU.add, [qn.b, t1.b], [qT.b])
                            if i == 0:
                                dbg("pq", pq.t[:, :], [pq.b])
                        dbg("qT", qT.t[:, :, :], [qT.b], BF16)
                    add_stage(l, P_AQ, st_aq)

                def st_kv(sl, c0=c0, lo=lo, hi=hi, m=m, qk_norm_rope=qk_norm_rope):
                    if en_att:
                        for (ca, n) in ((lo, c0 - lo), (c0, 4), (c0 + 4, hi - c0 - 4)):
                            if n <= 0:
                                continue
                            pk = pf.get()
                            proj_feat(sl, 0, 128, ca, n, pk)
                            ntk = n * CH
                            qn, pr, t1, qnb = qk_norm_rope(pk, gk, ntk)
                            if ca == c0:
                                cosap, sinap, rbufs = rope_t.t[:, 0, :], rope_t.t[:, 1, :], [rope_t.b]
                            else:
                                rt = f32r.get(); rt2 = f32r.get()
                                DMA(rt.t[:, 0:CH], C["c_rope"][:, 0, ca * CH:(ca + 1) * CH], [], [rt.b])
                                DMA(rt2.t[:, 0:CH], C["c_rope"][:, 1, ca * CH:(ca + 1) * CH], [], [rt2.b])
                                cosap, sinap, rbufs = rt.t[:, 0:CH], rt2.t[:, 0:CH], [rt.b, rt2.b]
                            TT("dve", t1.t[:, 0:ntk], pr.t[:, 0:ntk], sinap, ALU.mult, [pr.b] + rbufs, [t1.b])
                            TT("pool", qn.t[:, 0:ntk], qnb.t[:, 0:ntk], cosap, ALU.mult, [qnb.b] + rbufs, [qn.b])
                            o = (ca - lo) * CH
                            TT("pool", kT.t[:, o:o + ntk], qn.t[:, 0:ntk], t1.t[:, 0:ntk], ALU.add, [qn.b, t1.b], [kT.b])
                        for c in range(lo, hi):
                            pv = pf.get()
                            proj_tok(sl, 128, 128, c, pv)
                            CP("act", vaug.t[:, c - lo, :, 0:64], pv.t[:, 0:128].rearrange("p (g e) -> p g e", g=2), [pv.b], [vaug.b])
                        dbg("kT", kT.t[:, :], [kT.b], BF16)
                        dbg("vaug", vaug.t[:, :, :, :], [vaug.b], BF16)
                    if en_rec:
                        gate_rows(l, 0, sl, c0, m == 0, TSF.t[:, :, :], [tsf_b], ACSF.t[sD, :], [acsf_b])
                        DMA(TSBm.t[:, :, :], TSD[m].rearrange("p (j c) -> p j c", j=4), [tsd_buf[m]], [TSBm.b])
                        DMA(ACSBm.t[sD, :], ACSD[m], [acsd_buf[m]], [ACSBm.b])
                        mk_hilo(0, ACSF)
                        mk_hilo(1, ACSBm)
                add_stage(l, P_KV, st_kv)

                if en_att:
                    def st_az(sl, c0=c0, lo=lo, hi=hi):
                        for j in range(4):
                            pz = pf.get()
                            proj_tok(sl, 0, 512, c0 + j, pz)
                            ACT(gate_tok[0].t[:, j, :], pz.t[:, :], AF.Silu, [pz.b], [gate_tok[0].b])
                        for j in range(4):
                            c = c0 + j
                            po = pacc
                            kblocks = [cc for cc in (c - 1, c, c + 1) if 0 <= cc < nch]
                            for g in range(2):
                                pr_ = slice(g * 64, (g + 1) * 64)
                                for bi, cc in enumerate(kblocks):
                                    ps_ = pf.get()
                                    ko = (cc - lo) * CH
                                    for i in range(4):
                                        MM(ps_.t[:, i * 128:(i + 1) * 128], kT.t[pr_, ko:ko + CH], qT.t[pr_, i, j * CH:(j + 1) * CH],
                                           True, True, [kT.b, qT.b], [ps_.b])
                                    pt_ = b16r.get()
                                    ACT(pt_.t[:, :], ps_.t[:, :], AF.Exp, [ps_.b], [pt_.b])
                                    if cc != c:
                                        mi_ = 1 if cc < c else 0
                                        TT("pool", pt_.t[:, :], pt_.t[:, :], trib.t[:, mi_, :], ALU.mult, [pt_.b, trib.b], [pt_.b])
                                    for i in range(4):
                                        MM(po[g].t[:, i * 65:(i + 1) * 65], pt_.t[:, i * 128:(i + 1) * 128], vaug.t[:, cc - lo, g, :],
                                           bi == 0 and i == 0, bi == len(kblocks) - 1, [pt_.b, vaug.b], [po[g].b], skip=True)
                            ya = f32r.get(); den = st4.get()
                            for g in range(2):
                                pv3 = po[g].t[:, 0:260].rearrange("p (i e) -> p i e", i=4)
                                TT("dve", den.t[:, g * 4:(g + 1) * 4], pv3[:, :, 64], esink.t[:, g * 4:(g + 1) * 4], ALU.add,
                                   [po[g].b, esink.b], [den.b])
                            S.op("dve", (lambda den: lambda e: e.reciprocal(out=den.t[:, 0:8], in_=den.t[:, 0:8]))(den), [den.b], [den.b])
                            for g in range(2):
                                pv3 = po[g].t[:, 0:260].rearrange("p (i e) -> p i e", i=4)
                                TT("dve", ya.t[:, g * 256:(g + 1) * 256].rearrange("p (i e) -> p i e", i=4), pv3[:, :, 0:64],
                                   den.t[:, g * 4:(g + 1) * 4].unsqueeze(2).to_broadcast([128, 4, 64]), ALU.mult, [po[g].b, den.b], [ya.b])
                            dbg("ya", ya.t[:, :], [ya.b])
                            yg = b16r.get()
                            TT("pool", yg.t[:, :], ya.t[:, :], gate_tok[0].t[:, j, :], ALU.mult, [ya.b, gate_tok[0].b], [yg.b])
                            dbg("yg", yg.t[:, :], [yg.b], BF16)
                            p = pb.get()
                            for i in range(4):
                                TR(p.t[:, i * 128:(i + 1) * 128], yg.t[:, i * 128:(i + 1) * 128], identb.t[:], [yg.b, identb.b], [p.b])
                            CP("act", ybT.t[:, :, j * CH:(j + 1) * CH], p.t[:, 0:512].rearrange("p (i t) -> p i t", i=4), [p.b], [ybT.b])
                    add_stage(l, P_AZ, st_az)
                    add_branch_merge(l, 0, True, c0)

                if en_ml:
                    def st_mq(sl, c0=c0):
                        for h in range(4):
                            pq = pf.get()
                            proj_feat(sl, h * 128, 128, c0, 4, pq)
                            CP("act", mqT.t[:, h, :], pq.t[:, :], [pq.b], [mqT.b])
                    add_stage(l, P_MQ, st_mq)

                    def st_mk(_sl, c0=c0):
                        for j in range(4):
                            DMA(mktok.t[:, j, :], KTOK[c0 + j], [ktok_buf[c0 + j]], [mktok.b])
                            DMA(mvaug.t[:, j, :, 0:128], VTOK[c0 + j].rearrange("p (h e) -> p h e", h=4), [vtok_buf[c0 + j]], [mvaug.b])
                        for j in range(4):
                            p = pb.get()
                            for h in range(4):
                                TR(p.t[:, h * 128:(h + 1) * 128], mktok.t[:, j, h * 128:(h + 1) * 128], identb.t[:], [mktok.b, identb.b], [p.b])
                            CP("dve", mkT.t[:, :, j * CH:(j + 1) * CH], p.t[:, 0:512].rearrange("p (h t) -> p h t", h=4), [p.b], [mkT.b])
                    add_stage(l, None, st_mk)

                    def st_mo(sl, c0=c0):
                        for j in range(4):
                            pz = pf.get()
                            proj_tok(sl, 0, 512, c0 + j, pz)
                            ACT(gate_tok[0].t[:, j, :], pz.t[:, :], AF.Sigmoid, [pz.b], [gate_tok[0].b])
                    add_stage(l, P_MO, st_mo)

                    def st_mz(sl, c0=c0, m=m):
                        for j in range(4):
                            pz = pf.get()
                            proj_tok(sl, 0, 512, c0 + j, pz)
                            ACT(gate_tok[1].t[:, j, :], pz.t[:, :], AF.Silu, [pz.b], [gate_tok[1].b])
                        for j in range(4):
                            c = c0 + j
                            tok = slice(j * CH, (j + 1) * CH)
                            DMA(mCb[1].t[:, :], BSTM[c], [bstm_buf[c]], [mCb[1].b])
                            ps = pf.get()
                            for h in range(4):
                                MM(ps.t[:, h * 128:(h + 1) * 128], mkT.t[:, h, tok], mqT.t[:, h, tok], True, True, [mkT.b, mqT.b], [ps.b])
                            hsum = f32r.get()
                            g01 = f32r.get()
                            TT("pool", g01.t[:, :], gate_tok[0].t[:, j, :], gate_tok[1].t[:, j, :], ALU.mult, [gate_tok[0].b], [g01.b])
                            TT("pool", g01.t[:, :], g01.t[:, :], mng_bc.t[:, :], ALU.mult, [g01.b, mng_bc.b], [g01.b])
                            fwd_vw = {}
                            for d in range(2):
                                ts_ap = TSF.t[:, j, :] if d == 0 else TSBm.t[:, j, :]
                                ts_rd = [tsf_b] if d == 0 else [TSBm.b]
                                scm = b16r.get()
                                TT("dve", scm.t[:, :], ps.t[:, :], trib.t[:, d, :], ALU.mult, [ps.b, trib.b], [scm.b])
                                vw = b16r.get(); wkb = b16r.get()
                                if d == 0:
                                    fwd_vw["vw"], fwd_vw["wkb"] = vw, wkb
                                TT("pool", vw.t[:, :].rearrange("p (h e) -> p h e", h=4), mvaug.t[:, j, :, 0:128],
                                   ts_ap[:, 0:4].unsqueeze(2).to_broadcast([128, 4, 128]), ALU.mult, [mvaug.b] + ts_rd, [vw.b])
                                CP("dve", wkb.t[:, 0:4], ts_ap[:, 0:4], ts_rd, [wkb.b])
                                pnum, pint = pacc[0], pacc[1]
                                pden = pf.get()
                                for h in range(4):
                                    hs = slice(h * 128, (h + 1) * 128)
                                    MM(pnum.t[:, hs], scm.t[:, hs], vw.t[:, hs], True, True, [scm.b, vw.b], [pnum.b])
                                    MM(pden.t[:, h:h + 1], scm.t[:, hs], wkb.t[:, h:h + 1], True, True, [scm.b, wkb.b], [pden.b])
                                    MM(pint.t[:, hs], mqT.t[:, h, tok], mCb[d].t[:, hs], True, True, [mqT.b, mCb[d].b], [pint.b])
                                    MM(pden.t[:, 4 + h:5 + h], mqT.t[:, h, tok], mCb[d].t[:, 512 + h:513 + h], True, True,
                                       [mqT.b, mCb[d].b], [pden.b])
                                n1 = f32r.get(); n2 = f32r.get(); dd = st4.get()
                                u_bc = ts_ap[:, 4:8].unsqueeze(2).to_broadcast([128, 4, 128])
                                wi_bc = ts_ap[:, 8:12].unsqueeze(2).to_broadcast([128, 4, 128])
                                TT("dve", n1.t[:, :].rearrange("p (h e) -> p h e", h=4), pnum.t[:, :].rearrange("p (h e) -> p h e", h=4),
                                   u_bc, ALU.mult, [pnum.b] + ts_rd, [n1.b])
                                TT("dve", n2.t[:, :].rearrange("p (h e) -> p h e", h=4), pint.t[:, :].rearrange("p (h e) -> p h e", h=4),
                                   wi_bc, ALU.mult, [pint.b] + ts_rd, [n2.b])
                                TT("pool", n1.t[:, :], n1.t[:, :], n2.t[:, :], ALU.add, [n1.b, n2.b], [n1.b])
                                TT("dve", dd.t[:, 0:4], pden.t[:, 0:4], ts_ap[:, 4:8], ALU.mult, [pden.b] + ts_rd, [dd.b])
                                TT("dve", dd.t[:, 4:8], pden.t[:, 4:8], ts_ap[:, 8:12], ALU.mult, [pden.b] + ts_rd, [dd.b])
                                TT("dve", dd.t[:, 0:4], dd.t[:, 0:4], dd.t[:, 4:8], ALU.add, [dd.b], [dd.b])
                                ACT(dd.t[:, 0:4], dd.t[:, 0:4], AF.Abs, [dd.b], [dd.b])
                                TT("dve", dd.t[:, 0:4], dd.t[:, 0:4], ts_ap[:, 12:16], ALU.max, [dd.b] + ts_rd, [dd.b])
                                S.op("dve", (lambda dd: lambda e: e.reciprocal(out=dd.t[:, 0:4], in_=dd.t[:, 0:4]))(dd), [dd.b], [dd.b])
                                r_bc = dd.t[:, 0:4].unsqueeze(2).to_broadcast([128, 4, 128])
                                if d == 0:
                                    TT("pool", hsum.t[:, :].rearrange("p (h e) -> p h e", h=4), n1.t[:, :].rearrange("p (h e) -> p h e", h=4),
                                       r_bc, ALU.mult, [n1.b, dd.b], [hsum.b])
                                else:
                                    TT("pool", n1.t[:, :].rearrange("p (h e) -> p h e", h=4), n1.t[:, :].rearrange("p (h e) -> p h e", h=4),
                                       r_bc, ALU.mult, [n1.b, dd.b], [n1.b])
                                    TT("pool", hsum.t[:, :], hsum.t[:, :], n1.t[:, :], ALU.add, [hsum.b, n1.b], [hsum.b])
                            sq = f32r.get(); ss = st4.get()
                            ACT(sq.t[:, :], hsum.t[:, :], AF.Square, [hsum.b], [sq.b])
                            S.op("dve", (lambda sq, ss: lambda e: e.reduce_sum(out=ss.t[:, 0:4], in_=sq.t[:, :].rearrange("p (h e) -> p h e", h=4),
                                                                         axis=AX.X))(sq, ss), [sq.b], [ss.b])
                            rstd_from_sumsq(ss.t[:, 4:8], ss.t[:, 0:4], 128, [ss.b], [ss.b])
                            hn = f32r.get()
                            TT("dve", hn.t[:, :].rearrange("p (h e) -> p h e", h=4), hsum.t[:, :].rearrange("p (h e) -> p h e", h=4),
                               ss.t[:, 4:8].unsqueeze(2).to_broadcast([128, 4, 128]), ALU.mult, [hsum.b, ss.b], [hn.b])
                            yg = b16r.get()
                            TT("pool", yg.t[:, :], hn.t[:, :], g01.t[:, :], ALU.mult, [hn.b, g01.b], [yg.b])
                            p = pb.get()
                            for i in range(4):
                                TR(p.t[:, i * 128:(i + 1) * 128], yg.t[:, i * 128:(i + 1) * 128], identb.t[:], [yg.b, identb.b], [p.b])
                            CP("act", ybT.t[:, :, tok], p.t[:, 0:512].rearrange("p (i t) -> p i t", i=4), [p.b], [ybT.b])
                            mlstm_state_update(0, j, TSF.t[:, j, :], [tsf_b], pre=fwd_vw)
                            CP("act", mCb[0].t[:, :], mC[0].t[:, :], [mC[0].b], [mCb[0].b])
                    add_stage(l, P_MZ, st_mz)
                    add_branch_merge(l, 1, not en_att, c0)

                if en_ssd:
                    def st_sbc(sl, c0=c0):
                        for j in range(4):
                            DMA(sxtok.t[:, j, :], XTOK[c0 + j], [xtok_buf[c0 + j]], [sxtok.b])
                        ssd_conv(l, None, sl, c0, nch, [4, 5])
                        for which in range(2):
                            for g in range(2):
                                gs_ = slice(g * 64, (g + 1) * 64)
                                CP("pool", szero.t[gs_, which * 2 + g, :], sconv.t[gs_, 4 + which, :], [sconv.b], [szero.b])
                        for j in range(4):
                            ssd_tokmajor(j, do_x=False)
                    add_stage(l, P_SBC, st_sbc)

                    def st_sz(sl, c0=c0, m=m):
                        for j in range(4):
                            pz = pf.get()
                            proj_tok(sl, 0, 512, c0 + j, pz)
                            ACT(gate_tok[0].t[:, j, :], pz.t[:, :], AF.Silu, [pz.b], [gate_tok[0].b])
                        for j in range(4):
                            if "sz" in KSKIP:
                                break
                            c = c0 + j
                            tok = slice(j * CH, (j + 1) * CH)
                            if "cDma" not in KSKIP:
                                DMA(sSb[1].t[:, :], BSTS[c], [bsts_buf[c]], [sSb[1].b])
                            pcb = pf.get()
                            for g in range(2):
                                if "cPcb" in KSKIP:
                                    break
                                gs_ = slice(g * 64, (g + 1) * 64)
                                MM(pcb.t[:, g * 128:(g + 1) * 128], szero.t[:, g, tok], sconv.t[:, 5, tok], True, True, [sconv.b, szero.b], [pcb.b])
                            py = pf.get()
                            toff = []
                            for d in range(2):
                                ts_ap = TSF.t[:, j, :] if d == 0 else TSBm.t[:, j, :]
                                ts_rd = [tsf_b] if d == 0 else [TSBm.b]
                                acs_hi, acs_lo = acs_hl[d]
                                if d == 0:
                                    cbm = f32r.get()
                                    CP("act", cbm.t[:, 0:256], pcb.t[:, 0:256], [pcb.b], [cbm.b])
                                for h in range(8):
                                    if "cSel" in KSKIP:
                                        break
                                    pe_ = pacc[h // 4]
                                    hb_ = slice((h % 4) * 128, (h % 4 + 1) * 128)
                                    MM(pe_.t[:, hb_], selb.t[sD, h, :], acs_hi.t[sD, tok], h % 4 == 0, False, [selb.b, acs_hi.b], [pe_.b], skip=True)
                                    MM(pe_.t[:, hb_], selb.t[sD, h, :], acs_lo.t[sD, tok], False, False, [selb.b, acs_lo.b], [pe_.b], skip=True)
                                    MM(pe_.t[:, hb_], identb.t[:], negm.t[:, d, :], False, True, [identb.b, negm.b], [pe_.b], skip=True)
                                mts = []
                                if "cA" in KSKIP:
                                    continue
                                for g in range(2):
                                    lt = f32r.get()
                                    for e_ in range(4):
                                        h = g * 4 + e_
                                        ACT(lt.t[:, e_ * 128:(e_ + 1) * 128], pacc[g].t[:, e_ * 128:(e_ + 1) * 128], AF.Exp, [pacc[g].b] + ts_rd, [lt.b],
                                            bias=ts_ap[:, 20 + h:21 + h])
                                    mt = b16r.get()
                                    TT("dve" if g == 0 else "pool", mt.t[:, :].rearrange("p (h e) -> p h e", h=4), lt.t[:, :].rearrange("p (h e) -> p h e", h=4),
                                       cbm.t[:, g * 128:(g + 1) * 128].unsqueeze(1).to_broadcast([128, 4, 128]), ALU.mult, [lt.b, cbm.b], [mt.b])
                                    mts.append(mt)
                                if "cC" in KSKIP:
                                    continue
                                poff = pf.get()
                                for g in range(2):
                                    gs_ = slice(g * 64, (g + 1) * 64)
                                    MM(poff.t[:, g * 256:(g + 1) * 256], szero.t[:, 2 + g, tok], sSb[d].t[:, :], True, True, [szero.b, sSb[d].b], [poff.b])
                                for h in range(8):
                                    MM(py.t[:, h * 64:(h + 1) * 64], mts[h // 4].t[:, (h % 4) * 128:(h % 4 + 1) * 128], sxtok.t[:, j, h * 64:(h + 1) * 64],
                                       d == 0 and h == 0, d == 1, [mts[h // 4].b, sxtok.b], [py.b], skip=True)
                                to = f32r.get()
                                TT("dve", to.t[:, :].rearrange("p (h e) -> p h e", h=8), poff.t[:, :].rearrange("p (h e) -> p h e", h=8),
                                   ts_ap[:, 36:44].unsqueeze(2).to_broadcast([128, 8, 64]), ALU.mult, [poff.b] + ts_rd, [to.b])
                                toff.append(to)
                            if "cA" in KSKIP or "cC" in KSKIP or "cD" in KSKIP:
                                continue
                            y = f32r.get()
                            TT("dve", y.t[:, :], py.t[:, :], toff[0].t[:, :], ALU.add, [py.b, toff[0].b], [y.b])
                            TT("pool", y.t[:, :], y.t[:, :], toff[1].t[:, :], ALU.add, [y.b, toff[1].b], [y.b])
                            xs = toff[0]
                            TT("pool", xs.t[:, :].rearrange("p (h e) -> p h e", h=8), sxtok.t[:, j, :].rearrange("p (h e) -> p h e", h=8),
                               dsk_bc.t[:, 0:8].unsqueeze(2).to_broadcast([128, 8, 64]), ALU.mult, [sxtok.b, dsk_bc.b], [xs.b])
                            TT("pool", y.t[:, :], y.t[:, :], xs.t[:, :], ALU.add, [y.b, xs.b], [y.b])
                            TT("pool", y.t[:, :], y.t[:, :], gate_tok[0].t[:, j, :], ALU.mult, [y.b, gate_tok[0].b], [y.b])
                            sq = toff[1]; ss = st4.get()
                            ACT(sq.t[:, :], y.t[:, :], AF.Square, [y.b], [sq.b, ss.b], accum=ss.t[:, 0:1])
                            rstd_from_sumsq(ss.t[:, 1:2], ss.t[:, 0:1], 512, [ss.b], [ss.b])
                            yg = b16r.get()
                            STT("dve", yg.t[:, :], y.t[:, :], ss.t[:, 1:2], sng_bc.t[:, :], ALU.mult, ALU.mult, [y.b, ss.b, sng_bc.b], [yg.b])
                            p = pb.get()
                            for i in range(4):
                                TR(p.t[:, i * 128:(i + 1) * 128], yg.t[:, i * 128:(i + 1) * 128], identb.t[:], [yg.b, identb.b], [p.b])
                            CP("act", ybT.t[:, :, tok], p.t[:, 0:512].rearrange("p (i t) -> p i t", i=4), [p.b], [ybT.b])
                            if "cE" in KSKIP:
                                continue
                            ssd_state_update(0, j, TSF.t[:, j, :], [tsf_b])
                            CP("act", sSb[0].t[:, :], sS[0].t[:, :], [sS[0].b], [sSb[0].b])
                    add_stage(l, P_SZ, st_sz)
                    add_branch_merge(l, 2, not (en_att or en_ml), c0)

                if not (en_att or en_ml or en_ssd):
                    def st_zero(_sl):
                        S.op("dve", lambda e: e.memset(mergedT.t[:], 0.0), [], [mergedT.b])
                    add_stage(l, None, st_zero)
                add_final(l, tok0, c0, nch)

        n_mt_layer = sum(sl_ // TM for sl_ in seq_lens)
        casts_per_mt = 0
        if depth > 1:
            casts_per_mt = max(len(v_) for v_ in pending_casts.values()) // max(1, n_mt_layer - 1) + 1

        def flush_casts(l):
            def f(_sl):
                while pending_casts[l]:
                    pending_casts[l].pop(0)()
            return f
        for l in range(depth):
            if l >= 1:
                add_stage(l, None, flush_casts(l))
            add_stage(l, None, (lambda l: lambda _sl: load_layer_params(l))(l))
            pos = 0
            for slen in seq_lens:
                seq_layer(l, pos, slen)
                pos += slen
        run_stages()
        import os as _os
        S.schedule(window=int(_os.environ.get('KWIN', '24')), enable=_os.environ.get('KSCHED', '1') == '1')
        S._plan()
        print('sched ok:', S.check(), {e: len(S.order[e]) for e in S.ENGS}, 'est_us', getattr(S, 'est_time', 0) / 1e3, 'busy_us', {e: round(v / 1e3) for e, v in getattr(S, 'busy', {}).items()})
        if S.tagging:
            for kk_, v_ in sorted(S.gaps.items(), key=lambda x: -x[1])[:40]:
                print('GAP', kk_, round(v_ / 1e3, 1))
        S.emit()
    return nc


_CACHE = {}


def _run(seq_lens, depth, per_core_x, weights, enable=("att", "mlstm", "ssd"), debug=(), full=False):
    key = (tuple(seq_lens), depth, tuple(enable), tuple(debug))
    if key not in _CACHE:
        _CACHE[key] = build(list(seq_lens), depth, enable, debug)
    nc = _CACHE[key]
    consts = make_consts(max(seq_lens))
    in_maps = []
    for xc in per_core_x:
        m = {"x": np.ascontiguousarray(xc, dtype=np.float32)}
        for k, v in weights.items():
            m[k] = np.ascontiguousarray(v, dtype=np.float32)
        m.update(consts)
        in_maps.append(m)
    res = run_bass_kernel_spmd(nc, in_maps, core_ids=list(range(len(per_core_x))))
    if full:
        return res.results
    return [r["y"] for r in res.results]


def kernel(x_prompt, x_sample, norm_g, w_in, q_norm_g, k_norm_g, attn_sink, w_att_out,
           mlstm_i_b, mlstm_f_b, mlstm_norm_g, w_mlstm_out, conv_w, conv_b, a_log,
           dt_bias, d_skip, ssm_norm_g, w_ssm_out, w_out):
    x_prompt = np.asarray(x_prompt, dtype=np.float32)
    x_sample = np.asarray(x_sample, dtype=np.float32)
    weights = dict(norm_g=norm_g, w_in=w_in, q_norm_g=q_norm_g, k_norm_g=k_norm_g, attn_sink=attn_sink,
                   w_att_out=w_att_out, mlstm_i_b=mlstm_i_b, mlstm_f_b=mlstm_f_b, mlstm_norm_g=mlstm_norm_g,
                   w_mlstm_out=w_mlstm_out, conv_w=conv_w, conv_b=conv_b, a_log=a_log, dt_bias=dt_bias,
                   d_skip=d_skip, ssm_norm_g=ssm_norm_g, w_ssm_out=w_ssm_out, w_out=w_out)
    weights = {k: np.asarray(v, dtype=np.float32) for k, v in weights.items()}
    depth = weights["w_in"].shape[0]
    nb, sp = x_prompt.shape[0], x_prompt.shape[1]
    ns, ss = x_sample.shape[0], x_sample.shape[1]
    ppc, spc = nb // NCORES, ns // NCORES
    seq_lens = [sp] * ppc + [ss] * spc
    per_core = []
    for c in range(NCORES):
        parts = [x_prompt[c * ppc + i] for i in range(ppc)] + [x_sample[c * spc + i] for i in range(spc)]
        per_core.append(np.concatenate(parts, axis=0))
    outs = _run(seq_lens, depth, per_core, weights)
    y_prompt = np.empty_like(x_prompt)
    y_sample = np.empty_like(x_sample)
    for c in range(NCORES):
        o = outs[c]
        pos = 0
        for i in range(ppc):
            y_prompt[c * ppc + i] = o[pos:pos + sp]; pos += sp
        for i in range(spc):
            y_sample[c * spc + i] = o[pos:pos + ss]; pos += ss
    return (y_prompt, y_sample)
```

```python
import contextlib
import math
import numpy as np
import concourse.bass as bass
import concourse.mybir as mybir
from concourse.bass_utils import run_bass_kernel_spmd

F32 = mybir.dt.float32
BF16 = mybir.dt.bfloat16
AF = mybir.ActivationFunctionType
ALU = mybir.AluOpType
AX = mybir.AxisListType

D = 1024
NCORES = 8
TM = 512
CH = 128
NPIECE = 25
ROPE_THETA = 500000.0
EPS = 1e-6


class Buf:
    __slots__ = ("name", "last_w", "readers")

    def __init__(self, name=""):
        self.name = name
        self.last_w = None
        self.readers = []


class Sched:
    ENGS = ("pe", "act", "dve", "pool", "sp")
    NPOOL = 4

    def __init__(self, nc, n_dma_sems=24):
        self.nc = nc
        self.nodes = []
        self.n_dma_sems = n_dma_sems
        self.order = None
        import os as _os2
        self.tagging = bool(_os2.environ.get('KTAG'))
        self.gaps = {}

    def _add(self, eng, fn, reads, writes, dma, cost):
        deps = set()
        for b in reads:
            if b.last_w is not None:
                deps.add(b.last_w)
        for b in writes:
            if b.last_w is not None:
                deps.add(b.last_w)
            deps.update(b.readers)
        gid = len(self.nodes)
        tag = ""
        if self.tagging:
            import sys as _sys
            f = _sys._getframe(2)
            names = []
            while f is not None and len(names) < 3:
                nm = f.f_code.co_name
                if nm not in ("op", "dma", "ACT", "TT", "TSC", "STT", "CP", "MM", "TR", "DMA", "SCAN", "RECIP", "<lambda>", "proj_feat", "proj_tok"):
                    names.append(nm)
                f = f.f_back
            tag = "/".join(names[:2])
        self.nodes.append(dict(eng=eng, fn=fn, dma=dma, deps=deps, cost=cost, tag=tag))
        for b in reads:
            b.readers.append(gid)
        for b in writes:
            b.last_w = gid
            b.readers = []
        return gid

    def op(self, eng, fn, reads=(), writes=(), cost=300.0):
        return self._add(eng, fn, reads, writes, False, cost)

    def dma(self, eng, fn, reads=(), writes=(), cost=3000.0):
        return self._add(eng, fn, reads, writes, True, cost)

    def schedule(self, window=24, enable=True):
        nodes = self.nodes
        per = {e: [] for e in self.ENGS}
        for g, n in enumerate(nodes):
            per[n["eng"]].append(g)
        if not enable:
            self.order = per
            return
        ptr = {e: 0 for e in self.ENGS}
        done = {e: [False] * len(per[e]) for e in self.ENGS}
        finish = [None] * len(nodes)
        t_eng = {e: 0.0 for e in self.ENGS}
        new = {e: [] for e in self.ENGS}
        remaining = len(nodes)
        LAT = 250.0
        W = {e: window for e in self.ENGS}
        W["sp"] = 6
        while remaining:
            best = None
            for e in self.ENGS:
                lst = per[e]
                p = ptr[e]
                while p < len(lst) and done[e][p]:
                    p += 1
                ptr[e] = p
                if p >= len(lst):
                    continue
                cnt = 0
                q = p
                cand = None
                while q < len(lst) and cnt < W[e]:
                    if not done[e][q]:
                        cnt += 1
                        g = lst[q]
                        nd = nodes[g]
                        ready = t_eng[e]
                        ok = True
                        for d in nd["deps"]:
                            f = finish[d]
                            if f is None:
                                ok = False
                                break
                            if nodes[d]["eng"] != e or nodes[d]["dma"]:
                                f += LAT
                            if f > ready:
                                ready = f
                        if ok:
                            key = (ready, q)
                            if cand is None or key < cand[0]:
                                cand = (key, q, g, ready)
                            if ready <= t_eng[e]:
                                break
                    q += 1
                if cand is not None:
                    if best is None or (cand[3], cand[2]) < (best[3], best[2]):
                        best = (e, cand[1], cand[2], cand[3])
            assert best is not None, "scheduler stuck"
            e, q, g, start = best
            nd = nodes[g]
            if self.tagging and e == "pe" and start > t_eng[e] + 500.0:
                dmax = max(nd["deps"], key=lambda d: finish[d])
                key = (nd["tag"], nodes[dmax]["eng"], nodes[dmax]["tag"])
                self.gaps[key] = self.gaps.get(key, 0.0) + (start - t_eng[e])
            if nd["dma"]:
                finish[g] = start + nd["cost"]
                t_eng[e] = start + 60.0
            else:
                finish[g] = start + nd["cost"]
                t_eng[e] = finish[g]
            done[e][q] = True
            new[e].append(g)
            remaining -= 1
        self.order = new
        self.est_time = max(f for f in finish if f is not None)
        self.busy = {e: sum(nodes[g]['cost'] for g in new[e] if not nodes[g]['dma']) for e in self.ENGS}

    def _plan(self):
        nodes = self.nodes
        pos = {}
        for e in self.ENGS:
            for i, g in enumerate(self.order[e]):
                pos[g] = i
        npool = self.NPOOL
        nsp = self.n_dma_sems - npool
        rr = {"sp": 0, "pool": 0}
        dma_val = [0] * self.n_dma_sems
        sem_of = {}
        for e in self.ENGS:
            for g in self.order[e]:
                if nodes[g]["dma"]:
                    if e == "pool":
                        s = nsp + rr["pool"]; rr["pool"] = (rr["pool"] + 1) % npool
                    else:
                        s = rr["sp"]; rr["sp"] = (rr["sp"] + 1) % nsp
                    prev = dma_val[s]
                    dma_val[s] += 16
                    sem_of[g] = (s, dma_val[s], prev)
        self.dma_val = dma_val
        flag = [False] * len(nodes)
        plan = {e: [] for e in self.ENGS}
        for e in self.ENGS:
            waited_c = {}
            waited_d = {}
            for g in self.order[e]:
                nd = nodes[g]
                waits = []
                deps = list(nd["deps"])
                for d in deps:
                    dn = nodes[d]
                    if dn["dma"]:
                        s, v, _ = sem_of[d]
                        if waited_d.get(s, 0) < v:
                            waited_d[s] = v
                            waits.append(("d", s, v))
                    else:
                        pe_ = dn["eng"]
                        if pe_ == e and e in ("pe", "sp"):
                            continue
                        if waited_c.get(pe_, -1) < pos[d]:
                            waited_c[pe_] = pos[d]
                            flag[d] = True
                            waits.append(("c", pe_, d))
                if nd["dma"]:
                    s, v, prev = sem_of[g]
                    if prev > 0 and waited_d.get(s, 0) < prev:
                        waited_d[s] = prev
                        waits.append(("d", s, prev))
                plan[e].append((g, waits))
        counts = {}
        for e in self.ENGS:
            c = 0
            for g in self.order[e]:
                if (not nodes[g]["dma"]) and flag[g]:
                    c += 1
                counts[g] = c
        self.plan, self.flag, self.counts, self.sem_of = plan, flag, counts, sem_of

    def check(self):
        nodes = self.nodes
        ptr = {e: 0 for e in self.ENGS}
        csem = {e: 0 for e in self.ENGS}
        dsem = [0] * self.n_dma_sems
        while True:
            prog = False
            for e in self.ENGS:
                pl = self.plan[e]
                while ptr[e] < len(pl):
                    g, waits = pl[ptr[e]]
                    ok = True
                    for w in waits:
                        if w[0] == "c":
                            if csem[w[1]] < self.counts[w[2]]:
                                ok = False
                        elif dsem[w[1]] < w[2]:
                            ok = False
                    if not ok:
                        break
                    if nodes[g]["dma"]:
                        dsem[self.sem_of[g][0]] += 16
                    elif self.flag[g]:
                        csem[e] += 1
                    ptr[e] += 1
                    prog = True
            if all(ptr[e] == len(self.plan[e]) for e in self.ENGS):
                return True
            if not prog:
                for e in self.ENGS:
                    if ptr[e] < len(self.plan[e]):
                        print("STUCK", e, ptr[e], "/", len(self.plan[e]), self.plan[e][ptr[e]][1])
                return False

    def emit(self, final_wait_eng="sp"):
        nc = self.nc
        nodes = self.nodes
        with contextlib.ExitStack() as st:
            csem = {e: st.enter_context(nc.semaphore("cs_" + e)) for e in self.ENGS}
            dsem = [st.enter_context(nc.semaphore("ds_%d" % i)) for i in range(self.n_dma_sems)]
            final = [(i, v) for i, v in enumerate(self.dma_val) if v > 0]
            block = st.enter_context(nc.Block())
            engobj = {"pe": "tensor", "act": "scalar", "dve": "vector", "pool": "gpsimd", "sp": "sync"}

            def make(e):
                def body(eng):
                    for g, waits in self.plan[e]:
                        for w in waits:
                            if w[0] == "c":
                                eng.wait_ge(csem[w[1]], self.counts[w[2]])
                            else:
                                eng.wait_ge(dsem[w[1]], w[2])
                        ins = nodes[g]["fn"](eng)
                        if nodes[g]["dma"]:
                            ins.then_inc(dsem[self.sem_of[g][0]], 16)
                        elif self.flag[g]:
                            ins.then_inc(csem[e], 1)
                    if e == final_wait_eng:
                        for (i, v) in final:
                            eng.wait_ge(dsem[i], v)
                return body

            for e in self.ENGS:
                if self.plan[e] or e == final_wait_eng:
                    getattr(block, engobj[e])(make(e))


class Tl:
    __slots__ = ("t", "b")

    def __init__(self, t, name=""):
        self.t = t
        self.b = Buf(name)


def make_consts(smax):
    c = {}
    c["c_ident"] = np.eye(128, dtype=np.float32)
    s = np.arange(128)[:, None]
    t = np.arange(128)[None, :]
    tri = np.zeros((128, 2, 512), np.float32)
    tri[:, 0, :] = np.tile((s <= t).astype(np.float32), (1, 4))
    tri[:, 1, :] = np.tile((s >= t).astype(np.float32), (1, 4))
    c["c_tri"] = tri
    f = np.arange(128)
    d = f % 64
    pos = np.arange(smax, dtype=np.float32)
    rope = np.zeros((128, 2, smax), np.float32)
    rope[:, 0, :] = 1.0
    inv_freq = (ROPE_THETA ** (-np.arange(8, dtype=np.float32) * 2.0 / 16.0)).astype(np.float32)
    for ff in range(128):
        if d[ff] < 16:
            ang = pos * inv_freq[d[ff] % 8]
            rope[ff, 0, :] = np.cos(ang)
            rope[ff, 1, :] = np.sin(ang)
    c["c_rope"] = rope
    rot = np.zeros((128, 128), np.float32)
    for ff in range(128):
        if d[ff] < 8:
            rot[ff + 8, ff] = -1.0
        elif d[ff] < 16:
            rot[ff - 8, ff] = 1.0
    c["c_rot"] = rot
    blk = np.zeros((128, 128), np.float32)
    blk[:64, :64] = 1.0 / 64
    blk[64:, 64:] = 1.0 / 64
    c["c_blk"] = blk
    sel = np.zeros((8, 8, 128), np.float32)
    for h in range(8):
        sel[h, h, :] = 1.0
    c["c_sel"] = sel
    rst = np.ones((8, 512), np.float32)
    rst[:, ::128] = 0.0
    c["c_reset"] = rst
    return c


O_AQ, O_AK, O_AV, O_AZ = 0, 512, 640, 768
O_MQ, O_MK, O_MV, O_MO = 1280, 1792, 2304, 2816
O_MI, O_MF, O_MZ = 3328, 3336, 3344
O_SX, O_SB, O_SC, O_SDT, O_SZ = 3856, 4368, 4496, 4624, 4640
O_G = 5152
P_AQ, P_KV, P_AZ, P_MQ, P_MK, P_MV, P_MO, P_MZ, P_SX, P_SBC, P_SZ = range(11)
P_G0 = 11
P_WO0, P_WO1 = 23, 24


def build(seq_lens, depth, enable=("att", "mlstm", "ssd"), debug=()):
    ntok = sum(seq_lens)
    smax = max(seq_lens)
    nc = bass.Bass("TRN2", target_bir_lowering=False)
    S = Sched(nc)
    es = contextlib.ExitStack()

    def din(name, shape, dt=F32):
        return nc.dram_tensor(name, list(shape), dt, kind="ExternalInput").ap()

    x_in = din("x", [ntok, D])
    y_out = nc.dram_tensor("y", [ntok, D], F32, kind="ExternalOutput").ap()
    W = dict(
        norm_g=din("norm_g", [depth, D]), w_in=din("w_in", [depth, D, 8224]),
        q_norm_g=din("q_norm_g", [depth, 64]), k_norm_g=din("k_norm_g", [depth, 64]),
        attn_sink=din("attn_sink", [depth, 8]), w_att_out=din("w_att_out", [depth, 512, D]),
        mlstm_i_b=din("mlstm_i_b", [depth, 2, 4]), mlstm_f_b=din("mlstm_f_b", [depth, 2, 4]),
        mlstm_norm_g=din("mlstm_norm_g", [depth, 512]), w_mlstm_out=din("w_mlstm_out", [depth, 512, D]),
        conv_w=din("conv_w", [depth, 5, 768]), conv_b=din("conv_b", [depth, 768]),
        a_log=din("a_log", [depth, 2, 8]), dt_bias=din("dt_bias", [depth, 2, 8]),
        d_skip=din("d_skip", [depth, 8]), ssm_norm_g=din("ssm_norm_g", [depth, 512]),
        w_ssm_out=din("w_ssm_out", [depth, 512, D]), w_out=din("w_out", [depth, D, D]),
    )
    C = dict(c_ident=din("c_ident", [128, 128]), c_tri=din("c_tri", [128, 2, 512]),
             c_rope=din("c_rope", [128, 2, smax]), c_rot=din("c_rot", [128, 128]),
             c_blk=din("c_blk", [128, 128]), c_sel=din("c_sel", [8, 8, 128]),
             c_reset=din("c_reset", [8, 512]))
    WS = nc.dram_tensor("ws_bf16", [depth, NPIECE, 128, 4096], BF16, kind="Internal").ap()
    ws_buf = [[[] for p in range(NPIECE)] for l in range(depth)]
    nscr = max(1, min(2, depth - 1))
    YS = [nc.dram_tensor("yscr%d" % i, [ntok, D], F32, kind="Internal").ap() for i in range(nscr)]
    ys_buf = [[[Buf("yscr"), Buf("yscr")] for c in range(ntok // CH)] for i in range(nscr)]
    nchmax = smax // CH
    BSTM = nc.dram_tensor("bst_m", [nchmax, 128, 516], BF16, kind="Internal").ap()
    BSTS = nc.dram_tensor("bst_s", [nchmax, 128, 256], BF16, kind="Internal").ap()
    KTOK = nc.dram_tensor("ktok", [nchmax, 128, 512], BF16, kind="Internal").ap()
    VTOK = nc.dram_tensor("vtok", [nchmax, 128, 512], BF16, kind="Internal").ap()
    XTOK = nc.dram_tensor("xtok", [nchmax, 128, 512], BF16, kind="Internal").ap()
    ktok_buf = [Buf("ktok%d" % i) for i in range(nchmax)]
    vtok_buf = [Buf("vtok%d" % i) for i in range(nchmax)]
    xtok_buf = [Buf("xtok%d" % i) for i in range(nchmax)]
    bstm_buf = [Buf("bstm%d" % i) for i in range(nchmax)]
    bsts_buf = [Buf("bsts%d" % i) for i in range(nchmax)]

    dbg_done = {}
    import os
    KSKIP = set(os.environ.get("KSKIP", "").split(","))

    def dbg(name, ap, bufs, dt=F32):
        if name not in debug or name in dbg_done:
            return
        dbg_done[name] = True
        shp = list(ap.shape)
        o = nc.dram_tensor("dbg_" + name, shp, dt, kind="ExternalOutput").ap()
        S.dma("sp", lambda e: e.dma_start(out=o, in_=ap), bufs, [])

    def sb(name, shape, dt=F32):
        return Tl(es.enter_context(nc.sbuf_tensor(name, list(shape), dt)), name)

    def psum(name, shape, dt=F32):
        return Tl(es.enter_context(nc.psum_tensor(name, list(shape), dt)), name)

    def fsz(ap):
        n = 1
        for s_ in ap.shape[1:]:
            n *= s_
        return n

    def ecost(eng, ap, mult=1.0):
        n = fsz(ap)
        if eng == "act":
            return 220.0 + 0.85 * n
        if eng == "dve":
            return 60.0 + 1.3 * n * mult
        return 100.0 + 2.6 * n * mult

    def ACT(out, in_, func, rd, wr, bias=None, scale=None, accum=None):
        kw = {}
        if bias is not None:
            kw["bias"] = bias
        if scale is not None:
            kw["scale"] = scale
        if accum is not None:
            kw["accum_out"] = accum
        S.op("act", lambda e: e.activation(out=out, in_=in_, func=func, **kw), rd, wr, cost=ecost("act", out))

    def TT(eng, out, in0, in1, op, rd, wr):
        S.op(eng, lambda e: e.tensor_tensor(out=out, in0=in0, in1=in1, op=op), rd, wr, cost=ecost(eng, out))

    def TSC(eng, out, in0, s1, op0, rd, wr, s2=None, op1=None):
        if op1 is None:
            S.op(eng, lambda e: e.tensor_scalar(out=out, in0=in0, scalar1=s1, scalar2=None, op0=op0), rd, wr, cost=ecost(eng, out))
        else:
            S.op(eng, lambda e: e.tensor_scalar(out=out, in0=in0, scalar1=s1, scalar2=s2, op0=op0, op1=op1), rd, wr, cost=ecost(eng, out))

    def STT(eng, out, in0, scalar, in1, op0, op1, rd, wr):
        S.op(eng, lambda e: e.scalar_tensor_tensor(out=out, in0=in0, scalar=scalar, in1=in1, op0=op0, op1=op1), rd, wr,
             cost=ecost(eng, out))

    def CP(eng, out, in_, rd, wr):
        if eng == "act":
            S.op("act", lambda e: e.copy(out=out, in_=in_), rd, wr, cost=ecost("act", out))
        else:
            S.op(eng, lambda e: e.tensor_copy(out=out, in_=in_), rd, wr, cost=ecost(eng, out, 1.4 if eng == "pool" else 1.0))

    def RECIP(out, in_, rd, wr):
        S.op("dve", lambda e: e.reciprocal(out=out, in_=in_), rd, wr, cost=100.0 + 6.6 * fsz(out))

    def MM(out, lhsT, rhs, start, stop, rd, wr, skip=False):
        n = max(fsz(out), 32)
        passes = 4 if lhsT.dtype == F32 else 1
        c = 25.0 + 0.5 * n * passes
        if skip:
            S.op("pe", lambda e: e.matmul(out, lhsT, rhs, start=start, stop=stop, skip_group_check=True), rd, wr, cost=c)
        else:
            S.op("pe", lambda e: e.matmul(out, lhsT, rhs, start=start, stop=stop), rd, wr, cost=c)

    def TR(out, in_, ident, rd, wr):
        S.op("pe", lambda e: e.transpose(out, in_, ident), rd, wr, cost=90.0)

    def DMA(out, in_, rd, wr, eng="sp", slow=False):
        nbytes = out.shape[0] * fsz(out) * (2 if out.dtype == BF16 else 4)
        c = 2500.0 + nbytes / (40.0 if eng == "pool" else 120.0)
        if slow:
            S.dma(eng, lambda e: e.dma_start(out=out, in_=in_, allow_slow_non_contiguous=True), rd, wr, cost=c)
        else:
            S.dma(eng, lambda e: e.dma_start(out=out, in_=in_), rd, wr, cost=c)

    def SCAN(out, d0, d1, init, op0, op1, rd, wr):
        S.op("dve", lambda e: e.tensor_tensor_scan(out=out, data0=d0, data1=d1, initial=init, op0=op0, op1=op1), rd, wr,
             cost=100.0 + 2.0 * fsz(out))

    class Ring:
        def __init__(self, tiles):
            self.tiles = tiles
            self.i = 0

        def get(self):
            t = self.tiles[self.i]
            self.i = (self.i + 1) % len(self.tiles)
            return t

    with es:
        f32r = Ring([sb("f32r%d" % i, [128, 512]) for i in range(7)])
        b16r = Ring([sb("b16r%d" % i, [128, 512], BF16) for i in range(8)])
        identf = sb("identf", [128, 128])
        identb = sb("identb", [128, 128], BF16)
        trib = sb("trib", [128, 2, 512], BF16)
        rotb = sb("rotb", [128, 128], BF16)
        blkb = sb("blkb", [128, 128], BF16)
        self_ = sb("sel", [40, 8, 128])
        resetm = sb("resetm", [40, 512])
        zrow = sb("zrow", [8, 1])
        DMA(identf.t[:], C["c_ident"][:, :], [], [identf.b])
        CP("dve", identb.t[:], identf.t[:], [identf.b], [identb.b])
        for half in range(2):
            stg = f32r.get()
            DMA(stg.t[:, :], C["c_tri"][:, half, :], [], [stg.b])
            CP("dve", trib.t[:, half, :], stg.t[:, :], [stg.b], [trib.b])
        stg = f32r.get()
        DMA(stg.t[:, 0:128], C["c_rot"][:, :], [], [stg.b])
        CP("dve", rotb.t[:], stg.t[:, 0:128], [stg.b], [rotb.b])
        stg = f32r.get()
        DMA(stg.t[:, 0:128], C["c_blk"][:, :], [], [stg.b])
        CP("dve", blkb.t[:], stg.t[:, 0:128], [stg.b], [blkb.b])
        DMA(self_.t[32:40, :, :], C["c_sel"][:, :, :], [], [self_.b])
        DMA(resetm.t[32:40, :], C["c_reset"][:, :], [], [resetm.b])
        S.op("dve", lambda e: e.memset(zrow.t[:], 0.0), [], [zrow.b])
        epsc = sb("epsc", [128, 1])
        S.op("dve", lambda e: e.memset(epsc.t[:], EPS), [], [epsc.b])
        negm = sb("negm", [128, 2, 128], BF16)
        TSC("dve", negm.t[:, :, :], trib.t[:, :, 0:128], -1.0, ALU.add, [trib.b], [negm.b], s2=30000.0, op1=ALU.mult)

        pending_casts = {l: [] for l in range(depth)}
        _DMA_real = DMA

        def DMA(out, in_, rd, wr, eng="sp", slow=False, _defer=[None]):
            if _defer[0] is not None and eng == "pool":
                pending_casts[_defer[0]].append(lambda: _DMA_real(out, in_, rd, wr, eng=eng, slow=slow))
            else:
                _DMA_real(out, in_, rd, wr, eng=eng, slow=slow)
        _defer_box = DMA.__defaults__[2]
        for l in range(depth):
            _defer_box[0] = l if l >= 1 else None
            wi = W["w_in"][l].rearrange("(k p) c -> p k c", p=128)

            def wdst(p, off, n, cw=512, l=l):
                return WS[l, p].rearrange("p (k c) -> p k c", c=cw)[:, :, off:off + n]

            def cast(p, off, c0, n, l=l, wi=wi):
                b_ = Buf("ws"); ws_buf[l][p].append(b_)
                DMA(wdst(p, off, n), wi[:, :, c0:c0 + n], [], [b_], eng="pool")

            for i in range(4):
                cast(P_AQ, i * 128, O_AQ + i * 64, 64)
                cast(P_AQ, i * 128 + 64, O_AQ + (4 + i) * 64, 64)
            cast(P_KV, 0, O_AK, 256)
            for d_ in range(2):
                base_ = 256 + d_ * 72
                b0_ = Buf("ws"); ws_buf[l][P_KV].append(b0_)
                DMA(wdst(P_KV, base_, 72), wi[:, :, O_MI:O_MI + 72], [], [b0_], eng="pool")
                for (off_, c0_, n_) in ((0, O_MI + d_ * 4, 4), (32, O_MF + d_ * 4, 4), (64, O_SDT + d_ * 8, 8)):
                    b_ = Buf("ws"); ws_buf[l][P_KV].append(b_)
                    DMA(wdst(P_KV, base_ + off_, n_), wi[:, :, c0_:c0_ + n_], [b0_], [b_], eng="pool")
            cast(P_AZ, 0, O_AZ, 512)
            cast(P_MQ, 0, O_MQ, 512)
            cast(P_MK, 0, O_MK, 512)
            cast(P_MV, 0, O_MV, 512)
            cast(P_MO, 0, O_MO, 512)
            cast(P_MZ, 0, O_MZ, 512)
            cast(P_SX, 0, O_SX, 512)
            cast(P_SBC, 0, O_SB, 256)
            cast(P_SZ, 0, O_SZ, 512)
            for bi_, nm in enumerate(("w_att_out", "w_mlstm_out", "w_ssm_out")):
                src = W[nm][l].rearrange("(k p) c -> p k c", p=128)
                for q_ in range(4):
                    p = P_G0 + bi_ * 4 + q_
                    b_ = Buf("ws"); ws_buf[l][p].append(b_)
                    DMA(WS[l, p][:, 0:2048].rearrange("p (k c) -> p k c", c=256),
                        wi[:, :, O_G + bi_ * 1024 + q_ * 256:O_G + bi_ * 1024 + (q_ + 1) * 256], [], [b_], eng="pool")
                    b_ = Buf("ws"); ws_buf[l][p].append(b_)
                    DMA(WS[l, p][:, 2048:3072].rearrange("p (k c) -> p k c", c=256), src[:, :, q_ * 256:(q_ + 1) * 256], [], [b_], eng="pool")
            wo = W["w_out"][l].rearrange("(k p) c -> p k c", p=128)
            for p_, lo_ in ((P_WO0, 0), (P_WO1, 512)):
                b_ = Buf("ws"); ws_buf[l][p_].append(b_)
                DMA(wdst(p_, 0, 512), wo[:, :, lo_:lo_ + 512], [], [b_], eng="pool")

        _defer_box[0] = None
        NSLOT = 3
        wslots = [sb("wslot%d" % i, [128, 4096], BF16) for i in range(NSLOT)]

        ng_bc = sb("ng_bc", [128, D])
        gq = sb("gq", [128, 1]); gk = sb("gk", [128, 1])
        esink = sb("esink", [128, 8])
        ib = [sb("ib%d" % d, [4, 1]) for d in range(2)]
        nfb = [sb("nfb%d" % d, [4, 1]) for d in range(2)]
        mng_bc = sb("mng_bc", [128, 512]); sng_bc = sb("sng_bc", [128, 512])
        cw = sb("cw", [128, 6, 5]); cb = sb("cb", [128, 6])
        acoef = [sb("acoef%d" % d, [40, 1]) for d in range(2)]
        dtb = [sb("dtb%d" % d, [40, 1]) for d in range(2)]
        dsk_bc = sb("dsk_bc", [128, 8])

        def load_layer_params(l):
            DMA(ng_bc.t[:], W["norm_g"][l].partition_broadcast(128), [], [ng_bc.b])
            for half in range(2):
                DMA(gq.t[half * 64:(half + 1) * 64, :], W["q_norm_g"][l].rearrange("(d o) -> d o", o=1), [], [gq.b])
                DMA(gk.t[half * 64:(half + 1) * 64, :], W["k_norm_g"][l].rearrange("(d o) -> d o", o=1), [], [gk.b])
            S.op("act", lambda e: e.mul(out=gq.t[:], in_=gq.t[:], mul=0.125), [gq.b], [gq.b])
            DMA(esink.t[:], W["attn_sink"][l].partition_broadcast(128), [], [esink.b])
            ACT(esink.t[:], esink.t[:], AF.Exp, [esink.b], [esink.b])
            for d in range(2):
                DMA(ib[d].t[:], W["mlstm_i_b"][l, d].rearrange("(d o) -> d o", o=1), [], [ib[d].b])
                DMA(nfb[d].t[:], W["mlstm_f_b"][l, d].rearrange("(d o) -> d o", o=1), [], [nfb[d].b])
                S.op("act", (lambda d: lambda e: e.mul(out=nfb[d].t[:], in_=nfb[d].t[:], mul=-1.0))(d), [nfb[d].b], [nfb[d].b])
                DMA(acoef[d].t[32:40, :], W["a_log"][l, d].rearrange("(d o) -> d o", o=1), [], [acoef[d].b])
                ACT(acoef[d].t[32:40, :], acoef[d].t[32:40, :], AF.Exp, [acoef[d].b], [acoef[d].b])
                S.op("act", (lambda d: lambda e: e.mul(out=acoef[d].t[32:40, :], in_=acoef[d].t[32:40, :], mul=-1.0))(d), [acoef[d].b], [acoef[d].b])
                DMA(dtb[d].t[32:40, :], W["dt_bias"][l, d].rearrange("(d o) -> d o", o=1), [], [dtb[d].b])
            DMA(mng_bc.t[:], W["mlstm_norm_g"][l].partition_broadcast(128), [], [mng_bc.b])
            DMA(sng_bc.t[:], W["ssm_norm_g"][l].partition_broadcast(128), [], [sng_bc.b])
            for ti in range(6):
                DMA(cw.t[:, ti, :], W["conv_w"][l][:, ti * 128:(ti + 1) * 128].rearrange("k p -> p k"), [], [cw.b], slow=True)
                DMA(cb.t[:, ti:ti + 1], W["conv_b"][l][ti * 128:(ti + 1) * 128].rearrange("(p o) -> p o", o=1), [], [cb.b])
            DMA(dsk_bc.t[:], W["d_skip"][l].partition_broadcast(128), [], [dsk_bc.b])

        hT = sb("hT", [128, 8, 8 * CH], BF16)
        hT_b = [Buf("hT%d" % i) for i in range(8)]
        hslot_chunk = [None] * 8
        xring = Ring([sb("xt%d" % i, [128, D]) for i in range(2)])
        hbring = Ring([sb("hb%d" % i, [128, D], BF16) for i in range(2)])
        st4 = Ring([sb("st4_%d" % i, [128, 8]) for i in range(6)])

        pf = Ring([psum("pf%d" % i, [128, 512]) for i in range(5)])
        pacc = [psum("pacc%d" % i, [128, 512]) for i in range(2)]
        pb = Ring([psum("pb%d" % i, [128, 1024], BF16) for i in range(1)])

        mergedT = sb("mergedT", [128, 8, TM])
        GT = sb("GT", [128, 4096], BF16)

        class View:
            def __init__(self, ap, b):
                self.t = ap
                self.b = b
        mergedTb = View(GT.t[:, :].rearrange("p (k t) -> p k t", k=8), GT.b)
        gate_tok = [View(GT.t[:, i * 2048:(i + 1) * 2048].rearrange("p (j c) -> p j c", j=4), GT.b) for i in range(2)]
        ybT = sb("ybT", [128, 4, TM], BF16)
        rope_t = sb("rope_t", [128, 2, TM])
        kT = sb("kT", [128, 6 * CH], BF16)
        vaug = sb("vaug", [128, 6, 2, 65], BF16)
        mqT = sb("mqT", [128, 4, TM], BF16)
        qT = mqT
        mkT = sb("mkT", [128, 4, TM], BF16)
        mvaug = sb("mvaug", [128, 4, 4, 129], BF16)
        mktok = sb("mktok", [128, 4, 512], BF16)
        mC = [sb("mC%d" % d, [128, 516]) for d in range(2)]
        mCb = [sb("mCb%d" % d, [128, 516], BF16) for d in range(2)]
        srawr = Ring([sb("sraw%d" % i, [128, TM + 4]) for i in range(2)])
        sconv = sb("sconv", [128, 6, TM], BF16)
        sxtok = sb("sxtok", [128, 4, 512], BF16)
        szero = sb("szero", [128, 4, TM], BF16)
        S.op("pool", lambda e: e.memset(szero.t[:], 0.0), [], [szero.b])
        sbtok = sb("sbtok", [128, 4, 128], BF16)
        sS = [sb("sS%d" % d, [128, 256]) for d in range(2)]
        sSb = [sb("sSb%d" % d, [128, 256], BF16) for d in range(2)]
        RT = [sb("rt%d" % i, [40, 512]) for i in range(8)]
        for rt_ in RT:
            S.op("pool", (lambda rt_: lambda e: e.memset(rt_.t[:], 0.0))(rt_), [], [rt_.b])
        carryB = [sb("carryB%d" % d, [4, 1]) for d in range(2)]
        carryM = [sb("carryM%d" % d, [4, 1]) for d in range(2)]
        mprev = sb("mprev", [4, 4])
        NTS = 52
        TSF = sb("TSF", [128, 4, NTS])
        TSBm = sb("TSBm", [128, 4, NTS])
        ACSF = sb("ACSF", [40, TM])
        ACSBm = sb("ACSBm", [40, TM])
        acs_hl = {0: (sb("acsfh", [40, TM], BF16), sb("acsfl", [40, TM], BF16)),
                  1: (sb("acsbh", [40, TM], BF16), sb("acsbl", [40, TM], BF16))}
        selb = sb("selb", [40, 8, 128], BF16)
        CP("dve", selb.t[32:40, :, :], self_.t[32:40, :, :], [self_.b], [selb.b])

        def mk_hilo(d, src):
            hi, lo = acs_hl[d]
            CP("act", hi.t[32:40, :], src.t[32:40, :], [src.b], [hi.b])
            TT("dve", lo.t[32:40, :], src.t[32:40, :], hi.t[32:40, :], ALU.subtract, [src.b, hi.b], [lo.b])
        nmmax = smax // TM
        TSD = nc.dram_tensor("tsd", [nmmax, 128, 4 * NTS], F32, kind="Internal").ap()
        ACSD = nc.dram_tensor("acsd", [nmmax, 8, TM], F32, kind="Internal").ap()
        tsd_buf = [Buf("tsd%d" % i) for i in range(nmmax)]
        acsd_buf = [Buf("acsd%d" % i) for i in range(nmmax)]

        stages = []

        def add_stage(l, piece, fn):
            stages.append((l, piece, fn))

        def run_stages():
            loads = [i for i, s_ in enumerate(stages) if s_[1] is not None]
            slot_of = {}
            nxt = 0
            PRE = 2
            for i, (l, piece, fn) in enumerate(stages):
                while nxt < len(loads) and (nxt < PRE or loads[nxt - PRE] <= i):
                    j = loads[nxt]
                    sl = wslots[nxt % NSLOT]
                    lj, pj, _ = stages[j]
                    if pj == P_KV or pj == P_SBC:
                        nc_ = 400 if pj == P_KV else 256
                        DMA(sl.t[:, :].rearrange("p (k c) -> p k c", c=512)[:, :, 0:nc_],
                            WS[lj, pj].rearrange("p (k c) -> p k c", c=512)[:, :, 0:nc_], ws_buf[lj][pj], [sl.b])
                    elif P_G0 <= pj < P_WO0:
                        DMA(sl.t[:, 0:3072], WS[lj, pj][:, 0:3072], ws_buf[lj][pj], [sl.b])
                    else:
                        DMA(sl.t[:], WS[lj, pj], ws_buf[lj][pj], [sl.b])
                    slot_of[j] = sl
                    nxt += 1
                fn(slot_of.get(i))

        def src_of(l):
            return x_in if l == 0 else YS[(l - 1) % nscr]

        def src_buf(l, r0):
            return [] if l == 0 else ys_buf[(l - 1) % nscr][r0 // CH]

        def dst_of(l):
            return y_out if l == depth - 1 else YS[l % nscr]

        def dst_buf2(l, r0, half):
            return [] if l == depth - 1 else [ys_buf[l % nscr][r0 // CH][half]]

        def rstd_from_sumsq(dst, ssum, n, rd, wr, npart=128):
            TSC("dve", dst, ssum, 1.0 / n, ALU.mult, rd, wr, s2=EPS, op1=ALU.add)
            ACT(dst, dst, AF.Sqrt, wr, wr)
            S.op("dve", lambda e: e.reciprocal(out=dst, in_=dst), wr, wr)

        def ensure_h(l, tok0, c):
            sl = c % 8
            if hslot_chunk[sl] == (l, tok0, c):
                return
            hslot_chunk[sl] = (l, tok0, c)
            xt = xring.get()
            r0 = tok0 + c * CH
            DMA(xt.t[:], src_of(l)[r0:r0 + CH, :], src_buf(l, r0), [xt.b])
            st = st4.get()
            hb = hbring.get()
            ACT(hb.t[:], xt.t[:], AF.Square, [xt.b], [hb.b, st.b], accum=st.t[:, 0:1])
            rstd_from_sumsq(st.t[:, 1:2], st.t[:, 0:1], D, [st.b], [st.b])
            STT("dve", hb.t[:], xt.t[:], st.t[:, 1:2], ng_bc.t[:], ALU.mult, ALU.mult, [xt.b, st.b, ng_bc.b], [hb.b])
            p = pb.get()
            for k in range(8):
                TR(p.t[:, k * 128:(k + 1) * 128], hb.t[:, k * 128:(k + 1) * 128], identb.t[:], [hb.b, identb.b], [p.b])
            CP("act", hT.t[:, :, sl * CH:(sl + 1) * CH], p.t[:].rearrange("p (k t) -> p k t", k=8), [p.b], [hT_b[sl]])
            if c == 0:
                dbg("hT", hT.t[:, :, sl * CH:(sl + 1) * CH], [hT_b[sl]], BF16)

        def h_rhs(k, c0, n):
            s0 = c0 % 8
            assert s0 + n <= 8
            return hT.t[:, k, s0 * CH:(s0 + n) * CH]

        def h_bufs(c0, n):
            return [hT_b[(c0 + i) % 8] for i in range(n)]

        def proj_feat(sl, col0, ncols, c0, nchunks, pt, pcol0=0):
            for k in range(8):
                MM(pt.t[0:ncols, pcol0:pcol0 + nchunks * CH], sl.t[:, k * 512 + col0:k * 512 + col0 + ncols],
                   h_rhs(k, c0, nchunks), k == 0, k == 7, [sl.b] + h_bufs(c0, nchunks), [pt.b])

        def proj_tok(sl, col0, ncols, c, pt):
            s0 = c % 8
            for k in range(8):
                MM(pt.t[:, 0:ncols], hT.t[:, k, s0 * CH:(s0 + 1) * CH], sl.t[:, k * 512 + col0:k * 512 + col0 + ncols],
                   k == 0, k == 7, [sl.b, hT_b[s0]], [pt.b])

        mL = slice(0, 4)
        sD = slice(32, 40)

        def gate_rows(l, d, sl, c0, first, ts_dst, ts_bufs, acs_dst, acs_bufs):
            rev = (d == 1)

            def rv(ap2d):
                return ap2d[:, ::-1] if rev else ap2d

            pg_ = pf.get()
            proj_feat(sl, 256 + d * 72, 72, c0, 4, pg_)
            pmi = pmf = pdt = pg_
            R = RT
            IG, L1, Bp, Mg, WI, DEC = R[0], R[1], R[2], R[3], R[4], R[5]
            ACT(IG.t[mL, :], rv(pmi.t[0:4, :]), AF.Identity, [pmi.b, ib[d].b], [IG.b], bias=ib[d].t[:, 0:1])
            ACT(L1.t[mL, :], rv(pmf.t[32:36, :]), AF.Exp, [pmf.b, nfb[d].b], [L1.b], bias=nfb[d].t[:, 0:1], scale=-1.0)
            ACT(L1.t[mL, :], L1.t[mL, :], AF.Ln, [L1.b], [L1.b], bias=1.0)
            if first:
                S.op("dve", lambda e: e.memset(carryB[d].t[:], 0.0), [], [carryB[d].b])
                S.op("dve", lambda e: e.memset(carryM[d].t[:], 0.0), [], [carryM[d].b])
            SCAN(Bp.t[mL, :], L1.t[mL, :], zrow.t[0:4, 0:1].to_broadcast([4, 512]), carryB[d].t[:, 0:1], ALU.add, ALU.add,
                 [L1.b, zrow.b, carryB[d].b], [Bp.b])
            A = IG
            TT("dve", A.t[mL, :], IG.t[mL, :], Bp.t[mL, :], ALU.add, [IG.b, Bp.b], [A.b])
            SCAN(Mg.t[mL, :], A.t[mL, :], A.t[mL, :], carryM[d].t[:, 0:1], ALU.max, ALU.max, [A.b, carryM[d].b], [Mg.b])

            def r3(tl):
                return tl.t[mL, :].rearrange("p (c t) -> p c t", c=4)

            Mg3 = r3(Mg)
            CP("dve", mprev.t[:, 0:1], carryM[d].t[:, 0:1], [carryM[d].b], [mprev.b])
            CP("dve", mprev.t[:, 1:4], Mg3[:, 0:3, 127], [Mg.b], [mprev.b])
            CP("dve", carryB[d].t[:, 0:1], Bp.t[mL, 511:512], [Bp.b], [carryB[d].b])
            CP("dve", carryM[d].t[:, 0:1], Mg.t[mL, 511:512], [Mg.b], [carryM[d].b])
            mend_bc = Mg3[:, :, 127:128].to_broadcast([4, 4, 128])
            mprev_bc = mprev.t[:, :].unsqueeze(2).to_broadcast([4, 4, 128])
            U = L1
            TT("dve", r3(U), r3(Mg), mend_bc, ALU.subtract, [Mg.b, L1.b], [U.b])
            TSC("dve", U.t[mL, :], U.t[mL, :], -60.0, ALU.max, [U.b], [U.b])
            ACT(U.t[mL, :], U.t[mL, :], AF.Exp, [U.b], [U.b], scale=-1.0)
            TT("dve", r3(WI), r3(Mg), mprev_bc, ALU.subtract, [Mg.b, mprev.b], [WI.b])
            ACT(WI.t[mL, :], WI.t[mL, :], AF.Exp, [WI.b], [WI.b], scale=-1.0)
            FL = Bp
            TT("dve", FL.t[mL, :], Bp.t[mL, :], Mg.t[mL, :], ALU.subtract, [Bp.b, Mg.b], [FL.b])
            ACT(FL.t[mL, :], FL.t[mL, :], AF.Exp, [FL.b], [FL.b])
            TT("dve", r3(DEC), mprev_bc, mend_bc, ALU.subtract, [Mg.b, mprev.b], [DEC.b])
            ACT(DEC.t[mL, :], DEC.t[mL, :], AF.Exp, [DEC.b], [DEC.b])
            WK = A
            TT("dve", r3(WK), r3(A), mend_bc, ALU.subtract, [A.b, Mg.b], [WK.b])
            ACT(WK.t[mL, :], WK.t[mL, :], AF.Exp, [WK.b], [WK.b])
            DT, LDT, DA, ACS, EA = R[0], R[1], R[2], R[3], R[4]
            ACT(DT.t[sD, :], rv(pdt.t[64:72, :]), AF.Exp, [pdt.b, dtb[d].b], [DT.b], bias=dtb[d].t[sD, 0:1])
            ACT(DT.t[sD, :], DT.t[sD, :], AF.Ln, [DT.b], [DT.b], bias=1.0)
            ACT(LDT.t[sD, :], DT.t[sD, :], AF.Ln, [DT.b], [LDT.b])
            TSC("dve", DA.t[sD, :], DT.t[sD, :], acoef[d].t[sD, 0:1], ALU.mult, [DT.b, acoef[d].b], [DA.b])
            SCAN(ACS.t[sD, :], resetm.t[sD, :], DA.t[sD, :], 0.0, ALU.mult, ALU.add, [resetm.b, DA.b], [ACS.b])

            def r8(tl):
                return tl.t[sD, :].rearrange("p (c t) -> p c t", c=4)

            aend_bc = r8(ACS)[:, :, 127:128].to_broadcast([8, 4, 128])
            BL = LDT
            TT("dve", BL.t[sD, :], LDT.t[sD, :], ACS.t[sD, :], ALU.subtract, [LDT.b, ACS.b], [BL.b])
            WST = DA
            TT("dve", r8(WST), aend_bc, r8(ACS), ALU.subtract, [ACS.b, DA.b], [WST.b])
            ACT(WST.t[sD, :], WST.t[sD, :], AF.Exp, [WST.b], [WST.b])
            TT("dve", WST.t[sD, :], WST.t[sD, :], DT.t[sD, :], ALU.mult, [WST.b, DT.b], [WST.b])
            ACT(EA.t[sD, :], ACS.t[sD, :], AF.Exp, [ACS.b], [EA.b])
            CD = DT
            CP("dve", r8(CD), aend_bc, [ACS.b, WST.b, DT.b], [CD.b])
            ACT(CD.t[sD, :], CD.t[sD, :], AF.Exp, [CD.b], [CD.b])
            quants = [(WK, mL, 4, 0), (U, mL, 4, 4), (WI, mL, 4, 8), (FL, mL, 4, 12), (DEC, mL, 4, 16),
                      (BL, sD, 8, 20), (WST, sD, 8, 28), (EA, sD, 8, 36), (CD, sD, 8, 44)]
            pts = pf.get()
            if rev:
                order_ = [R[0], R[1], R[2], R[4], R[5]]
                rmap = {}
                for ti_, tl_ in enumerate(order_):
                    q2 = R[6 + (ti_ % 2)]
                    rmap[id(tl_)] = (q2, ti_)
                quants = sorted(quants, key=lambda x: rmap[id(x[0])][1])
                done_ = set()
            for qi, (q, ps_, r, off) in enumerate(quants):
                if rev:
                    q2, ti_ = rmap[id(q)]
                    if ti_ not in done_:
                        done_.add(ti_)
                        CP("dve", q2.t[0:40, :], q.t[0:40, ::-1], [q.b], [q2.b])
                    q = q2
                for j in range(4):
                    MM(pts.t[:, j * 64 + off:j * 64 + off + r], q.t[ps_, j * CH:(j + 1) * CH], identf.t[ps_, ps_],
                       True, True, [q.b, identf.b], [pts.b])
            CP("act", ts_dst, pts.t[:, 0:256].rearrange("p (j c) -> p j c", j=4)[:, :, 0:NTS], [pts.b], ts_bufs)
            if rev:
                CP("pool", acs_dst, ACS.t[sD, ::-1], [ACS.b], acs_bufs)
            else:
                CP("pool", acs_dst, ACS.t[sD, :], [ACS.b], acs_bufs)

        def ssd_conv(l, slx, slbc, c0, nch_seq, tiles):
            for ti in tiles:
                if "conv" in KSKIP:
                    break
                sl, col0 = (slx, ti * 128) if ti < 4 else (slbc, (ti - 4) * 128)
                sraw = srawr.get()
                pm = pf.get()
                proj_feat(sl, col0, 128, c0, 4, pm)
                CP("act", sraw.t[:, 2:2 + TM], pm.t[:, :], [pm.b], [sraw.b])
                ph = pf.get()
                if c0 > 0:
                    s0 = (c0 - 1) % 8
                    for k in range(8):
                        MM(ph.t[:, 0:32], sl.t[:, k * 512 + col0:k * 512 + col0 + 128], hT.t[:, k, s0 * CH + 96:s0 * CH + 128],
                           k == 0, k == 7, [sl.b, hT_b[s0]], [ph.b])
                    CP("dve", sraw.t[:, 0:2], ph.t[:, 30:32], [ph.b], [sraw.b])
                else:
                    S.op("dve", (lambda sraw: lambda e: e.memset(sraw.t[:, 0:2], 0.0))(sraw), [], [sraw.b])
                if c0 + 4 < nch_seq:
                    s0 = (c0 + 4) % 8
                    for k in range(8):
                        MM(ph.t[:, 32:64], sl.t[:, k * 512 + col0:k * 512 + col0 + 128], hT.t[:, k, s0 * CH:s0 * CH + 32],
                           k == 0, k == 7, [sl.b, hT_b[s0]], [ph.b])
                    CP("dve", sraw.t[:, TM + 2:TM + 4], ph.t[:, 32:34], [ph.b], [sraw.b])
                else:
                    S.op("dve", (lambda sraw: lambda e: e.memset(sraw.t[:, TM + 2:TM + 4], 0.0))(sraw), [], [sraw.b])
                acc = f32r.get()
                TSC("dve", acc.t[:, :], sraw.t[:, 0:TM], cw.t[:, ti, 0:1], ALU.mult, [sraw.b, cw.b, cb.b], [acc.b],
                    s2=cb.t[:, ti:ti + 1], op1=ALU.add)
                for kk in range(1, 5):
                    STT("dve", acc.t[:, :], sraw.t[:, kk:kk + TM], cw.t[:, ti, kk:kk + 1], acc.t[:, :], ALU.mult, ALU.add,
                        [sraw.b, cw.b, acc.b], [acc.b])
                ACT(sconv.t[:, ti, :], acc.t[:, :], AF.Silu, [acc.b], [sconv.b])

        def ssd_tokmajor(j, do_x=True):
            if "tokm" in KSKIP:
                return
            if do_x:
                p = pb.get()
                for ti in range(4):
                    TR(p.t[:, ti * 128:(ti + 1) * 128], sconv.t[:, ti, j * CH:(j + 1) * CH], identb.t[:], [sconv.b, identb.b], [p.b])
                CP("act", sxtok.t[:, j, :], p.t[:, 0:512], [p.b], [sxtok.b])
            if "tokb" in KSKIP:
                return
            p2 = pb.get()
            TR(p2.t[:, 0:128], sconv.t[:, 4, j * CH:(j + 1) * CH], identb.t[:], [sconv.b, identb.b], [p2.b])
            CP("act", sbtok.t[:, j, :], p2.t[:, 0:128], [p2.b], [sbtok.b])

        def ssd_state_update(d, j, ts_ap, ts_rd):
            xw = b16r.get()
            TT("pool" if d == 0 else "dve", xw.t[:, :].rearrange("p (h e) -> p h e", h=8), sxtok.t[:, j, :].rearrange("p (h e) -> p h e", h=8),
               ts_ap[:, 28:36].unsqueeze(2).to_broadcast([128, 8, 64]), ALU.mult, [sxtok.b] + ts_rd, [xw.b])
            pS = pf.get()
            for g in range(2):
                MM(pS.t[:, g * 256:(g + 1) * 256], sbtok.t[:, j, :], xw.t[:, g * 256:(g + 1) * 256], True, True, [sbtok.b, xw.b], [pS.b])
            for g in range(2):
                ps_ = slice(g * 64, (g + 1) * 64)
                TT("dve", sS[d].t[ps_, :].rearrange("p (h e) -> p h e", h=4), sS[d].t[ps_, :].rearrange("p (h e) -> p h e", h=4),
                   ts_ap[ps_, 44 + g * 4:44 + g * 4 + 4].unsqueeze(2).to_broadcast([64, 4, 64]), ALU.mult, [sS[d].b] + ts_rd, [sS[d].b])
                TT("dve", sS[d].t[ps_, :], sS[d].t[ps_, :], pS.t[ps_, g * 256:(g + 1) * 256], ALU.add, [sS[d].b, pS.b], [sS[d].b])

        def mlstm_state_update(d, j, ts_ap, ts_rd, pre=None):
            if pre is not None:
                vw, wkb = pre["vw"], pre["wkb"]
            else:
                vw = b16r.get()
                TT("pool" if d == 0 else "dve", vw.t[:, :].rearrange("p (h e) -> p h e", h=4), mvaug.t[:, j, :, 0:128],
                   ts_ap[:, 0:4].unsqueeze(2).to_broadcast([128, 4, 128]), ALU.mult, [mvaug.b] + ts_rd, [vw.b])
                wkb = b16r.get()
                CP("dve", wkb.t[:, 0:4], ts_ap[:, 0:4], ts_rd, [wkb.b])
            pC = pf.get(); pn = pf.get()
            for h in range(4):
                MM(pC.t[:, h * 128:(h + 1) * 128], mktok.t[:, j, h * 128:(h + 1) * 128], vw.t[:, h * 128:(h + 1) * 128], True, True,
                   [mktok.b, vw.b], [pC.b])
                MM(pn.t[:, h:h + 1], mktok.t[:, j, h * 128:(h + 1) * 128], wkb.t[:, h:h + 1], True, True, [mktok.b, wkb.b], [pn.b])
            dec_bc = ts_ap[:, 16:20]
            TT("dve", mC[d].t[:, 0:512].rearrange("p (h e) -> p h e", h=4), mC[d].t[:, 0:512].rearrange("p (h e) -> p h e", h=4),
               dec_bc.unsqueeze(2).to_broadcast([128, 4, 128]), ALU.mult, [mC[d].b] + ts_rd, [mC[d].b])
            TT("dve", mC[d].t[:, 512:516], mC[d].t[:, 512:516], dec_bc, ALU.mult, [mC[d].b] + ts_rd, [mC[d].b])
            TT("dve", mC[d].t[:, 0:512], mC[d].t[:, 0:512], pC.t[:, :], ALU.add, [mC[d].b, pC.b], [mC[d].b])
            TT("dve", mC[d].t[:, 512:516], mC[d].t[:, 512:516], pn.t[:, 0:4], ALU.add, [mC[d].b, pn.b], [mC[d].b])

        def mlstm_v_tok(sl, c, j):
            pv = pf.get()
            proj_tok(sl, 0, 512, c, pv)
            CP("act", mvaug.t[:, j, :, 0:128], pv.t[:, :].rearrange("p (h e) -> p h e", h=4), [pv.b], [mvaug.b])

        def add_branch_merge(l, bi, first, c0):
            for q_ in range(4):
                def st_g(sl, q_=q_):
                    for oo in range(2):
                        o = q_ * 2 + oo
                        pg = pf.get()
                        for k_ in range(8):
                            MM(pg.t[:, :], sl.t[:, k_ * 256 + oo * 128:k_ * 256 + (oo + 1) * 128], h_rhs(k_, c0, 4), k_ == 0, k_ == 7,
                               [sl.b] + h_bufs(c0, 4), [pg.b])
                        gs = f32r.get()
                        ACT(gs.t[:, :], pg.t[:, :], AF.Sigmoid, [pg.b], [gs.b])
                        pp = pf.get()
                        for k_ in range(4):
                            MM(pp.t[:, :], sl.t[:, 2048 + k_ * 256 + oo * 128:2048 + k_ * 256 + (oo + 1) * 128], ybT.t[:, k_, :], k_ == 0, k_ == 3,
                               [sl.b, ybT.b], [pp.b])
                        if first:
                            TT("dve", mergedT.t[:, o, :], gs.t[:, :], pp.t[:, :], ALU.mult, [gs.b, pp.b], [mergedT.b])
                        else:
                            TT("dve", gs.t[:, :], gs.t[:, :], pp.t[:, :], ALU.mult, [gs.b, pp.b], [gs.b])
                            TT("pool", mergedT.t[:, o, :], mergedT.t[:, o, :], gs.t[:, :], ALU.add, [gs.b, mergedT.b], [mergedT.b])
                add_stage(l, P_G0 + bi * 4 + q_, st_g)

        def add_final(l, tok0, c0, nch):
            def st_pre(_sl):
                for c_ in range(c0 + 5, min(nch, c0 + 9)):
                    ensure_h(l, tok0, c_)
                dbg("ybT", ybT.t[:, :, :], [ybT.b], BF16)
                dbg("mergedT", mergedT.t[:, :, :], [mergedT.b])
                CP("act", mergedTb.t[:, 0:4, :], mergedT.t[:, 0:4, :], [mergedT.b], [mergedTb.b])
                CP("dve", mergedTb.t[:, 4:8, :], mergedT.t[:, 4:8, :], [mergedT.b], [mergedTb.b])
            add_stage(l, None, st_pre)
            for half in range(2):
                def st_o(sl, half=half):
                    hs = slice(half * 512, (half + 1) * 512)
                    for j in range(4):
                        r0 = tok0 + (c0 + j) * CH
                        xt = xring2.get()
                        DMA(xt.t[:, :], src_of(l)[r0:r0 + CH, hs], src_buf(l, r0), [xt.b])
                        po = pf.get()
                        for k in range(8):
                            MM(po.t[:, :], mergedTb.t[:, k, j * CH:(j + 1) * CH], sl.t[:, k * 512:(k + 1) * 512], k == 0, k == 7,
                               [mergedTb.b, sl.b], [po.b])
                        TT("dve", xt.t[:, :], xt.t[:, :], po.t[:, :], ALU.add, [xt.b, po.b], [xt.b])
                        DMA(dst_of(l)[r0:r0 + CH, hs], xt.t[:, :], [xt.b], dst_buf2(l, r0, half))
                add_stage(l, P_WO0 + half, st_o)

        xring2 = Ring([sb("xo%d" % i, [128, 512]) for i in range(3)])
        f32r.tiles.append(sb("f32r_x", [128, 512]))
        tsf_b = TSF.b
        acsf_b = ACSF.b

        def seq_layer(l, tok0, slen):
            nch = slen // CH
            nm = slen // TM
            en_att, en_ml, en_ssd = ("att" in enable), ("mlstm" in enable), ("ssd" in enable)
            en_rec = en_ml or en_ssd

            if en_rec:
                for m in reversed(range(nm)):
                    c0 = 4 * m

                    def st_h(_sl, c0=c0):
                        for c in range(max(0, c0 - 1), min(nch, c0 + 5)):
                            ensure_h(l, tok0, c)
                    add_stage(l, None, st_h)

                    def st_gates(sl, c0=c0, m=m):
                        if m == nm - 1:
                            S.op("dve", lambda e: e.memset(mC[1].t[:], 0.0), [], [mC[1].b])
                            S.op("dve", lambda e: e.memset(sS[1].t[:], 0.0), [], [sS[1].b])
                            S.op("dve", lambda e: e.memset(mvaug.t[:, :, :, 128:129], 1.0), [], [mvaug.b])
                        gate_rows(l, 1, sl, c0, m == nm - 1, TSBm.t[:, :, :], [TSBm.b], ACSBm.t[sD, :], [ACSBm.b])
                        DMA(TSD[m].rearrange("p (j c) -> p j c", j=4), TSBm.t[:, :, :], [TSBm.b], [tsd_buf[m]])
                        DMA(ACSD[m], ACSBm.t[sD, :], [ACSBm.b], [acsd_buf[m]])
                    add_stage(l, P_KV, st_gates)

                    if en_ml:
                        def st_mk(sl, c0=c0):
                            for j in range(4):
                                pk = pf.get()
                                proj_tok(sl, 0, 512, c0 + j, pk)
                                S.op("act", (lambda j, pk: lambda e: e.mul(out=mktok.t[:, j, :], in_=pk.t[:, :], mul=128 ** -0.5))(j, pk),
                                     [pk.b], [mktok.b])
                                DMA(KTOK[c0 + j], mktok.t[:, j, :], [mktok.b], [ktok_buf[c0 + j]])
                        add_stage(l, P_MK, st_mk)

                        def st_mv(sl, c0=c0, m=m):
                            for j in range(4):
                                mlstm_v_tok(sl, c0 + j, j)
                                DMA(VTOK[c0 + j].rearrange("p (h e) -> p h e", h=4), mvaug.t[:, j, :, 0:128], [mvaug.b], [vtok_buf[c0 + j]])
                            for j in reversed(range(4)):
                                c = c0 + j
                                CP("act", mCb[1].t[:, :], mC[1].t[:, :], [mC[1].b], [mCb[1].b])
                                DMA(BSTM[c], mCb[1].t[:, :], [mCb[1].b], [bstm_buf[c]])
                                mlstm_state_update(1, j, TSBm.t[:, j, :], [TSBm.b])
                        add_stage(l, P_MV, st_mv)

                    if en_ssd and "p2ssd" not in KSKIP:
                        def st_sx(sl, c0=c0):
                            ssd_conv(l, sl, None, c0, nch, [0, 1, 2, 3])
                        add_stage(l, P_SX, st_sx)

                        def st_sbc(sl, c0=c0, m=m):
                            ssd_conv(l, None, sl, c0, nch, [4])
                            for j in reversed(range(4)):
                                c = c0 + j
                                ssd_tokmajor(j)
                                DMA(XTOK[c], sxtok.t[:, j, :], [sxtok.b], [xtok_buf[c]])
                                CP("act", sSb[1].t[:, :], sS[1].t[:, :], [sS[1].b], [sSb[1].b])
                                DMA(BSTS[c], sSb[1].t[:, :], [sSb[1].b], [bsts_buf[c]])
                                ssd_state_update(1, j, TSBm.t[:, j, :], [TSBm.b])
                        add_stage(l, P_SBC, st_sbc)

                    def st_pref(_sl, c0=c0):
                        for c_ in range(max(0, c0 - 5), c0 - 1):
                            ensure_h(l, tok0, c_)
                    add_stage(l, None, st_pref)

            for m in range(nm):
                c0 = 4 * m
                lo = max(0, c0 - 1)
                hi = min(nch, c0 + 5)

                def st_h(_sl, c0=c0, lo=lo, hi=hi, m=m):
                    if l + 1 < depth:
                        for _ in range(casts_per_mt):
                            if pending_casts[l + 1]:
                                pending_casts[l + 1].pop(0)()
                    for c in range(lo, hi):
                        ensure_h(l, tok0, c)
                    DMA(rope_t.t[:], C["c_rope"][:, :, c0 * CH:c0 * CH + TM], [], [rope_t.b])
                    if m == 0:
                        S.op("dve", lambda e: e.memset(mC[0].t[:], 0.0), [], [mC[0].b])
                        S.op("dve", lambda e: e.memset(sS[0].t[:], 0.0), [], [sS[0].b])
                        S.op("dve", lambda e: e.memset(mCb[0].t[:], 0.0), [], [mCb[0].b])
                        S.op("dve", lambda e: e.memset(sSb[0].t[:], 0.0), [], [sSb[0].b])
                        S.op("dve", lambda e: e.memset(mvaug.t[:, :, :, 128:129], 1.0), [], [mvaug.b])
                        S.op("dve", lambda e: e.memset(vaug.t[:, :, :, 64:65], 1.0), [], [vaug.b])
                add_stage(l, None, st_h)

                def qk_norm_rope(pq, gain, ntok):
                    sq = b16r.get()
                    ACT(sq.t[:, 0:ntok], pq.t[:, 0:ntok], AF.Square, [pq.b], [sq.b])
                    pss = pf.get()
                    MM(pss.t[:, 0:ntok], blkb.t[:], sq.t[:, 0:ntok], True, True, [blkb.b, sq.b], [pss.b])
                    rs = f32r.get()
                    ACT(rs.t[:, 0:ntok], pss.t[:, 0:ntok], AF.Ln, [pss.b, epsc.b], [rs.b], bias=epsc.t[:, 0:1])
                    ACT(rs.t[:, 0:ntok], rs.t[:, 0:ntok], AF.Exp, [rs.b], [rs.b], scale=-0.5)
                    qn = f32r.get(); qnb = b16r.get()
                    STT("dve", qnb.t[:, 0:ntok], pq.t[:, 0:ntok], gain.t[:, 0:1], rs.t[:, 0:ntok], ALU.mult, ALU.mult,
                        [pq.b, gain.b, rs.b], [qnb.b])
                    pr = pf.get()
                    MM(pr.t[:, 0:ntok], rotb.t[:], qnb.t[:, 0:ntok], True, True, [rotb.b, qnb.b], [pr.b])
                    t1 = f32r.get()
                    return qn, pr, t1, qnb

                if en_att:
                    def st_aq(sl, c0=c0, qk_norm_rope=qk_norm_rope):
                        for i in range(4):
                            pq = pf.get()
                            proj_feat(sl, i * 128, 128, c0, 4, pq)
                            qn, pr, t1, qnb = qk_norm_rope(pq, gq, TM)
                            TT("dve", t1.t[:, :], pr.t[:, :], rope_t.t[:, 1, :], ALU.mult, [pr.b, rope_t.b], [t1.b])
                            TT("pool", qn.t[:, :], qnb.t[:, :], rope_t.t[:, 0, :], ALU.mult, [qnb.b, rope_t.b], [qn.b])
                            TT("pool", qT.t[:, i, :], qn.t[:, :], t1.t[:, :], ALU.add, [qn.b, t1.b], [qT.b])
                            if i == 0:
                                dbg("pq", pq.t[:, :], [pq.b])
                        dbg("qT", qT.t[:, :, :], [qT.b], BF16)
                    add_stage(l, P_AQ, st_aq)

                def st_kv(sl, c0=c0, lo=lo, hi=hi, m=m, qk_norm_rope=qk_norm_rope):
                    if en_att:
                        for (ca, n) in ((lo, c0 - lo), (c0, 4), (c0 + 4, hi - c0 - 4)):
                            if n <= 0:
                                continue
                            pk = pf.get()
                            proj_feat(sl, 0, 128, ca, n, pk)
                            ntk = n * CH
                            qn, pr, t1, qnb = qk_norm_rope(pk, gk, ntk)
                            if ca == c0:
                                cosap, sinap, rbufs = rope_t.t[:, 0, :], rope_t.t[:, 1, :], [rope_t.b]
                            else:
                                rt = f32r.get(); rt2 = f32r.get()
                                DMA(rt.t[:, 0:CH], C["c_rope"][:, 0, ca * CH:(ca + 1) * CH], [], [rt.b])
                                DMA(rt2.t[:, 0:CH], C["c_rope"][:, 1, ca * CH:(ca + 1) * CH], [], [rt2.b])
                                cosap, sinap, rbufs = rt.t[:, 0:CH], rt2.t[:, 0:CH], [rt.b, rt2.b]
                            TT("dve", t1.t[:, 0:ntk], pr.t[:, 0:ntk], sinap, ALU.mult, [pr.b] + rbufs, [t1.b])
                            TT("pool", qn.t[:, 0:ntk], qnb.t[:, 0:ntk], cosap, ALU.mult, [qnb.b] + rbufs, [qn.b])
                            o = (ca - lo) * CH
                            TT("pool", kT.t[:, o:o + ntk], qn.t[:, 0:ntk], t1.t[:, 0:ntk], ALU.add, [qn.b, t1.b], [kT.b])
                        for c in range(lo, hi):
                            pv = pf.get()
                            proj_tok(sl, 128, 128, c, pv)
                            CP("act", vaug.t[:, c - lo, :, 0:64], pv.t[:, 0:128].rearrange("p (g e) -> p g e", g=2), [pv.b], [vaug.b])
                        dbg("kT", kT.t[:, :], [kT.b], BF16)
                        dbg("vaug", vaug.t[:, :, :, :], [vaug.b], BF16)
                    if en_rec:
                        gate_rows(l, 0, sl, c0, m == 0, TSF.t[:, :, :], [tsf_b], ACSF.t[sD, :], [acsf_b])
                        DMA(TSBm.t[:, :, :], TSD[m].rearrange("p (j c) -> p j c", j=4), [tsd_buf[m]], [TSBm.b])
                        DMA(ACSBm.t[sD, :], ACSD[m], [acsd_buf[m]], [ACSBm.b])
                        mk_hilo(0, ACSF)
                        mk_hilo(1, ACSBm)
                add_stage(l, P_KV, st_kv)

                if en_att:
                    def st_az(sl, c0=c0, lo=lo, hi=hi):
                        for j in range(4):
                            pz = pf.get()
                            proj_tok(sl, 0, 512, c0 + j, pz)
                            ACT(gate_tok[0].t[:, j, :], pz.t[:, :], AF.Silu, [pz.b], [gate_tok[0].b])
                        for j in range(4):
                            c = c0 + j
                            po = pacc
                            kblocks = [cc for cc in (c - 1, c, c + 1) if 0 <= cc < nch]
                            for g in range(2):
                                pr_ = slice(g * 64, (g + 1) * 64)
                                for bi, cc in enumerate(kblocks):
                                    ps_ = pf.get()
                                    ko = (cc - lo) * CH
                                    for i in range(4):
                                        MM(ps_.t[:, i * 128:(i + 1) * 128], kT.t[pr_, ko:ko + CH], qT.t[pr_, i, j * CH:(j + 1) * CH],
                                           True, True, [kT.b, qT.b], [ps_.b])
                                    pt_ = b16r.get()
                                    ACT(pt_.t[:, :], ps_.t[:, :], AF.Exp, [ps_.b], [pt_.b])
                                    if cc != c:
                                        mi_ = 1 if cc < c else 0
                                        TT("pool", pt_.t[:, :], pt_.t[:, :], trib.t[:, mi_, :], ALU.mult, [pt_.b, trib.b], [pt_.b])
                                    for i in range(4):
                                        MM(po[g].t[:, i * 65:(i + 1) * 65], pt_.t[:, i * 128:(i + 1) * 128], vaug.t[:, cc - lo, g, :],
                                           bi == 0 and i == 0, bi == len(kblocks) - 1, [pt_.b, vaug.b], [po[g].b], skip=True)
                            ya = f32r.get(); den = st4.get()
                            for g in range(2):
                                pv3 = po[g].t[:, 0:260].rearrange("p (i e) -> p i e", i=4)
                                TT("dve", den.t[:, g * 4:(g + 1) * 4], pv3[:, :, 64], esink.t[:, g * 4:(g + 1) * 4], ALU.add,
                                   [po[g].b, esink.b], [den.b])
                            S.op("dve", (lambda den: lambda e: e.reciprocal(out=den.t[:, 0:8], in_=den.t[:, 0:8]))(den), [den.b], [den.b])
                            for g in range(2):
                                pv3 = po[g].t[:, 0:260].rearrange("p (i e) -> p i e", i=4)
                                TT("dve", ya.t[:, g * 256:(g + 1) * 256].rearrange("p (i e) -> p i e", i=4), pv3[:, :, 0:64],
                                   den.t[:, g * 4:(g + 1) * 4].unsqueeze(2).to_broadcast([128, 4, 64]), ALU.mult, [po[g].b, den.b], [ya.b])
                            dbg("ya", ya.t[:, :], [ya.b])
                            yg = b16r.get()
                            TT("pool", yg.t[:, :], ya.t[:, :], gate_tok[0].t[:, j, :], ALU.mult, [ya.b, gate_tok[0].b], [yg.b])
                            dbg("yg", yg.t[:, :], [yg.b], BF16)
                            p = pb.get()
                            for i in range(4):
                                TR(p.t[:, i * 128:(i + 1) * 128], yg.t[:, i * 128:(i + 1) * 128], identb.t[:], [yg.b, identb.b], [p.b])
                            CP("act", ybT.t[:, :, j * CH:(j + 1) * CH], p.t[:, 0:512].rearrange("p (i t) -> p i t", i=4), [p.b], [ybT.b])
                    add_stage(l, P_AZ, st_az)
                    add_branch_merge(l, 0, True, c0)

                if en_ml:
                    def st_mq(sl, c0=c0):
                        for h in range(4):
                            pq = pf.get()
                            proj_feat(sl, h * 128, 128, c0, 4, pq)
                            CP("act", mqT.t[:, h, :], pq.t[:, :], [pq.b], [mqT.b])
                    add_stage(l, P_MQ, st_mq)

                    def st_mk(_sl, c0=c0):
                        for j in range(4):
                            DMA(mktok.t[:, j, :], KTOK[c0 + j], [ktok_buf[c0 + j]], [mktok.b])
                            DMA(mvaug.t[:, j, :, 0:128], VTOK[c0 + j].rearrange("p (h e) -> p h e", h=4), [vtok_buf[c0 + j]], [mvaug.b])
                        for j in range(4):
                            p = pb.get()
                            for h in range(4):
                                TR(p.t[:, h * 128:(h + 1) * 128], mktok.t[:, j, h * 128:(h + 1) * 128], identb.t[:], [mktok.b, identb.b], [p.b])
                            CP("dve", mkT.t[:, :, j * CH:(j + 1) * CH], p.t[:, 0:512].rearrange("p (h t) -> p h t", h=4), [p.b], [mkT.b])
                    add_stage(l, None, st_mk)

                    def st_mo(sl, c0=c0):
                        for j in range(4):
                            pz = pf.get()
                            proj_tok(sl, 0, 512, c0 + j, pz)
                            ACT(gate_tok[0].t[:, j, :], pz.t[:, :], AF.Sigmoid, [pz.b], [gate_tok[0].b])
                    add_stage(l, P_MO, st_mo)

                    def st_mz(sl, c0=c0, m=m):
                        for j in range(4):
                            pz = pf.get()
                            proj_tok(sl, 0, 512, c0 + j, pz)
                            ACT(gate_tok[1].t[:, j, :], pz.t[:, :], AF.Silu, [pz.b], [gate_tok[1].b])
                        for j in range(4):
                            c = c0 + j
                            tok = slice(j * CH, (j + 1) * CH)
                            DMA(mCb[1].t[:, :], BSTM[c], [bstm_buf[c]], [mCb[1].b])
                            ps = pf.get()
                            for h in range(4):
                                MM(ps.t[:, h * 128:(h + 1) * 128], mkT.t[:, h, tok], mqT.t[:, h, tok], True, True, [mkT.b, mqT.b], [ps.b])
                            hsum = f32r.get()
                            g01 = f32r.get()
                            TT("pool", g01.t[:, :], gate_tok[0].t[:, j, :], gate_tok[1].t[:, j, :], ALU.mult, [gate_tok[0].b], [g01.b])
                            TT("pool", g01.t[:, :], g01.t[:, :], mng_bc.t[:, :], ALU.mult, [g01.b, mng_bc.b], [g01.b])
                            fwd_vw = {}
                            for d in range(2):
                                ts_ap = TSF.t[:, j, :] if d == 0 else TSBm.t[:, j, :]
                                ts_rd = [tsf_b] if d == 0 else [TSBm.b]
                                scm = b16r.get()
                                TT("dve", scm.t[:, :], ps.t[:, :], trib.t[:, d, :], ALU.mult, [ps.b, trib.b], [scm.b])
                                vw = b16r.get(); wkb = b16r.get()
                                if d == 0:
                                    fwd_vw["vw"], fwd_vw["wkb"] = vw, wkb
                                TT("pool", vw.t[:, :].rearrange("p (h e) -> p h e", h=4), mvaug.t[:, j, :, 0:128],
                                   ts_ap[:, 0:4].unsqueeze(2).to_broadcast([128, 4, 128]), ALU.mult, [mvaug.b] + ts_rd, [vw.b])
                                CP("dve", wkb.t[:, 0:4], ts_ap[:, 0:4], ts_rd, [wkb.b])
                                pnum, pint = pacc[0], pacc[1]
                                pden = pf.get()
                                for h in range(4):
                                    hs = slice(h * 128, (h + 1) * 128)
                                    MM(pnum.t[:, hs], scm.t[:, hs], vw.t[:, hs], True, True, [scm.b, vw.b], [pnum.b])
                                    MM(pden.t[:, h:h + 1], scm.t[:, hs], wkb.t[:, h:h + 1], True, True, [scm.b, wkb.b], [pden.b])
                                    MM(pint.t[:, hs], mqT.t[:, h, tok], mCb[d].t[:, hs], True, True, [mqT.b, mCb[d].b], [pint.b])
                                    MM(pden.t[:, 4 + h:5 + h], mqT.t[:, h, tok], mCb[d].t[:, 512 + h:513 + h], True, True,
                                       [mqT.b, mCb[d].b], [pden.b])
                                n1 = f32r.get(); n2 = f32r.get(); dd = st4.get()
                                u_bc = ts_ap[:, 4:8].unsqueeze(2).to_broadcast([128, 4, 128])
                                wi_bc = ts_ap[:, 8:12].unsqueeze(2).to_broadcast([128, 4, 128])
                                TT("dve", n1.t[:, :].rearrange("p (h e) -> p h e", h=4), pnum.t[:, :].rearrange("p (h e) -> p h e", h=4),
                                   u_bc, ALU.mult, [pnum.b] + ts_rd, [n1.b])
                                TT("dve", n2.t[:, :].rearrange("p (h e) -> p h e", h=4), pint.t[:, :].rearrange("p (h e) -> p h e", h=4),
                                   wi_bc, ALU.mult, [pint.b] + ts_rd, [n2.b])
                                TT("pool", n1.t[:, :], n1.t[:, :], n2.t[:, :], ALU.add, [n1.b, n2.b], [n1.b])
                                TT("dve", dd.t[:, 0:4], pden.t[:, 0:4], ts_ap[:, 4:8], ALU.mult, [pden.b] + ts_rd, [dd.b])
                                TT("dve", dd.t[:, 4:8], pden.t[:, 4:8], ts_ap[:, 8:12], ALU.mult, [pden.b] + ts_rd, [dd.b])
                                TT("dve", dd.t[:, 0:4], dd.t[:, 0:4], dd.t[:, 4:8], ALU.add, [dd.b], [dd.b])
                                ACT(dd.t[:, 0:4], dd.t[:, 0:4], AF.Abs, [dd.b], [dd.b])
                                TT("dve", dd.t[:, 0:4], dd.t[:, 0:4], ts_ap[:, 12:16], ALU.max, [dd.b] + ts_rd, [dd.b])
                                S.op("dve", (lambda dd: lambda e: e.reciprocal(out=dd.t[:, 0:4], in_=dd.t[:, 0:4]))(dd), [dd.b], [dd.b])
                                r_bc = dd.t[:, 0:4].unsqueeze(2).to_broadcast([128, 4, 128])
                                if d == 0:
                                    TT("pool", hsum.t[:, :].rearrange("p (h e) -> p h e", h=4), n1.t[:, :].rearrange("p (h e) -> p h e", h=4),
                                       r_bc, ALU.mult, [n1.b, dd.b], [hsum.b])
                                else:
                                    TT("pool", n1.t[:, :].rearrange("p (h e) -> p h e", h=4), n1.t[:, :].rearrange("p (h e) -> p h e", h=4),
                                       r_bc, ALU.mult, [n1.b, dd.b], [n1.b])
                                    TT("pool", hsum.t[:, :], hsum.t[:, :], n1.t[:, :], ALU.add, [hsum.b, n1.b], [hsum.b])
                            sq = f32r.get(); ss = st4.get()
                            ACT(sq.t[:, :], hsum.t[:, :], AF.Square, [hsum.b], [sq.b])
                            S.op("dve", (lambda sq, ss: lambda e: e.reduce_sum(out=ss.t[:, 0:4], in_=sq.t[:, :].rearrange("p (h e) -> p h e", h=4),
                                                                         axis=AX.X))(sq, ss), [sq.b], [ss.b])
                            rstd_from_sumsq(ss.t[:, 4:8], ss.t[:, 0:4], 128, [ss.b], [ss.b])
                            hn = f32r.get()
                            TT("dve", hn.t[:, :].rearrange("p (h e) -> p h e", h=4), hsum.t[:, :].rearrange("p (h e) -> p h e", h=4),
                               ss.t[:, 4:8].unsqueeze(2).to_broadcast([128, 4, 128]), ALU.mult, [hsum.b, ss.b], [hn.b])
                            yg = b16r.get()
                            TT("pool", yg.t[:, :], hn.t[:, :], g01.t[:, :], ALU.mult, [hn.b, g01.b], [yg.b])
                            p = pb.get()
                            for i in range(4):
                                TR(p.t[:, i * 128:(i + 1) * 128], yg.t[:, i * 128:(i + 1) * 128], identb.t[:], [yg.b, identb.b], [p.b])
                            CP("act", ybT.t[:, :, tok], p.t[:, 0:512].rearrange("p (i t) -> p i t", i=4), [p.b], [ybT.b])
                            mlstm_state_update(0, j, TSF.t[:, j, :], [tsf_b], pre=fwd_vw)
                            CP("act", mCb[0].t[:, :], mC[0].t[:, :], [mC[0].b], [mCb[0].b])
                    add_stage(l, P_MZ, st_mz)
                    add_branch_merge(l, 1, not en_att, c0)

                if en_ssd:
                    def st_sbc(sl, c0=c0):
                        for j in range(4):
                            DMA(sxtok.t[:, j, :], XTOK[c0 + j], [xtok_buf[c0 + j]], [sxtok.b])
                        ssd_conv(l, None, sl, c0, nch, [4, 5])
                        for which in range(2):
                            for g in range(2):
                                gs_ = slice(g * 64, (g + 1) * 64)
                                CP("pool", szero.t[gs_, which * 2 + g, :], sconv.t[gs_, 4 + which, :], [sconv.b], [szero.b])
                        for j in range(4):
                            ssd_tokmajor(j, do_x=False)
                    add_stage(l, P_SBC, st_sbc)

                    def st_sz(sl, c0=c0, m=m):
                        for j in range(4):
                            pz = pf.get()
                            proj_tok(sl, 0, 512, c0 + j, pz)
                            ACT(gate_tok[0].t[:, j, :], pz.t[:, :], AF.Silu, [pz.b], [gate_tok[0].b])
                        for j in range(4):
                            if "sz" in KSKIP:
                                break
                            c = c0 + j
                            tok = slice(j * CH, (j + 1) * CH)
                            if "cDma" not in KSKIP:
                                DMA(sSb[1].t[:, :], BSTS[c], [bsts_buf[c]], [sSb[1].b])
                            pcb = pf.get()
                            for g in range(2):
                                if "cPcb" in KSKIP:
                                    break
                                gs_ = slice(g * 64, (g + 1) * 64)
                                MM(pcb.t[:, g * 128:(g + 1) * 128], szero.t[:, g, tok], sconv.t[:, 5, tok], True, True, [sconv.b, szero.b], [pcb.b])
                            py = pf.get()
                            toff = []
                            for d in range(2):
                                ts_ap = TSF.t[:, j, :] if d == 0 else TSBm.t[:, j, :]
                                ts_rd = [tsf_b] if d == 0 else [TSBm.b]
                                acs_hi, acs_lo = acs_hl[d]
                                if d == 0:
                                    cbm = f32r.get()
                                    CP("act", cbm.t[:, 0:256], pcb.t[:, 0:256], [pcb.b], [cbm.b])
                                for h in range(8):
                                    if "cSel" in KSKIP:
                                        break
                                    pe_ = pacc[h // 4]
                                    hb_ = slice((h % 4) * 128, (h % 4 + 1) * 128)
                                    MM(pe_.t[:, hb_], selb.t[sD, h, :], acs_hi.t[sD, tok], h % 4 == 0, False, [selb.b, acs_hi.b], [pe_.b], skip=True)
                                    MM(pe_.t[:, hb_], selb.t[sD, h, :], acs_lo.t[sD, tok], False, False, [selb.b, acs_lo.b], [pe_.b], skip=True)
                                    MM(pe_.t[:, hb_], identb.t[:], negm.t[:, d, :], False, True, [identb.b, negm.b], [pe_.b], skip=True)
                                mts = []
                                if "cA" in KSKIP:
                                    continue
                                for g in range(2):
                                    lt = f32r.get()
                                    for e_ in range(4):
                                        h = g * 4 + e_
                                        ACT(lt.t[:, e_ * 128:(e_ + 1) * 128], pacc[g].t[:, e_ * 128:(e_ + 1) * 128], AF.Exp, [pacc[g].b] + ts_rd, [lt.b],
                                            bias=ts_ap[:, 20 + h:21 + h])
                                    mt = b16r.get()
                                    TT("dve" if g == 0 else "pool", mt.t[:, :].rearrange("p (h e) -> p h e", h=4), lt.t[:, :].rearrange("p (h e) -> p h e", h=4),
                                       cbm.t[:, g * 128:(g + 1) * 128].unsqueeze(1).to_broadcast([128, 4, 128]), ALU.mult, [lt.b, cbm.b], [mt.b])
                                    mts.append(mt)
                                if "cC" in KSKIP:
                                    continue
                                poff = pf.get()
                                for g in range(2):
                                    gs_ = slice(g * 64, (g + 1) * 64)
                                    MM(poff.t[:, g * 256:(g + 1) * 256], szero.t[:, 2 + g, tok], sSb[d].t[:, :], True, True, [szero.b, sSb[d].b], [poff.b])
                                for h in range(8):
                                    MM(py.t[:, h * 64:(h + 1) * 64], mts[h // 4].t[:, (h % 4) * 128:(h % 4 + 1) * 128], sxtok.t[:, j, h * 64:(h + 1) * 64],
                                       d == 0 and h == 0, d == 1, [mts[h // 4].b, sxtok.b], [py.b], skip=True)
                                to = f32r.get()
                                TT("dve", to.t[:, :].rearrange("p (h e) -> p h e", h=8), poff.t[:, :].rearrange("p (h e) -> p h e", h=8),
                                   ts_ap[:, 36:44].unsqueeze(2).to_broadcast([128, 8, 64]), ALU.mult, [poff.b] + ts_rd, [to.b])
                                toff.append(to)
                            if "cA" in KSKIP or "cC" in KSKIP or "cD" in KSKIP:
                                continue
                            y = f32r.get()
                            TT("dve", y.t[:, :], py.t[:, :], toff[0].t[:, :], ALU.add, [py.b, toff[0].b], [y.b])
                            TT("pool", y.t[:, :], y.t[:, :], toff[1].t[:, :], ALU.add, [y.b, toff[1].b], [y.b])
                            xs = toff[0]
                            TT("pool", xs.t[:, :].rearrange("p (h e) -> p h e", h=8), sxtok.t[:, j, :].rearrange("p (h e) -> p h e", h=8),
                               dsk_bc.t[:, 0:8].unsqueeze(2).to_broadcast([128, 8, 64]), ALU.mult, [sxtok.b, dsk_bc.b], [xs.b])
                            TT("pool", y.t[:, :], y.t[:, :], xs.t[:, :], ALU.add, [y.b, xs.b], [y.b])
                            TT("pool", y.t[:, :], y.t[:, :], gate_tok[0].t[:, j, :], ALU.mult, [y.b, gate_tok[0].b], [y.b])
                            sq = toff[1]; ss = st4.get()
                            ACT(sq.t[:, :], y.t[:, :], AF.Square, [y.b], [sq.b, ss.b], accum=ss.t[:, 0:1])
                            rstd_from_sumsq(ss.t[:, 1:2], ss.t[:, 0:1], 512, [ss.b], [ss.b])
                            yg = b16r.get()
                            STT("dve", yg.t[:, :], y.t[:, :], ss.t[:, 1:2], sng_bc.t[:, :], ALU.mult, ALU.mult, [y.b, ss.b, sng_bc.b], [yg.b])
                            p = pb.get()
                            for i in range(4):
                                TR(p.t[:, i * 128:(i + 1) * 128], yg.t[:, i * 128:(i + 1) * 128], identb.t[:], [yg.b, identb.b], [p.b])
                            CP("act", ybT.t[:, :, tok], p.t[:, 0:512].rearrange("p (i t) -> p i t", i=4), [p.b], [ybT.b])
                            if "cE" in KSKIP:
                                continue
                            ssd_state_update(0, j, TSF.t[:, j, :], [tsf_b])
                            CP("act", sSb[0].t[:, :], sS[0].t[:, :], [sS[0].b], [sSb[0].b])
                    add_stage(l, P_SZ, st_sz)
                    add_branch_merge(l, 2, not (en_att or en_ml), c0)

                if not (en_att or en_ml or en_ssd):
                    def st_zero(_sl):
                        S.op("dve", lambda e: e.memset(mergedT.t[:], 0.0), [], [mergedT.b])
                    add_stage(l, None, st_zero)
                add_final(l, tok0, c0, nch)

        n_mt_layer = sum(sl_ // TM for sl_ in seq_lens)
        casts_per_mt = 0
        if depth > 1:
            casts_per_mt = max(len(v_) for v_ in pending_casts.values()) // max(1, n_mt_layer - 1) + 1

        def flush_casts(l):
            def f(_sl):
                while pending_casts[l]:
                    pending_casts[l].pop(0)()
            return f
        for l in range(depth):
            if l >= 1:
                add_stage(l, None, flush_casts(l))
            add_stage(l, None, (lambda l: lambda _sl: load_layer_params(l))(l))
            pos = 0
            for slen in seq_lens:
                seq_layer(l, pos, slen)
                pos += slen
        run_stages()
        import os as _os
        S.schedule(window=int(_os.environ.get('KWIN', '24')), enable=_os.environ.get('KSCHED', '1') == '1')
        S._plan()
        print('sched ok:', S.check(), {e: len(S.order[e]) for e in S.ENGS}, 'est_us', getattr(S, 'est_time', 0) / 1e3, 'busy_us', {e: round(v / 1e3) for e, v in getattr(S, 'busy', {}).items()})
        if S.tagging:
            for kk_, v_ in sorted(S.gaps.items(), key=lambda x: -x[1])[:40]:
                print('GAP', kk_, round(v_ / 1e3, 1))
        S.emit()
    return nc


_CACHE = {}


def _run(seq_lens, depth, per_core_x, weights, enable=("att", "mlstm", "ssd"), debug=(), full=False):
    key = (tuple(seq_lens), depth, tuple(enable), tuple(debug))
    if key not in _CACHE:
        _CACHE[key] = build(list(seq_lens), depth, enable, debug)
    nc = _CACHE[key]
    consts = make_consts(max(seq_lens))
    in_maps = []
    for xc in per_core_x:
        m = {"x": np.ascontiguousarray(xc, dtype=np.float32)}
        for k, v in weights.items():
            m[k] = np.ascontiguousarray(v, dtype=np.float32)
        m.update(consts)
        in_maps.append(m)
    res = run_bass_kernel_spmd(nc, in_maps, core_ids=list(range(len(per_core_x))))
    if full:
        return res.results
    return [r["y"] for r in res.results]


def kernel(x_prompt, x_sample, norm_g, w_in, q_norm_g, k_norm_g, attn_sink, w_att_out,
           mlstm_i_b, mlstm_f_b, mlstm_norm_g, w_mlstm_out, conv_w, conv_b, a_log,
           dt_bias, d_skip, ssm_norm_g, w_ssm_out, w_out):
    x_prompt = np.asarray(x_prompt, dtype=np.float32)
    x_sample = np.asarray(x_sample, dtype=np.float32)
    weights = dict(norm_g=norm_g, w_in=w_in, q_norm_g=q_norm_g, k_norm_g=k_norm_g, attn_sink=attn_sink,
                   w_att_out=w_att_out, mlstm_i_b=mlstm_i_b, mlstm_f_b=mlstm_f_b, mlstm_norm_g=mlstm_norm_g,
                   w_mlstm_out=w_mlstm_out, conv_w=conv_w, conv_b=conv_b, a_log=a_log, dt_bias=dt_bias,
                   d_skip=d_skip, ssm_norm_g=ssm_norm_g, w_ssm_out=w_ssm_out, w_out=w_out)
    weights = {k: np.asarray(v, dtype=np.float32) for k, v in weights.items()}
    depth = weights["w_in"].shape[0]
    nb, sp = x_prompt.shape[0], x_prompt.shape[1]
    ns, ss = x_sample.shape[0], x_sample.shape[1]
    ppc, spc = nb // NCORES, ns // NCORES
    seq_lens = [sp] * ppc + [ss] * spc
    per_core = []
    for c in range(NCORES):
        parts = [x_prompt[c * ppc + i] for i in range(ppc)] + [x_sample[c * spc + i] for i in range(spc)]
        per_core.append(np.concatenate(parts, axis=0))
    outs = _run(seq_lens, depth, per_core, weights)
    y_prompt = np.empty_like(x_prompt)
    y_sample = np.empty_like(x_sample)
    for c in range(NCORES):
        o = outs[c]
        pos = 0
        for i in range(ppc):
            y_prompt[c * ppc + i] = o[pos:pos + sp]; pos += sp
        for i in range(spc):
            y_sample[c * spc + i] = o[pos:pos + ss]; pos += ss
    return (y_prompt, y_sample)
```

```python
import contextlib
import math
import numpy as np
import concourse.bass as bass
import concourse.mybir as mybir
from concourse.bass_utils import run_bass_kernel_spmd

F32 = mybir.dt.float32
BF16 = mybir.dt.bfloat16
AF = mybir.ActivationFunctionType
ALU = mybir.AluOpType
AX = mybir.AxisListType

D = 1024
NCORES = 8
TM = 512
CH = 128
NPIECE = 25
ROPE_THETA = 500000.0
EPS = 1e-6


class Buf:
    __slots__ = ("name", "last_w", "readers")

    def __init__(self, name=""):
        self.name = name
        self.last_w = None
        self.readers = []


class Sched:
    ENGS = ("pe", "act", "dve", "pool", "sp")
    NPOOL = 4

    def __init__(self, nc, n_dma_sems=24):
        self.nc = nc
        self.nodes = []
        self.n_dma_sems = n_dma_sems
        self.order = None
        import os as _os2
        self.tagging = bool(_os2.environ.get('KTAG'))
        self.gaps = {}

    def _add(self, eng, fn, reads, writes, dma, cost):
        deps = set()
        for b in reads:
            if b.last_w is not None:
                deps.add(b.last_w)
        for b in writes:
            if b.last_w is not None:
                deps.add(b.last_w)
            deps.update(b.readers)
        gid = len(self.nodes)
        tag = ""
        if self.tagging:
            import sys as _sys
            f = _sys._getframe(2)
            names = []
            while f is not None and len(names) < 3:
                nm = f.f_code.co_name
                if nm not in ("op", "dma", "ACT", "TT", "TSC", "STT", "CP", "MM", "TR", "DMA", "SCAN", "RECIP", "<lambda>", "proj_feat", "proj_tok"):
                    names.append(nm)
                f = f.f_back
            tag = "/".join(names[:2])
        self.nodes.append(dict(eng=eng, fn=fn, dma=dma, deps=deps, cost=cost, tag=tag))
        for b in reads:
            b.readers.append(gid)
        for b in writes:
            b.last_w = gid
            b.readers = []
        return gid

    def op(self, eng, fn, reads=(), writes=(), cost=300.0):
        return self._add(eng, fn, reads, writes, False, cost)

    def dma(self, eng, fn, reads=(), writes=(), cost=3000.0):
        return self._add(eng, fn, reads, writes, True, cost)

    def schedule(self, window=24, enable=True):
        nodes = self.nodes
        per = {e: [] for e in self.ENGS}
        for g, n in enumerate(nodes):
            per[n["eng"]].append(g)
        if not enable:
            self.order = per
            return
        ptr = {e: 0 for e in self.ENGS}
        done = {e: [False] * len(per[e]) for e in self.ENGS}
        finish = [None] * len(nodes)
        t_eng = {e: 0.0 for e in self.ENGS}
        new = {e: [] for e in self.ENGS}
        remaining = len(nodes)
        LAT = 250.0
        W = {e: window for e in self.ENGS}
        W["sp"] = 6
        while remaining:
            best = None
            for e in self.ENGS:
                lst = per[e]
                p = ptr[e]
                while p < len(lst) and done[e][p]:
                    p += 1
                ptr[e] = p
                if p >= len(lst):
                    continue
                cnt = 0
                q = p
                cand = None
                while q < len(lst) and cnt < W[e]:
                    if not done[e][q]:
                        cnt += 1
                        g = lst[q]
                        nd = nodes[g]
                        ready = t_eng[e]
                        ok = True
                        for d in nd["deps"]:
                            f = finish[d]
                            if f is None:
                                ok = False
                                break
                            if nodes[d]["eng"] != e or nodes[d]["dma"]:
                                f += LAT
                            if f > ready:
                                ready = f
                        if ok:
                            key = (ready, q)
                            if cand is None or key < cand[0]:
                                cand = (key, q, g, ready)
                            if ready <= t_eng[e]:
                                break
                    q += 1
                if cand is not None:
                    if best is None or (cand[3], cand[2]) < (best[3], best[2]):
                        best = (e, cand[1], cand[2], cand[3])
            assert best is not None, "scheduler stuck"
            e, q, g, start = best
            nd = nodes[g]
            if self.tagging and e == "pe" and start > t_eng[e] + 500.0:
                dmax = max(nd["deps"], key=lambda d: finish[d])
                key = (nd["tag"], nodes[dmax]["eng"], nodes[dmax]["tag"])
                self.gaps[key] = self.gaps.get(key, 0.0) + (start - t_eng[e])
            if nd["dma"]:
                finish[g] = start + nd["cost"]
                t_eng[e] = start + 60.0
            else:
                finish[g] = start + nd["cost"]
                t_eng[e] = finish[g]
            done[e][q] = True
            new[e].append(g)
            remaining -= 1
        self.order = new
        self.est_time = max(f for f in finish if f is not None)
        self.busy = {e: sum(nodes[g]['cost'] for g in new[e] if not nodes[g]['dma']) for e in self.ENGS}

    def _plan(self):
        nodes = self.nodes
        pos = {}
        for e in self.ENGS:
            for i, g in enumerate(self.order[e]):
                pos[g] = i
        npool = self.NPOOL
        nsp = self.n_dma_sems - npool
        rr = {"sp": 0, "pool": 0}
        dma_val = [0] * self.n_dma_sems
        sem_of = {}
        for e in self.ENGS:
            for g in self.order[e]:
                if nodes[g]["dma"]:
                    if e == "pool":
                        s = nsp + rr["pool"]; rr["pool"] = (rr["pool"] + 1) % npool
                    else:
                        s = rr["sp"]; rr["sp"] = (rr["sp"] + 1) % nsp
                    prev = dma_val[s]
                    dma_val[s] += 16
                    sem_of[g] = (s, dma_val[s], prev)
        self.dma_val = dma_val
        flag = [False] * len(nodes)
        plan = {e: [] for e in self.ENGS}
        for e in self.ENGS:
            waited_c = {}
            waited_d = {}
            for g in self.order[e]:
                nd = nodes[g]
                waits = []
                deps = list(nd["deps"])
                for d in deps:
                    dn = nodes[d]
                    if dn["dma"]:
                        s, v, _ = sem_of[d]
                        if waited_d.get(s, 0) < v:
                            waited_d[s] = v
                            waits.append(("d", s, v))
                    else:
                        pe_ = dn["eng"]
                        if pe_ == e and e in ("pe", "sp"):
                            continue
                        if waited_c.get(pe_, -1) < pos[d]:
                            waited_c[pe_] = pos[d]
                            flag[d] = True
                            waits.append(("c", pe_, d))
                if nd["dma"]:
                    s, v, prev = sem_of[g]
                    if prev > 0 and waited_d.get(s, 0) < prev:
                        waited_d[s] = prev
                        waits.append(("d", s, prev))
                plan[e].append((g, waits))
        counts = {}
        for e in self.ENGS:
            c = 0
            for g in self.order[e]:
                if (not nodes[g]["dma"]) and flag[g]:
                    c += 1
                counts[g] = c
        self.plan, self.flag, self.counts, self.sem_of = plan, flag, counts, sem_of

    def check(self):
        nodes = self.nodes
        ptr = {e: 0 for e in self.ENGS}
        csem = {e: 0 for e in self.ENGS}
        dsem = [0] * self.n_dma_sems
        while True:
            prog = False
            for e in self.ENGS:
                pl = self.plan[e]
                while ptr[e] < len(pl):
                    g, waits = pl[ptr[e]]
                    ok = True
                    for w in waits:
                        if w[0] == "c":
                            if csem[w[1]] < self.counts[w[2]]:
                                ok = False
                        elif dsem[w[1]] < w[2]:
                            ok = False
                    if not ok:
                        break
                    if nodes[g]["dma"]:
                        dsem[self.sem_of[g][0]] += 16
                    elif self.flag[g]:
                        csem[e] += 1
                    ptr[e] += 1
                    prog = True
            if all(ptr[e] == len(self.plan[e]) for e in self.ENGS):
                return True
            if not prog:
                for e in self.ENGS:
                    if ptr[e] < len(self.plan[e]):
                        print("STUCK", e, ptr[e], "/", len(self.plan[e]), self.plan[e][ptr[e]][1])
                return False

    def emit(self, final_wait_eng="sp"):
        nc = self.nc
        nodes = self.nodes
        with contextlib.ExitStack() as st:
            csem = {e: st.enter_context(nc.semaphore("cs_" + e)) for e in self.ENGS}
            dsem = [st.enter_context(nc.semaphore("ds_%d" % i)) for i in range(self.n_dma_sems)]
            final = [(i, v) for i, v in enumerate(self.dma_val) if v > 0]
            block = st.enter_context(nc.Block())
            engobj = {"pe": "tensor", "act": "scalar", "dve": "vector", "pool": "gpsimd", "sp": "sync"}

            def make(e):
                def body(eng):
                    for g, waits in self.plan[e]:
                        for w in waits:
                            if w[0] == "c":
                                eng.wait_ge(csem[w[1]], self.counts[w[2]])
                            else:
                                eng.wait_ge(dsem[w[1]], w[2])
                        ins = nodes[g]["fn"](eng)
                        if nodes[g]["dma"]:
                            ins.then_inc(dsem[self.sem_of[g][0]], 16)
                        elif self.flag[g]:
                            ins.then_inc(csem[e], 1)
                    if e == final_wait_eng:
                        for (i, v) in final:
                            eng.wait_ge(dsem[i], v)
                return body

            for e in self.ENGS:
                if self.plan[e] or e == final_wait_eng:
                    getattr(block, engobj[e])(make(e))


class Tl:
    __slots__ = ("t", "b")

    def __init__(self, t, name=""):
        self.t = t
        self.b = Buf(name)


def make_consts(smax):
    c = {}
    c["c_ident"] = np.eye(128, dtype=np.float32)
    s = np.arange(128)[:, None]
    t = np.arange(128)[None, :]
    tri = np.zeros((128, 2, 512), np.float32)
    tri[:, 0, :] = np.tile((s <= t).astype(np.float32), (1, 4))
    tri[:, 1, :] = np.tile((s >= t).astype(np.float32), (1, 4))
    c["c_tri"] = tri
    f = np.arange(128)
    d = f % 64
    pos = np.arange(smax, dtype=np.float32)
    rope = np.zeros((128, 2, smax), np.float32)
    rope[:, 0, :] = 1.0
    inv_freq = (ROPE_THETA ** (-np.arange(8, dtype=np.float32) * 2.0 / 16.0)).astype(np.float32)
    for ff in range(128):
        if d[ff] < 16:
            ang = pos * inv_freq[d[ff] % 8]
            rope[ff, 0, :] = np.cos(ang)
            rope[ff, 1, :] = np.sin(ang)
    c["c_rope"] = rope
    rot = np.zeros((128, 128), np.float32)
    for ff in range(128):
        if d[ff] < 8:
            rot[ff + 8, ff] = -1.0
        elif d[ff] < 16:
            rot[ff - 8, ff] = 1.0
    c["c_rot"] = rot
    blk = np.zeros((128, 128), np.float32)
    blk[:64, :64] = 1.0 / 64
    blk[64:, 64:] = 1.0 / 64
    c["c_blk"] = blk
    sel = np.zeros((8, 8, 128), np.float32)
    for h in range(8):
        sel[h, h, :] = 1.0
    c["c_sel"] = sel
    rst = np.ones((8, 512), np.float32)
    rst[:, ::128] = 0.0
    c["c_reset"] = rst
    return c


O_AQ, O_AK, O_AV, O_AZ = 0, 512, 640, 768
O_MQ, O_MK, O_MV, O_MO = 1280, 1792, 2304, 2816
O_MI, O_MF, O_MZ = 3328, 3336, 3344
O_SX, O_SB, O_SC, O_SDT, O_SZ = 3856, 4368, 4496, 4624, 4640
O_G = 5152
P_AQ, P_KV, P_AZ, P_MQ, P_MK, P_MV, P_MO, P_MZ, P_SX, P_SBC, P_SZ = range(11)
P_G0 = 11
P_WO0, P_WO1 = 23, 24


def build(seq_lens, depth, enable=("att", "mlstm", "ssd"), debug=()):
    ntok = sum(seq_lens)
    smax = max(seq_lens)
    nc = bass.Bass("TRN2", target_bir_lowering=False)
    S = Sched(nc)
    es = contextlib.ExitStack()

    def din(name, shape, dt=F32):
        return nc.dram_tensor(name, list(shape), dt, kind="ExternalInput").ap()

    x_in = din("x", [ntok, D])
    y_out = nc.dram_tensor("y", [ntok, D], F32, kind="ExternalOutput").ap()
    W = dict(
        norm_g=din("norm_g", [depth, D]), w_in=din("w_in", [depth, D, 8224]),
        q_norm_g=din("q_norm_g", [depth, 64]), k_norm_g=din("k_norm_g", [depth, 64]),
        attn_sink=din("attn_sink", [depth, 8]), w_att_out=din("w_att_out", [depth, 512, D]),
        mlstm_i_b=din("mlstm_i_b", [depth, 2, 4]), mlstm_f_b=din("mlstm_f_b", [depth, 2, 4]),
        mlstm_norm_g=din("mlstm_norm_g", [depth, 512]), w_mlstm_out=din("w_mlstm_out", [depth, 512, D]),
        conv_w=din("conv_w", [depth, 5, 768]), conv_b=din("conv_b", [depth, 768]),
        a_log=din("a_log", [depth, 2, 8]), dt_bias=din("dt_bias", [depth, 2, 8]),
        d_skip=din("d_skip", [depth, 8]), ssm_norm_g=din("ssm_norm_g", [depth, 512]),
        w_ssm_out=din("w_ssm_out", [depth, 512, D]), w_out=din("w_out", [depth, D, D]),
    )
    C = dict(c_ident=din("c_ident", [128, 128]), c_tri=din("c_tri", [128, 2, 512]),
             c_rope=din("c_rope", [128, 2, smax]), c_rot=din("c_rot", [128, 128]),
             c_blk=din("c_blk", [128, 128]), c_sel=din("c_sel", [8, 8, 128]),
             c_reset=din("c_reset", [8, 512]))
    WS = nc.dram_tensor("ws_bf16", [depth, NPIECE, 128, 4096], BF16, kind="Internal").ap()
    ws_buf = [[[] for p in range(NPIECE)] for l in range(depth)]
    nscr = max(1, min(2, depth - 1))
    YS = [nc.dram_tensor("yscr%d" % i, [ntok, D], F32, kind="Internal").ap() for i in range(nscr)]
    ys_buf = [[[Buf("yscr"), Buf("yscr")] for c in range(ntok // CH)] for i in range(nscr)]
    nchmax = smax // CH
    BSTM = nc.dram_tensor("bst_m", [nchmax, 128, 516], BF16, kind="Internal").ap()
    BSTS = nc.dram_tensor("bst_s", [nchmax, 128, 256], BF16, kind="Internal").ap()
    KTOK = nc.dram_tensor("ktok", [nchmax, 128, 512], BF16, kind="Internal").ap()
    VTOK = nc.dram_tensor("vtok", [nchmax, 128, 512], BF16, kind="Internal").ap()
    XTOK = nc.dram_tensor("xtok", [nchmax, 128, 512], BF16, kind="Internal").ap()
    ktok_buf = [Buf("ktok%d" % i) for i in range(nchmax)]
    vtok_buf = [Buf("vtok%d" % i) for i in range(nchmax)]
    xtok_buf = [Buf("xtok%d" % i) for i in range(nchmax)]
    bstm_buf = [Buf("bstm%d" % i) for i in range(nchmax)]
    bsts_buf = [Buf("bsts%d" % i) for i in range(nchmax)]

    dbg_done = {}
    import os
    KSKIP = set(os.environ.get("KSKIP", "").split(","))

    def dbg(name, ap, bufs, dt=F32):
        if name not in debug or name in dbg_done:
            return
        dbg_done[name] = True
        shp = list(ap.shape)
        o = nc.dram_tensor("dbg_" + name, shp, dt, kind="ExternalOutput").ap()
        S.dma("sp", lambda e: e.dma_start(out=o, in_=ap), bufs, [])

    def sb(name, shape, dt=F32):
        return Tl(es.enter_context(nc.sbuf_tensor(name, list(shape), dt)), name)

    def psum(name, shape, dt=F32):
        return Tl(es.enter_context(nc.psum_tensor(name, list(shape), dt)), name)

    def fsz(ap):
        n = 1
        for s_ in ap.shape[1:]:
            n *= s_
        return n

    def ecost(eng, ap, mult=1.0):
        n = fsz(ap)
        if eng == "act":
            return 220.0 + 0.85 * n
        if eng == "dve":
            return 60.0 + 1.3 * n * mult
        return 100.0 + 2.6 * n * mult

    def ACT(out, in_, func, rd, wr, bias=None, scale=None, accum=None):
        kw = {}
        if bias is not None:
            kw["bias"] = bias
        if scale is not None:
            kw["scale"] = scale
        if accum is not None:
            kw["accum_out"] = accum
        S.op("act", lambda e: e.activation(out=out, in_=in_, func=func, **kw), rd, wr, cost=ecost("act", out))

    def TT(eng, out, in0, in1, op, rd, wr):
        S.op(eng, lambda e: e.tensor_tensor(out=out, in0=in0, in1=in1, op=op), rd, wr, cost=ecost(eng, out))

    def TSC(eng, out, in0, s1, op0, rd, wr, s2=None, op1=None):
        if op1 is None:
            S.op(eng, lambda e: e.tensor_scalar(out=out, in0=in0, scalar1=s1, scalar2=None, op0=op0), rd, wr, cost=ecost(eng, out))
        else:
            S.op(eng, lambda e: e.tensor_scalar(out=out, in0=in0, scalar1=s1, scalar2=s2, op0=op0, op1=op1), rd, wr, cost=ecost(eng, out))

    def STT(eng, out, in0, scalar, in1, op0, op1, rd, wr):
        S.op(eng, lambda e: e.scalar_tensor_tensor(out=out, in0=in0, scalar=scalar, in1=in1, op0=op0, op1=op1), rd, wr,
             cost=ecost(eng, out))

    def CP(eng, out, in_, rd, wr):
        if eng == "act":
            S.op("act", lambda e: e.copy(out=out, in_=in_), rd, wr, cost=ecost("act", out))
        else:
            S.op(eng, lambda e: e.tensor_copy(out=out, in_=in_), rd, wr, cost=ecost(eng, out, 1.4 if eng == "pool" else 1.0))

    def RECIP(out, in_, rd, wr):
        S.op("dve", lambda e: e.reciprocal(out=out, in_=in_), rd, wr, cost=100.0 + 6.6 * fsz(out))

    def MM(out, lhsT, rhs, start, stop, rd, wr, skip=False):
        n = max(fsz(out), 32)
        passes = 4 if lhsT.dtype == F32 else 1
        c = 25.0 + 0.5 * n * passes
        if skip:
            S.op("pe", lambda e: e.matmul(out, lhsT, rhs, start=start, stop=stop, skip_group_check=True), rd, wr, cost=c)
        else:
            S.op("pe", lambda e: e.matmul(out, lhsT, rhs, start=start, stop=stop), rd, wr, cost=c)

    def TR(out, in_, ident, rd, wr):
        S.op("pe", lambda e: e.transpose(out, in_, ident), rd, wr, cost=90.0)

    def DMA(out, in_, rd, wr, eng="sp", slow=False):
        nbytes = out.shape[0] * fsz(out) * (2 if out.dtype == BF16 else 4)
        c = 2000.0 + nbytes / (40.0 if eng == "pool" else 280.0)
        if slow:
            S.dma(eng, lambda e: e.dma_start(out=out, in_=in_, allow_slow_non_contiguous=True), rd, wr, cost=c)
        else:
            S.dma(eng, lambda e: e.dma_start(out=out, in_=in_), rd, wr, cost=c)

    def SCAN(out, d0, d1, init, op0, op1, rd, wr):
        S.op("dve", lambda e: e.tensor_tensor_scan(out=out, data0=d0, data1=d1, initial=init, op0=op0, op1=op1), rd, wr,
             cost=100.0 + 2.0 * fsz(out))

    class Ring:
        def __init__(self, tiles):
            self.tiles = tiles
            self.i = 0

        def get(self):
            t = self.tiles[self.i]
            self.i = (self.i + 1) % len(self.tiles)
            return t

    with es:
        f32r = Ring([sb("f32r%d" % i, [128, 512]) for i in range(7)])
        b16r = Ring([sb("b16r%d" % i, [128, 512], BF16) for i in range(8)])
        identf = sb("identf", [128, 128])
        identb = sb("identb", [128, 128], BF16)
        trib = sb("trib", [128, 2, 512], BF16)
        rotb = sb("rotb", [128, 128], BF16)
        blkb = sb("blkb", [128, 128], BF16)
        self_ = sb("sel", [40, 8, 128])
        resetm = sb("resetm", [40, 512])
        zrow = sb("zrow", [8, 1])
        DMA(identf.t[:], C["c_ident"][:, :], [], [identf.b])
        CP("dve", identb.t[:], identf.t[:], [identf.b], [identb.b])
        for half in range(2):
            stg = f32r.get()
            DMA(stg.t[:, :], C["c_tri"][:, half, :], [], [stg.b])
            CP("dve", trib.t[:, half, :], stg.t[:, :], [stg.b], [trib.b])
        stg = f32r.get()
        DMA(stg.t[:, 0:128], C["c_rot"][:, :], [], [stg.b])
        CP("dve", rotb.t[:], stg.t[:, 0:128], [stg.b], [rotb.b])
        stg = f32r.get()
        DMA(stg.t[:, 0:128], C["c_blk"][:, :], [], [stg.b])
        CP("dve", blkb.t[:], stg.t[:, 0:128], [stg.b], [blkb.b])
        DMA(self_.t[32:40, :, :], C["c_sel"][:, :, :], [], [self_.b])
        DMA(resetm.t[32:40, :], C["c_reset"][:, :], [], [resetm.b])
        S.op("dve", lambda e: e.memset(zrow.t[:], 0.0), [], [zrow.b])
        epsc = sb("epsc", [128, 1])
        S.op("dve", lambda e: e.memset(epsc.t[:], EPS), [], [epsc.b])
        negm = sb("negm", [128, 2, 128], BF16)
        TSC("dve", negm.t[:, :, :], trib.t[:, :, 0:128], -1.0, ALU.add, [trib.b], [negm.b], s2=30000.0, op1=ALU.mult)

        pending_casts = {l: [] for l in range(depth)}
        _DMA_real = DMA

        def DMA(out, in_, rd, wr, eng="sp", slow=False, _defer=[None]):
            if _defer[0] is not None and eng == "pool":
                pending_casts[_defer[0]].append(lambda: _DMA_real(out, in_, rd, wr, eng=eng, slow=slow))
            else:
                _DMA_real(out, in_, rd, wr, eng=eng, slow=slow)
        _defer_box = DMA.__defaults__[2]
        for l in range(depth):
            _defer_box[0] = l if l >= 1 else None
            wi = W["w_in"][l].rearrange("(k p) c -> p k c", p=128)

            def wdst(p, off, n, cw=512, l=l):
                return WS[l, p].rearrange("p (k c) -> p k c", c=cw)[:, :, off:off + n]

            def cast(p, off, c0, n, l=l, wi=wi):
                b_ = Buf("ws"); ws_buf[l][p].append(b_)
                DMA(wdst(p, off, n), wi[:, :, c0:c0 + n], [], [b_], eng="pool")

            for i in range(4):
                cast(P_AQ, i * 128, O_AQ + i * 64, 64)
                cast(P_AQ, i * 128 + 64, O_AQ + (4 + i) * 64, 64)
            cast(P_KV, 0, O_AK, 256)
            for d_ in range(2):
                base_ = 256 + d_ * 72
                b0_ = Buf("ws"); ws_buf[l][P_KV].append(b0_)
                DMA(wdst(P_KV, base_, 72), wi[:, :, O_MI:O_MI + 72], [], [b0_], eng="pool")
                for (off_, c0_, n_) in ((0, O_MI + d_ * 4, 4), (32, O_MF + d_ * 4, 4), (64, O_SDT + d_ * 8, 8)):
                    b_ = Buf("ws"); ws_buf[l][P_KV].append(b_)
                    DMA(wdst(P_KV, base_ + off_, n_), wi[:, :, c0_:c0_ + n_], [b0_], [b_], eng="pool")
            cast(P_AZ, 0, O_AZ, 512)
            cast(P_MQ, 0, O_MQ, 512)
            cast(P_MK, 0, O_MK, 512)
            cast(P_MV, 0, O_MV, 512)
            cast(P_MO, 0, O_MO, 512)
            cast(P_MZ, 0, O_MZ, 512)
            cast(P_SX, 0, O_SX, 512)
            cast(P_SBC, 0, O_SB, 256)
            cast(P_SZ, 0, O_SZ, 512)
            for bi_, nm in enumerate(("w_att_out", "w_mlstm_out", "w_ssm_out")):
                src = W[nm][l].rearrange("(k p) c -> p k c", p=128)
                for q_ in range(4):
                    p = P_G0 + bi_ * 4 + q_
                    b_ = Buf("ws"); ws_buf[l][p].append(b_)
                    DMA(WS[l, p][:, 0:2048].rearrange("p (k c) -> p k c", c=256),
                        wi[:, :, O_G + bi_ * 1024 + q_ * 256:O_G + bi_ * 1024 + (q_ + 1) * 256], [], [b_], eng="pool")
                    b_ = Buf("ws"); ws_buf[l][p].append(b_)
                    DMA(WS[l, p][:, 2048:3072].rearrange("p (k c) -> p k c", c=256), src[:, :, q_ * 256:(q_ + 1) * 256], [], [b_], eng="pool")
            wo = W["w_out"][l].rearrange("(k p) c -> p k c", p=128)
            for p_, lo_ in ((P_WO0, 0), (P_WO1, 512)):
                b_ = Buf("ws"); ws_buf[l][p_].append(b_)
                DMA(wdst(p_, 0, 512), wo[:, :, lo_:lo_ + 512], [], [b_], eng="pool")

        _defer_box[0] = None
        NSLOT = 3
        wslots = [sb("wslot%d" % i, [128, 4096], BF16) for i in range(NSLOT)]

        ng_bc = sb("ng_bc", [128, D])
        gq = sb("gq", [128, 1]); gk = sb("gk", [128, 1])
        esink = sb("esink", [128, 8])
        ib = [sb("ib%d" % d, [4, 1]) for d in range(2)]
        nfb = [sb("nfb%d" % d, [4, 1]) for d in range(2)]
        mng_bc = sb("mng_bc", [128, 512]); sng_bc = sb("sng_bc", [128, 512])
        cw = sb("cw", [128, 6, 5]); cb = sb("cb", [128, 6])
        acoef = [sb("acoef%d" % d, [40, 1]) for d in range(2)]
        dtb = [sb("dtb%d" % d, [40, 1]) for d in range(2)]
        dsk_bc = sb("dsk_bc", [128, 8])

        def load_layer_params(l):
            DMA(ng_bc.t[:], W["norm_g"][l].partition_broadcast(128), [], [ng_bc.b])
            for half in range(2):
                DMA(gq.t[half * 64:(half + 1) * 64, :], W["q_norm_g"][l].rearrange("(d o) -> d o", o=1), [], [gq.b])
                DMA(gk.t[half * 64:(half + 1) * 64, :], W["k_norm_g"][l].rearrange("(d o) -> d o", o=1), [], [gk.b])
            S.op("act", lambda e: e.mul(out=gq.t[:], in_=gq.t[:], mul=0.125), [gq.b], [gq.b])
            DMA(esink.t[:], W["attn_sink"][l].partition_broadcast(128), [], [esink.b])
            ACT(esink.t[:], esink.t[:], AF.Exp, [esink.b], [esink.b])
            for d in range(2):
                DMA(ib[d].t[:], W["mlstm_i_b"][l, d].rearrange("(d o) -> d o", o=1), [], [ib[d].b])
                DMA(nfb[d].t[:], W["mlstm_f_b"][l, d].rearrange("(d o) -> d o", o=1), [], [nfb[d].b])
                S.op("act", (lambda d: lambda e: e.mul(out=nfb[d].t[:], in_=nfb[d].t[:], mul=-1.0))(d), [nfb[d].b], [nfb[d].b])
                DMA(acoef[d].t[32:40, :], W["a_log"][l, d].rearrange("(d o) -> d o", o=1), [], [acoef[d].b])
                ACT(acoef[d].t[32:40, :], acoef[d].t[32:40, :], AF.Exp, [acoef[d].b], [acoef[d].b])
                S.op("act", (lambda d: lambda e: e.mul(out=acoef[d].t[32:40, :], in_=acoef[d].t[32:40, :], mul=-1.0))(d), [acoef[d].b], [acoef[d].b])
                DMA(dtb[d].t[32:40, :], W["dt_bias"][l, d].rearrange("(d o) -> d o", o=1), [], [dtb[d].b])
            DMA(mng_bc.t[:], W["mlstm_norm_g"][l].partition_broadcast(128), [], [mng_bc.b])
            DMA(sng_bc.t[:], W["ssm_norm_g"][l].partition_broadcast(128), [], [sng_bc.b])
            for ti in range(6):
                DMA(cw.t[:, ti, :], W["conv_w"][l][:, ti * 128:(ti + 1) * 128].rearrange("k p -> p k"), [], [cw.b], slow=True)
                DMA(cb.t[:, ti:ti + 1], W["conv_b"][l][ti * 128:(ti + 1) * 128].rearrange("(p o) -> p o", o=1), [], [cb.b])
            DMA(dsk_bc.t[:], W["d_skip"][l].partition_broadcast(128), [], [dsk_bc.b])

        hT = sb("hT", [128, 8, 8 * CH], BF16)
        hT_b = [Buf("hT%d" % i) for i in range(8)]
        hslot_chunk = [None] * 8
        xring = Ring([sb("xt%d" % i, [128, D]) for i in range(2)])
        hbring = Ring([sb("hb%d" % i, [128, D], BF16) for i in range(2)])
        st4 = Ring([sb("st4_%d" % i, [128, 8]) for i in range(6)])

        pf = Ring([psum("pf%d" % i, [128, 512]) for i in range(5)])
        pacc = [psum("pacc%d" % i, [128, 512]) for i in range(2)]
        pb = Ring([psum("pb%d" % i, [128, 1024], BF16) for i in range(1)])

        mergedT = sb("mergedT", [128, 8, TM])
        GT = sb("GT", [128, 4096], BF16)

        class View:
            def __init__(self, ap, b):
                self.t = ap
                self.b = b
        mergedTb = View(GT.t[:, :].rearrange("p (k t) -> p k t", k=8), GT.b)
        gate_tok = [View(GT.t[:, i * 2048:(i + 1) * 2048].rearrange("p (j c) -> p j c", j=4), GT.b) for i in range(2)]
        ybT = sb("ybT", [128, 4, TM], BF16)
        rope_t = sb("rope_t", [128, 2, TM])
        kT = sb("kT", [128, 6 * CH], BF16)
        vaug = sb("vaug", [128, 6, 2, 65], BF16)
        mqT = sb("mqT", [128, 4, TM], BF16)
        qT = mqT
        mkT = sb("mkT", [128, 4, TM], BF16)
        mvaug = sb("mvaug", [128, 4, 4, 129], BF16)
        mktok = sb("mktok", [128, 4, 512], BF16)
        mC = [sb("mC%d" % d, [128, 516]) for d in range(2)]
        mCb = [sb("mCb%d" % d, [128, 516], BF16) for d in range(2)]
        srawr = Ring([sb("sraw%d" % i, [128, TM + 4]) for i in range(2)])
        sconv = sb("sconv", [128, 6, TM], BF16)
        sxtok = sb("sxtok", [128, 4, 512], BF16)
        szero = sb("szero", [128, 4, TM], BF16)
        S.op("pool", lambda e: e.memset(szero.t[:], 0.0), [], [szero.b])
        sbtok = sb("sbtok", [128, 4, 128], BF16)
        sS = [sb("sS%d" % d, [128, 256]) for d in range(2)]
        sSb = [sb("sSb%d" % d, [128, 256], BF16) for d in range(2)]
        RT = [sb("rt%d" % i, [40, 512]) for i in range(8)]
        for rt_ in RT:
            S.op("pool", (lambda rt_: lambda e: e.memset(rt_.t[:], 0.0))(rt_), [], [rt_.b])
        carryB = [sb("carryB%d" % d, [4, 1]) for d in range(2)]
        carryM = [sb("carryM%d" % d, [4, 1]) for d in range(2)]
        mprev = sb("mprev", [4, 4])
        NTS = 52
        TSF = sb("TSF", [128, 4, NTS])
        TSBm = sb("TSBm", [128, 4, NTS])
        ACSF = sb("ACSF", [40, TM])
        ACSBm = sb("ACSBm", [40, TM])
        acs_hl = {0: (sb("acsfh", [40, TM], BF16), sb("acsfl", [40, TM], BF16)),
                  1: (sb("acsbh", [40, TM], BF16), sb("acsbl", [40, TM], BF16))}
        selb = sb("selb", [40, 8, 128], BF16)
        CP("dve", selb.t[32:40, :, :], self_.t[32:40, :, :], [self_.b], [selb.b])

        def mk_hilo(d, src):
            hi, lo = acs_hl[d]
            CP("act", hi.t[32:40, :], src.t[32:40, :], [src.b], [hi.b])
            TT("dve", lo.t[32:40, :], src.t[32:40, :], hi.t[32:40, :], ALU.subtract, [src.b, hi.b], [lo.b])
        nmmax = smax // TM
        TSD = nc.dram_tensor("tsd", [nmmax, 128, 4 * NTS], F32, kind="Internal").ap()
        ACSD = nc.dram_tensor("acsd", [nmmax, 8, TM], F32, kind="Internal").ap()
        tsd_buf = [Buf("tsd%d" % i) for i in range(nmmax)]
        acsd_buf = [Buf("acsd%d" % i) for i in range(nmmax)]

        stages = []

        def add_stage(l, piece, fn):
            stages.append((l, piece, fn))

        def run_stages():
            loads = [i for i, s_ in enumerate(stages) if s_[1] is not None]
            slot_of = {}
            nxt = 0
            PRE = 2
            for i, (l, piece, fn) in enumerate(stages):
                while nxt < len(loads) and (nxt < PRE or loads[nxt - PRE] <= i):
                    j = loads[nxt]
                    sl = wslots[nxt % NSLOT]
                    lj, pj, _ = stages[j]
                    if pj == P_KV or pj == P_SBC:
                        nc_ = 400 if pj == P_KV else 256
                        DMA(sl.t[:, :].rearrange("p (k c) -> p k c", c=512)[:, :, 0:nc_],
                            WS[lj, pj].rearrange("p (k c) -> p k c", c=512)[:, :, 0:nc_], ws_buf[lj][pj], [sl.b])
                    elif P_G0 <= pj < P_WO0:
                        DMA(sl.t[:, 0:3072], WS[lj, pj][:, 0:3072], ws_buf[lj][pj], [sl.b])
                    else:
                        DMA(sl.t[:], WS[lj, pj], ws_buf[lj][pj], [sl.b])
                    slot_of[j] = sl
                    nxt += 1
                fn(slot_of.get(i))

        def src_of(l):
            return x_in if l == 0 else YS[(l - 1) % nscr]

        def src_buf(l, r0):
            return [] if l == 0 else ys_buf[(l - 1) % nscr][r0 // CH]

        def dst_of(l):
            return y_out if l == depth - 1 else YS[l % nscr]

        def dst_buf2(l, r0, half):
            return [] if l == depth - 1 else [ys_buf[l % nscr][r0 // CH][half]]

        def rstd_from_sumsq(dst, ssum, n, rd, wr, npart=128):
            TSC("dve", dst, ssum, 1.0 / n, ALU.mult, rd, wr, s2=EPS, op1=ALU.add)
            ACT(dst, dst, AF.Sqrt, wr, wr)
            S.op("dve", lambda e: e.reciprocal(out=dst, in_=dst), wr, wr)

        def ensure_h(l, tok0, c):
            sl = c % 8
            if hslot_chunk[sl] == (l, tok0, c):
                return
            hslot_chunk[sl] = (l, tok0, c)
            xt = xring.get()
            r0 = tok0 + c * CH
            DMA(xt.t[:], src_of(l)[r0:r0 + CH, :], src_buf(l, r0), [xt.b])
            st = st4.get()
            hb = hbring.get()
            ACT(hb.t[:], xt.t[:], AF.Square, [xt.b], [hb.b, st.b], accum=st.t[:, 0:1])
            rstd_from_sumsq(st.t[:, 1:2], st.t[:, 0:1], D, [st.b], [st.b])
            STT("dve", hb.t[:], xt.t[:], st.t[:, 1:2], ng_bc.t[:], ALU.mult, ALU.mult, [xt.b, st.b, ng_bc.b], [hb.b])
            p = pb.get()
            for k in range(8):
                TR(p.t[:, k * 128:(k + 1) * 128], hb.t[:, k * 128:(k + 1) * 128], identb.t[:], [hb.b, identb.b], [p.b])
            CP("act", hT.t[:, :, sl * CH:(sl + 1) * CH], p.t[:].rearrange("p (k t) -> p k t", k=8), [p.b], [hT_b[sl]])
            if c == 0:
                dbg("hT", hT.t[:, :, sl * CH:(sl + 1) * CH], [hT_b[sl]], BF16)

        def h_rhs(k, c0, n):
            s0 = c0 % 8
            assert s0 + n <= 8
            return hT.t[:, k, s0 * CH:(s0 + n) * CH]

        def h_bufs(c0, n):
            return [hT_b[(c0 + i) % 8] for i in range(n)]

        def proj_feat(sl, col0, ncols, c0, nchunks, pt, pcol0=0):
            for k in range(8):
                MM(pt.t[0:ncols, pcol0:pcol0 + nchunks * CH], sl.t[:, k * 512 + col0:k * 512 + col0 + ncols],
                   h_rhs(k, c0, nchunks), k == 0, k == 7, [sl.b] + h_bufs(c0, nchunks), [pt.b])

        def proj_tok(sl, col0, ncols, c, pt):
            s0 = c % 8
            for k in range(8):
                MM(pt.t[:, 0:ncols], hT.t[:, k, s0 * CH:(s0 + 1) * CH], sl.t[:, k * 512 + col0:k * 512 + col0 + ncols],
                   k == 0, k == 7, [sl.b, hT_b[s0]], [pt.b])

        mL = slice(0, 4)
        sD = slice(32, 40)

        def gate_rows(l, d, sl, c0, first, ts_dst, ts_bufs, acs_dst, acs_bufs):
            rev = (d == 1)

            def rv(ap2d):
                return ap2d[:, ::-1] if rev else ap2d

            pg_ = pf.get()
            proj_feat(sl, 256 + d * 72, 72, c0, 4, pg_)
            pmi = pmf = pdt = pg_
            R = RT
            IG, L1, Bp, Mg, WI, DEC = R[0], R[1], R[2], R[3], R[4], R[5]
            ACT(IG.t[mL, :], rv(pmi.t[0:4, :]), AF.Identity, [pmi.b, ib[d].b], [IG.b], bias=ib[d].t[:, 0:1])
            ACT(L1.t[mL, :], rv(pmf.t[32:36, :]), AF.Exp, [pmf.b, nfb[d].b], [L1.b], bias=nfb[d].t[:, 0:1], scale=-1.0)
            ACT(L1.t[mL, :], L1.t[mL, :], AF.Ln, [L1.b], [L1.b], bias=1.0)
            if first:
                S.op("dve", lambda e: e.memset(carryB[d].t[:], 0.0), [], [carryB[d].b])
                S.op("dve", lambda e: e.memset(carryM[d].t[:], 0.0), [], [carryM[d].b])
            SCAN(Bp.t[mL, :], L1.t[mL, :], zrow.t[0:4, 0:1].to_broadcast([4, 512]), carryB[d].t[:, 0:1], ALU.add, ALU.add,
                 [L1.b, zrow.b, carryB[d].b], [Bp.b])
            A = IG
            TT("dve", A.t[mL, :], IG.t[mL, :], Bp.t[mL, :], ALU.add, [IG.b, Bp.b], [A.b])
            SCAN(Mg.t[mL, :], A.t[mL, :], A.t[mL, :], carryM[d].t[:, 0:1], ALU.max, ALU.max, [A.b, carryM[d].b], [Mg.b])

            def r3(tl):
                return tl.t[mL, :].rearrange("p (c t) -> p c t", c=4)

            Mg3 = r3(Mg)
            CP("dve", mprev.t[:, 0:1], carryM[d].t[:, 0:1], [carryM[d].b], [mprev.b])
            CP("dve", mprev.t[:, 1:4], Mg3[:, 0:3, 127], [Mg.b], [mprev.b])
            CP("dve", carryB[d].t[:, 0:1], Bp.t[mL, 511:512], [Bp.b], [carryB[d].b])
            CP("dve", carryM[d].t[:, 0:1], Mg.t[mL, 511:512], [Mg.b], [carryM[d].b])
            mend_bc = Mg3[:, :, 127:128].to_broadcast([4, 4, 128])
            mprev_bc = mprev.t[:, :].unsqueeze(2).to_broadcast([4, 4, 128])
            U = L1
            TT("dve", r3(U), r3(Mg), mend_bc, ALU.subtract, [Mg.b, L1.b], [U.b])
            TSC("dve", U.t[mL, :], U.t[mL, :], -60.0, ALU.max, [U.b], [U.b])
            ACT(U.t[mL, :], U.t[mL, :], AF.Exp, [U.b], [U.b], scale=-1.0)
            TT("dve", r3(WI), r3(Mg), mprev_bc, ALU.subtract, [Mg.b, mprev.b], [WI.b])
            ACT(WI.t[mL, :], WI.t[mL, :], AF.Exp, [WI.b], [WI.b], scale=-1.0)
            FL = Bp
            TT("dve", FL.t[mL, :], Bp.t[mL, :], Mg.t[mL, :], ALU.subtract, [Bp.b, Mg.b], [FL.b])
            ACT(FL.t[mL, :], FL.t[mL, :], AF.Exp, [FL.b], [FL.b])
            TT("dve", r3(DEC), mprev_bc, mend_bc, ALU.subtract, [Mg.b, mprev.b], [DEC.b])
            ACT(DEC.t[mL, :], DEC.t[mL, :], AF.Exp, [DEC.b], [DEC.b])
            WK = A
            TT("dve", r3(WK), r3(A), mend_bc, ALU.subtract, [A.b, Mg.b], [WK.b])
            ACT(WK.t[mL, :], WK.t[mL, :], AF.Exp, [WK.b], [WK.b])
            DT, LDT, DA, ACS, EA = R[0], R[1], R[2], R[3], R[4]
            ACT(DT.t[sD, :], rv(pdt.t[64:72, :]), AF.Exp, [pdt.b, dtb[d].b], [DT.b], bias=dtb[d].t[sD, 0:1])
            ACT(DT.t[sD, :], DT.t[sD, :], AF.Ln, [DT.b], [DT.b], bias=1.0)
            ACT(LDT.t[sD, :], DT.t[sD, :], AF.Ln, [DT.b], [LDT.b])
            TSC("dve", DA.t[sD, :], DT.t[sD, :], acoef[d].t[sD, 0:1], ALU.mult, [DT.b, acoef[d].b], [DA.b])
            SCAN(ACS.t[sD, :], resetm.t[sD, :], DA.t[sD, :], 0.0, ALU.mult, ALU.add, [resetm.b, DA.b], [ACS.b])

            def r8(tl):
                return tl.t[sD, :].rearrange("p (c t) -> p c t", c=4)

            aend_bc = r8(ACS)[:, :, 127:128].to_broadcast([8, 4, 128])
            BL = LDT
            TT("dve", BL.t[sD, :], LDT.t[sD, :], ACS.t[sD, :], ALU.subtract, [LDT.b, ACS.b], [BL.b])
            WST = DA
            TT("dve", r8(WST), aend_bc, r8(ACS), ALU.subtract, [ACS.b, DA.b], [WST.b])
            ACT(WST.t[sD, :], WST.t[sD, :], AF.Exp, [WST.b], [WST.b])
            TT("dve", WST.t[sD, :], WST.t[sD, :], DT.t[sD, :], ALU.mult, [WST.b, DT.b], [WST.b])
            ACT(EA.t[sD, :], ACS.t[sD, :], AF.Exp, [ACS.b], [EA.b])
            CD = DT
            CP("dve", r8(CD), aend_bc, [ACS.b, WST.b, DT.b], [CD.b])
            ACT(CD.t[sD, :], CD.t[sD, :], AF.Exp, [CD.b], [CD.b])
            quants = [(WK, mL, 4, 0), (U, mL, 4, 4), (WI, mL, 4, 8), (FL, mL, 4, 12), (DEC, mL, 4, 16),
                      (BL, sD, 8, 20), (WST, sD, 8, 28), (EA, sD, 8, 36), (CD, sD, 8, 44)]
            pts = pf.get()
            if rev:
                order_ = [R[0], R[1], R[2], R[4], R[5]]
                rmap = {}
                for ti_, tl_ in enumerate(order_):
                    q2 = R[6 + (ti_ % 2)]
                    rmap[id(tl_)] = (q2, ti_)
                quants = sorted(quants, key=lambda x: rmap[id(x[0])][1])
                done_ = set()
            for qi, (q, ps_, r, off) in enumerate(quants):
                if rev:
                    q2, ti_ = rmap[id(q)]
                    if ti_ not in done_:
                        done_.add(ti_)
                        CP("dve", q2.t[0:40, :], q.t[0:40, ::-1], [q.b], [q2.b])
                    q = q2
                for j in range(4):
                    MM(pts.t[:, j * 64 + off:j * 64 + off + r], q.t[ps_, j * CH:(j + 1) * CH], identf.t[ps_, ps_],
                       True, True, [q.b, identf.b], [pts.b])
            CP("act", ts_dst, pts.t[:, 0:256].rearrange("p (j c) -> p j c", j=4)[:, :, 0:NTS], [pts.b], ts_bufs)
            if rev:
                CP("pool", acs_dst, ACS.t[sD, ::-1], [ACS.b], acs_bufs)
            else:
                CP("pool", acs_dst, ACS.t[sD, :], [ACS.b], acs_bufs)

        def ssd_conv(l, slx, slbc, c0, nch_seq, tiles):
            for ti in tiles:
                if "conv" in KSKIP:
                    break
                sl, col0 = (slx, ti * 128) if ti < 4 else (slbc, (ti - 4) * 128)
                sraw = srawr.get()
                pm = pf.get()
                proj_feat(sl, col0, 128, c0, 4, pm)
                CP("act", sraw.t[:, 2:2 + TM], pm.t[:, :], [pm.b], [sraw.b])
                ph = pf.get()
                if c0 > 0:
                    s0 = (c0 - 1) % 8
                    for k in range(8):
                        MM(ph.t[:, 0:32], sl.t[:, k * 512 + col0:k * 512 + col0 + 128], hT.t[:, k, s0 * CH + 96:s0 * CH + 128],
                           k == 0, k == 7, [sl.b, hT_b[s0]], [ph.b])
                    CP("dve", sraw.t[:, 0:2], ph.t[:, 30:32], [ph.b], [sraw.b])
                else:
                    S.op("dve", (lambda sraw: lambda e: e.memset(sraw.t[:, 0:2], 0.0))(sraw), [], [sraw.b])
                if c0 + 4 < nch_seq:
                    s0 = (c0 + 4) % 8
                    for k in range(8):
                        MM(ph.t[:, 32:64], sl.t[:, k * 512 + col0:k * 512 + col0 + 128], hT.t[:, k, s0 * CH:s0 * CH + 32],
                           k == 0, k == 7, [sl.b, hT_b[s0]], [ph.b])
                    CP("dve", sraw.t[:, TM + 2:TM + 4], ph.t[:, 32:34], [ph.b], [sraw.b])
                else:
                    S.op("dve", (lambda sraw: lambda e: e.memset(sraw.t[:, TM + 2:TM + 4], 0.0))(sraw), [], [sraw.b])
                acc = f32r.get()
                TSC("dve", acc.t[:, :], sraw.t[:, 0:TM], cw.t[:, ti, 0:1], ALU.mult, [sraw.b, cw.b, cb.b], [acc.b],
                    s2=cb.t[:, ti:ti + 1], op1=ALU.add)
                for kk in range(1, 5):
                    STT("dve", acc.t[:, :], sraw.t[:, kk:kk + TM], cw.t[:, ti, kk:kk + 1], acc.t[:, :], ALU.mult, ALU.add,
                        [sraw.b, cw.b, acc.b], [acc.b])
                ACT(sconv.t[:, ti, :], acc.t[:, :], AF.Silu, [acc.b], [sconv.b])

        def ssd_tokmajor(j, do_x=True):
            if "tokm" in KSKIP:
                return
            if do_x:
                p = pb.get()
                for ti in range(4):
                    TR(p.t[:, ti * 128:(ti + 1) * 128], sconv.t[:, ti, j * CH:(j + 1) * CH], identb.t[:], [sconv.b, identb.b], [p.b])
                CP("act", sxtok.t[:, j, :], p.t[:, 0:512], [p.b], [sxtok.b])
            if "tokb" in KSKIP:
                return
            p2 = pb.get()
            TR(p2.t[:, 0:128], sconv.t[:, 4, j * CH:(j + 1) * CH], identb.t[:], [sconv.b, identb.b], [p2.b])
            CP("act", sbtok.t[:, j, :], p2.t[:, 0:128], [p2.b], [sbtok.b])

        def ssd_state_update(d, j, ts_ap, ts_rd):
            xw = b16r.get()
            TT("pool" if d == 0 else "dve", xw.t[:, :].rearrange("p (h e) -> p h e", h=8), sxtok.t[:, j, :].rearrange("p (h e) -> p h e", h=8),
               ts_ap[:, 28:36].unsqueeze(2).to_broadcast([128, 8, 64]), ALU.mult, [sxtok.b] + ts_rd, [xw.b])
            pS = pf.get()
            for g in range(2):
                MM(pS.t[:, g * 256:(g + 1) * 256], sbtok.t[:, j, :], xw.t[:, g * 256:(g + 1) * 256], True, True, [sbtok.b, xw.b], [pS.b])
            for g in range(2):
                ps_ = slice(g * 64, (g + 1) * 64)
                TT("dve", sS[d].t[ps_, :].rearrange("p (h e) -> p h e", h=4), sS[d].t[ps_, :].rearrange("p (h e) -> p h e", h=4),
                   ts_ap[ps_, 44 + g * 4:44 + g * 4 + 4].unsqueeze(2).to_broadcast([64, 4, 64]), ALU.mult, [sS[d].b] + ts_rd, [sS[d].b])
                TT("dve", sS[d].t[ps_, :], sS[d].t[ps_, :], pS.t[ps_, g * 256:(g + 1) * 256], ALU.add, [sS[d].b, pS.b], [sS[d].b])

        def mlstm_state_update(d, j, ts_ap, ts_rd, pre=None):
            if pre is not None:
                vw, wkb = pre["vw"], pre["wkb"]
            else:
                vw = b16r.get()
                TT("pool" if d == 0 else "dve", vw.t[:, :].rearrange("p (h e) -> p h e", h=4), mvaug.t[:, j, :, 0:128],
                   ts_ap[:, 0:4].unsqueeze(2).to_broadcast([128, 4, 128]), ALU.mult, [mvaug.b] + ts_rd, [vw.b])
                wkb = b16r.get()
                CP("dve", wkb.t[:, 0:4], ts_ap[:, 0:4], ts_rd, [wkb.b])
            pC = pf.get(); pn = pf.get()
            for h in range(4):
                MM(pC.t[:, h * 128:(h + 1) * 128], mktok.t[:, j, h * 128:(h + 1) * 128], vw.t[:, h * 128:(h + 1) * 128], True, True,
                   [mktok.b, vw.b], [pC.b])
                MM(pn.t[:, h:h + 1], mktok.t[:, j, h * 128:(h + 1) * 128], wkb.t[:, h:h + 1], True, True, [mktok.b, wkb.b], [pn.b])
            dec_bc = ts_ap[:, 16:20]
            TT("dve", mC[d].t[:, 0:512].rearrange("p (h e) -> p h e", h=4), mC[d].t[:, 0:512].rearrange("p (h e) -> p h e", h=4),
               dec_bc.unsqueeze(2).to_broadcast([128, 4, 128]), ALU.mult, [mC[d].b] + ts_rd, [mC[d].b])
            TT("dve", mC[d].t[:, 512:516], mC[d].t[:, 512:516], dec_bc, ALU.mult, [mC[d].b] + ts_rd, [mC[d].b])
            TT("dve", mC[d].t[:, 0:512], mC[d].t[:, 0:512], pC.t[:, :], ALU.add, [mC[d].b, pC.b], [mC[d].b])
            TT("dve", mC[d].t[:, 512:516], mC[d].t[:, 512:516], pn.t[:, 0:4], ALU.add, [mC[d].b, pn.b], [mC[d].b])

        def mlstm_v_tok(sl, c, j):
            pv = pf.get()
            proj_tok(sl, 0, 512, c, pv)
            CP("act", mvaug.t[:, j, :, 0:128], pv.t[:, :].rearrange("p (h e) -> p h e", h=4), [pv.b], [mvaug.b])

        def add_branch_merge(l, bi, first, c0):
            for q_ in range(4):
                def st_g(sl, q_=q_):
                    for oo in range(2):
                        o = q_ * 2 + oo
                        pg = pf.get()
                        for k_ in range(8):
                            MM(pg.t[:, :], sl.t[:, k_ * 256 + oo * 128:k_ * 256 + (oo + 1) * 128], h_rhs(k_, c0, 4), k_ == 0, k_ == 7,
                               [sl.b] + h_bufs(c0, 4), [pg.b])
                        gs = f32r.get()
                        ACT(gs.t[:, :], pg.t[:, :], AF.Sigmoid, [pg.b], [gs.b])
                        pp = pf.get()
                        for k_ in range(4):
                            MM(pp.t[:, :], sl.t[:, 2048 + k_ * 256 + oo * 128:2048 + k_ * 256 + (oo + 1) * 128], ybT.t[:, k_, :], k_ == 0, k_ == 3,
                               [sl.b, ybT.b], [pp.b])
                        if first:
                            TT("dve", mergedT.t[:, o, :], gs.t[:, :], pp.t[:, :], ALU.mult, [gs.b, pp.b], [mergedT.b])
                        else:
                            TT("dve", gs.t[:, :], gs.t[:, :], pp.t[:, :], ALU.mult, [gs.b, pp.b], [gs.b])
                            TT("pool", mergedT.t[:, o, :], mergedT.t[:, o, :], gs.t[:, :], ALU.add, [gs.b, mergedT.b], [mergedT.b])
                add_stage(l, P_G0 + bi * 4 + q_, st_g)

        def add_final(l, tok0, c0, nch):
            def st_pre(_sl):
                for c_ in range(c0 + 5, min(nch, c0 + 9)):
                    ensure_h(l, tok0, c_)
                dbg("ybT", ybT.t[:, :, :], [ybT.b], BF16)
                dbg("mergedT", mergedT.t[:, :, :], [mergedT.b])
                CP("act", mergedTb.t[:, 0:4, :], mergedT.t[:, 0:4, :], [mergedT.b], [mergedTb.b])
                CP("dve", mergedTb.t[:, 4:8, :], mergedT.t[:, 4:8, :], [mergedT.b], [mergedTb.b])
            add_stage(l, None, st_pre)
            for half in range(2):
                def st_o(sl, half=half):
                    hs = slice(half * 512, (half + 1) * 512)
                    for j in range(4):
                        r0 = tok0 + (c0 + j) * CH
                        xt = xring2.get()
                        DMA(xt.t[:, :], src_of(l)[r0:r0 + CH, hs], src_buf(l, r0), [xt.b])
                        po = pf.get()
                        for k in range(8):
                            MM(po.t[:, :], mergedTb.t[:, k, j * CH:(j + 1) * CH], sl.t[:, k * 512:(k + 1) * 512], k == 0, k == 7,
                               [mergedTb.b, sl.b], [po.b])
                        TT("dve", xt.t[:, :], xt.t[:, :], po.t[:, :], ALU.add, [xt.b, po.b], [xt.b])
                        DMA(dst_of(l)[r0:r0 + CH, hs], xt.t[:, :], [xt.b], dst_buf2(l, r0, half))
                add_stage(l, P_WO0 + half, st_o)

        xring2 = Ring([sb("xo%d" % i, [128, 512]) for i in range(3)])
        f32r.tiles.append(sb("f32r_x", [128, 512]))
        tsf_b = TSF.b
        acsf_b = ACSF.b

        def seq_layer(l, tok0, slen):
            nch = slen // CH
            nm = slen // TM
            en_att, en_ml, en_ssd = ("att" in enable), ("mlstm" in enable), ("ssd" in enable)
            en_rec = en_ml or en_ssd

            if en_rec:
                for m in reversed(range(nm)):
                    c0 = 4 * m

                    def st_h(_sl, c0=c0):
                        for c in range(max(0, c0 - 1), min(nch, c0 + 5)):
                            ensure_h(l, tok0, c)
                    add_stage(l, None, st_h)

                    def st_gates(sl, c0=c0, m=m):
                        if m == nm - 1:
                            S.op("dve", lambda e: e.memset(mC[1].t[:], 0.0), [], [mC[1].b])
                            S.op("dve", lambda e: e.memset(sS[1].t[:], 0.0), [], [sS[1].b])
                            S.op("dve", lambda e: e.memset(mvaug.t[:, :, :, 128:129], 1.0), [], [mvaug.b])
                        gate_rows(l, 1, sl, c0, m == nm - 1, TSBm.t[:, :, :], [TSBm.b], ACSBm.t[sD, :], [ACSBm.b])
                        DMA(TSD[m].rearrange("p (j c) -> p j c", j=4), TSBm.t[:, :, :], [TSBm.b], [tsd_buf[m]])
                        DMA(ACSD[m], ACSBm.t[sD, :], [ACSBm.b], [acsd_buf[m]])
                    add_stage(l, P_KV, st_gates)

                    if en_ml:
                        def st_mk(sl, c0=c0):
                            for j in range(4):
                                pk = pf.get()
                                proj_tok(sl, 0, 512, c0 + j, pk)
                                S.op("act", (lambda j, pk: lambda e: e.mul(out=mktok.t[:, j, :], in_=pk.t[:, :], mul=128 ** -0.5))(j, pk),
                                     [pk.b], [mktok.b])
                                DMA(KTOK[c0 + j], mktok.t[:, j, :], [mktok.b], [ktok_buf[c0 + j]])
                        add_stage(l, P_MK, st_mk)

                        def st_mv(sl, c0=c0, m=m):
                            for j in range(4):
                                mlstm_v_tok(sl, c0 + j, j)
                                DMA(VTOK[c0 + j].rearrange("p (h e) -> p h e", h=4), mvaug.t[:, j, :, 0:128], [mvaug.b], [vtok_buf[c0 + j]])
                            for j in reversed(range(4)):
                                c = c0 + j
                                CP("act", mCb[1].t[:, :], mC[1].t[:, :], [mC[1].b], [mCb[1].b])
                                DMA(BSTM[c], mCb[1].t[:, :], [mCb[1].b], [bstm_buf[c]])
                                mlstm_state_update(1, j, TSBm.t[:, j, :], [TSBm.b])
                        add_stage(l, P_MV, st_mv)

                    if en_ssd and "p2ssd" not in KSKIP:
                        def st_sx(sl, c0=c0):
                            ssd_conv(l, sl, None, c0, nch, [0, 1, 2, 3])
                        add_stage(l, P_SX, st_sx)

                        def st_sbc(sl, c0=c0, m=m):
                            ssd_conv(l, None, sl, c0, nch, [4])
                            for j in reversed(range(4)):
                                c = c0 + j
                                ssd_tokmajor(j)
                                DMA(XTOK[c], sxtok.t[:, j, :], [sxtok.b], [xtok_buf[c]])
                                CP("act", sSb[1].t[:, :], sS[1].t[:, :], [sS[1].b], [sSb[1].b])
                                DMA(BSTS[c], sSb[1].t[:, :], [sSb[1].b], [bsts_buf[c]])
                                ssd_state_update(1, j, TSBm.t[:, j, :], [TSBm.b])
                        add_stage(l, P_SBC, st_sbc)

                    def st_pref(_sl, c0=c0):
                        for c_ in range(max(0, c0 - 5), c0 - 1):
                            ensure_h(l, tok0, c_)
                    add_stage(l, None, st_pref)

            for m in range(nm):
                c0 = 4 * m
                lo = max(0, c0 - 1)
                hi = min(nch, c0 + 5)

                def st_h(_sl, c0=c0, lo=lo, hi=hi, m=m):
                    if l + 1 < depth:
                        for _ in range(casts_per_mt):
                            if pending_casts[l + 1]:
                                pending_casts[l + 1].pop(0)()
                    for c in range(lo, hi):
                        ensure_h(l, tok0, c)
                    DMA(rope_t.t[:], C["c_rope"][:, :, c0 * CH:c0 * CH + TM], [], [rope_t.b])
                    if m == 0:
                        S.op("dve", lambda e: e.memset(mC[0].t[:], 0.0), [], [mC[0].b])
                        S.op("dve", lambda e: e.memset(sS[0].t[:], 0.0), [], [sS[0].b])
                        S.op("dve", lambda e: e.memset(mCb[0].t[:], 0.0), [], [mCb[0].b])
                        S.op("dve", lambda e: e.memset(sSb[0].t[:], 0.0), [], [sSb[0].b])
                        S.op("dve", lambda e: e.memset(mvaug.t[:, :, :, 128:129], 1.0), [], [mvaug.b])
                        S.op("dve", lambda e: e.memset(vaug.t[:, :, :, 64:65], 1.0), [], [vaug.b])
                add_stage(l, None, st_h)

                def qk_norm_rope(pq, gain, ntok):
                    sq = b16r.get()
                    ACT(sq.t[:, 0:ntok], pq.t[:, 0:ntok], AF.Square, [pq.b], [sq.b])
                    pss = pf.get()
                    MM(pss.t[:, 0:ntok], blkb.t[:], sq.t[:, 0:ntok], True, True, [blkb.b, sq.b], [pss.b])
                    rs = f32r.get()
                    ACT(rs.t[:, 0:ntok], pss.t[:, 0:ntok], AF.Ln, [pss.b, epsc.b], [rs.b], bias=epsc.t[:, 0:1])
                    ACT(rs.t[:, 0:ntok], rs.t[:, 0:ntok], AF.Exp, [rs.b], [rs.b], scale=-0.5)
                    qn = f32r.get(); qnb = b16r.get()
                    STT("dve", qnb.t[:, 0:ntok], pq.t[:, 0:ntok], gain.t[:, 0:1], rs.t[:, 0:ntok], ALU.mult, ALU.mult,
                        [pq.b, gain.b, rs.b], [qnb.b])
                    pr = pf.get()
                    MM(pr.t[:, 0:ntok], rotb.t[:], qnb.t[:, 0:ntok], True, True, [rotb.b, qnb.b], [pr.b])
                    t1 = f32r.get()
                    return qn, pr, t1, qnb

                if en_att:
                    def st_aq(sl, c0=c0, qk_norm_rope=qk_norm_rope):
                        for i in range(4):
                            pq = pf.get()
                            proj_feat(sl, i * 128, 128, c0, 4, pq)
                            qn, pr, t1, qnb = qk_norm_rope(pq, gq, TM)
                            TT("dve", t1.t[:, :], pr.t[:, :], rope_t.t[:, 1, :], ALU.mult, [pr.b, rope_t.b], [t1.b])
                            TT("pool", qn.t[:, :], qnb.t[:, :], rope_t.t[:, 0, :], ALU.mult, [qnb.b, rope_t.b], [qn.b])
                            TT("pool", qT.t[:, i, :], qn.t[:, :], t1.t[:, :], ALU.add, [qn.b, t1.b], [qT.b])
                            if i == 0:
                                dbg("pq", pq.t[:, :], [pq.b])
                        dbg("qT", qT.t[:, :, :], [qT.b], BF16)
                    add_stage(l, P_AQ, st_aq)

                def st_kv(sl, c0=c0, lo=lo, hi=hi, m=m, qk_norm_rope=qk_norm_rope):
                    if en_att:
                        for (ca, n) in ((lo, c0 - lo), (c0, 4), (c0 + 4, hi - c0 - 4)):
                            if n <= 0:
                                continue
                            pk = pf.get()
                            proj_feat(sl, 0, 128, ca, n, pk)
                            ntk = n * CH
                            qn, pr, t1, qnb = qk_norm_rope(pk, gk, ntk)
                            if ca == c0:
                                cosap, sinap, rbufs = rope_t.t[:, 0, :], rope_t.t[:, 1, :], [rope_t.b]
                            else:
                                rt = f32r.get(); rt2 = f32r.get()
                                DMA(rt.t[:, 0:CH], C["c_rope"][:, 0, ca * CH:(ca + 1) * CH], [], [rt.b])
                                DMA(rt2.t[:, 0:CH], C["c_rope"][:, 1, ca * CH:(ca + 1) * CH], [], [rt2.b])
                                cosap, sinap, rbufs = rt.t[:, 0:CH], rt2.t[:, 0:CH], [rt.b, rt2.b]
                            TT("dve", t1.t[:, 0:ntk], pr.t[:, 0:ntk], sinap, ALU.mult, [pr.b] + rbufs, [t1.b])
                            TT("pool", qn.t[:, 0:ntk], qnb.t[:, 0:ntk], cosap, ALU.mult, [qnb.b] + rbufs, [qn.b])
                            o = (ca - lo) * CH
                            TT("pool", kT.t[:, o:o + ntk], qn.t[:, 0:ntk], t1.t[:, 0:ntk], ALU.add, [qn.b, t1.b], [kT.b])
                        for c in range(lo, hi):
                            pv = pf.get()
                            proj_tok(sl, 128, 128, c, pv)
                            CP("act", vaug.t[:, c - lo, :, 0:64], pv.t[:, 0:128].rearrange("p (g e) -> p g e", g=2), [pv.b], [vaug.b])
                        dbg("kT", kT.t[:, :], [kT.b], BF16)
                        dbg("vaug", vaug.t[:, :, :, :], [vaug.b], BF16)
                    if en_rec:
                        gate_rows(l, 0, sl, c0, m == 0, TSF.t[:, :, :], [tsf_b], ACSF.t[sD, :], [acsf_b])
                        DMA(TSBm.t[:, :, :], TSD[m].rearrange("p (j c) -> p j c", j=4), [tsd_buf[m]], [TSBm.b])
                        DMA(ACSBm.t[sD, :], ACSD[m], [acsd_buf[m]], [ACSBm.b])
                        mk_hilo(0, ACSF)
                        mk_hilo(1, ACSBm)
                add_stage(l, P_KV, st_kv)

                if en_att:
                    def st_az(sl, c0=c0, lo=lo, hi=hi):
                        for j in range(4):
                            pz = pf.get()
                            proj_tok(sl, 0, 512, c0 + j, pz)
                            ACT(gate_tok[0].t[:, j, :], pz.t[:, :], AF.Silu, [pz.b], [gate_tok[0].b])
                        for j in range(4):
                            c = c0 + j
                            po = pacc
                            kblocks = [cc for cc in (c - 1, c, c + 1) if 0 <= cc < nch]
                            for g in range(2):
                                pr_ = slice(g * 64, (g + 1) * 64)
                                for bi, cc in enumerate(kblocks):
                                    ps_ = pf.get()
                                    ko = (cc - lo) * CH
                                    for i in range(4):
                                        MM(ps_.t[:, i * 128:(i + 1) * 128], kT.t[pr_, ko:ko + CH], qT.t[pr_, i, j * CH:(j + 1) * CH],
                                           True, True, [kT.b, qT.b], [ps_.b])
                                    pt_ = b16r.get()
                                    ACT(pt_.t[:, :], ps_.t[:, :], AF.Exp, [ps_.b], [pt_.b])
                                    if cc != c:
                                        mi_ = 1 if cc < c else 0
                                        TT("pool", pt_.t[:, :], pt_.t[:, :], trib.t[:, mi_, :], ALU.mult, [pt_.b, trib.b], [pt_.b])
                                    for i in range(4):
                                        MM(po[g].t[:, i * 65:(i + 1) * 65], pt_.t[:, i * 128:(i + 1) * 128], vaug.t[:, cc - lo, g, :],
                                           bi == 0 and i == 0, bi == len(kblocks) - 1, [pt_.b, vaug.b], [po[g].b], skip=True)
                            ya = f32r.get(); den = st4.get()
                            for g in range(2):
                                pv3 = po[g].t[:, 0:260].rearrange("p (i e) -> p i e", i=4)
                                TT("dve", den.t[:, g * 4:(g + 1) * 4], pv3[:, :, 64], esink.t[:, g * 4:(g + 1) * 4], ALU.add,
                                   [po[g].b, esink.b], [den.b])
                            S.op("dve", (lambda den: lambda e: e.reciprocal(out=den.t[:, 0:8], in_=den.t[:, 0:8]))(den), [den.b], [den.b])
                            for g in range(2):
                                pv3 = po[g].t[:, 0:260].rearrange("p (i e) -> p i e", i=4)
                                TT("dve", ya.t[:, g * 256:(g + 1) * 256].rearrange("p (i e) -> p i e", i=4), pv3[:, :, 0:64],
                                   den.t[:, g * 4:(g + 1) * 4].unsqueeze(2).to_broadcast([128, 4, 64]), ALU.mult, [po[g].b, den.b], [ya.b])
                            dbg("ya", ya.t[:, :], [ya.b])
                            yg = b16r.get()
                            TT("pool", yg.t[:, :], ya.t[:, :], gate_tok[0].t[:, j, :], ALU.mult, [ya.b, gate_tok[0].b], [yg.b])
                            dbg("yg", yg.t[:, :], [yg.b], BF16)
                            p = pb.get()
                            for i in range(4):
                                TR(p.t[:, i * 128:(i + 1) * 128], yg.t[:, i * 128:(i + 1) * 128], identb.t[:], [yg.b, identb.b], [p.b])
                            CP("act", ybT.t[:, :, j * CH:(j + 1) * CH], p.t[:, 0:512].rearrange("p (i t) -> p i t", i=4), [p.b], [ybT.b])
                    add_stage(l, P_AZ, st_az)
                    add_branch_merge(l, 0, True, c0)

                if en_ml:
                    def st_mq(sl, c0=c0):
                        for h in range(4):
                            pq = pf.get()
                            proj_feat(sl, h * 128, 128, c0, 4, pq)
                            CP("act", mqT.t[:, h, :], pq.t[:, :], [pq.b], [mqT.b])
                    add_stage(l, P_MQ, st_mq)

                    def st_mk(_sl, c0=c0):
                        for j in range(4):
                            DMA(mktok.t[:, j, :], KTOK[c0 + j], [ktok_buf[c0 + j]], [mktok.b])
                            DMA(mvaug.t[:, j, :, 0:128], VTOK[c0 + j].rearrange("p (h e) -> p h e", h=4), [vtok_buf[c0 + j]], [mvaug.b])
                        for j in range(4):
                            p = pb.get()
                            for h in range(4):
                                TR(p.t[:, h * 128:(h + 1) * 128], mktok.t[:, j, h * 128:(h + 1) * 128], identb.t[:], [mktok.b, identb.b], [p.b])
                            CP("dve", mkT.t[:, :, j * CH:(j + 1) * CH], p.t[:, 0:512].rearrange("p (h t) -> p h t", h=4), [p.b], [mkT.b])
                    add_stage(l, None, st_mk)

                    def st_mo(sl, c0=c0):
                        for j in range(4):
                            pz = pf.get()
                            proj_tok(sl, 0, 512, c0 + j, pz)
                            ACT(gate_tok[0].t[:, j, :], pz.t[:, :], AF.Sigmoid, [pz.b], [gate_tok[0].b])
                    add_stage(l, P_MO, st_mo)

                    def st_mz(sl, c0=c0, m=m):
                        for j in range(4):
                            pz = pf.get()
                            proj_tok(sl, 0, 512, c0 + j, pz)
                            ACT(gate_tok[1].t[:, j, :], pz.t[:, :], AF.Silu, [pz.b], [gate_tok[1].b])
                        for j in range(4):
                            c = c0 + j
                            tok = slice(j * CH, (j + 1) * CH)
                            DMA(mCb[1].t[:, :], BSTM[c], [bstm_buf[c]], [mCb[1].b])
                            ps = pf.get()
                            for h in range(4):
                                MM(ps.t[:, h * 128:(h + 1) * 128], mkT.t[:, h, tok], mqT.t[:, h, tok], True, True, [mkT.b, mqT.b], [ps.b])
                            hsum = f32r.get()
                            g01 = f32r.get()
                            TT("pool", g01.t[:, :], gate_tok[0].t[:, j, :], gate_tok[1].t[:, j, :], ALU.mult, [gate_tok[0].b], [g01.b])
                            TT("pool", g01.t[:, :], g01.t[:, :], mng_bc.t[:, :], ALU.mult, [g01.b, mng_bc.b], [g01.b])
                            fwd_vw = {}
                            for d in range(2):
                                ts_ap = TSF.t[:, j, :] if d == 0 else TSBm.t[:, j, :]
                                ts_rd = [tsf_b] if d == 0 else [TSBm.b]
                                scm = b16r.get()
                                TT("dve", scm.t[:, :], ps.t[:, :], trib.t[:, d, :], ALU.mult, [ps.b, trib.b], [scm.b])
                                vw = b16r.get(); wkb = b16r.get()
                                if d == 0:
                                    fwd_vw["vw"], fwd_vw["wkb"] = vw, wkb
                                TT("pool", vw.t[:, :].rearrange("p (h e) -> p h e", h=4), mvaug.t[:, j, :, 0:128],
                                   ts_ap[:, 0:4].unsqueeze(2).to_broadcast([128, 4, 128]), ALU.mult, [mvaug.b] + ts_rd, [vw.b])
                                CP("dve", wkb.t[:, 0:4], ts_ap[:, 0:4], ts_rd, [wkb.b])
                                pnum, pint = pacc[0], pacc[1]
                                pden = pf.get()
                                for h in range(4):
                                    hs = slice(h * 128, (h + 1) * 128)
                                    MM(pnum.t[:, hs], scm.t[:, hs], vw.t[:, hs], True, True, [scm.b, vw.b], [pnum.b])
                                    MM(pden.t[:, h:h + 1], scm.t[:, hs], wkb.t[:, h:h + 1], True, True, [scm.b, wkb.b], [pden.b])
                                    MM(pint.t[:, hs], mqT.t[:, h, tok], mCb[d].t[:, hs], True, True, [mqT.b, mCb[d].b], [pint.b])
                                    MM(pden.t[:, 4 + h:5 + h], mqT.t[:, h, tok], mCb[d].t[:, 512 + h:513 + h], True, True,
                                       [mqT.b, mCb[d].b], [pden.b])
                                n1 = f32r.get(); n2 = f32r.get(); dd = st4.get()
                                u_bc = ts_ap[:, 4:8].unsqueeze(2).to_broadcast([128, 4, 128])
                                wi_bc = ts_ap[:, 8:12].unsqueeze(2).to_broadcast([128, 4, 128])
                                TT("dve", n1.t[:, :].rearrange("p (h e) -> p h e", h=4), pnum.t[:, :].rearrange("p (h e) -> p h e", h=4),
                                   u_bc, ALU.mult, [pnum.b] + ts_rd, [n1.b])
                                TT("dve", n2.t[:, :].rearrange("p (h e) -> p h e", h=4), pint.t[:, :].rearrange("p (h e) -> p h e", h=4),
                                   wi_bc, ALU.mult, [pint.b] + ts_rd, [n2.b])
                                TT("pool", n1.t[:, :], n1.t[:, :], n2.t[:, :], ALU.add, [n1.b, n2.b], [n1.b])
                                TT("dve", dd.t[:, 0:4], pden.t[:, 0:4], ts_ap[:, 4:8], ALU.mult, [pden.b] + ts_rd, [dd.b])
                                TT("dve", dd.t[:, 4:8], pden.t[:, 4:8], ts_ap[:, 8:12], ALU.mult, [pden.b] + ts_rd, [dd.b])
                                TT("dve", dd.t[:, 0:4], dd.t[:, 0:4], dd.t[:, 4:8], ALU.add, [dd.b], [dd.b])
                                ACT(dd.t[:, 0:4], dd.t[:, 0:4], AF.Abs, [dd.b], [dd.b])
                                TT("dve", dd.t[:, 0:4], dd.t[:, 0:4], ts_ap[:, 12:16], ALU.max, [dd.b] + ts_rd, [dd.b])
                                S.op("dve", (lambda dd: lambda e: e.reciprocal(out=dd.t[:, 0:4], in_=dd.t[:, 0:4]))(dd), [dd.b], [dd.b])
                                r_bc = dd.t[:, 0:4].unsqueeze(2).to_broadcast([128, 4, 128])
                                if d == 0:
                                    TT("pool", hsum.t[:, :].rearrange("p (h e) -> p h e", h=4), n1.t[:, :].rearrange("p (h e) -> p h e", h=4),
                                       r_bc, ALU.mult, [n1.b, dd.b], [hsum.b])
                                else:
                                    TT("pool", n1.t[:, :].rearrange("p (h e) -> p h e", h=4), n1.t[:, :].rearrange("p (h e) -> p h e", h=4),
                                       r_bc, ALU.mult, [n1.b, dd.b], [n1.b])
                                    TT("pool", hsum.t[:, :], hsum.t[:, :], n1.t[:, :], ALU.add, [hsum.b, n1.b], [hsum.b])
                            sq = f32r.get(); ss = st4.get()
                            ACT(sq.t[:, :], hsum.t[:, :], AF.Square, [hsum.b], [sq.b])
                            S.op("dve", (lambda sq, ss: lambda e: e.reduce_sum(out=ss.t[:, 0:4], in_=sq.t[:, :].rearrange("p (h e) -> p h e", h=4),
                                                                         axis=AX.X))(sq, ss), [sq.b], [ss.b])
                            rstd_from_sumsq(ss.t[:, 4:8], ss.t[:, 0:4], 128, [ss.b], [ss.b])
                            hn = f32r.get()
                            TT("dve", hn.t[:, :].rearrange("p (h e) -> p h e", h=4), hsum.t[:, :].rearrange("p (h e) -> p h e", h=4),
                               ss.t[:, 4:8].unsqueeze(2).to_broadcast([128, 4, 128]), ALU.mult, [hsum.b, ss.b], [hn.b])
                            yg = b16r.get()
                            TT("pool", yg.t[:, :], hn.t[:, :], g01.t[:, :], ALU.mult, [hn.b, g01.b], [yg.b])
                            p = pb.get()
                            for i in range(4):
                                TR(p.t[:, i * 128:(i + 1) * 128], yg.t[:, i * 128:(i + 1) * 128], identb.t[:], [yg.b, identb.b], [p.b])
                            CP("act", ybT.t[:, :, tok], p.t[:, 0:512].rearrange("p (i t) -> p i t", i=4), [p.b], [ybT.b])
                            mlstm_state_update(0, j, TSF.t[:, j, :], [tsf_b], pre=fwd_vw)
                            CP("act", mCb[0].t[:, :], mC[0].t[:, :], [mC[0].b], [mCb[0].b])
                    add_stage(l, P_MZ, st_mz)
                    add_branch_merge(l, 1, not en_att, c0)

                if en_ssd:
                    def st_sbc(sl, c0=c0):
                        for j in range(4):
                            DMA(sxtok.t[:, j, :], XTOK[c0 + j], [xtok_buf[c0 + j]], [sxtok.b])
                        ssd_conv(l, None, sl, c0, nch, [4, 5])
                        for which in range(2):
                            for g in range(2):
                                gs_ = slice(g * 64, (g + 1) * 64)
                                CP("pool", szero.t[gs_, which * 2 + g, :], sconv.t[gs_, 4 + which, :], [sconv.b], [szero.b])
                        for j in range(4):
                            ssd_tokmajor(j, do_x=False)
                    add_stage(l, P_SBC, st_sbc)

                    def st_sz(sl, c0=c0, m=m):
                        for j in range(4):
                            pz = pf.get()
                            proj_tok(sl, 0, 512, c0 + j, pz)
                            ACT(gate_tok[0].t[:, j, :], pz.t[:, :], AF.Silu, [pz.b], [gate_tok[0].b])
                        for j in range(4):
                            if "sz" in KSKIP:
                                break
                            c = c0 + j
                            tok = slice(j * CH, (j + 1) * CH)
                            if "cDma" not in KSKIP:
                                DMA(sSb[1].t[:, :], BSTS[c], [bsts_buf[c]], [sSb[1].b])
                            pcb = pf.get()
                            for g in range(2):
                                if "cPcb" in KSKIP:
                                    break
                                gs_ = slice(g * 64, (g + 1) * 64)
                                MM(pcb.t[:, g * 128:(g + 1) * 128], szero.t[:, g, tok], sconv.t[:, 5, tok], True, True, [sconv.b, szero.b], [pcb.b])
                            py = pf.get()
                            toff = []
                            for d in range(2):
                                ts_ap = TSF.t[:, j, :] if d == 0 else TSBm.t[:, j, :]
                                ts_rd = [tsf_b] if d == 0 else [TSBm.b]
                                acs_hi, acs_lo = acs_hl[d]
                                if d == 0:
                                    cbm = f32r.get()
                                    CP("act", cbm.t[:, 0:256], pcb.t[:, 0:256], [pcb.b], [cbm.b])
                                for h in range(8):
                                    if "cSel" in KSKIP:
                                        break
                                    pe_ = pacc[h // 4]
                                    hb_ = slice((h % 4) * 128, (h % 4 + 1) * 128)
                                    MM(pe_.t[:, hb_], selb.t[sD, h, :], acs_hi.t[sD, tok], h % 4 == 0, False, [selb.b, acs_hi.b], [pe_.b], skip=True)
                                    MM(pe_.t[:, hb_], selb.t[sD, h, :], acs_lo.t[sD, tok], False, False, [selb.b, acs_lo.b], [pe_.b], skip=True)
                                    MM(pe_.t[:, hb_], identb.t[:], negm.t[:, d, :], False, True, [identb.b, negm.b], [pe_.b], skip=True)
                                mts = []
                                if "cA" in KSKIP:
                                    continue
                                for g in range(2):
                                    lt = f32r.get()
                                    for e_ in range(4):
                                        h = g * 4 + e_
                                        ACT(lt.t[:, e_ * 128:(e_ + 1) * 128], pacc[g].t[:, e_ * 128:(e_ + 1) * 128], AF.Exp, [pacc[g].b] + ts_rd, [lt.b],
                                            bias=ts_ap[:, 20 + h:21 + h])
                                    mt = b16r.get()
                                    TT("dve" if g == 0 else "pool", mt.t[:, :].rearrange("p (h e) -> p h e", h=4), lt.t[:, :].rearrange("p (h e) -> p h e", h=4),
                                       cbm.t[:, g * 128:(g + 1) * 128].unsqueeze(1).to_broadcast([128, 4, 128]), ALU.mult, [lt.b, cbm.b], [mt.b])
                                    mts.append(mt)
                                if "cC" in KSKIP:
                                    continue
                                poff = pf.get()
                                for g in range(2):
                                    gs_ = slice(g * 64, (g + 1) * 64)
                                    MM(poff.t[:, g * 256:(g + 1) * 256], szero.t[:, 2 + g, tok], sSb[d].t[:, :], True, True, [szero.b, sSb[d].b], [poff.b])
                                for h in range(8):
                                    MM(py.t[:, h * 64:(h + 1) * 64], mts[h // 4].t[:, (h % 4) * 128:(h % 4 + 1) * 128], sxtok.t[:, j, h * 64:(h + 1) * 64],
                                       d == 0 and h == 0, d == 1, [mts[h // 4].b, sxtok.b], [py.b], skip=True)
                                to = f32r.get()
                                TT("dve", to.t[:, :].rearrange("p (h e) -> p h e", h=8), poff.t[:, :].rearrange("p (h e) -> p h e", h=8),
                                   ts_ap[:, 36:44].unsqueeze(2).to_broadcast([128, 8, 64]), ALU.mult, [poff.b] + ts_rd, [to.b])
                                toff.append(to)
                            if "cA" in KSKIP or "cC" in KSKIP or "cD" in KSKIP:
                                continue
                            y = f32r.get()
                            TT("dve", y.t[:, :], py.t[:, :], toff[0].t[:, :], ALU.add, [py.b, toff[0].b], [y.b])
                            TT("pool", y.t[:, :], y.t[:, :], toff[1].t[:, :], ALU.add, [y.b, toff[1].b], [y.b])
                            xs = toff[0]
                            TT("pool", xs.t[:, :].rearrange("p (h e) -> p h e", h=8), sxtok.t[:, j, :].rearrange("p (h e) -> p h e", h=8),
                               dsk_bc.t[:, 0:8].unsqueeze(2).to_broadcast([128, 8, 64]), ALU.mult, [sxtok.b, dsk_bc.b], [xs.b])
                            TT("pool", y.t[:, :], y.t[:, :], xs.t[:, :], ALU.add, [y.b, xs.b], [y.b])
                            TT("pool", y.t[:, :], y.t[:, :], gate_tok[0].t[:, j, :], ALU.mult, [y.b, gate_tok[0].b], [y.b])
                            sq = toff[1]; ss = st4.get()
                            ACT(sq.t[:, :], y.t[:, :], AF.Square, [y.b], [sq.b, ss.b], accum=ss.t[:, 0:1])
                            rstd_from_sumsq(ss.t[:, 1:2], ss.t[:, 0:1], 512, [ss.b], [ss.b])
                            yg = b16r.get()
                            STT("dve", yg.t[:, :], y.t[:, :], ss.t[:, 1:2], sng_bc.t[:, :], ALU.mult, ALU.mult, [y.b, ss.b, sng_bc.b], [yg.b])
                            p = pb.get()
                            for i in range(4):
                                TR(p.t[:, i * 128:(i + 1) * 128], yg.t[:, i * 128:(i + 1) * 128], identb.t[:], [yg.b, identb.b], [p.b])
                            CP("act", ybT.t[:, :, tok], p.t[:, 0:512].rearrange("p (i t) -> p i t", i=4), [p.b], [ybT.b])
                            if "cE" in KSKIP:
                                continue
                            ssd_state_update(0, j, TSF.t[:, j, :], [tsf_b])
                            CP("act", sSb[0].t[:, :], sS[0].t[:, :], [sS[0].b], [sSb[0].b])
                    add_stage(l, P_SZ, st_sz)
                    add_branch_merge(l, 2, not (en_att or en_ml), c0)

                if not (en_att or en_ml or en_ssd):
                    def st_zero(_sl):
                        S.op("dve", lambda e: e.memset(mergedT.t[:], 0.0), [], [mergedT.b])
                    add_stage(l, None, st_zero)
                add_final(l, tok0, c0, nch)

        n_mt_layer = sum(sl_ // TM for sl_ in seq_lens)
        casts_per_mt = 0
        if depth > 1:
            casts_per_mt = max(len(v_) for v_ in pending_casts.values()) // max(1, n_mt_layer - 1) + 1

        def flush_casts(l):
            def f(_sl):
                while pending_casts[l]:
                    pending_casts[l].pop(0)()
            return f
        for l in range(depth):
            if l >= 1:
                add_stage(l, None, flush_casts(l))
            add_stage(l, None, (lambda l: lambda _sl: load_layer_params(l))(l))
            pos = 0
            for slen in seq_lens:
                seq_layer(l, pos, slen)
                pos += slen
        run_stages()
        import os as _os
        S.schedule(window=int(_os.environ.get('KWIN', '24')), enable=_os.environ.get('KSCHED', '1') == '1')
        S._plan()
        print('sched ok:', S.check(), {e: len(S.order[e]) for e in S.ENGS}, 'est_us', getattr(S, 'est_time', 0) / 1e3, 'busy_us', {e: round(v / 1e3) for e, v in getattr(S, 'busy', {}).items()})
        if S.tagging:
            for kk_, v_ in sorted(S.gaps.items(), key=lambda x: -x[1])[:40]:
                print('GAP', kk_, round(v_ / 1e3, 1))
        S.emit()
    return nc


_CACHE = {}


def _run(seq_lens, depth, per_core_x, weights, enable=("att", "mlstm", "ssd"), debug=(), full=False):
    key = (tuple(seq_lens), depth, tuple(enable), tuple(debug))
    if key not in _CACHE:
        _CACHE[key] = build(list(seq_lens), depth, enable, debug)
    nc = _CACHE[key]
    consts = make_consts(max(seq_lens))
    in_maps = []
    for xc in per_core_x:
        m = {"x": np.ascontiguousarray(xc, dtype=np.float32)}
        for k, v in weights.items():
            m[k] = np.ascontiguousarray(v, dtype=np.float32)
        m.update(consts)
        in_maps.append(m)
    res = run_bass_kernel_spmd(nc, in_maps, core_ids=list(range(len(per_core_x))))
    if full:
        return res.results
    return [r["y"] for r in res.results]


def kernel(x_prompt, x_sample, norm_g, w_in, q_norm_g, k_norm_g, attn_sink, w_att_out,
           mlstm_i_b, mlstm_f_b, mlstm_norm_g, w_mlstm_out, conv_w, conv_b, a_log,
           dt_bias, d_skip, ssm_norm_g, w_ssm_out, w_out):
    x_prompt = np.asarray(x_prompt, dtype=np.float32)
    x_sample = np.asarray(x_sample, dtype=np.float32)
    weights = dict(norm_g=norm_g, w_in=w_in, q_norm_g=q_norm_g, k_norm_g=k_norm_g, attn_sink=attn_sink,
                   w_att_out=w_att_out, mlstm_i_b=mlstm_i_b, mlstm_f_b=mlstm_f_b, mlstm_norm_g=mlstm_norm_g,
                   w_mlstm_out=w_mlstm_out, conv_w=conv_w, conv_b=conv_b, a_log=a_log, dt_bias=dt_bias,
                   d_skip=d_skip, ssm_norm_g=ssm_norm_g, w_ssm_out=w_ssm_out, w_out=w_out)
    weights = {k: np.asarray(v, dtype=np.float32) for k, v in weights.items()}
    depth = weights["w_in"].shape[0]
    nb, sp = x_prompt.shape[0], x_prompt.shape[1]
    ns, ss = x_sample.shape[0], x_sample.shape[1]
    ppc, spc = nb // NCORES, ns // NCORES
    seq_lens = [sp] * ppc + [ss] * spc
    per_core = []
    for c in range(NCORES):
        parts = [x_prompt[c * ppc + i] for i in range(ppc)] + [x_sample[c * spc + i] for i in range(spc)]
        per_core.append(np.concatenate(parts, axis=0))
    outs = _run(seq_lens, depth, per_core, weights)
    y_prompt = np.empty_like(x_prompt)
    y_sample = np.empty_like(x_sample)
    for c in range(NCORES):
        o = outs[c]
        pos = 0
        for i in range(ppc):
            y_prompt[c * ppc + i] = o[pos:pos + sp]; pos += sp
        for i in range(spc):
            y_sample[c * spc + i] = o[pos:pos + ss]; pos += ss
    return (y_prompt, y_sample)
```
